# Optimizing a Trainium2 kernel written in Bass

```python
import math
import jax, jax.numpy as jnp
from jax import lax
import numpy as np

D_MODEL = 1024
BATCH = 8
SEQ = 4096
DEPTH = 1

RWKV_WIDTH = D_MODEL // 2
RWKV_HEAD = 64
RWKV_HEADS = RWKV_WIDTH // RWKV_HEAD
DECAY_RANK = 64
AAA_RANK = 64
GATE_RANK = 128
RWKV_SPLITS = (RWKV_WIDTH, 2 * RWKV_WIDTH, 3 * RWKV_WIDTH,
               3 * RWKV_WIDTH + DECAY_RANK, 3 * RWKV_WIDTH + DECAY_RANK + AAA_RANK)
N_RWKV_COLS = 3 * RWKV_WIDTH + DECAY_RANK + AAA_RANK + GATE_RANK
LNX_EPS = 64e-5
S5_WIDTH = D_MODEL // 2
S5_GROUP = 16
S5_GROUPS = S5_WIDTH // S5_GROUP
S5_STATE = 64
STEP_MIN = 1e-3
STEP_MAX = 1e-1
N_IN_COLS = N_RWKV_COLS + S5_WIDTH + 2 * D_MODEL
D_FF = 256 * ((8 * D_MODEL // 3 + 255) // 256)
CONV_WIDTH = 3
NORM_EPS = 1e-6

kernel_name = 'hybrid_rwkv7_s5_gated_merge_convffn'


def rms_norm(x, g):
    xf = x.astype(jnp.float32)
    y = xf * lax.rsqrt(jnp.mean(xf * xf, axis=-1, keepdims=True) + NORM_EPS)
    return y.astype(x.dtype) * g


def token_shift(p, mu):
    prev = jnp.pad(p, ((0, 0), (1, 0), (0, 0)))[:, :-1]
    return p + (prev - p) * mu


def wkv7_scan(r, decay, k, v, a_vec, b_vec):
    bsz, _, nh, n = r.shape

    def step(S, inp):
        r_t, w_t, k_t, v_t, a_t, b_t = inp
        sa = jnp.einsum('bhij,bhj->bhi', S, a_t)
        S = (S * w_t[:, :, None, :] + sa[..., None] * b_t[:, :, None, :]
             + v_t[..., None] * k_t[:, :, None, :])
        return S, jnp.einsum('bhij,bhj->bhi', S, r_t)

    xs = tuple(jnp.moveaxis(t, 1, 0) for t in (r, decay, k, v, a_vec, b_vec))
    S0 = jnp.zeros((bsz, nh, n, n), jnp.float32)
    _, ys = lax.scan(step, S0, xs)
    return jnp.moveaxis(ys, 0, 1)


def rwkv7_branch(p, mu, w0, w2, a0, a2, g2, k_k, k_a, r_k, lnx_w, lnx_b):
    dtype = p.dtype
    f32 = jnp.float32
    bsz, L, _ = p.shape
    p = token_shift(p.astype(f32), mu.astype(f32))
    r, k, v, wd, ad, gd = jnp.split(p, RWKV_SPLITS, axis=-1)
    w = -jax.nn.softplus(-(w0 + jnp.tanh(wd) @ w2)) - 0.5
    decay = jnp.exp(-jnp.exp(w))
    a = jax.nn.sigmoid(a0 + ad @ a2)
    g = jax.nn.sigmoid(gd) @ g2
    hs = (bsz, L, RWKV_HEADS, RWKV_HEAD)
    kk = (k * k_k).reshape(hs)
    kk = kk / jnp.maximum(jnp.linalg.norm(kk, axis=-1, keepdims=True), 1e-12)
    k = k * (1.0 + (a - 1.0) * k_a)
    rh, kh, vh = r.reshape(hs), k.reshape(hs), v.reshape(hs)
    ah = a.reshape(hs)
    y = wkv7_scan(rh, decay.reshape(hs), kh, vh, -kk, kk * ah)
    mean = jnp.mean(y, axis=-1, keepdims=True)
    var = jnp.mean(jnp.square(y - mean), axis=-1, keepdims=True)
    y = ((y - mean) * lax.rsqrt(var + LNX_EPS)).reshape(bsz, L, RWKV_WIDTH) * lnx_w + lnx_b
    bonus = jnp.sum(rh * kh * r_k, axis=-1, keepdims=True) * vh
    y = (y + bonus.reshape(bsz, L, RWKV_WIDTH)) * g
    return y.astype(dtype)


def s5_combine(left, right):
    a_i, b_i = left
    a_j, b_j = right
    return a_j * a_i, a_j * b_i + b_j


def s5_branch(u, a_re, a_im, b_re, b_im, c_re, c_im, d, log_step, w_glu, b_glu):
    f32 = jnp.float32
    bsz, L, _ = u.shape
    lam = lax.complex(a_re.astype(f32), a_im.astype(f32))
    dt = jnp.exp(log_step.astype(f32))[:, None]
    a_bar = jnp.exp(lam * dt)
    b_bar = ((a_bar - 1.0) / lam)[..., None] * lax.complex(b_re.astype(f32), b_im.astype(f32))
    c = lax.complex(c_re.astype(f32), c_im.astype(f32))
    ug = u.astype(f32).reshape(bsz, L, S5_GROUPS, S5_GROUP)
    bu = jnp.einsum('gpc,blgc->blgp', b_bar, ug.astype(jnp.complex64))
    a_elems = jnp.broadcast_to(a_bar, (1, L, S5_GROUPS, S5_STATE))
    _, states = lax.associative_scan(s5_combine, (a_elems, bu), axis=1)
    y = jnp.real(jnp.einsum('gcp,blgp->blgc', c, states)) + d.astype(f32).reshape(S5_GROUPS, S5_GROUP) * ug
    y = jax.nn.gelu(y.reshape(bsz, L, S5_WIDTH)).astype(u.dtype)
    return y * jax.nn.sigmoid(y @ w_glu + b_glu)


def conv_ffn(h, w_up, conv_w, conv_b, w_down):
    L = h.shape[1]
    z = h @ w_up
    zp = jnp.pad(z, ((0, 0), (CONV_WIDTH - 1, 0), (0, 0)))
    z = conv_b + sum(conv_w[j] * zp[:, j:j + L] for j in range(CONV_WIDTH))
    gate, val = jnp.split(z, 2, axis=-1)
    return (jax.nn.gelu(gate) * val) @ w_down


def setup_inputs(seed: int = 0) -> dict:
    key = jax.random.key(seed)
    ks = jax.random.split(key, 40)
    f32 = jnp.float32
    Ld, W, G, P, C, F = DEPTH, RWKV_WIDTH, S5_GROUPS, S5_STATE, S5_GROUP, D_FF

    def nrm(k, shape, scale):
        return jax.random.normal(k, shape, f32) * scale

    def gain(k, shape):
        return 1.0 + 0.05 * jax.random.normal(k, shape, f32)

    n = jnp.arange(P, dtype=f32)
    return {
        'x': nrm(ks[0], (BATCH, SEQ, D_MODEL), 1.0),
        'norm_mix_pre': gain(ks[1], (Ld, D_MODEL)),
        'norm_mix_post': gain(ks[2], (Ld, D_MODEL)),
        'norm_ffn_pre': gain(ks[3], (Ld, D_MODEL)),
        'norm_ffn_post': gain(ks[4], (Ld, D_MODEL)),
        'w_in': nrm(ks[5], (Ld, D_MODEL, N_IN_COLS), D_MODEL ** -0.5),
        'b_gate': nrm(ks[6], (Ld, 2 * D_MODEL), 0.02),
        'rwkv_shift_mu': jax.random.uniform(ks[7], (Ld, N_RWKV_COLS), f32, 0.0, 1.0),
        'rwkv_w0': jax.random.uniform(ks[8], (Ld, W), f32, -6.0, -1.0),
        'rwkv_w2': nrm(ks[9], (Ld, DECAY_RANK, W), 0.1 * DECAY_RANK ** -0.5),
        'rwkv_a0': nrm(ks[10], (Ld, W), 0.1),
        'rwkv_a2': nrm(ks[11], (Ld, AAA_RANK, W), 0.1 * AAA_RANK ** -0.5),
        'rwkv_g2': nrm(ks[12], (Ld, GATE_RANK, W), GATE_RANK ** -0.5),
        'rwkv_k_k': 0.85 + 0.05 * jax.random.normal(ks[13], (Ld, W), f32),
        'rwkv_k_a': gain(ks[14], (Ld, W)),
        'rwkv_r_k': nrm(ks[15], (Ld, RWKV_HEADS, RWKV_HEAD), 0.3),
        'rwkv_lnx_w': gain(ks[16], (Ld, W)),
        'rwkv_lnx_b': nrm(ks[17], (Ld, W), 0.02),
        's5_a_re': -0.5 + 0.01 * jax.random.normal(ks[18], (Ld, G, P), f32),
        's5_a_im': math.pi * n + 0.01 * jax.random.normal(ks[19], (Ld, G, P), f32),
        's5_b_re': nrm(ks[20], (Ld, G, P, C), (2 * C) ** -0.5),
        's5_b_im': nrm(ks[21], (Ld, G, P, C), (2 * C) ** -0.5),
        's5_c_re': nrm(ks[22], (Ld, G, C, P), P ** -0.5),
        's5_c_im': nrm(ks[23], (Ld, G, C, P), P ** -0.5),
        's5_d': nrm(ks[24], (Ld, S5_WIDTH), 1.0),
        's5_log_step': jax.random.uniform(ks[25], (Ld, G), f32, math.log(STEP_MIN), math.log(STEP_MAX)),
        's5_w_glu': nrm(ks[26], (Ld, S5_WIDTH, S5_WIDTH), S5_WIDTH ** -0.5),
        's5_b_glu': nrm(ks[27], (Ld, S5_WIDTH), 0.02),
        'w_branch_rwkv': nrm(ks[28], (Ld, W, D_MODEL), W ** -0.5),
        'w_branch_s5': nrm(ks[29], (Ld, S5_WIDTH, D_MODEL), S5_WIDTH ** -0.5),
        'w_out': nrm(ks[30], (Ld, D_MODEL, D_MODEL), D_MODEL ** -0.5),
        'ffn_w_up': nrm(ks[31], (Ld, D_MODEL, 2 * F), D_MODEL ** -0.5),
        'ffn_conv_w': nrm(ks[32], (Ld, CONV_WIDTH, 2 * F), CONV_WIDTH ** -0.5),
        'ffn_conv_b': nrm(ks[33], (Ld, 2 * F), 0.02),
        'ffn_w_down': nrm(ks[34], (Ld, F, D_MODEL), F ** -0.5),
    }


def reference(x, norm_mix_pre, norm_mix_post, norm_ffn_pre, norm_ffn_post, w_in, b_gate,
              rwkv_shift_mu, rwkv_w0, rwkv_w2, rwkv_a0, rwkv_a2, rwkv_g2, rwkv_k_k, rwkv_k_a,
              rwkv_r_k, rwkv_lnx_w, rwkv_lnx_b, s5_a_re, s5_a_im, s5_b_re, s5_b_im, s5_c_re,
              s5_c_im, s5_d, s5_log_step, s5_w_glu, s5_b_glu, w_branch_rwkv, w_branch_s5, w_out,
              ffn_w_up, ffn_conv_w, ffn_conv_b, ffn_w_down):
    for l in range(DEPTH):
        h = rms_norm(x, norm_mix_pre[l])
        proj = h @ w_in[l]
        p_rwkv = proj[..., :N_RWKV_COLS]
        u_s5 = proj[..., N_RWKV_COLS:N_RWKV_COLS + S5_WIDTH]
        gates = jax.nn.sigmoid(proj[..., N_RWKV_COLS + S5_WIDTH:] + b_gate[l])
        g_rwkv, g_s5 = jnp.split(gates, 2, axis=-1)
        o_rwkv = rwkv7_branch(p_rwkv, rwkv_shift_mu[l], rwkv_w0[l], rwkv_w2[l], rwkv_a0[l],
                              rwkv_a2[l], rwkv_g2[l], rwkv_k_k[l], rwkv_k_a[l], rwkv_r_k[l],
                              rwkv_lnx_w[l], rwkv_lnx_b[l]) @ w_branch_rwkv[l]
        o_s5 = s5_branch(u_s5, s5_a_re[l], s5_a_im[l], s5_b_re[l], s5_b_im[l], s5_c_re[l],
                         s5_c_im[l], s5_d[l], s5_log_step[l], s5_w_glu[l], s5_b_glu[l]) @ w_branch_s5[l]
        mixed = (g_rwkv * o_rwkv + g_s5 * o_s5) @ w_out[l]
        x = x + rms_norm(mixed, norm_mix_post[l])
        h = rms_norm(x, norm_ffn_pre[l])
        f = conv_ffn(h, ffn_w_up[l], ffn_conv_w[l], ffn_conv_b[l], ffn_w_down[l])
        x = x + rms_norm(f, norm_ffn_post[l])
    return x
```

```python
import contextlib
import math
import os
import numpy as np
import concourse.bass as bass
import concourse.mybir as mybir
from concourse.bass_utils import run_bass_kernel_spmd

F32 = mybir.dt.float32
BF16 = mybir.dt.bfloat16
I32 = mybir.dt.int32
ALU = mybir.AluOpType
AF = mybir.ActivationFunctionType

L = 4096
D = 1024
NR = 1792
NS5 = 512
NIN = 4352
FF = 2816
PI = math.pi


class Reg:
    __slots__ = ("name", "w", "r", "dsem", "dcnt", "last", "excl")

    def __init__(self, name=""):
        self.name = name
        self.w = None
        self.r = []
        self.dsem = None
        self.dcnt = 0
        self.last = 0
        self.excl = False


class Sched:
    def __init__(self, nc):
        self.nc = nc
        self.eng = {"pe": nc.tensor, "dve": nc.vector, "act": nc.scalar,
                    "pool": nc.gpsimd, "sp": nc.sync}
        self.sem = {k: nc.alloc_semaphore(name="sem_" + k) for k in self.eng}
        self.cnt = {k: 0 for k in self.eng}
        self.seen = {k: {} for k in self.eng}
        self.ninst = 0
        self.nds = 0
        self.dregs = []
        self.dmap = {}
        self.maxops = int(os.environ.get("K_MAXOPS", "100000000"))
        self.rec = None

    def _wait(self, e, tok):
        sem, val = tok
        key = sem.name
        if key in self.dmap:
            val = max(val, self.dmap[key].dcnt)
        if self.seen[e].get(key, 0) >= val:
            return
        if e == "pe" and sem is self.sem["pe"]:
            return
        self.eng[e].wait_ge(sem, val)
        self.seen[e][key] = val

    def _deps(self, e, reads, writes, skip=None):
        for r in reads:
            if r.w is not None:
                self._wait(e, r.w)
        for w in writes:
            if w.w is not None and w.w[0] is not skip:
                self._wait(e, w.w)
            for t in w.r:
                self._wait(e, t)

    def _commit(self, tok, reads, writes):
        for r in reads:
            r.last = self.ninst
            r.r.append(tok)
            if len(r.r) > 16:
                d = {}
                for s, v in r.r:
                    if d.get(s.name, (None, -1))[1] < v:
                        d[s.name] = (s, v)
                r.r = list(d.values())
        for w in writes:
            w.last = self.ninst
            w.w = tok
            w.r = []

    def op(self, e, fn, reads=(), writes=(), cost=None):
        if self.rec is not None:
            self.rec.append(("op", e, fn, list(reads), list(writes), None, cost))
            return None
        if self.ninst >= self.maxops:
            return None
        reads = [x.r if isinstance(x, Buf) else x for x in reads]
        writes = [x.r if isinstance(x, Buf) else x for x in writes]
        ex = [x for x in reads if x.excl and x not in writes]
        if ex:
            reads = [x for x in reads if not x.excl]
            writes = list(writes) + ex
        self._deps(e, reads, writes)
        ins = fn(self.eng[e])
        self.cnt[e] += 1
        ins.then_inc(self.sem[e], 1)
        tok = (self.sem[e], self.cnt[e])
        self._commit(tok, reads, writes)
        self.ninst += 1
        return tok

    def dma(self, e, out, in_, reads=(), writes=(), dreg=None, **kw):
        if self.rec is not None:
            self.rec.append(("dma", e, (out, in_), list(reads), list(writes), (dreg, kw), None))
            return None
        if self.ninst >= self.maxops and not kw.pop("force", False):
            return None
        kw.pop("force", None)
        reads = [x.r if isinstance(x, Buf) else x for x in reads]
        writes = [x.r if isinstance(x, Buf) else x for x in writes]
        if dreg is None:
            dreg = writes[0] if writes else reads[0]
        elif isinstance(dreg, Buf):
            dreg = dreg.r
        if dreg.dsem is None:
            self.nds += 1
            dreg.dsem = self.nc.alloc_semaphore(name="ds%d_%s" % (self.nds, dreg.name))
            self.dregs.append(dreg)
            self.dmap[dreg.dsem.name] = dreg
        self._deps(e, reads, writes, skip=dreg.dsem)
        ins = self.eng[e].dma_start(out=out, in_=in_, **kw)
        dreg.dcnt += 16
        ins.then_inc(dreg.dsem, 16)
        tok = (dreg.dsem, dreg.dcnt)
        self._commit(tok, reads, writes)
        self.ninst += 1
        return tok

    def replay(self, item):
        kind, e, a, reads, writes, extra = item[:6]
        if kind == "op":
            return self.op(e, a, reads, writes)
        dreg, kw = extra
        return self.dma(e, a[0], a[1], reads=reads, writes=writes, dreg=dreg, **kw)

    def schedule_emit(self, items, window=1500):
        import heapq
        n = len(items)
        norm = []
        for it in items:
            kind, e, a, reads, writes, extra, cost = it
            reads = [x.r if isinstance(x, Buf) else x for x in reads]
            writes = [x.r if isinstance(x, Buf) else x for x in writes]
            dreg = None
            if kind == "dma":
                dreg = extra[0]
                if dreg is None:
                    dreg = writes[0] if writes else reads[0]
                elif isinstance(dreg, Buf):
                    dreg = dreg.r
            ex = [x for x in reads if x.excl and x not in writes]
            if ex:
                reads = [x for x in reads if not x.excl]
                writes = list(writes) + ex
            norm.append((kind, e, a, reads, writes, extra, cost, dreg))
        lw = {}
        rd = {}
        first = [[] for _ in range(n)]
        preds = [set() for _ in range(n)]
        for i, (kind, e, a, reads, writes, extra, cost, dreg) in enumerate(norm):
            for r in reads:
                k = id(r)
                if k in lw:
                    preds[i].add(lw[k])
                else:
                    first[i].append((r, "r"))
            for w in writes:
                k = id(w)
                if k in lw:
                    if not (kind == "dma" and norm[lw[k]][0] == "dma" and norm[lw[k]][7] is dreg):
                        preds[i].add(lw[k])
                else:
                    first[i].append((w, "w"))
                for j in rd.get(k, ()):
                    preds[i].add(j)
            for r in reads:
                rd.setdefault(id(r), []).append(i)
            for w in writes:
                lw[id(w)] = i
                rd[id(w)] = []
            preds[i].discard(i)
        succs = [[] for _ in range(n)]
        npred = [len(p) for p in preds]
        for i, p in enumerate(preds):
            for j in p:
                succs[j].append(i)
        def dur(it):
            kind, e, a, reads, writes, extra, cost, dreg = it
            if cost is not None:
                return cost
            return {"pe": 0.08, "dve": 0.4, "act": 0.4, "pool": 1.0, "sp": 2.0}[e] if kind == "op" else 2.5
        LAT = 0.15
        free = {e: 0.0 for e in self.eng}
        fin = [0.0] * n
        ready_t = [0.0] * n
        heaps = {e: [] for e in self.eng}
        low = 0
        done = [False] * n
        avail = [False] * n
        for i in range(n):
            if npred[i] == 0:
                heapq.heappush(heaps[norm[i][1]], (0.0, i)); avail[i] = True
        order = []
        deferred = {e: [] for e in self.eng}
        while len(order) < n:
            best = None
            for e, h in heaps.items():
                while h and h[0][1] >= low + window:
                    deferred[e].append(heapq.heappop(h))
                if not h:
                    continue
                rt, i = h[0]
                st = max(rt, free[e])
                if best is None or (st, i) < (best[0], best[1]):
                    best = (st, i, e)
            if best is None:
                for e in deferred:
                    for x in deferred[e]:
                        heapq.heappush(heaps[e], x)
                    deferred[e] = []
                window *= 2
                continue
            st, i, e = best
            heapq.heappop(heaps[e])
            order.append(i)
            done[i] = True
            f = st + dur(norm[i])
            fin[i] = f
            free[e] = f
            for j in succs[i]:
                npred[j] -= 1
                ready_t[j] = max(ready_t[j], f + LAT)
                if npred[j] == 0:
                    heapq.heappush(heaps[norm[j][1]], (ready_t[j], j)); avail[j] = True
            if i == low:
                while low < n and done[low]:
                    low += 1
                for e2 in deferred:
                    keep = []
                    for x in deferred[e2]:
                        if x[1] < low + window:
                            heapq.heappush(heaps[e2], x)
                        else:
                            keep.append(x)
                    deferred[e2] = keep
        self.sched_makespan = max(fin) if n else 0.0
        toks = [None] * n
        for i in order:
            kind, e, a, reads, writes, extra, cost, dreg = norm[i]
            for (reg, mode) in first[i]:
                if reg.w is not None and not (kind == "dma" and mode == "w" and reg.w[0] is (dreg.dsem if dreg is not None else None)):
                    self._wait(e, reg.w)
                if mode == "w":
                    for t in reg.r:
                        self._wait(e, t)
            for j in preds[i]:
                self._wait(e, toks[j])
            if kind == "op":
                ins = a(self.eng[e])
                self.cnt[e] += 1
                ins.then_inc(self.sem[e], 1)
                toks[i] = (self.sem[e], self.cnt[e])
            else:
                dg, kw = extra
                kw = dict(kw); kw.pop("force", None)
                if dreg.dsem is None:
                    self.nds += 1
                    dreg.dsem = self.nc.alloc_semaphore(name="ds%d_%s" % (self.nds, dreg.name))
                    self.dregs.append(dreg)
                    self.dmap[dreg.dsem.name] = dreg
                ins = self.eng[e].dma_start(out=a[0], in_=a[1], **kw)
                dreg.dcnt += 16
                ins.then_inc(dreg.dsem, 16)
                toks[i] = (dreg.dsem, dreg.dcnt)
            self.ninst += 1
        touched = {}
        for i, it in enumerate(norm):
            for r in it[3]:
                touched[id(r)] = r
            for w in it[4]:
                touched[id(w)] = w
        for k, reg in touched.items():
            if k in lw:
                reg.w = toks[lw[k]]
                reg.r = [toks[j] for j in rd.get(k, ())]
            else:
                reg.r = list(reg.r) + [toks[j] for j in rd.get(k, ())]
            reg.last = self.ninst

    def barrier(self):
        for e in self.eng:
            for f in self.eng:
                if f != e and self.cnt[f] > 0:
                    self._wait(e, (self.sem[f], self.cnt[f]))
            for d in self.dregs:
                if d.dcnt > 0:
                    self._wait(e, (d.dsem, d.dcnt))

    def final_wait(self, e, regs):
        for r in regs:
            r = r.r if isinstance(r, Buf) else r
            if r.w is not None:
                self._wait(e, r.w)
            for t in r.r:
                self._wait(e, t)


class Buf:
    def __init__(self, t, name):
        self.t = t
        self.a = t.ap()
        self.r = Reg(name)


class Builder:
    def __init__(self, dbg=None, ntiles=8):
        self.nc = bass.Bass("TRN2", target_bir_lowering=False)
        self.S = Sched(self.nc)
        self.dbg = dbg or {}
        self.ntiles = ntiles
        self.din = {}
        self.dout = {}
        self.nbuf = 0
        self.psb = None
        self.psi = 0
        self.ps_pool = None
        self.pclock = 0
        self.scopes = []

    def push(self):
        self.scopes.append(contextlib.ExitStack())

    def pop(self):
        self.S.barrier()
        self.scopes.pop().close()

    def inp(self, name, shape):
        b = Buf(self.nc.dram_tensor(name, list(shape), F32, kind="ExternalInput"), name)
        self.din[name] = b
        return b

    def outp(self, name, shape, dt=F32):
        b = Buf(self.nc.dram_tensor(name, list(shape), dt, kind="ExternalOutput"), name)
        self.dout[name] = b
        return b

    def sb(self, name, shape, dt=F32):
        self.nbuf += 1
        nm = "%s_%d" % (name, self.nbuf)
        if self.scopes:
            return Buf(self.scopes[-1].enter_context(self.nc.sbuf_tensor(nm, list(shape), dt)), name)
        return Buf(self.nc.alloc_sbuf_tensor(nm, list(shape), dt), name)

    def view(self, buf, ap):
        v = Buf.__new__(Buf)
        v.t = buf.t
        v.a = ap
        v.r = buf.r
        return v

    def init_psum(self):
        self.psb = []
        for i in range(8):
            t = self.nc.alloc_psum_tensor("psb%d" % i, [128, 512], F32)
            self.psb.append(Buf(t, "psb%d" % i))
            self.psb[-1].r.excl = True

    def ps(self):
        if self.ps_pool is not None:
            b = self.psb[self.ps_pool[self.psi % len(self.ps_pool)]]
            self.psi += 1
            return b
        b = min(self.psb, key=lambda t: t.r.last)
        self.pclock = max(self.pclock, self.S.ninst) + 1
        b.r.last = self.pclock
        return b

    def op(self, e, fn, reads=(), writes=(), cost=None):
        return self.S.op(e, fn, reads, writes, cost=cost)

    @staticmethod
    def fsz(ap):
        n = 1
        for d in list(ap.shape)[1:]:
            n *= int(d)
        return n

    def mm(self, out, lhsT, rhs, start, stop, reads, writes, **kw):
        return self.op("pe", lambda e: e.matmul(out, lhsT=lhsT, rhs=rhs, start=start, stop=stop, **kw), reads, writes,
                       cost=0.03 + max(self.fsz(rhs), 64) * 0.00052)

    def tr(self, out, in_, ident, reads, writes):
        return self.op("pe", lambda e: e.transpose(out, in_, ident), reads, writes, cost=0.1)

    def act(self, eng_out, in_, func, reads, writes, **kw):
        return self.op("act", lambda e: e.activation(out=eng_out, in_=in_, func=func, **kw), reads, writes,
                       cost=0.22 + self.fsz(in_) * 0.00075)

    def tt(self, e, out, in0, in1, op, reads, writes):
        c = (0.1 + self.fsz(in0) * 0.00115) if e == "dve" else (0.6 + self.fsz(in0) * 0.0013)
        return self.op(e, lambda g: g.tensor_tensor(out=out, in0=in0, in1=in1, op=op), reads, writes, cost=c)

    def ts(self, e, out, in0, s1, op0, reads, writes, s2=None, op1=None):
        c = (0.1 + self.fsz(in0) * 0.0008) if e == "dve" else (0.6 + self.fsz(in0) * 0.0013)
        if op1 is None:
            return self.op(e, lambda g: g.tensor_scalar(out=out, in0=in0, scalar1=s1, scalar2=None, op0=op0), reads, writes, cost=c)
        return self.op(e, lambda g: g.tensor_scalar(out=out, in0=in0, scalar1=s1, scalar2=s2, op0=op0, op1=op1), reads, writes, cost=c)

    def stt(self, out, in0, scalar, in1, op0, op1, reads, writes):
        return self.op("dve", lambda g: g.scalar_tensor_tensor(out=out, in0=in0, scalar=scalar, in1=in1, op0=op0, op1=op1), reads, writes,
                       cost=0.1 + self.fsz(in0) * 0.00115)

    def cp(self, e, out, in_, reads, writes):
        if e == "act":
            return self.op("act", lambda g: g.activation(out=out, in_=in_, func=AF.Copy), reads, writes, cost=0.22 + self.fsz(in_) * 0.00075)
        c = (0.1 + self.fsz(in_) * 0.0008) if e == "dve" else (0.6 + self.fsz(in_) * 0.0013)
        return self.op(e, lambda g: g.tensor_copy(out, in_), reads, writes, cost=c)

    def memset(self, e, out, val, writes):
        return self.op(e, lambda g: g.memset(out, val), (), writes)

    def cmul(self, e, o_re, o_im, a_re, a_im, b_re, b_im, t1, t2, reads, writes, tmp):
        R = list(reads)
        self.tt(e, t1, a_re, b_re, ALU.mult, R, [tmp])
        self.tt(e, t2, a_im, b_im, ALU.mult, R, [tmp])
        self.tt(e, o_re, t1, t2, ALU.subtract, [tmp], writes)
        self.tt(e, t1, a_re, b_im, ALU.mult, R, [tmp])
        self.tt(e, t2, a_im, b_re, ALU.mult, R, [tmp])
        self.tt(e, o_im, t1, t2, ALU.add, [tmp], writes)


def build(dbg=None, ntiles=8, phases=("1a", "1b", "1c", "2")):
    B = Builder(dbg, ntiles)
    nc, S = B.nc, B.S
    dbg = B.dbg

    x = B.inp("x", [L, D])
    shapes = {
        "norm_mix_pre": [D], "norm_mix_post": [D], "norm_ffn_pre": [D], "norm_ffn_post": [D],
        "w_in": [D, NIN], "b_gate": [2048], "rwkv_shift_mu": [NR], "rwkv_w0": [512],
        "rwkv_w2": [64, 512], "rwkv_a0": [512], "rwkv_a2": [64, 512], "rwkv_g2": [128, 512],
        "rwkv_k_k": [512], "rwkv_k_a": [512], "rwkv_r_k": [512], "rwkv_lnx_w": [512],
        "rwkv_lnx_b": [512], "s5_a_re": [32, 64], "s5_a_im": [32, 64], "s5_b_re": [32, 64, 16],
        "s5_b_im": [32, 64, 16], "s5_c_re": [32, 16, 64], "s5_c_im": [32, 16, 64], "s5_d": [512],
        "s5_log_step": [32], "s5_w_glu": [512, 512], "s5_b_glu": [512], "w_branch_rwkv": [512, D],
        "w_branch_s5": [512, D], "w_out": [D, D], "ffn_w_up": [D, 2 * FF], "ffn_conv_w": [3, 2 * FF],
        "ffn_conv_b": [2 * FF], "ffn_w_down": [FF, D],
    }
    W = {k: B.inp(k, v) for k, v in shapes.items()}
    out = B.outp("out", [L, D])
    dbo = {k: B.outp("dbg_" + k, shp, dt) for k, (shp, dt) in dbg.items()}

    B.init_psum()

    ident_f = B.sb("ident_f", [128, 128], F32)
    ident_b = B.sb("ident_b", [128, 128], BF16)
    B.memset("pool", ident_f.a, 1.0, [ident_f])
    B.op("pool", lambda e: e.affine_select(out=ident_f.a, in_=ident_f.a, pattern=[[-1, 128]],
                                           compare_op=ALU.is_equal, fill=0.0, base=0, channel_multiplier=1),
         [ident_f], [ident_f])
    B.cp("pool", ident_b.a, ident_f.a, [ident_f], [ident_b])

    def bc_load(name, n, q="sp"):
        t = B.sb(name + "_bc", [128, n], F32)
        S.dma(q, t.a, W[name].a.partition_broadcast(128), reads=[W[name]], writes=[t])
        return t

    def col_load(name, nt, q="sp"):
        t = B.sb(name + "_col", [128, nt], F32)
        S.dma(q, t.a, W[name].a.rearrange("(t p) -> p t", p=128), reads=[W[name]], writes=[t],
              allow_slow_non_contiguous=True)
        return t

    def frontend(src, ti, ntok_tiles, gbc, xt, hb, hT, scr, st):
        nt = ntok_tiles
        T = 128 * nt
        S.dma("sp", xt.a, src.a[ti * T:(ti + 1) * T, :].rearrange("(s p) d -> p s d", p=128),
              reads=[src], writes=[xt])
        for s in range(nt):
            B.act(scr.a, xt.a[:, s, :], AF.Square, [xt], [scr, st], accum_out=st.a[:, s:s + 1])
        B.ts("dve", st.a[:, nt:2 * nt], st.a[:, 0:nt], 1.0 / D, ALU.mult, [st], [st], s2=1e-6, op1=ALU.add)
        B.act(st.a[:, nt:2 * nt], st.a[:, nt:2 * nt], AF.Sqrt, [st], [st])
        B.op("dve", lambda e: e.reciprocal(st.a[:, 2 * nt:3 * nt], st.a[:, nt:2 * nt]), [st], [st])
        for s in range(nt):
            B.stt(hb.a[:, s, :], xt.a[:, s, :], st.a[:, 2 * nt + s:2 * nt + s + 1], gbc.a, ALU.mult, ALU.mult,
                  [xt, st, gbc], [hb])
        for c in range(8):
            pb = B.ps()
            pv = pb.a.bitcast(BF16)
            for s in range(nt):
                B.tr(pv[:, s * 128:(s + 1) * 128], hb.a[:, s, c * 128:(c + 1) * 128], ident_b.a,
                     [hb, ident_b], [pb])
            B.cp("act" if c % 2 == 0 else "dve", hT.a[:, c, :], pv[:, 0:T], [pb], [hT])

    T1 = 512
    YGLU = Buf(nc.dram_tensor("yglu_d", [128, 4, L], BF16, kind="Internal"), "yglu_d")

    B.push()
    gpre = bc_load("norm_mix_pre", D)
    x1s = Buf(nc.dram_tensor("x1s", [L, D], F32, kind="Internal"), "x1s")
    YR = Buf(nc.dram_tensor("yr_d", [128, 4, L], BF16, kind="Internal"), "yr_d")
    if "1a" in phases:
        B.push()
        ws5 = B.sb("ws5", [128, 8, 512], BF16)
        S.dma("pool", ws5.a, W["w_in"].a.rearrange("(c p) n -> p c n", p=128)[:, :, NR:NR + 512],
              reads=[W["w_in"]], writes=[ws5])
        wglu = B.sb("wglu", [128, 4, 512], BF16)
        S.dma("pool", wglu.a, W["s5_w_glu"].a.rearrange("(c p) n -> p c n", p=128), reads=[W["s5_w_glu"]], writes=[wglu])
        bglu = col_load("s5_b_glu", 4)
        dcol = col_load("s5_d", 4)

        MS = B.sb("ms", [128, 16, 2, 2])
        ER = B.sb("ER", [128, 16, 64]); EI = B.sb("EI", [128, 16, 64]); R8 = B.sb("R8", [128, 16])
        Wt = B.sb("Wt", [128, 4, 8, 2, 128], BF16)
        CAW = B.sb("CAW", [128, 16, 2, 9, 64], BF16)
        mk = B.sb("mk", [128, 2])
        B.memset("pool", mk.a[:, 0:1], 0.0, [mk]); B.memset("pool", mk.a[0:32, 0:1], 1.0, [mk]); B.memset("pool", mk.a[64:96, 0:1], 1.0, [mk])
        mk4 = B.sb("mk4", [128, 4])
        B.memset("pool", mk4.a, 0.0, [mk4])
        B.memset("pool", mk4.a[0:32, 0:1], 1.0, [mk4]); B.memset("pool", mk4.a[32:64, 1:2], 1.0, [mk4])
        B.memset("pool", mk4.a[64:96, 2:3], 1.0, [mk4]); B.memset("pool", mk4.a[64:128, 3:4], 1.0, [mk4]); B.memset("pool", mk4.a[64:96, 3:4], 0.0, [mk4])
        B.memset("pool", mk.a[:, 1:2], 1.0, [mk]); B.memset("pool", mk.a[0:32, 1:2], 0.0, [mk]); B.memset("pool", mk.a[64:96, 1:2], 0.0, [mk])
        Kbd = B.sb("Kbd", [128, 4, 8, 128], BF16)
        B.push()
        are = B.sb("are", [128, 16]); aim = B.sb("aim", [128, 16]); ls = B.sb("ls", [128, 16])
        for gl in range(2):
            S.dma("sp", are.a[gl * 64:(gl + 1) * 64, :], W["s5_a_re"].a.rearrange("(q gl) n -> gl n q", gl=2)[gl],
                  reads=[W["s5_a_re"]], writes=[are], allow_slow_non_contiguous=True)
            S.dma("sp", aim.a[gl * 64:(gl + 1) * 64, :], W["s5_a_im"].a.rearrange("(q gl) n -> gl n q", gl=2)[gl],
                  reads=[W["s5_a_im"]], writes=[aim], allow_slow_non_contiguous=True)
            S.dma("sp", ls.a[gl * 64:(gl + 1) * 64, :],
                  W["s5_log_step"].a.rearrange("(q gl) -> gl q", gl=2)[gl].partition_broadcast(64),
                  reads=[W["s5_log_step"]], writes=[ls], allow_slow_non_contiguous=True)
        braw = [B.sb("braw%d" % i, [128, 16, 16]) for i in range(2)]
        for i, nm in enumerate(("s5_b_re", "s5_b_im")):
            S.dma("sp", braw[i].a, W[nm].a.rearrange("(q gl) n c -> (gl n) q c", gl=2), reads=[W[nm]], writes=[braw[i]])
        craw = [B.sb("craw%d" % i, [128, 16, 16]) for i in range(2)]
        ctmp = B.sb("ctmp", [128, 128])
        for i, nm in enumerate(("s5_c_re", "s5_c_im")):
            for blk in range(2):
                src = W[nm].a.rearrange("(b qq gl) c n -> b qq c gl n", b=2, gl=2)[blk]
                for qq in range(8):
                    S.dma("sp", ctmp.a[qq * 16:(qq + 1) * 16, :].rearrange("p (gl n) -> p gl n", gl=2),
                          src[qq], reads=[W[nm]], writes=[ctmp])
                pb = B.ps()
                B.tr(pb.a[:, 0:128], ctmp.a, ident_f.a, [ctmp, ident_f], [pb])
                B.cp("dve", craw[i].a[:, blk * 8:(blk + 1) * 8, :],
                     pb.a[:, 0:128].rearrange("p (qq c) -> p qq c", c=16), [pb], [craw[i]])

        tm = B.sb("s5tmp", [128, 12, 32])
        tmr = tm.r

        def row(i, n=16):
            return tm.a[:, i, 0:n]

        dt_ = row(0)
        B.act(dt_, ls.a, AF.Exp, [ls], [tm])
        xr = row(1)
        B.tt("dve", xr, are.a, dt_, ALU.mult, [are, tm], [tm])
        rho = row(2)
        B.ts("dve", rho, xr, 1.0 / 720, ALU.mult, [tm], [tm], s2=1.0 / 120, op1=ALU.add)
        for cf in (1.0 / 24, 1.0 / 6, 0.5, 1.0, 1.0):
            B.tt("dve", rho, rho, xr, ALU.mult, [tm], [tm])
            B.ts("dve", rho, rho, cf, ALU.add, [tm], [tm])
        th2 = tm.a[:, 3, :]
        B.tt("dve", th2[:, 0:16], aim.a, dt_, ALU.mult, [aim, tm], [tm])
        B.ts("dve", th2[:, 16:32], th2[:, 0:16], PI / 2, ALU.add, [tm], [tm])
        kf = tm.a[:, 4, :]
        B.ts("dve", kf, th2, 1.0 / (2 * PI), ALU.mult, [tm], [tm])
        ki = B.sb("ki", [128, 32], I32)
        B.cp("dve", ki.a, kf, [tm], [ki])
        B.cp("dve", kf, ki.a, [ki], [tm])
        r1 = tm.a[:, 5, :]
        B.stt(r1, kf, -2 * PI, th2, ALU.mult, ALU.add, [tm], [tm])
        B.ts("dve", kf, r1, PI, ALU.is_gt, [tm], [tm], s2=-2 * PI, op1=ALU.mult)
        B.tt("dve", r1, r1, kf, ALU.add, [tm], [tm])
        B.ts("dve", kf, r1, -PI, ALU.is_lt, [tm], [tm], s2=2 * PI, op1=ALU.mult)
        B.tt("dve", r1, r1, kf, ALU.add, [tm], [tm])
        sc = tm.a[:, 6, :]
        B.act(sc, r1, AF.Sin, [tm], [tm])
        n2 = row(7)
        B.tt("dve", kf, sc, sc, ALU.mult, [tm], [tm])
        B.tt("dve", n2, kf[:, 0:16], kf[:, 16:32], ALU.add, [tm], [tm])
        B.ts("dve", n2, n2, -0.5, ALU.mult, [tm], [tm], s2=1.5, op1=ALU.add)
        B.tt("dve", n2, n2, rho, ALU.mult, [tm], [tm])
        PW = B.sb("pw", [128, 9, 2, 16])
        B.memset("dve", PW.a[:, 0, 0, :], 1.0, [PW])
        B.memset("dve", PW.a[:, 0, 1, :], 0.0, [PW])
        B.tt("dve", PW.a[:, 1, 0, :], sc[:, 16:32], n2, ALU.mult, [tm], [PW])
        B.tt("dve", PW.a[:, 1, 1, :], sc[:, 0:16], n2, ALU.mult, [tm], [PW])
        pt = B.sb("ptmp", [128, 2, 4, 16])
        for (lo, n, s) in ((2, 1, 1), (3, 2, 2), (5, 4, 4)):
            bre = PW.a[:, s:s + 1, 0, :].broadcast_to([128, n, 16])
            bim = PW.a[:, s:s + 1, 1, :].broadcast_to([128, n, 16])
            B.cmul("dve", PW.a[:, lo:lo + n, 0, :], PW.a[:, lo:lo + n, 1, :],
                   PW.a[:, lo - s:lo - s + n, 0, :], PW.a[:, lo - s:lo - s + n, 1, :], bre, bim,
                   pt.a[:, 0, 0:n, :], pt.a[:, 1, 0:n, :], [PW], [PW], pt)
        B.cp("dve", MS.a[:, :, 0, 0], PW.a[:, 8, 0, :], [PW], [MS])
        B.cp("dve", MS.a[:, :, 1, 1], PW.a[:, 8, 0, :], [PW], [MS])
        B.cp("dve", MS.a[:, :, 1, 0], PW.a[:, 8, 1, :], [PW], [MS])
        B.ts("dve", MS.a[:, :, 0, 1], PW.a[:, 8, 1, :], -1.0, ALU.mult, [PW], [MS])
        B.tt("dve", R8.a, rho, rho, ALU.mult, [tm], [R8])
        B.tt("dve", R8.a, R8.a, R8.a, ALU.mult, [R8], [R8])
        B.tt("dve", R8.a, R8.a, R8.a, ALU.mult, [R8], [R8])
        r8i = row(8)
        B.op("dve", lambda e: e.reciprocal(r8i, R8.a), [R8], [tm])
        B.tt("dve", ER.a[:, :, 0], PW.a[:, 8, 0, :], r8i, ALU.mult, [PW, tm], [ER])
        B.tt("dve", EI.a[:, :, 0], PW.a[:, 8, 1, :], r8i, ALU.mult, [PW, tm], [EI])
        et = B.sb("etmp", [128, 2, 16, 32])
        n_ = 1
        while n_ < 64:
            bre = ER.a[:, :, n_ - 1:n_].broadcast_to([128, 16, n_]); bim = EI.a[:, :, n_ - 1:n_].broadcast_to([128, 16, n_])
            B.cmul("dve", ER.a[:, :, n_:2 * n_], EI.a[:, :, n_:2 * n_], ER.a[:, :, 0:n_], EI.a[:, :, 0:n_], bre, bim,
                   et.a[:, 0, :, 0:n_], et.a[:, 1, :, 0:n_], [ER, EI], [ER, EI], et)
            n_ *= 2
        am1 = row(8); nre = row(9); nim = row(10); den = row(11); t0 = row(4); t1 = row(5)
        B.ts("dve", am1, PW.a[:, 1, 0, :], -1.0, ALU.add, [PW], [tm])
        B.tt("dve", nre, am1, are.a, ALU.mult, [tm, are], [tm])
        B.tt("dve", t0, PW.a[:, 1, 1, :], aim.a, ALU.mult, [PW, aim], [tm])
        B.tt("dve", nre, nre, t0, ALU.add, [tm], [tm])
        B.tt("dve", nim, PW.a[:, 1, 1, :], are.a, ALU.mult, [PW, are], [tm])
        B.tt("dve", t0, am1, aim.a, ALU.mult, [tm, aim], [tm])
        B.tt("dve", nim, nim, t0, ALU.subtract, [tm], [tm])
        B.tt("dve", den, are.a, are.a, ALU.mult, [are], [tm])
        B.tt("dve", t0, aim.a, aim.a, ALU.mult, [aim], [tm])
        B.tt("dve", den, den, t0, ALU.add, [tm], [tm])
        B.op("dve", lambda e: e.reciprocal(t1, den), [tm], [tm])
        B.tt("dve", nre, nre, t1, ALU.mult, [tm], [tm])
        B.tt("dve", nim, nim, t1, ALU.mult, [tm], [tm])
        bb = [B.sb("bb%d" % i, [128, 16, 16]) for i in range(2)]
        btmp = B.sb("btmp", [128, 2, 16, 16])
        cre = nre[:, :, None].broadcast_to([128, 16, 16]); cim = nim[:, :, None].broadcast_to([128, 16, 16])
        B.cmul("dve", bb[0].a, bb[1].a, cre, cim, braw[0].a, braw[1].a, btmp.a[:, 0], btmp.a[:, 1],
               [tm, braw[0], braw[1]], [bb[0], bb[1]], btmp)
        X = [B.sb("X%d" % i, [128, 16, 32]) for i in range(2)]
        Xb = [B.sb("Xb%d" % i, [128, 16, 64], BF16) for i in range(2)]
        for i in range(2):
            B.memset("pool", X[i].a, 0.0, [X[i]])
            for gl in range(2):
                B.cp("pool", X[i].a[gl * 64:(gl + 1) * 64, :, gl * 16:(gl + 1) * 16], bb[i].a[gl * 64:(gl + 1) * 64], [bb[i]], [X[i]])
            B.memset("pool", Xb[i].a, 0.0, [Xb[i]])
            for kk in range(2):
                B.cp("pool", Xb[i].a[:, kk::2, 32 * kk:32 * kk + 32], X[i].a[:, kk::2, :], [X[i]], [Xb[i]])
        B.push()
        WX = [B.sb("WX%d" % i, [128, 8, 16, 32]) for i in range(2)]
        wtmp = B.sb("wtmp", [128, 2, 8, 16, 32])
        pre = PW.a[:, 0:8, 0, :][:, :, :, None].broadcast_to([128, 8, 16, 32])
        pim = PW.a[:, 0:8, 1, :][:, :, :, None].broadcast_to([128, 8, 16, 32])
        xre = X[0].a[:, None, :, :].broadcast_to([128, 8, 16, 32])
        xim = X[1].a[:, None, :, :].broadcast_to([128, 8, 16, 32])
        B.cmul("dve", WX[0].a, WX[1].a, pre, pim, xre, xim, wtmp.a[:, 0], wtmp.a[:, 1], [PW, X[0], X[1]], [WX[0], WX[1]], wtmp)
        for tile in range(4):
            for e_ in range(8):
                pb = B.ps()
                for ri in range(2):
                    B.tr(pb.a[:, ri * 128:(ri + 1) * 128],
                         WX[ri].a[:, e_, 4 * tile:4 * tile + 4, :].rearrange("p k c -> p (k c)"), ident_f.a,
                         [WX[ri], ident_f], [pb])
                B.cp("act" if e_ % 2 else "dve", Wt.a[:, tile, e_, :, :],
                     pb.a[:, 0:256].rearrange("p (r n) -> p r n", r=2), [pb], [Wt])
        B.pop()
        B.push()
        CA = [B.sb("CA%d" % i, [128, 9, 16, 16]) for i in range(2)]
        catmp = B.sb("catmp", [128, 2, 9, 16, 16])
        pre9 = PW.a[:, :, 0, :][:, :, :, None].broadcast_to([128, 9, 16, 16])
        pim9 = PW.a[:, :, 1, :][:, :, :, None].broadcast_to([128, 9, 16, 16])
        cre9 = craw[0].a[:, None, :, :].broadcast_to([128, 9, 16, 16])
        cim9 = craw[1].a[:, None, :, :].broadcast_to([128, 9, 16, 16])
        B.cmul("dve", CA[0].a, CA[1].a, pre9, pim9, cre9, cim9, catmp.a[:, 0], catmp.a[:, 1],
               [PW, craw[0], craw[1]], [CA[0], CA[1]], catmp)
        B.memset("pool", CAW.a, 0.0, [CAW])
        for gl in range(2):
            hs = slice(gl * 64, (gl + 1) * 64)
            for tau in range(9):
                for kk in range(2):
                    o = 32 * kk + 16 * gl
                    B.cp("pool", CAW.a[hs, kk::2, 0, tau, o:o + 16], CA[0].a[hs, tau, kk::2], [CA[0]], [CAW])
                    B.ts("pool", CAW.a[hs, kk::2, 1, tau, o:o + 16], CA[1].a[hs, tau, kk::2], -1.0, ALU.mult, [CA[1]], [CAW])
        B.memset("pool", Kbd.a, 0.0, [Kbd])
        k0 = B.sb("k0", [128, 128])
        for tile in range(4):
            pb = B.ps()
            for h in range(2):
                for tau in range(8):
                    n = 0
                    for kk in range(2):
                        q = 4 * tile + 2 * h + kk
                        o = 32 * kk
                        for ri in range(2):
                            B.mm(pb.a[64 * h:64 * h + 64, tau * 32:(tau + 1) * 32], Xb[ri].a[:, q, :],
                                 CAW.a[:, q, ri, tau, o:o + 32], n == 0, n == 3, [Xb[ri], CAW], [pb])
                            n += 1
            for h in range(2):
                hs = slice(64 * h, 64 * h + 64)
                for kk in range(2):
                    cs = slice(64 * h + 32 * kk, 64 * h + 32 * kk + 32)
                    B.ts("dve", Kbd.a[hs, tile, 1:8, cs], pb.a[hs, 32:256].rearrange("p (t c) -> p t c", c=32),
                         mk.a[hs, kk:kk + 1], ALU.mult, [pb, mk], [Kbd])
            B.memset("dve", k0.a, 0.0, [k0])
            for h in range(2):
                hs = slice(64 * h, 64 * h + 64)
                for kk in range(2):
                    cs = slice(64 * h + 32 * kk, 64 * h + 32 * kk + 32)
                    B.ts("dve", k0.a[hs, cs], pb.a[hs, 0:32], mk.a[hs, kk:kk + 1], ALU.mult, [pb, mk], [k0])
            B.stt(k0.a, ident_f.a, dcol.a[:, tile:tile + 1], k0.a, ALU.mult, ALU.add, [ident_f, dcol, k0], [k0])
            B.cp("dve", Kbd.a[:, tile, 0, :], k0.a, [k0], [Kbd])
        B.pop()
        B.pop()
        for nm_, b_ in (("Wt", Wt), ("CAW", CAW), ("Kbd", Kbd), ("MS", MS)):
            if nm_ in dbo:
                S.dma("sp", dbo[nm_].a, b_.a, reads=[b_], writes=[dbo[nm_]], dreg=b_)
        xt = B.sb("xt", [128, 4, D]); hT = B.sb("hT", [128, 8, T1], BF16)
        st = B.sb("st", [128, 12])
        u = B.sb("u", [128, 4, 8, T1 // 8], BF16)
        um = B.sb("um", [128, 4, 4, 8, T1 // 8], BF16)
        hb = B.view(um, um.a.rearrange("p a b c d -> p (a b c d)")[:, 0:4 * D].rearrange("p (s d) -> p s d", s=4))
        NM = T1 // 8
        Dm = B.sb("Dm", [128, 16, 2, NM])
        St = B.sb("St", [128, 16, 2, NM + 1])
        Sb = B.sb("Sb", [128, 16, 2, NM], BF16)
        rt1 = B.sb("rt1", [128, 16, NM]); rt2 = B.sb("rt2", [128, 16, NM]); Dr = B.sb("Dr", [128, 16, 2, NM])
        Qs = B.sb("Qs", [128, 16, 2, NM]); Sfin = B.sb("Sfin", [128, 16, 2]); B.memset("pool", Sfin.a, 0.0, [Sfin])
        y2 = B.sb("y2", [128, 4, T1]); yg = B.sb("yg", [128, 4, T1])
        yy = B.view(Dm, Dm.a.rearrange("p a b c -> p (a b c)").rearrange("p (t n) -> p t n", t=4))
        scr = B.view(y2, y2.a.rearrange("p a b -> p (a b)")[:, 0:D])
        ygb = B.sb("ygb", [128, 4, T1], BF16); sg = y2
        ygl = B.sb("ygl", [128, 4, T1], BF16)
        B.memset("pool", St.a[:, :, :, 0], 0.0, [St])
        for ti in range(ntiles):
            frontend(x, ti, 4, gpre, xt, hb, hT, scr, st)
            for cb in range(4):
                pb = B.ps()
                for c in range(8):
                    B.mm(pb.a, ws5.a[:, c, cb * 128:(cb + 1) * 128], hT.a[:, c, :], c == 0, c == 7, [ws5, hT], [pb])
                pperm = pb.a.rearrange("p (m t) -> p t m", t=8)
                B.cp("act", u.a[:, cb, :, :], pperm, [pb], [u])
                for k in range(4):
                    B.act(um.a[:, k, cb, :, :], pperm, AF.Copy, [pb, mk4], [um], scale=mk4.a[:, k:k + 1])
            for qb in range(4):
                pb = B.ps()
                for qq in range(4):
                    q = 4 * qb + qq
                    tile, k = q // 4, q % 4
                    ks = slice(32 * k, 32 * k + 32)
                    for ri in range(2):
                        col = (qq * 2 + ri) * NM
                        for j0 in range(8):
                            B.mm(pb.a[:, col:col + NM], Wt.a[:, tile, 7 - j0, ri, :], um.a[:, k, tile, j0, :],
                                 j0 == 0, j0 == 7, [Wt, um], [pb])
                B.cp("dve", Dm.a[:, 4 * qb:4 * qb + 4, :, :],
                     pb.a.rearrange("p (q r m) -> p q r m", q=4, r=2), [pb], [Dm])
            erb = ER.a[:, :, :]; eib = EI.a[:, :, :]
            B.tt("dve", rt1.a, erb, Dm.a[:, :, 0, :], ALU.mult, [ER, Dm], [rt1])
            B.tt("dve", rt2.a, eib, Dm.a[:, :, 1, :], ALU.mult, [EI, Dm], [rt2])
            B.tt("dve", Dr.a[:, :, 0, :], rt1.a, rt2.a, ALU.add, [rt1, rt2], [Dr])
            B.tt("dve", rt1.a, erb, Dm.a[:, :, 1, :], ALU.mult, [ER, Dm], [rt1])
            B.tt("dve", rt2.a, eib, Dm.a[:, :, 0, :], ALU.mult, [EI, Dm], [rt2])
            B.tt("dve", Dr.a[:, :, 1, :], rt1.a, rt2.a, ALU.subtract, [rt1, rt2], [Dr])
            for q in range(16):
                for ri in range(2):
                    B.op("dve", lambda e, q=q, ri=ri: e.tensor_tensor_scan(
                        out=Qs.a[:, q, ri, :], data0=R8.a[:, q:q + 1].broadcast_to([128, NM]), data1=Dr.a[:, q, ri, :],
                        initial=Sfin.a[:, q, ri:ri + 1], op0=ALU.mult, op1=ALU.add), [R8, Dr, Sfin], [Qs])
            B.tt("dve", rt1.a, erb, Qs.a[:, :, 0, :], ALU.mult, [ER, Qs], [rt1])
            B.tt("dve", rt2.a, eib, Qs.a[:, :, 1, :], ALU.mult, [EI, Qs], [rt2])
            B.tt("dve", St.a[:, :, 0, 1:NM + 1], rt1.a, rt2.a, ALU.subtract, [rt1, rt2], [St])
            B.tt("dve", rt1.a, erb, Qs.a[:, :, 1, :], ALU.mult, [ER, Qs], [rt1])
            B.tt("dve", rt2.a, eib, Qs.a[:, :, 0, :], ALU.mult, [EI, Qs], [rt2])
            B.tt("dve", St.a[:, :, 1, 1:NM + 1], rt1.a, rt2.a, ALU.add, [rt1, rt2], [St])
            B.cp("act", Sb.a, St.a[:, :, :, 0:NM], [St], [Sb])
            B.cp("act", Sfin.a, St.a[:, :, :, NM], [St], [Sfin])
            B.cp("pool", St.a[:, :, :, 0], St.a[:, :, :, NM], [St], [St])
            for tile in range(4):
                pb = B.ps()
                for h in range(2):
                    hs = slice(64 * h, 64 * h + 64)
                    for t0_ in range(8):
                        n = 0
                        for kk in range(2):
                            q = 4 * tile + 2 * h + kk
                            for ri in range(2):
                                B.mm(pb.a[hs, t0_ * NM:(t0_ + 1) * NM], CAW.a[:, q, ri, t0_ + 1, :], Sb.a[:, q, ri, :], n == 0, n == 3,
                                     [CAW, Sb], [pb], skip_group_check=True)
                                n += 1
                B.cp("act", y2.a[:, tile, :], pb.a, [pb], [y2])
                pb = B.ps()
                for t0o in range(8):
                    for tau in range(t0o + 1):
                        B.mm(pb.a[:, t0o * NM:(t0o + 1) * NM], Kbd.a[:, tile, tau, :], u.a[:, tile, t0o - tau, :],
                             tau == 0, tau == t0o, [Kbd, u], [pb])
                B.tt("dve", yy.a[:, tile, :].rearrange("p (m t) -> p t m", t=8), pb.a.rearrange("p (t m) -> p t m", t=8),
                     y2.a[:, tile, :].rearrange("p (t m) -> p t m", t=8), ALU.add, [pb, y2], [yy])
            if "s5y" in dbo:
                S.dma("sp", dbo["s5y"].a[:, :, ti * T1:(ti + 1) * T1], yy.a, reads=[yy], writes=[dbo["s5y"]], dreg=yy)
            for tile in range(4):
                B.act(yg.a[:, tile, :], yy.a[:, tile, :], AF.Gelu_apprx_tanh, [yy], [yg])
            B.cp("act", ygb.a, yg.a, [yg], [ygb])
            for cb in range(4):
                pb = B.ps()
                for c in range(4):
                    B.mm(pb.a, wglu.a[:, c, cb * 128:(cb + 1) * 128], ygb.a[:, c, :], c == 0, c == 3, [wglu, ygb], [pb])
                B.act(sg.a[:, cb, :], pb.a, AF.Sigmoid, [pb, bglu], [sg], bias=bglu.a[:, cb:cb + 1])
                B.tt("dve", ygl.a[:, cb, :], yg.a[:, cb, :], sg.a[:, cb, :], ALU.mult, [yg, sg], [ygl])
            S.dma("sp", YGLU.a[:, :, ti * T1:(ti + 1) * T1], ygl.a, reads=[ygl], writes=[YGLU], dreg=ygl)
            if "yglu" in dbo:
                S.dma("sp", dbo["yglu"].a[:, :, ti * T1:(ti + 1) * T1], ygl.a, reads=[ygl], writes=[dbo["yglu"]], dreg=ygl)
        B.pop()

    if "1b" in phases:
        phase_1b(B, W, x, YR, gpre, ident_f, ident_b, frontend, bc_load, col_load, dbo, ntiles * (T1 // 128))

    if "1c" in phases:
        phase_1c(B, W, x, x1s, YR, YGLU, gpre, frontend, bc_load, col_load, dbo, ntiles)

    B.pop()
    if "2" in phases:
        phase_2(B, W, x1s, out, frontend, bc_load, dbo, ntiles)

    S.final_wait("sp", list(B.dout.values()))
    B.Wshapes = shapes
    return B


def phase_1b(B, W, x, YR, gpre, ident_f, ident_b, frontend, bc_load, col_load, dbo, ntiles):
    nc, S = B.nc, B.S
    TB = 128
    C = 128
    NW = 3
    C0 = math.exp(-0.5)
    B.push()
    wrg = B.sb("wrg", [128, 8, NR], BF16)
    win = W["w_in"].a.rearrange("(c p) n -> p c n", p=128)
    for c in range(8):
        S.dma("pool", wrg.a[:, c, :], win[:, c, 0:NR], reads=[W["w_in"]], writes=[wrg])
    w2p = B.sb("w2p", [128, 512], BF16); a2p = B.sb("a2p", [128, 512], BF16); g2b = B.sb("g2b", [128, 512], BF16)
    B.memset("pool", w2p.a, 0.0, [w2p]); B.memset("pool", a2p.a, 0.0, [a2p])
    S.dma("pool", w2p.a[0:64, :], W["rwkv_w2"].a, reads=[W["rwkv_w2"]], writes=[w2p])
    S.dma("pool", a2p.a[64:128, :], W["rwkv_a2"].a, reads=[W["rwkv_a2"]], writes=[a2p])
    S.dma("pool", g2b.a, W["rwkv_g2"].a, reads=[W["rwkv_g2"]], writes=[g2b])
    mu = col_load("rwkv_shift_mu", 14); w0c = col_load("rwkv_w0", 4); a0c = col_load("rwkv_a0", 4)
    kkc = col_load("rwkv_k_k", 4); kac = col_load("rwkv_k_a", 4); rkc = col_load("rwkv_r_k", 4)
    lwc = col_load("rwkv_lnx_w", 4); lbc = col_load("rwkv_lnx_b", 4)
    omu = B.sb("omu", [128, 14]); oka = B.sb("oka", [128, 4])
    B.ts("dve", omu.a, mu.a, -1.0, ALU.mult, [mu], [omu], s2=1.0, op1=ALU.add)
    B.ts("dve", oka.a, kac.a, -1.0, ALU.mult, [kac], [oka], s2=1.0, op1=ALU.add)
    bones = B.sb("bones", [128, 128]); bavg = B.sb("bavg", [128, 128]); ones = B.sb("ones", [128, 128])
    B.memset("pool", ones.a, 1.0, [ones])
    B.memset("pool", bones.a, 0.0, [bones])
    B.memset("pool", bones.a[0:64, 0:64], 1.0, [bones]); B.memset("pool", bones.a[64:128, 64:128], 1.0, [bones])
    B.ts("pool", bavg.a, bones.a, 1.0 / 64, ALU.mult, [bones], [bavg])
    hm = B.sb("hm", [128, 2])
    B.memset("pool", hm.a, 0.0, [hm]); B.memset("pool", hm.a[0:64, 0:1], 1.0, [hm]); B.memset("pool", hm.a[64:128, 1:2], 1.0, [hm])
    mf = B.sb("mf", [128, 128])
    MU4 = B.sb("MU4", [128, 4, 128], BF16); MI4 = B.sb("MI4", [128, 4, 128], BF16)
    ML4 = B.sb("ML4", [128, 4, 128], BF16); I4 = B.sb("I4", [128, 4, 128], BF16)
    for (mt, pat, cm, cop) in ((MU4, 1, -1, ALU.is_gt), (MI4, 1, -1, ALU.is_ge), (ML4, -1, 1, ALU.is_gt)):
        B.memset("pool", mf.a, 1.0, [mf])
        B.op("pool", lambda e, pat=pat, cm=cm, cop=cop: e.affine_select(out=mf.a, in_=mf.a, pattern=[[pat, 128]], compare_op=cop,
                                                                      fill=0.0, base=0, channel_multiplier=cm), [mf], [mf])
        for h in range(4):
            B.cp("pool", mt.a[:, h, :], mf.a, [mf], [mt])
    for h in range(4):
        B.cp("pool", I4.a[:, h, :], ident_f.a, [ident_f], [I4])
    pc = B.sb("pc", [128, 14]); B.memset("pool", pc.a, 0.0, [pc])
    H32 = B.sb("H32", [128, 4, 64]); Hb = B.sb("Hb", [128, 4, 64], BF16); Ht = B.sb("Ht", [128, 4, 64])
    B.memset("pool", H32.a, 0.0, [H32]); B.memset("pool", Hb.a, 0.0, [Hb])

    def bc4(col):
        return col.a[:, :, None].broadcast_to([128, 4, TB])

    class BS:
        pass

    sets = []
    for w in range(NW):
        b = BS()
        f4 = lambda nm: B.sb(nm + str(w), [128, 4, TB])
        h4 = lambda nm: B.sb(nm + str(w), [128, 4, TB], BF16)
        b.xt = B.sb("xt%d" % w, [128, 1, D]); b.hb = B.sb("hb%d" % w, [128, 1, D], BF16); b.hT = B.sb("hT%d" % w, [128, 8, TB], BF16)
        b.st = B.sb("st%d" % w, [128, 6])
        b.PS = B.sb("PS%d" % w, [128, 14, TB]); b.t1 = B.sb("t1%d" % w, [128, 4, TB]); b.t2 = B.sb("t2%d" % w, [128, 4, TB])
        b.scr = B.view(b.PS, b.PS.a.rearrange("p a b -> p (a b)")[:, 0:D])
        b.twa = B.sb("twa%d" % w, [128, TB], BF16); b.sgd = B.sb("sgd%d" % w, [128, TB], BF16)
        b.sgw = f4("sgw"); b.asg = f4("asg"); b.gg = f4("gg"); b.kk = f4("kk"); b.tq = f4("tq"); b.kmod = f4("kmod")
        b.cum = f4("cum"); b.eg = b.t1; b.egx = f4("egx"); b.eng = b.t2; b.bon = f4("bon")
        b.am = B.sb("am%d" % w, [128, 4, 2, TB], BF16); b.rm = B.sb("rm%d" % w, [128, 4, 2, TB], BF16)
        b.bf = h4("bf"); b.kf = h4("kf"); b.vb = h4("vb")
        b.Btok = B.sb("Btok%d" % w, [128, 512], BF16); b.Ktok = B.sb("Ktok%d" % w, [128, 512], BF16); b.Vtok = B.sb("Vtok%d" % w, [128, 512], BF16)
        for nm, src in (("Pm", b.sgw), ("Qm", b.asg), ("Rm", b.kk), ("Nak", b.kmod), ("Nrb", b.egx), ("Nrk", b.eng)):
            setattr(b, nm, B.view(src, src.a.rearrange("p a b -> p (a b)").bitcast(BF16).rearrange("p (h t) -> p h t", h=8)))
        hTf = b.hT.a.rearrange("p a b -> p (a b)")
        b.Xb = B.view(b.hT, hTf[:, 0:512]); b.Ub = B.view(b.hT, hTf[:, 512:1024])
        b.Yf = B.view(b.xt, b.xt.a.rearrange("p a b -> p (a b)")[:, 0:4 * TB].rearrange("p (j t) -> p j t", j=4))
        b.dd = B.view(b.hb, b.hb.a.rearrange("p a b -> p (a b)").bitcast(F32).rearrange("p (j t) -> p j t", j=4))
        b.yrb = h4("yrb")
        sets.append(b)

    def chunk(ci):
        b = sets[ci % NW]
        PS, tq, cum, kk, kmod, eg, egx, eng, asg, sgw, gg, bon = b.PS, b.tq, b.cum, b.kk, b.kmod, b.eg, b.egx, b.eng, b.asg, b.sgw, b.gg, b.bon
        am, rm, Pm, Qm, Rm, Nak, Nrb, Nrk = b.am, b.rm, b.Pm, b.Qm, b.Rm, b.Nak, b.Nrb, b.Nrk
        frontend(x, ci, 1, gpre, b.xt, b.hb, b.hT, b.scr, b.st)
        yield
        for j0 in range(0, 14, 4):
            nj = min(4, 14 - j0)
            pb = B.ps()
            for jj in range(nj):
                for c in range(8):
                    B.mm(pb.a[:, jj * TB:(jj + 1) * TB], wrg.a[:, c, (j0 + jj) * 128:(j0 + jj + 1) * 128], b.hT.a[:, c, :],
                         c == 0, c == 7, [wrg, b.hT], [pb])
            pv = pb.a[:, 0:nj * TB].rearrange("p (j t) -> p j t", j=nj)
            om_b = omu.a[:, j0:j0 + nj, None].broadcast_to([128, nj, TB])
            mu_b = mu.a[:, j0:j0 + nj, None].broadcast_to([128, nj, TB - 1])
            B.tt("dve", b.t1.a[:, 0:nj, :], pv, om_b, ALU.mult, [pb, omu], [b.t1])
            B.tt("dve", b.t2.a[:, 0:nj, 1:TB], pv[:, :, 0:TB - 1], mu_b, ALU.mult, [pb, mu], [b.t2])
            B.tt("dve", b.t2.a[:, 0:nj, 0:1], pc.a[:, j0:j0 + nj, None], mu.a[:, j0:j0 + nj, None], ALU.mult, [pc, mu], [b.t2])
            B.cp("act", pc.a[:, j0:j0 + nj, None], pv[:, :, TB - 1:TB], [pb], [pc])
            B.tt("pool", PS.a[:, j0:j0 + nj, :], b.t1.a[:, 0:nj, :], b.t2.a[:, 0:nj, :], ALU.add, [b.t1, b.t2], [PS])
            yield
        if "pshift" in dbo:
            S.dma("sp", dbo["pshift"].a[:, :, ci * TB:(ci + 1) * TB], PS.a, reads=[PS], writes=[dbo["pshift"]], dreg=PS)
        r_ = PS.a[:, 0:4, :]; k_ = PS.a[:, 4:8, :]; v_ = PS.a[:, 8:12, :]
        B.act(b.twa.a[0:64, :], PS.a[0:64, 12, :], AF.Tanh, [PS], [b.twa])
        B.cp("act", b.twa.a[64:128, :], PS.a[64:128, 12, :], [PS], [b.twa])
        B.act(b.sgd.a, PS.a[:, 13, :], AF.Sigmoid, [PS], [b.sgd])
        pw_ = B.ps()
        for j in range(4):
            B.mm(pw_.a[:, j * TB:(j + 1) * TB], w2p.a[:, j * 128:(j + 1) * 128], b.twa.a, True, True, [w2p, b.twa], [pw_])
        for j in range(4):
            B.act(sgw.a[:, j, :], pw_.a[:, j * TB:(j + 1) * TB], AF.Sigmoid, [pw_, w0c], [sgw], bias=w0c.a[:, j:j + 1])
        pa_ = B.ps()
        for j in range(4):
            B.mm(pa_.a[:, j * TB:(j + 1) * TB], a2p.a[:, j * 128:(j + 1) * 128], b.twa.a, True, True, [a2p, b.twa], [pa_])
        for j in range(4):
            B.act(asg.a[:, j, :], pa_.a[:, j * TB:(j + 1) * TB], AF.Sigmoid, [pa_, a0c], [asg], bias=a0c.a[:, j:j + 1])
        pg_ = B.ps()
        for j in range(4):
            B.mm(pg_.a[:, j * TB:(j + 1) * TB], g2b.a[:, j * 128:(j + 1) * 128], b.sgd.a, True, True, [g2b, b.sgd], [pg_])
        B.cp("act", gg.a, pg_.a.rearrange("p (j t) -> p j t", j=4), [pg_], [gg])
        yield
        B.tt("dve", kk.a, k_, bc4(kkc), ALU.mult, [PS, kkc], [kk])
        B.tt("pool", tq.a, kk.a, kk.a, ALU.mult, [kk], [tq])
        pb = B.ps()
        for j in range(4):
            B.mm(pb.a[:, j * TB:(j + 1) * TB], bones.a, tq.a[:, j, :], True, True, [bones, tq], [pb])
        B.act(cum.a, pb.a.rearrange("p (j t) -> p j t", j=4), AF.Sqrt, [pb], [cum])
        B.ts("dve", cum.a, cum.a, 1e-12, ALU.max, [cum], [cum])
        B.op("dve", lambda e: e.reciprocal(cum.a, cum.a), [cum], [cum])
        B.tt("pool", kk.a, kk.a, cum.a, ALU.mult, [kk, cum], [kk])
        yield
        B.tt("dve", tq.a, asg.a, bc4(kac), ALU.mult, [asg, kac], [tq])
        B.tt("dve", tq.a, tq.a, bc4(oka), ALU.add, [tq, oka], [tq])
        B.tt("dve", kmod.a, k_, tq.a, ALU.mult, [PS, tq], [kmod])
        B.tt("pool", tq.a, r_, kmod.a, ALU.mult, [PS, kmod], [tq])
        B.tt("dve", tq.a, tq.a, bc4(rkc), ALU.mult, [tq, rkc], [tq])
        pbon = B.ps()
        for j in range(4):
            B.mm(pbon.a[:, j * TB:(j + 1) * TB], bones.a, tq.a[:, j, :], True, True, [bones, tq], [pbon])
        B.tt("dve", bon.a, pbon.a.rearrange("p (j t) -> p j t", j=4), v_, ALU.mult, [pbon, PS], [bon])
        yield
        for j in range(4):
            B.op("dve", lambda e, j=j: e.tensor_tensor_scan(out=cum.a[:, j, :], data0=ones.a, data1=sgw.a[:, j, :], initial=0.0,
                                                            op0=ALU.mult, op1=ALU.add), [ones, sgw], [cum])
        B.act(eg.a, cum.a, AF.Exp, [cum], [eg], scale=-C0)
        B.act(eng.a, cum.a, AF.Exp, [cum], [eng], scale=C0)
        B.tt("pool", tq.a, cum.a, sgw.a, ALU.subtract, [cum, sgw], [tq])
        B.act(egx.a, tq.a, AF.Exp, [tq], [egx], scale=-C0)
        yield
        B.stt(tq.a, kk.a, -1.0, egx.a, ALU.mult, ALU.mult, [kk, egx], [tq])
        for hh in range(2):
            B.act(am.a[:, :, hh, :], tq.a, AF.Copy, [tq, hm], [am], scale=hm.a[:, hh:hh + 1])
        B.tt("dve", egx.a, r_, eg.a, ALU.mult, [PS, eg], [egx])
        for hh in range(2):
            B.act(rm.a[:, :, hh, :], egx.a, AF.Copy, [egx, hm], [rm], scale=hm.a[:, hh:hh + 1])
        B.tt("pool", tq.a, kk.a, asg.a, ALU.mult, [kk, asg], [tq])
        B.tt("dve", b.bf.a, tq.a, eng.a, ALU.mult, [tq, eng], [b.bf])
        B.tt("dve", b.kf.a, kmod.a, eng.a, ALU.mult, [kmod, eng], [b.kf])
        B.cp("act", b.vb.a, v_, [PS], [b.vb])
        yield
        cs = slice(0, C)
        for n_, (src, dst) in enumerate(((b.bf, b.Btok), (b.kf, b.Ktok), (b.vb, b.Vtok))):
            pb = B.ps()
            pv = pb.a.bitcast(BF16)
            for j in range(4):
                B.tr(pv[:, j * 128:(j + 1) * 128], src.a[:, j, cs], ident_b.a, [src, ident_b], [pb])
            B.cp("act" if n_ != 1 else "dve", dst.a, pv[:, 0:512], [pb], [dst])
        yield
        for g in range(2):
            gs = slice(4 * g, 4 * g + 4)
            for kind in range(5):
                pbk = B.ps()
                for hq in range(4):
                    h = 4 * g + hq
                    j, hh = h // 2, h % 2
                    o = slice(hq * 128, (hq + 1) * 128)
                    bfs, kfs, ams, rms = b.bf.a[:, j, cs], b.kf.a[:, j, cs], am.a[:, j, hh, cs], rm.a[:, j, hh, cs]
                    lhsT, rhs, rd = ((bfs, ams, [b.bf, am]), (ams, bfs, [b.bf, am]), (kfs, ams, [b.kf, am]),
                                     (bfs, rms, [b.bf, rm]), (kfs, rms, [b.kf, rm]))[kind]
                    B.mm(pbk.a[:, o], lhsT, rhs, True, True, rd, [pbk])
                msk, dst = ((MU4, Pm), (ML4, Qm), (MU4, Nak), (MI4, Nrb), (MI4, Nrk))[kind]
                B.tt("dve", dst.a[:, gs, :], pbk.a.rearrange("p (h t) -> p h t", h=4), msk.a, ALU.mult, [pbk, msk], [dst])
            B.tt("pool", Rm.a[:, gs, :], Pm.a[:, gs, :], I4.a, ALU.add, [Pm, I4], [Rm])
            yield
        for lvl in range(1, 7):
            for g in range(2):
                gs = slice(4 * g, 4 * g + 4)
                pq = B.ps()
                pp = B.ps() if lvl < 6 else None
                for hq in range(4):
                    h = 4 * g + hq
                    o = slice(hq * 128, (hq + 1) * 128)
                    B.mm(pq.a[:, o], Pm.a[:, h, :], Qm.a[:, h, :], True, True, [Pm, Qm], [pq])
                    if pp is not None:
                        B.mm(pp.a[:, o], Qm.a[:, h, :], Pm.a[:, h, :], True, True, [Pm, Qm], [pp])
                B.cp("act", Qm.a[:, gs, :], pq.a.rearrange("p (h t) -> p h t", h=4), [pq], [Qm])
                if pp is not None:
                    B.cp("act", Pm.a[:, gs, :], pp.a.rearrange("p (h t) -> p h t", h=4), [pp], [Pm])
                pr = B.ps()
                for hq in range(4):
                    h = 4 * g + hq
                    o = slice(hq * 128, (hq + 1) * 128)
                    B.mm(pr.a[:, o], Qm.a[:, h, :], Rm.a[:, h, :], True, True, [Qm, Rm], [pr])
                B.tt("dve", Rm.a[:, gs, :], pr.a.rearrange("p (h t) -> p h t", h=4), Rm.a[:, gs, :], ALU.add, [pr, Rm], [Rm])
                yield
        px = B.ps()
        for h in range(8):
            j, hh = h // 2, h % 2
            o = slice(h * 64, (h + 1) * 64)
            B.mm(px.a[:, o], am.a[:, j, hh, cs], Hb.a[:, j, :], True, False, [am, Hb], [px])
            B.mm(px.a[:, o], Nak.a[:, h, :], b.Vtok.a[:, o], False, True, [Nak, b.Vtok], [px])
        B.cp("act", b.Xb.a, px.a, [px], [b.Xb])
        pu = B.ps()
        for h in range(8):
            o = slice(h * 64, (h + 1) * 64)
            B.mm(pu.a[:, o], Rm.a[:, h, :], b.Xb.a[:, o], True, True, [Rm, b.Xb], [pu])
        B.cp("act", b.Ub.a, pu.a, [pu], [b.Ub])
        ph = B.ps()
        for h in range(8):
            j, hh = h // 2, h % 2
            o = slice(h * 64, (h + 1) * 64)
            ho = ph.a[hh * 64:(hh + 1) * 64, j * 64:(j + 1) * 64]
            B.mm(ho, b.Btok.a[:, o], b.Ub.a[:, o], True, False, [b.Btok, b.Ub], [ph])
            B.mm(ho, b.Ktok.a[:, o], b.Vtok.a[:, o], False, True, [b.Ktok, b.Vtok], [ph])
        py = B.ps()
        for h in range(8):
            j, hh = h // 2, h % 2
            o = slice(h * 64, (h + 1) * 64)
            yo = py.a[hh * 64:(hh + 1) * 64, j * 128:(j + 1) * 128]
            B.mm(yo, Hb.a[:, j, :], rm.a[:, j, hh, cs], True, False, [Hb, rm], [py])
            B.mm(yo, b.Ub.a[:, o], Nrb.a[:, h, :], False, False, [b.Ub, Nrb], [py])
            B.mm(yo, b.Vtok.a[:, o], Nrk.a[:, h, :], False, True, [b.Vtok, Nrk], [py])
        B.tt("dve", Ht.a, ph.a[:, 0:256].rearrange("p (j i) -> p j i", j=4), H32.a, ALU.add, [ph, H32], [Ht])
        gC = eg.a[:, :, C - 1:C].broadcast_to([128, 4, 64])
        B.tt("dve", H32.a, Ht.a, gC, ALU.mult, [Ht, eg], [H32])
        B.cp("act", Hb.a, H32.a, [H32], [Hb])
        B.cp("act", b.Yf.a, py.a.rearrange("p (j t) -> p j t", j=4), [py], [b.Yf])
        yield
        if "wkv" in dbo:
            S.dma("sp", dbo["wkv"].a[:, :, ci * TB:(ci + 1) * TB], b.Yf.a, reads=[b.Yf], writes=[dbo["wkv"]], dreg=b.Yf)
        Yf, dd = b.Yf, b.dd
        pm_ = B.ps()
        for j in range(4):
            B.mm(pm_.a[:, j * TB:(j + 1) * TB], bavg.a, Yf.a[:, j, :], True, True, [bavg, Yf], [pm_])
        B.tt("dve", dd.a, Yf.a, pm_.a.rearrange("p (j t) -> p j t", j=4), ALU.subtract, [Yf, pm_], [dd])
        B.act(tq.a, dd.a, AF.Square, [dd], [tq])
        pv_ = B.ps()
        for j in range(4):
            B.mm(pv_.a[:, j * TB:(j + 1) * TB], bavg.a, tq.a[:, j, :], True, True, [bavg, tq], [pv_])
        B.ts("dve", cum.a, pv_.a.rearrange("p (j t) -> p j t", j=4), 64e-5, ALU.add, [pv_], [cum])
        B.act(cum.a, cum.a, AF.Sqrt, [cum], [cum])
        B.op("dve", lambda e: e.reciprocal(cum.a, cum.a), [cum], [cum])
        yield
        B.tt("pool", dd.a, dd.a, cum.a, ALU.mult, [dd, cum], [dd])
        B.tt("dve", dd.a, dd.a, bc4(lwc), ALU.mult, [dd, lwc], [dd])
        B.tt("dve", dd.a, dd.a, bc4(lbc), ALU.add, [dd, lbc], [dd])
        B.tt("pool", dd.a, dd.a, bon.a, ALU.add, [dd, bon], [dd])
        B.tt("dve", b.yrb.a, dd.a, gg.a, ALU.mult, [dd, gg], [b.yrb])
        if "rwkv_y" in dbo:
            S.dma("sp", dbo["rwkv_y"].a[:, :, ci * TB:(ci + 1) * TB], b.yrb.a, reads=[b.yrb], writes=[dbo["rwkv_y"]], dreg=b.yrb)
        S.dma("sp", YR.a[:, :, ci * TB:(ci + 1) * TB], b.yrb.a, reads=[b.yrb], writes=[YR], dreg=b.yrb)
        yield

    pools = [[0, 1, 2], [3, 4, 5], [6, 7]] if NW == 3 else ([[0, 1, 2, 3], [4, 5, 6, 7]] if NW == 2 else [list(range(8))])
    S.rec = []
    for ci in range(ntiles):
        B.ps_pool = pools[ci % NW]
        for _ in chunk(ci):
            pass
    items = S.rec
    S.rec = None
    B.ps_pool = None
    S.schedule_emit(items, window=int(os.environ.get("K_WIN", "1400")))
    B.pop()


def phase_1c(B, W, x, x1s, YR, YGLU, gpre, frontend, bc_load, col_load, dbo, ntiles):
    nc, S = B.nc, B.S
    TB = 512
    NS = TB // 128
    B.push()
    wg = B.sb("wg", [128, 8, 2048], BF16)
    win = W["w_in"].a.rearrange("(c p) n -> p c n", p=128)
    for c in range(8):
        S.dma("pool", wg.a[:, c, :], win[:, c, NR + 512:NIN], reads=[W["w_in"]], writes=[wg])
    wbr = B.sb("wbr", [128, 4, D], BF16); wbs = B.sb("wbs", [128, 4, D], BF16); wout = B.sb("wout", [128, 8, D], BF16)
    S.dma("pool", wbr.a, W["w_branch_rwkv"].a.rearrange("(c p) n -> p c n", p=128), reads=[W["w_branch_rwkv"]], writes=[wbr])
    S.dma("pool", wbs.a, W["w_branch_s5"].a.rearrange("(c p) n -> p c n", p=128), reads=[W["w_branch_s5"]], writes=[wbs])
    for c in range(8):
        S.dma("pool", wout.a[:, c, :], W["w_out"].a[c * 128:(c + 1) * 128, :], reads=[W["w_out"]], writes=[wout])
    gpost = bc_load("norm_mix_post", D)
    bgc = col_load("b_gate", 16)
    xt = B.sb("xt", [128, NS, D]); hb = B.sb("hb", [128, NS, D], BF16); hT = B.sb("hT", [128, 8, TB], BF16)
    scr = B.sb("scr", [128, D]); st = B.sb("st", [128, 12])
    yrt = B.sb("yrt", [128, 4, TB], BF16); ygt = B.sb("ygt", [128, 4, TB], BF16)
    gA = B.sb("gA", [128, TB]); gB = B.sb("gB", [128, TB]); mt1 = B.sb("mt1", [128, TB]); mt2 = B.sb("mt2", [128, TB])
    mixb = B.sb("mixb", [128, 8, TB], BF16)
    for ti in range(ntiles):
        frontend(x, ti, NS, gpre, xt, hb, hT, scr, st)
        S.dma("sp", yrt.a, YR.a[:, :, ti * TB:(ti + 1) * TB], reads=[YR], writes=[yrt])
        S.dma("sp", ygt.a, YGLU.a[:, :, ti * TB:(ti + 1) * TB], reads=[YGLU], writes=[ygt])
        for cb in range(8):
            pa = B.ps(); pbb = B.ps(); po = B.ps(); ps_ = B.ps()
            for c in range(8):
                B.mm(pa.a, wg.a[:, c, cb * 128:(cb + 1) * 128], hT.a[:, c, :], c == 0, c == 7, [wg, hT], [pa])
            for c in range(8):
                B.mm(pbb.a, wg.a[:, c, (8 + cb) * 128:(9 + cb) * 128], hT.a[:, c, :], c == 0, c == 7, [wg, hT], [pbb])
            for j in range(4):
                B.mm(po.a, wbr.a[:, j, cb * 128:(cb + 1) * 128], yrt.a[:, j, :], j == 0, j == 3, [wbr, yrt], [po])
            for j in range(4):
                B.mm(ps_.a, wbs.a[:, j, cb * 128:(cb + 1) * 128], ygt.a[:, j, :], j == 0, j == 3, [wbs, ygt], [ps_])
            B.act(gA.a, pa.a, AF.Sigmoid, [pa, bgc], [gA], bias=bgc.a[:, cb:cb + 1])
            B.act(gB.a, pbb.a, AF.Sigmoid, [pbb, bgc], [gB], bias=bgc.a[:, 8 + cb:9 + cb])
            B.tt("dve", mt1.a, po.a, gA.a, ALU.mult, [po, gA], [mt1])
            B.tt("dve", mt2.a, ps_.a, gB.a, ALU.mult, [ps_, gB], [mt2])
            B.tt("pool", mixb.a[:, cb, :], mt1.a, mt2.a, ALU.add, [mt1, mt2], [mixb])
        for s_ in range(NS):
            pbs = [B.ps(), B.ps()]
            for half in range(2):
                for c8 in range(8):
                    B.mm(pbs[half].a, mixb.a[:, c8, s_ * 128:(s_ + 1) * 128], wout.a[:, c8, half * 512:(half + 1) * 512],
                         c8 == 0, c8 == 7, [mixb, wout], [pbs[half]])
            for half in range(2):
                B.act(scr.a[:, 0:512], pbs[half].a, AF.Square, [pbs[half]], [scr, st], accum_out=st.a[:, half:half + 1])
            B.tt("dve", st.a[:, 2:3], st.a[:, 0:1], st.a[:, 1:2], ALU.add, [st], [st])
            B.ts("dve", st.a[:, 2:3], st.a[:, 2:3], 1.0 / D, ALU.mult, [st], [st], s2=1e-6, op1=ALU.add)
            B.act(st.a[:, 2:3], st.a[:, 2:3], AF.Sqrt, [st], [st])
            B.op("dve", lambda e: e.reciprocal(st.a[:, 3:4], st.a[:, 2:3]), [st], [st])
            for half in range(2):
                hsl = slice(half * 512, (half + 1) * 512)
                B.stt(scr.a[:, hsl], pbs[half].a, st.a[:, 3:4], gpost.a[:, hsl], ALU.mult, ALU.mult, [pbs[half], st, gpost], [scr])
            B.tt("pool", xt.a[:, s_, :], xt.a[:, s_, :], scr.a, ALU.add, [xt, scr], [xt])
        S.dma("sp", x1s.a[ti * TB:(ti + 1) * TB, :].rearrange("(s p) d -> p s d", p=128), xt.a, reads=[xt], writes=[x1s], dreg=xt)
        if "x1" in dbo:
            S.dma("sp", dbo["x1"].a[ti * TB:(ti + 1) * TB, :].rearrange("(s p) d -> p s d", p=128), xt.a, reads=[xt],
                  writes=[dbo["x1"]], dreg=xt)
    B.pop()


def phase_2(B, W, x1s, out, frontend, bc_load, dbo, ntiles):
    nc, S = B.nc, B.S
    TB = 512
    NS = TB // 128
    ntiles = ntiles * (512 // TB)
    B.push()
    wup = B.sb("wup", [128, 8, 2 * FF], BF16)
    wsrc = W["ffn_w_up"].a.rearrange("(c p) n -> p c n", p=128)
    for c in range(8):
        for (a, b) in ((0, 2048), (2048, 4096), (4096, 2 * FF)):
            S.dma("pool", wup.a[:, c, a:b], wsrc[:, c, a:b], reads=[W["ffn_w_up"]], writes=[wup])
    wdn = B.sb("wdn", [128, 22, D], BF16)
    for i in range(22):
        S.dma("pool", wdn.a[:, i, :], W["ffn_w_down"].a[i * 128:(i + 1) * 128, :], reads=[W["ffn_w_down"]], writes=[wdn])
    g1 = bc_load("norm_ffn_pre", D)
    g2 = bc_load("norm_ffn_post", D)
    cw = B.sb("cw", [128, 3, 44]); cbias = B.sb("cbias", [128, 44])
    S.dma("sp", cw.a, W["ffn_conv_w"].a.rearrange("j (b p) -> p j b", p=128), reads=[W["ffn_conv_w"]], writes=[cw],
          allow_slow_non_contiguous=True)
    S.dma("sp", cbias.a, W["ffn_conv_b"].a.rearrange("(b p) -> p b", p=128), reads=[W["ffn_conv_b"]], writes=[cbias],
          allow_slow_non_contiguous=True)
    halo = B.sb("halo", [128, 44, 2]); B.memset("pool", halo.a, 0.0, [halo])
    class BS:
        pass
    sets = []
    for w in range(1):
        q = BS()
        q.xt = B.sb("xt2_%d" % w, [128, NS, D]); q.hT = B.sb("hT2_%d" % w, [128, 8, TB], BF16)
        q.actb = B.sb("actb%d" % w, [128, 22, TB], BF16)
        q.hb = B.view(q.actb, q.actb.a.rearrange("p a b -> p (a b)")[:, 0:NS * D].rearrange("p (s d) -> p s d", s=NS))
        q.st = B.sb("st2_%d" % w, [128, 12])
        sets.append(q)
    scrF_ = B.sb("scrF", [128, D])
    a0s_ = (B.sb("a0g", [128, TB]), B.sb("a0v", [128, TB]))
    a1s_ = (B.sb("a1g", [128, TB]), B.sb("a1v", [128, TB]))
    for q in sets:
        q.scrF, q.a0s, q.a1s = scrF_, a0s_, a1s_
    def tile(ti):
        q = sets[0]
        xt, hT, actb, hb, st, scrF, a0s, a1s = q.xt, q.hT, q.actb, q.hb, q.st, q.scrF, q.a0s, q.a1s
        accg, accv = a1s
        frontend(x1s, ti, NS, g1, xt, hb, hT, scrF, st)
        for i in range(22):
            accs = a1s
            for gv in range(2):
                b = i + 22 * gv
                pb = B.ps()
                for c in range(8):
                    B.mm(pb.a[:, 0:TB], wup.a[:, c, b * 128:(b + 1) * 128], hT.a[:, c, :], c == 0, c == 7, [wup, hT], [pb])
                acc = accs[gv]; a0 = a0s[gv]; a1 = a1s[gv]
                B.act(a0.a, pb.a[:, 0:TB], AF.Identity, [pb, cw, cbias], [a0], scale=cw.a[:, 2, b:b + 1], bias=cbias.a[:, b:b + 1])
                B.act(a1.a[:, 1:TB], pb.a[:, 0:TB - 1], AF.Copy, [pb, cw], [a1], scale=cw.a[:, 1, b:b + 1])
                B.stt(a0.a[:, 2:TB], pb.a[:, 0:TB - 2], cw.a[:, 0, b:b + 1], a0.a[:, 2:TB], ALU.mult, ALU.add, [pb, cw, a0], [a0])
                B.ts("dve", a1.a[:, 0:1], halo.a[:, b, 1:2], cw.a[:, 1, b:b + 1], ALU.mult, [halo, cw], [a1])
                B.stt(a0.a[:, 0:2], halo.a[:, b, 0:2], cw.a[:, 0, b:b + 1], a0.a[:, 0:2], ALU.mult, ALU.add, [halo, cw, a0], [a0])
                B.cp("dve", halo.a[:, b, :], pb.a[:, TB - 2:TB], [pb], [halo])
                B.tt("pool", acc.a, a0.a, a1.a, ALU.add, [a0, a1], [acc])
            if "zc" in dbo and ti == 0 and i == 0:
                S.dma("sp", dbo["zc"].a[:, 0:TB], accg.a, reads=[accg], writes=[dbo["zc"]], dreg=accg)
            B.act(accg.a, accg.a, AF.Gelu_apprx_tanh, [accg], [accg])
            B.tt("pool", actb.a[:, i, :], accg.a, accv.a, ALU.mult, [accg, accv], [actb])
        for s_ in range(NS):
            pbs = [B.ps(), B.ps()]
            for half in range(2):
                for i in range(22):
                    B.mm(pbs[half].a, actb.a[:, i, s_ * 128:(s_ + 1) * 128], wdn.a[:, i, half * 512:(half + 1) * 512],
                         i == 0, i == 21, [actb, wdn], [pbs[half]])
            for half in range(2):
                B.act(scrF.a[:, 0:512], pbs[half].a, AF.Square, [pbs[half]], [scrF, st], accum_out=st.a[:, half:half + 1])
            B.tt("dve", st.a[:, 2:3], st.a[:, 0:1], st.a[:, 1:2], ALU.add, [st], [st])
            B.ts("dve", st.a[:, 2:3], st.a[:, 2:3], 1.0 / D, ALU.mult, [st], [st], s2=1e-6, op1=ALU.add)
            B.act(st.a[:, 2:3], st.a[:, 2:3], AF.Sqrt, [st], [st])
            B.op("dve", lambda e: e.reciprocal(st.a[:, 3:4], st.a[:, 2:3]), [st], [st])
            for half in range(2):
                hsl = slice(half * 512, (half + 1) * 512)
                B.stt(scrF.a[:, hsl], pbs[half].a, st.a[:, 3:4], g2.a[:, hsl], ALU.mult, ALU.mult, [pbs[half], st, g2], [scrF])
            B.tt("pool", xt.a[:, s_, :], xt.a[:, s_, :], scrF.a, ALU.add, [xt, scrF], [xt])
        S.dma("sp", out.a[ti * TB:(ti + 1) * TB, :].rearrange("(s p) d -> p s d", p=128), xt.a, reads=[xt], writes=[out], dreg=xt)

    S.rec = []
    for ti in range(ntiles):
        tile(ti)
    items = S.rec
    S.rec = None
    S.schedule_emit(items, window=int(os.environ.get("K_WIN2", "1500")))
    B.pop()


_CACHE = {}


def kernel(**inputs):
    if "B" not in _CACHE:
        _CACHE["B"] = build()
    Bd = _CACHE["B"]
    x = np.ascontiguousarray(inputs["x"], dtype=np.float32)
    wmap = {k: np.ascontiguousarray(np.asarray(inputs[k], dtype=np.float32).reshape(shp)) for k, shp in Bd.Wshapes.items()}
    in_maps = []
    for c in range(8):
        m = dict(wmap)
        m["x"] = x[c]
        in_maps.append(m)
    res = run_bass_kernel_spmd(Bd.nc, in_maps, core_ids=list(range(8)))
    return np.stack([np.asarray(res.results[c]["out"], dtype=np.float32) for c in range(8)], axis=0)
```

```python
import contextlib
import math
import os
import numpy as np
import concourse.bass as bass
import concourse.mybir as mybir
from concourse.bass_utils import run_bass_kernel_spmd

F32 = mybir.dt.float32
BF16 = mybir.dt.bfloat16
I32 = mybir.dt.int32
ALU = mybir.AluOpType
AF = mybir.ActivationFunctionType

L = 4096
D = 1024
NR = 1792
NS5 = 512
NIN = 4352
FF = 2816
PI = math.pi


class Reg:
    __slots__ = ("name", "w", "r", "dsem", "dcnt", "last", "excl")

    def __init__(self, name=""):
        self.name = name
        self.w = None
        self.r = []
        self.dsem = None
        self.dcnt = 0
        self.last = 0
        self.excl = False


class Sched:
    def __init__(self, nc):
        self.nc = nc
        self.eng = {"pe": nc.tensor, "dve": nc.vector, "act": nc.scalar,
                    "pool": nc.gpsimd, "sp": nc.sync}
        self.sem = {k: nc.alloc_semaphore(name="sem_" + k) for k in self.eng}
        self.cnt = {k: 0 for k in self.eng}
        self.seen = {k: {} for k in self.eng}
        self.ninst = 0
        self.nds = 0
        self.dregs = []
        self.dmap = {}
        self.maxops = int(os.environ.get("K_MAXOPS", "100000000"))
        self.rec = None

    def _wait(self, e, tok):
        sem, val = tok
        key = sem.name
        if key in self.dmap:
            val = max(val, self.dmap[key].dcnt)
        if self.seen[e].get(key, 0) >= val:
            return
        if e == "pe" and sem is self.sem["pe"]:
            return
        self.eng[e].wait_ge(sem, val)
        self.seen[e][key] = val

    def _deps(self, e, reads, writes, skip=None):
        for r in reads:
            if r.w is not None:
                self._wait(e, r.w)
        for w in writes:
            if w.w is not None and w.w[0] is not skip:
                self._wait(e, w.w)
            for t in w.r:
                self._wait(e, t)

    def _commit(self, tok, reads, writes):
        for r in reads:
            r.last = self.ninst
            r.r.append(tok)
            if len(r.r) > 16:
                d = {}
                for s, v in r.r:
                    if d.get(s.name, (None, -1))[1] < v:
                        d[s.name] = (s, v)
                r.r = list(d.values())
        for w in writes:
            w.last = self.ninst
            w.w = tok
            w.r = []

    def op(self, e, fn, reads=(), writes=(), cost=None):
        if self.rec is not None:
            self.rec.append(("op", e, fn, list(reads), list(writes), None, cost))
            return None
        if self.ninst >= self.maxops:
            return None
        reads = [x.r if isinstance(x, Buf) else x for x in reads]
        writes = [x.r if isinstance(x, Buf) else x for x in writes]
        ex = [x for x in reads if x.excl and x not in writes]
        if ex:
            reads = [x for x in reads if not x.excl]
            writes = list(writes) + ex
        self._deps(e, reads, writes)
        ins = fn(self.eng[e])
        self.cnt[e] += 1
        ins.then_inc(self.sem[e], 1)
        tok = (self.sem[e], self.cnt[e])
        self._commit(tok, reads, writes)
        self.ninst += 1
        return tok

    def dma(self, e, out, in_, reads=(), writes=(), dreg=None, **kw):
        if self.rec is not None:
            self.rec.append(("dma", e, (out, in_), list(reads), list(writes), (dreg, kw), None))
            return None
        if self.ninst >= self.maxops and not kw.pop("force", False):
            return None
        kw.pop("force", None)
        reads = [x.r if isinstance(x, Buf) else x for x in reads]
        writes = [x.r if isinstance(x, Buf) else x for x in writes]
        if dreg is None:
            dreg = writes[0] if writes else reads[0]
        elif isinstance(dreg, Buf):
            dreg = dreg.r
        if dreg.dsem is None:
            self.nds += 1
            dreg.dsem = self.nc.alloc_semaphore(name="ds%d_%s" % (self.nds, dreg.name))
            self.dregs.append(dreg)
            self.dmap[dreg.dsem.name] = dreg
        self._deps(e, reads, writes, skip=dreg.dsem)
        ins = self.eng[e].dma_start(out=out, in_=in_, **kw)
        dreg.dcnt += 16
        ins.then_inc(dreg.dsem, 16)
        tok = (dreg.dsem, dreg.dcnt)
        self._commit(tok, reads, writes)
        self.ninst += 1
        return tok

    def replay(self, item):
        kind, e, a, reads, writes, extra = item[:6]
        if kind == "op":
            return self.op(e, a, reads, writes)
        dreg, kw = extra
        return self.dma(e, a[0], a[1], reads=reads, writes=writes, dreg=dreg, **kw)

    def schedule_emit(self, items, window=1500):
        import heapq
        n = len(items)
        norm = []
        for it in items:
            kind, e, a, reads, writes, extra, cost = it
            reads = [x.r if isinstance(x, Buf) else x for x in reads]
            writes = [x.r if isinstance(x, Buf) else x for x in writes]
            dreg = None
            if kind == "dma":
                dreg = extra[0]
                if dreg is None:
                    dreg = writes[0] if writes else reads[0]
                elif isinstance(dreg, Buf):
                    dreg = dreg.r
            ex = [x for x in reads if x.excl and x not in writes]
            if ex:
                reads = [x for x in reads if not x.excl]
                writes = list(writes) + ex
            norm.append((kind, e, a, reads, writes, extra, cost, dreg))
        lw = {}
        rd = {}
        first = [[] for _ in range(n)]
        preds = [set() for _ in range(n)]
        for i, (kind, e, a, reads, writes, extra, cost, dreg) in enumerate(norm):
            for r in reads:
                k = id(r)
                if k in lw:
                    preds[i].add(lw[k])
                else:
                    first[i].append((r, "r"))
            for w in writes:
                k = id(w)
                if k in lw:
                    if not (kind == "dma" and norm[lw[k]][0] == "dma" and norm[lw[k]][7] is dreg):
                        preds[i].add(lw[k])
                else:
                    first[i].append((w, "w"))
                for j in rd.get(k, ()):
                    preds[i].add(j)
            for r in reads:
                rd.setdefault(id(r), []).append(i)
            for w in writes:
                lw[id(w)] = i
                rd[id(w)] = []
            preds[i].discard(i)
        succs = [[] for _ in range(n)]
        npred = [len(p) for p in preds]
        for i, p in enumerate(preds):
            for j in p:
                succs[j].append(i)
        def dur(it):
            kind, e, a, reads, writes, extra, cost, dreg = it
            if cost is not None:
                return cost
            if kind == "op":
                return {"pe": 0.08, "dve": 0.4, "act": 0.4, "pool": 1.0, "sp": 2.0}[e]
            o = a[0]
            nbytes = 1
            for d in list(o.shape):
                nbytes *= int(d)
            nbytes *= mybir.dt.size(o.dtype)
            return 2.5 + nbytes / 140e3
        LAT = 0.15
        free = {e: 0.0 for e in self.eng}
        fin = [0.0] * n
        ready_t = [0.0] * n
        heaps = {e: [] for e in self.eng}
        low = 0
        done = [False] * n
        avail = [False] * n
        for i in range(n):
            if npred[i] == 0:
                heapq.heappush(heaps[norm[i][1]], (0.0, i)); avail[i] = True
        order = []
        deferred = {e: [] for e in self.eng}
        while len(order) < n:
            best = None
            for e, h in heaps.items():
                while h and h[0][1] >= low + window:
                    deferred[e].append(heapq.heappop(h))
                if not h:
                    continue
                rt, i = h[0]
                st = max(rt, free[e])
                if best is None or (st, i) < (best[0], best[1]):
                    best = (st, i, e)
            if best is None:
                for e in deferred:
                    for x in deferred[e]:
                        heapq.heappush(heaps[e], x)
                    deferred[e] = []
                window *= 2
                continue
            st, i, e = best
            heapq.heappop(heaps[e])
            order.append(i)
            done[i] = True
            f = st + dur(norm[i])
            fin[i] = f
            free[e] = f
            for j in succs[i]:
                npred[j] -= 1
                ready_t[j] = max(ready_t[j], f + LAT)
                if npred[j] == 0:
                    heapq.heappush(heaps[norm[j][1]], (ready_t[j], j)); avail[j] = True
            if i == low:
                while low < n and done[low]:
                    low += 1
                for e2 in deferred:
                    keep = []
                    for x in deferred[e2]:
                        if x[1] < low + window:
                            heapq.heappush(heaps[e2], x)
                        else:
                            keep.append(x)
                    deferred[e2] = keep
        self.sched_makespan = max(fin) if n else 0.0
        toks = [None] * n
        for i in order:
            kind, e, a, reads, writes, extra, cost, dreg = norm[i]
            for (reg, mode) in first[i]:
                if reg.w is not None and not (kind == "dma" and mode == "w" and reg.w[0] is (dreg.dsem if dreg is not None else None)):
                    self._wait(e, reg.w)
                if mode == "w":
                    for t in reg.r:
                        self._wait(e, t)
            for j in preds[i]:
                self._wait(e, toks[j])
            if kind == "op":
                ins = a(self.eng[e])
                self.cnt[e] += 1
                ins.then_inc(self.sem[e], 1)
                toks[i] = (self.sem[e], self.cnt[e])
            else:
                dg, kw = extra
                kw = dict(kw); kw.pop("force", None)
                if dreg.dsem is None:
                    self.nds += 1
                    dreg.dsem = self.nc.alloc_semaphore(name="ds%d_%s" % (self.nds, dreg.name))
                    self.dregs.append(dreg)
                    self.dmap[dreg.dsem.name] = dreg
                ins = self.eng[e].dma_start(out=a[0], in_=a[1], **kw)
                dreg.dcnt += 16
                ins.then_inc(dreg.dsem, 16)
                toks[i] = (dreg.dsem, dreg.dcnt)
            self.ninst += 1
        touched = {}
        for i, it in enumerate(norm):
            for r in it[3]:
                touched[id(r)] = r
            for w in it[4]:
                touched[id(w)] = w
        for k, reg in touched.items():
            if k in lw:
                reg.w = toks[lw[k]]
                reg.r = [toks[j] for j in rd.get(k, ())]
            else:
                reg.r = list(reg.r) + [toks[j] for j in rd.get(k, ())]
            reg.last = self.ninst

    def barrier(self):
        for e in self.eng:
            for f in self.eng:
                if f != e and self.cnt[f] > 0:
                    self._wait(e, (self.sem[f], self.cnt[f]))
            for d in self.dregs:
                if d.dcnt > 0:
                    self._wait(e, (d.dsem, d.dcnt))

    def final_wait(self, e, regs):
        for r in regs:
            r = r.r if isinstance(r, Buf) else r
            if r.w is not None:
                self._wait(e, r.w)
            for t in r.r:
                self._wait(e, t)


class Buf:
    def __init__(self, t, name):
        self.t = t
        self.a = t.ap()
        self.r = Reg(name)


class Builder:
    def __init__(self, dbg=None, ntiles=8):
        self.nc = bass.Bass("TRN2", target_bir_lowering=False)
        self.S = Sched(self.nc)
        self.dbg = dbg or {}
        self.ntiles = ntiles
        self.din = {}
        self.dout = {}
        self.nbuf = 0
        self.psb = None
        self.psi = 0
        self.ps_pool = None
        self.pclock = 0
        self.scopes = []

    def push(self):
        self.scopes.append(contextlib.ExitStack())

    def pop(self):
        self.S.barrier()
        self.scopes.pop().close()

    def inp(self, name, shape):
        b = Buf(self.nc.dram_tensor(name, list(shape), F32, kind="ExternalInput"), name)
        self.din[name] = b
        return b

    def outp(self, name, shape, dt=F32):
        b = Buf(self.nc.dram_tensor(name, list(shape), dt, kind="ExternalOutput"), name)
        self.dout[name] = b
        return b

    def sb(self, name, shape, dt=F32):
        self.nbuf += 1
        nm = "%s_%d" % (name, self.nbuf)
        if self.scopes:
            return Buf(self.scopes[-1].enter_context(self.nc.sbuf_tensor(nm, list(shape), dt)), name)
        return Buf(self.nc.alloc_sbuf_tensor(nm, list(shape), dt), name)

    def view(self, buf, ap):
        v = Buf.__new__(Buf)
        v.t = buf.t
        v.a = ap
        v.r = buf.r
        return v

    def init_psum(self):
        self.psb = []
        for i in range(8):
            t = self.nc.alloc_psum_tensor("psb%d" % i, [128, 512], F32)
            self.psb.append(Buf(t, "psb%d" % i))
            self.psb[-1].r.excl = True

    def ps(self):
        if self.ps_pool is not None:
            b = self.psb[self.ps_pool[self.psi % len(self.ps_pool)]]
            self.psi += 1
            return b
        b = min(self.psb, key=lambda t: t.r.last)
        self.pclock = max(self.pclock, self.S.ninst) + 1
        b.r.last = self.pclock
        return b

    def op(self, e, fn, reads=(), writes=(), cost=None):
        return self.S.op(e, fn, reads, writes, cost=cost)

    @staticmethod
    def fsz(ap):
        n = 1
        for d in list(ap.shape)[1:]:
            n *= int(d)
        return n

    def mm(self, out, lhsT, rhs, start, stop, reads, writes, **kw):
        return self.op("pe", lambda e: e.matmul(out, lhsT=lhsT, rhs=rhs, start=start, stop=stop, **kw), reads, writes,
                       cost=0.03 + max(self.fsz(rhs), 64) * 0.00052)

    def tr(self, out, in_, ident, reads, writes):
        return self.op("pe", lambda e: e.transpose(out, in_, ident), reads, writes, cost=0.1)

    def act(self, eng_out, in_, func, reads, writes, **kw):
        return self.op("act", lambda e: e.activation(out=eng_out, in_=in_, func=func, **kw), reads, writes,
                       cost=0.22 + self.fsz(in_) * 0.00075)

    def tt(self, e, out, in0, in1, op, reads, writes):
        c = (0.1 + self.fsz(in0) * 0.00115) if e == "dve" else (0.6 + self.fsz(in0) * 0.0013)
        return self.op(e, lambda g: g.tensor_tensor(out=out, in0=in0, in1=in1, op=op), reads, writes, cost=c)

    def ts(self, e, out, in0, s1, op0, reads, writes, s2=None, op1=None):
        c = (0.1 + self.fsz(in0) * 0.0008) if e == "dve" else (0.6 + self.fsz(in0) * 0.0013)
        if op1 is None:
            return self.op(e, lambda g: g.tensor_scalar(out=out, in0=in0, scalar1=s1, scalar2=None, op0=op0), reads, writes, cost=c)
        return self.op(e, lambda g: g.tensor_scalar(out=out, in0=in0, scalar1=s1, scalar2=s2, op0=op0, op1=op1), reads, writes, cost=c)

    def stt(self, out, in0, scalar, in1, op0, op1, reads, writes):
        return self.op("dve", lambda g: g.scalar_tensor_tensor(out=out, in0=in0, scalar=scalar, in1=in1, op0=op0, op1=op1), reads, writes,
                       cost=0.1 + self.fsz(in0) * 0.00115)

    def cp(self, e, out, in_, reads, writes):
        if e == "act":
            return self.op("act", lambda g: g.activation(out=out, in_=in_, func=AF.Copy), reads, writes, cost=0.22 + self.fsz(in_) * 0.00075)
        c = (0.1 + self.fsz(in_) * 0.0008) if e == "dve" else (0.6 + self.fsz(in_) * 0.0013)
        return self.op(e, lambda g: g.tensor_copy(out, in_), reads, writes, cost=c)

    def memset(self, e, out, val, writes):
        return self.op(e, lambda g: g.memset(out, val), (), writes)

    def cmul(self, e, o_re, o_im, a_re, a_im, b_re, b_im, t1, t2, reads, writes, tmp):
        R = list(reads)
        self.tt(e, t1, a_re, b_re, ALU.mult, R, [tmp])
        self.tt(e, t2, a_im, b_im, ALU.mult, R, [tmp])
        self.tt(e, o_re, t1, t2, ALU.subtract, [tmp], writes)
        self.tt(e, t1, a_re, b_im, ALU.mult, R, [tmp])
        self.tt(e, t2, a_im, b_re, ALU.mult, R, [tmp])
        self.tt(e, o_im, t1, t2, ALU.add, [tmp], writes)


def build(dbg=None, ntiles=8, phases=("1a", "1b", "1c", "2")):
    B = Builder(dbg, ntiles)
    nc, S = B.nc, B.S
    dbg = B.dbg

    x = B.inp("x", [L, D])
    shapes = {
        "norm_mix_pre": [D], "norm_mix_post": [D], "norm_ffn_pre": [D], "norm_ffn_post": [D],
        "w_in": [D, NIN], "b_gate": [2048], "rwkv_shift_mu": [NR], "rwkv_w0": [512],
        "rwkv_w2": [64, 512], "rwkv_a0": [512], "rwkv_a2": [64, 512], "rwkv_g2": [128, 512],
        "rwkv_k_k": [512], "rwkv_k_a": [512], "rwkv_r_k": [512], "rwkv_lnx_w": [512],
        "rwkv_lnx_b": [512], "s5_a_re": [32, 64], "s5_a_im": [32, 64], "s5_b_re": [32, 64, 16],
        "s5_b_im": [32, 64, 16], "s5_c_re": [32, 16, 64], "s5_c_im": [32, 16, 64], "s5_d": [512],
        "s5_log_step": [32], "s5_w_glu": [512, 512], "s5_b_glu": [512], "w_branch_rwkv": [512, D],
        "w_branch_s5": [512, D], "w_out": [D, D], "ffn_w_up": [D, 2 * FF], "ffn_conv_w": [3, 2 * FF],
        "ffn_conv_b": [2 * FF], "ffn_w_down": [FF, D],
    }
    W = {k: B.inp(k, v) for k, v in shapes.items()}
    out = B.outp("out", [L, D])
    dbo = {k: B.outp("dbg_" + k, shp, dt) for k, (shp, dt) in dbg.items()}

    B.init_psum()

    ident_f = B.sb("ident_f", [128, 128], F32)
    ident_b = B.sb("ident_b", [128, 128], BF16)
    B.memset("pool", ident_f.a, 1.0, [ident_f])
    B.op("pool", lambda e: e.affine_select(out=ident_f.a, in_=ident_f.a, pattern=[[-1, 128]],
                                           compare_op=ALU.is_equal, fill=0.0, base=0, channel_multiplier=1),
         [ident_f], [ident_f])
    B.cp("pool", ident_b.a, ident_f.a, [ident_f], [ident_b])

    def bc_load(name, n, q="sp"):
        t = B.sb(name + "_bc", [128, n], F32)
        S.dma(q, t.a, W[name].a.partition_broadcast(128), reads=[W[name]], writes=[t])
        return t

    def col_load(name, nt, q="sp"):
        t = B.sb(name + "_col", [128, nt], F32)
        S.dma(q, t.a, W[name].a.rearrange("(t p) -> p t", p=128), reads=[W[name]], writes=[t],
              allow_slow_non_contiguous=True)
        return t

    def frontend(src, ti, ntok_tiles, gbc, xt, hb, hT, scr, st):
        nt = ntok_tiles
        T = 128 * nt
        S.dma("sp", xt.a, src.a[ti * T:(ti + 1) * T, :].rearrange("(s p) d -> p s d", p=128),
              reads=[src], writes=[xt])
        for s in range(nt):
            B.act(scr.a, xt.a[:, s, :], AF.Square, [xt], [scr, st], accum_out=st.a[:, s:s + 1])
        B.ts("dve", st.a[:, nt:2 * nt], st.a[:, 0:nt], 1.0 / D, ALU.mult, [st], [st], s2=1e-6, op1=ALU.add)
        B.act(st.a[:, nt:2 * nt], st.a[:, nt:2 * nt], AF.Sqrt, [st], [st])
        B.op("dve", lambda e: e.reciprocal(st.a[:, 2 * nt:3 * nt], st.a[:, nt:2 * nt]), [st], [st])
        for s in range(nt):
            B.stt(hb.a[:, s, :], xt.a[:, s, :], st.a[:, 2 * nt + s:2 * nt + s + 1], gbc.a, ALU.mult, ALU.mult,
                  [xt, st, gbc], [hb])
        for c in range(8):
            pb = B.ps()
            pv = pb.a.bitcast(BF16)
            for s in range(nt):
                B.tr(pv[:, s * 128:(s + 1) * 128], hb.a[:, s, c * 128:(c + 1) * 128], ident_b.a,
                     [hb, ident_b], [pb])
            B.cp("act" if c % 2 == 0 else "dve", hT.a[:, c, :], pv[:, 0:T], [pb], [hT])

    T1 = 512
    YGLU = Buf(nc.dram_tensor("yglu_d", [128, 4, L], BF16, kind="Internal"), "yglu_d")

    B.push()
    gpre = bc_load("norm_mix_pre", D)
    x1s = Buf(nc.dram_tensor("x1s", [L, D], F32, kind="Internal"), "x1s")
    YR = Buf(nc.dram_tensor("yr_d", [128, 4, L], BF16, kind="Internal"), "yr_d")
    if "1a" in phases:
        B.push()
        ws5 = B.sb("ws5", [128, 8, 512], BF16)
        S.dma("pool", ws5.a, W["w_in"].a.rearrange("(c p) n -> p c n", p=128)[:, :, NR:NR + 512],
              reads=[W["w_in"]], writes=[ws5])
        wglu = B.sb("wglu", [128, 4, 512], BF16)
        S.dma("pool", wglu.a, W["s5_w_glu"].a.rearrange("(c p) n -> p c n", p=128), reads=[W["s5_w_glu"]], writes=[wglu])
        bglu = col_load("s5_b_glu", 4)
        dcol = col_load("s5_d", 4)

        MS = B.sb("ms", [128, 16, 2, 2])
        ER = B.sb("ER", [128, 16, 64]); EI = B.sb("EI", [128, 16, 64]); R8 = B.sb("R8", [128, 16])
        Wt = B.sb("Wt", [128, 4, 8, 2, 128], BF16)
        CAW = B.sb("CAW", [128, 16, 2, 9, 64], BF16)
        mk = B.sb("mk", [128, 2])
        B.memset("pool", mk.a[:, 0:1], 0.0, [mk]); B.memset("pool", mk.a[0:32, 0:1], 1.0, [mk]); B.memset("pool", mk.a[64:96, 0:1], 1.0, [mk])
        mk4 = B.sb("mk4", [128, 4])
        B.memset("pool", mk4.a, 0.0, [mk4])
        B.memset("pool", mk4.a[0:32, 0:1], 1.0, [mk4]); B.memset("pool", mk4.a[32:64, 1:2], 1.0, [mk4])
        B.memset("pool", mk4.a[64:96, 2:3], 1.0, [mk4]); B.memset("pool", mk4.a[64:128, 3:4], 1.0, [mk4]); B.memset("pool", mk4.a[64:96, 3:4], 0.0, [mk4])
        B.memset("pool", mk.a[:, 1:2], 1.0, [mk]); B.memset("pool", mk.a[0:32, 1:2], 0.0, [mk]); B.memset("pool", mk.a[64:96, 1:2], 0.0, [mk])
        Kbd = B.sb("Kbd", [128, 4, 8, 128], BF16)
        B.push()
        are = B.sb("are", [128, 16]); aim = B.sb("aim", [128, 16]); ls = B.sb("ls", [128, 16])
        for gl in range(2):
            S.dma("sp", are.a[gl * 64:(gl + 1) * 64, :], W["s5_a_re"].a.rearrange("(q gl) n -> gl n q", gl=2)[gl],
                  reads=[W["s5_a_re"]], writes=[are], allow_slow_non_contiguous=True)
            S.dma("sp", aim.a[gl * 64:(gl + 1) * 64, :], W["s5_a_im"].a.rearrange("(q gl) n -> gl n q", gl=2)[gl],
                  reads=[W["s5_a_im"]], writes=[aim], allow_slow_non_contiguous=True)
            S.dma("sp", ls.a[gl * 64:(gl + 1) * 64, :],
                  W["s5_log_step"].a.rearrange("(q gl) -> gl q", gl=2)[gl].partition_broadcast(64),
                  reads=[W["s5_log_step"]], writes=[ls], allow_slow_non_contiguous=True)
        braw = [B.sb("braw%d" % i, [128, 16, 16]) for i in range(2)]
        for i, nm in enumerate(("s5_b_re", "s5_b_im")):
            S.dma("sp", braw[i].a, W[nm].a.rearrange("(q gl) n c -> (gl n) q c", gl=2), reads=[W[nm]], writes=[braw[i]])
        craw = [B.sb("craw%d" % i, [128, 16, 16]) for i in range(2)]
        ctmp = B.sb("ctmp", [128, 128])
        for i, nm in enumerate(("s5_c_re", "s5_c_im")):
            for blk in range(2):
                src = W[nm].a.rearrange("(b qq gl) c n -> b qq c gl n", b=2, gl=2)[blk]
                for qq in range(8):
                    S.dma("sp", ctmp.a[qq * 16:(qq + 1) * 16, :].rearrange("p (gl n) -> p gl n", gl=2),
                          src[qq], reads=[W[nm]], writes=[ctmp])
                pb = B.ps()
                B.tr(pb.a[:, 0:128], ctmp.a, ident_f.a, [ctmp, ident_f], [pb])
                B.cp("dve", craw[i].a[:, blk * 8:(blk + 1) * 8, :],
                     pb.a[:, 0:128].rearrange("p (qq c) -> p qq c", c=16), [pb], [craw[i]])

        tm = B.sb("s5tmp", [128, 12, 32])
        tmr = tm.r

        def row(i, n=16):
            return tm.a[:, i, 0:n]

        dt_ = row(0)
        B.act(dt_, ls.a, AF.Exp, [ls], [tm])
        xr = row(1)
        B.tt("dve", xr, are.a, dt_, ALU.mult, [are, tm], [tm])
        rho = row(2)
        B.ts("dve", rho, xr, 1.0 / 720, ALU.mult, [tm], [tm], s2=1.0 / 120, op1=ALU.add)
        for cf in (1.0 / 24, 1.0 / 6, 0.5, 1.0, 1.0):
            B.tt("dve", rho, rho, xr, ALU.mult, [tm], [tm])
            B.ts("dve", rho, rho, cf, ALU.add, [tm], [tm])
        th2 = tm.a[:, 3, :]
        B.tt("dve", th2[:, 0:16], aim.a, dt_, ALU.mult, [aim, tm], [tm])
        B.ts("dve", th2[:, 16:32], th2[:, 0:16], PI / 2, ALU.add, [tm], [tm])
        kf = tm.a[:, 4, :]
        B.ts("dve", kf, th2, 1.0 / (2 * PI), ALU.mult, [tm], [tm])
        ki = B.sb("ki", [128, 32], I32)
        B.cp("dve", ki.a, kf, [tm], [ki])
        B.cp("dve", kf, ki.a, [ki], [tm])
        r1 = tm.a[:, 5, :]
        B.stt(r1, kf, -2 * PI, th2, ALU.mult, ALU.add, [tm], [tm])
        B.ts("dve", kf, r1, PI, ALU.is_gt, [tm], [tm], s2=-2 * PI, op1=ALU.mult)
        B.tt("dve", r1, r1, kf, ALU.add, [tm], [tm])
        B.ts("dve", kf, r1, -PI, ALU.is_lt, [tm], [tm], s2=2 * PI, op1=ALU.mult)
        B.tt("dve", r1, r1, kf, ALU.add, [tm], [tm])
        sc = tm.a[:, 6, :]
        B.act(sc, r1, AF.Sin, [tm], [tm])
        n2 = row(7)
        B.tt("dve", kf, sc, sc, ALU.mult, [tm], [tm])
        B.tt("dve", n2, kf[:, 0:16], kf[:, 16:32], ALU.add, [tm], [tm])
        B.ts("dve", n2, n2, -0.5, ALU.mult, [tm], [tm], s2=1.5, op1=ALU.add)
        B.tt("dve", n2, n2, rho, ALU.mult, [tm], [tm])
        PW = B.sb("pw", [128, 9, 2, 16])
        B.memset("dve", PW.a[:, 0, 0, :], 1.0, [PW])
        B.memset("dve", PW.a[:, 0, 1, :], 0.0, [PW])
        B.tt("dve", PW.a[:, 1, 0, :], sc[:, 16:32], n2, ALU.mult, [tm], [PW])
        B.tt("dve", PW.a[:, 1, 1, :], sc[:, 0:16], n2, ALU.mult, [tm], [PW])
        pt = B.sb("ptmp", [128, 2, 4, 16])
        for (lo, n, s) in ((2, 1, 1), (3, 2, 2), (5, 4, 4)):
            bre = PW.a[:, s:s + 1, 0, :].broadcast_to([128, n, 16])
            bim = PW.a[:, s:s + 1, 1, :].broadcast_to([128, n, 16])
            B.cmul("dve", PW.a[:, lo:lo + n, 0, :], PW.a[:, lo:lo + n, 1, :],
                   PW.a[:, lo - s:lo - s + n, 0, :], PW.a[:, lo - s:lo - s + n, 1, :], bre, bim,
                   pt.a[:, 0, 0:n, :], pt.a[:, 1, 0:n, :], [PW], [PW], pt)
        B.cp("dve", MS.a[:, :, 0, 0], PW.a[:, 8, 0, :], [PW], [MS])
        B.cp("dve", MS.a[:, :, 1, 1], PW.a[:, 8, 0, :], [PW], [MS])
        B.cp("dve", MS.a[:, :, 1, 0], PW.a[:, 8, 1, :], [PW], [MS])
        B.ts("dve", MS.a[:, :, 0, 1], PW.a[:, 8, 1, :], -1.0, ALU.mult, [PW], [MS])
        B.tt("dve", R8.a, rho, rho, ALU.mult, [tm], [R8])
        B.tt("dve", R8.a, R8.a, R8.a, ALU.mult, [R8], [R8])
        B.tt("dve", R8.a, R8.a, R8.a, ALU.mult, [R8], [R8])
        r8i = row(8)
        B.op("dve", lambda e: e.reciprocal(r8i, R8.a), [R8], [tm])
        B.tt("dve", ER.a[:, :, 0], PW.a[:, 8, 0, :], r8i, ALU.mult, [PW, tm], [ER])
        B.tt("dve", EI.a[:, :, 0], PW.a[:, 8, 1, :], r8i, ALU.mult, [PW, tm], [EI])
        et = B.sb("etmp", [128, 2, 16, 32])
        n_ = 1
        while n_ < 64:
            bre = ER.a[:, :, n_ - 1:n_].broadcast_to([128, 16, n_]); bim = EI.a[:, :, n_ - 1:n_].broadcast_to([128, 16, n_])
            B.cmul("dve", ER.a[:, :, n_:2 * n_], EI.a[:, :, n_:2 * n_], ER.a[:, :, 0:n_], EI.a[:, :, 0:n_], bre, bim,
                   et.a[:, 0, :, 0:n_], et.a[:, 1, :, 0:n_], [ER, EI], [ER, EI], et)
            n_ *= 2
        am1 = row(8); nre = row(9); nim = row(10); den = row(11); t0 = row(4); t1 = row(5)
        B.ts("dve", am1, PW.a[:, 1, 0, :], -1.0, ALU.add, [PW], [tm])
        B.tt("dve", nre, am1, are.a, ALU.mult, [tm, are], [tm])
        B.tt("dve", t0, PW.a[:, 1, 1, :], aim.a, ALU.mult, [PW, aim], [tm])
        B.tt("dve", nre, nre, t0, ALU.add, [tm], [tm])
        B.tt("dve", nim, PW.a[:, 1, 1, :], are.a, ALU.mult, [PW, are], [tm])
        B.tt("dve", t0, am1, aim.a, ALU.mult, [tm, aim], [tm])
        B.tt("dve", nim, nim, t0, ALU.subtract, [tm], [tm])
        B.tt("dve", den, are.a, are.a, ALU.mult, [are], [tm])
        B.tt("dve", t0, aim.a, aim.a, ALU.mult, [aim], [tm])
        B.tt("dve", den, den, t0, ALU.add, [tm], [tm])
        B.op("dve", lambda e: e.reciprocal(t1, den), [tm], [tm])
        B.tt("dve", nre, nre, t1, ALU.mult, [tm], [tm])
        B.tt("dve", nim, nim, t1, ALU.mult, [tm], [tm])
        bb = [B.sb("bb%d" % i, [128, 16, 16]) for i in range(2)]
        btmp = B.sb("btmp", [128, 2, 16, 16])
        cre = nre[:, :, None].broadcast_to([128, 16, 16]); cim = nim[:, :, None].broadcast_to([128, 16, 16])
        B.cmul("dve", bb[0].a, bb[1].a, cre, cim, braw[0].a, braw[1].a, btmp.a[:, 0], btmp.a[:, 1],
               [tm, braw[0], braw[1]], [bb[0], bb[1]], btmp)
        X = [B.sb("X%d" % i, [128, 16, 32]) for i in range(2)]
        Xb = [B.sb("Xb%d" % i, [128, 16, 64], BF16) for i in range(2)]
        for i in range(2):
            B.memset("pool", X[i].a, 0.0, [X[i]])
            for gl in range(2):
                B.cp("pool", X[i].a[gl * 64:(gl + 1) * 64, :, gl * 16:(gl + 1) * 16], bb[i].a[gl * 64:(gl + 1) * 64], [bb[i]], [X[i]])
            B.memset("pool", Xb[i].a, 0.0, [Xb[i]])
            for kk in range(2):
                B.cp("pool", Xb[i].a[:, kk::2, 32 * kk:32 * kk + 32], X[i].a[:, kk::2, :], [X[i]], [Xb[i]])
        B.push()
        WX = [B.sb("WX%d" % i, [128, 8, 16, 32]) for i in range(2)]
        wtmp = B.sb("wtmp", [128, 2, 8, 16, 32])
        pre = PW.a[:, 0:8, 0, :][:, :, :, None].broadcast_to([128, 8, 16, 32])
        pim = PW.a[:, 0:8, 1, :][:, :, :, None].broadcast_to([128, 8, 16, 32])
        xre = X[0].a[:, None, :, :].broadcast_to([128, 8, 16, 32])
        xim = X[1].a[:, None, :, :].broadcast_to([128, 8, 16, 32])
        B.cmul("dve", WX[0].a, WX[1].a, pre, pim, xre, xim, wtmp.a[:, 0], wtmp.a[:, 1], [PW, X[0], X[1]], [WX[0], WX[1]], wtmp)
        for tile in range(4):
            for e_ in range(8):
                pb = B.ps()
                for ri in range(2):
                    B.tr(pb.a[:, ri * 128:(ri + 1) * 128],
                         WX[ri].a[:, e_, 4 * tile:4 * tile + 4, :].rearrange("p k c -> p (k c)"), ident_f.a,
                         [WX[ri], ident_f], [pb])
                B.cp("act" if e_ % 2 else "dve", Wt.a[:, tile, e_, :, :],
                     pb.a[:, 0:256].rearrange("p (r n) -> p r n", r=2), [pb], [Wt])
        B.pop()
        B.push()
        CA = [B.sb("CA%d" % i, [128, 9, 16, 16]) for i in range(2)]
        catmp = B.sb("catmp", [128, 2, 9, 16, 16])
        pre9 = PW.a[:, :, 0, :][:, :, :, None].broadcast_to([128, 9, 16, 16])
        pim9 = PW.a[:, :, 1, :][:, :, :, None].broadcast_to([128, 9, 16, 16])
        cre9 = craw[0].a[:, None, :, :].broadcast_to([128, 9, 16, 16])
        cim9 = craw[1].a[:, None, :, :].broadcast_to([128, 9, 16, 16])
        B.cmul("dve", CA[0].a, CA[1].a, pre9, pim9, cre9, cim9, catmp.a[:, 0], catmp.a[:, 1],
               [PW, craw[0], craw[1]], [CA[0], CA[1]], catmp)
        B.memset("pool", CAW.a, 0.0, [CAW])
        for gl in range(2):
            hs = slice(gl * 64, (gl + 1) * 64)
            for tau in range(9):
                for kk in range(2):
                    o = 32 * kk + 16 * gl
                    B.cp("pool", CAW.a[hs, kk::2, 0, tau, o:o + 16], CA[0].a[hs, tau, kk::2], [CA[0]], [CAW])
                    B.ts("pool", CAW.a[hs, kk::2, 1, tau, o:o + 16], CA[1].a[hs, tau, kk::2], -1.0, ALU.mult, [CA[1]], [CAW])
        B.memset("pool", Kbd.a, 0.0, [Kbd])
        k0 = B.sb("k0", [128, 128])
        for tile in range(4):
            pb = B.ps()
            for h in range(2):
                for tau in range(8):
                    n = 0
                    for kk in range(2):
                        q = 4 * tile + 2 * h + kk
                        o = 32 * kk
                        for ri in range(2):
                            B.mm(pb.a[64 * h:64 * h + 64, tau * 32:(tau + 1) * 32], Xb[ri].a[:, q, :],
                                 CAW.a[:, q, ri, tau, o:o + 32], n == 0, n == 3, [Xb[ri], CAW], [pb])
                            n += 1
            for h in range(2):
                hs = slice(64 * h, 64 * h + 64)
                for kk in range(2):
                    cs = slice(64 * h + 32 * kk, 64 * h + 32 * kk + 32)
                    B.ts("dve", Kbd.a[hs, tile, 1:8, cs], pb.a[hs, 32:256].rearrange("p (t c) -> p t c", c=32),
                         mk.a[hs, kk:kk + 1], ALU.mult, [pb, mk], [Kbd])
            B.memset("dve", k0.a, 0.0, [k0])
            for h in range(2):
                hs = slice(64 * h, 64 * h + 64)
                for kk in range(2):
                    cs = slice(64 * h + 32 * kk, 64 * h + 32 * kk + 32)
                    B.ts("dve", k0.a[hs, cs], pb.a[hs, 0:32], mk.a[hs, kk:kk + 1], ALU.mult, [pb, mk], [k0])
            B.stt(k0.a, ident_f.a, dcol.a[:, tile:tile + 1], k0.a, ALU.mult, ALU.add, [ident_f, dcol, k0], [k0])
            B.cp("dve", Kbd.a[:, tile, 0, :], k0.a, [k0], [Kbd])
        B.pop()
        B.pop()
        for nm_, b_ in (("Wt", Wt), ("CAW", CAW), ("Kbd", Kbd), ("MS", MS)):
            if nm_ in dbo:
                S.dma("sp", dbo[nm_].a, b_.a, reads=[b_], writes=[dbo[nm_]], dreg=b_)
        xt = B.sb("xt", [128, 4, D]); hT = B.sb("hT", [128, 8, T1], BF16)
        st = B.sb("st", [128, 12])
        u = B.sb("u", [128, 4, 8, T1 // 8], BF16)
        um = B.sb("um", [128, 4, 4, 8, T1 // 8], BF16)
        hb = B.view(um, um.a.rearrange("p a b c d -> p (a b c d)")[:, 0:4 * D].rearrange("p (s d) -> p s d", s=4))
        NM = T1 // 8
        Dm = B.sb("Dm", [128, 16, 2, NM])
        St = B.sb("St", [128, 16, 2, NM + 1])
        Sb = B.sb("Sb", [128, 16, 2, NM], BF16)
        rt1 = B.sb("rt1", [128, 16, NM]); rt2 = B.sb("rt2", [128, 16, NM]); Dr = B.sb("Dr", [128, 16, 2, NM])
        Qs = B.sb("Qs", [128, 16, 2, NM]); Sfin = B.sb("Sfin", [128, 16, 2]); B.memset("pool", Sfin.a, 0.0, [Sfin])
        y2 = B.sb("y2", [128, 4, T1]); yg = B.sb("yg", [128, 4, T1])
        yy = B.view(Dm, Dm.a.rearrange("p a b c -> p (a b c)").rearrange("p (t n) -> p t n", t=4))
        scr = B.view(y2, y2.a.rearrange("p a b -> p (a b)")[:, 0:D])
        ygb = B.sb("ygb", [128, 4, T1], BF16); sg = y2
        ygl = B.sb("ygl", [128, 4, T1], BF16)
        B.memset("pool", St.a[:, :, :, 0], 0.0, [St])
        for ti in range(ntiles):
            frontend(x, ti, 4, gpre, xt, hb, hT, scr, st)
            for cb in range(4):
                pb = B.ps()
                for c in range(8):
                    B.mm(pb.a, ws5.a[:, c, cb * 128:(cb + 1) * 128], hT.a[:, c, :], c == 0, c == 7, [ws5, hT], [pb])
                pperm = pb.a.rearrange("p (m t) -> p t m", t=8)
                B.cp("act", u.a[:, cb, :, :], pperm, [pb], [u])
                for k in range(4):
                    B.act(um.a[:, k, cb, :, :], pperm, AF.Copy, [pb, mk4], [um], scale=mk4.a[:, k:k + 1])
            for qb in range(4):
                pb = B.ps()
                for qq in range(4):
                    q = 4 * qb + qq
                    tile, k = q // 4, q % 4
                    ks = slice(32 * k, 32 * k + 32)
                    for ri in range(2):
                        col = (qq * 2 + ri) * NM
                        for j0 in range(8):
                            B.mm(pb.a[:, col:col + NM], Wt.a[:, tile, 7 - j0, ri, :], um.a[:, k, tile, j0, :],
                                 j0 == 0, j0 == 7, [Wt, um], [pb])
                B.cp("dve", Dm.a[:, 4 * qb:4 * qb + 4, :, :],
                     pb.a.rearrange("p (q r m) -> p q r m", q=4, r=2), [pb], [Dm])
            erb = ER.a[:, :, :]; eib = EI.a[:, :, :]
            B.tt("dve", rt1.a, erb, Dm.a[:, :, 0, :], ALU.mult, [ER, Dm], [rt1])
            B.tt("dve", rt2.a, eib, Dm.a[:, :, 1, :], ALU.mult, [EI, Dm], [rt2])
            B.tt("dve", Dr.a[:, :, 0, :], rt1.a, rt2.a, ALU.add, [rt1, rt2], [Dr])
            B.tt("dve", rt1.a, erb, Dm.a[:, :, 1, :], ALU.mult, [ER, Dm], [rt1])
            B.tt("dve", rt2.a, eib, Dm.a[:, :, 0, :], ALU.mult, [EI, Dm], [rt2])
            B.tt("dve", Dr.a[:, :, 1, :], rt1.a, rt2.a, ALU.subtract, [rt1, rt2], [Dr])
            for q in range(16):
                for ri in range(2):
                    B.op("dve", lambda e, q=q, ri=ri: e.tensor_tensor_scan(
                        out=Qs.a[:, q, ri, :], data0=R8.a[:, q:q + 1].broadcast_to([128, NM]), data1=Dr.a[:, q, ri, :],
                        initial=Sfin.a[:, q, ri:ri + 1], op0=ALU.mult, op1=ALU.add), [R8, Dr, Sfin], [Qs])
            B.tt("dve", rt1.a, erb, Qs.a[:, :, 0, :], ALU.mult, [ER, Qs], [rt1])
            B.tt("dve", rt2.a, eib, Qs.a[:, :, 1, :], ALU.mult, [EI, Qs], [rt2])
            B.tt("dve", St.a[:, :, 0, 1:NM + 1], rt1.a, rt2.a, ALU.subtract, [rt1, rt2], [St])
            B.tt("dve", rt1.a, erb, Qs.a[:, :, 1, :], ALU.mult, [ER, Qs], [rt1])
            B.tt("dve", rt2.a, eib, Qs.a[:, :, 0, :], ALU.mult, [EI, Qs], [rt2])
            B.tt("dve", St.a[:, :, 1, 1:NM + 1], rt1.a, rt2.a, ALU.add, [rt1, rt2], [St])
            B.cp("act", Sb.a, St.a[:, :, :, 0:NM], [St], [Sb])
            B.cp("act", Sfin.a, St.a[:, :, :, NM], [St], [Sfin])
            B.cp("pool", St.a[:, :, :, 0], St.a[:, :, :, NM], [St], [St])
            for tile in range(4):
                pb = B.ps()
                for h in range(2):
                    hs = slice(64 * h, 64 * h + 64)
                    for t0_ in range(8):
                        n = 0
                        for kk in range(2):
                            q = 4 * tile + 2 * h + kk
                            for ri in range(2):
                                B.mm(pb.a[hs, t0_ * NM:(t0_ + 1) * NM], CAW.a[:, q, ri, t0_ + 1, :], Sb.a[:, q, ri, :], n == 0, n == 3,
                                     [CAW, Sb], [pb], skip_group_check=True)
                                n += 1
                B.cp("act", y2.a[:, tile, :], pb.a, [pb], [y2])
                pb = B.ps()
                for t0o in range(8):
                    for tau in range(t0o + 1):
                        B.mm(pb.a[:, t0o * NM:(t0o + 1) * NM], Kbd.a[:, tile, tau, :], u.a[:, tile, t0o - tau, :],
                             tau == 0, tau == t0o, [Kbd, u], [pb])
                B.tt("dve", yy.a[:, tile, :].rearrange("p (m t) -> p t m", t=8), pb.a.rearrange("p (t m) -> p t m", t=8),
                     y2.a[:, tile, :].rearrange("p (t m) -> p t m", t=8), ALU.add, [pb, y2], [yy])
            if "s5y" in dbo:
                S.dma("sp", dbo["s5y"].a[:, :, ti * T1:(ti + 1) * T1], yy.a, reads=[yy], writes=[dbo["s5y"]], dreg=yy)
            for tile in range(4):
                B.act(yg.a[:, tile, :], yy.a[:, tile, :], AF.Gelu_apprx_tanh, [yy], [yg])
            B.cp("act", ygb.a, yg.a, [yg], [ygb])
            for cb in range(4):
                pb = B.ps()
                for c in range(4):
                    B.mm(pb.a, wglu.a[:, c, cb * 128:(cb + 1) * 128], ygb.a[:, c, :], c == 0, c == 3, [wglu, ygb], [pb])
                B.act(sg.a[:, cb, :], pb.a, AF.Sigmoid, [pb, bglu], [sg], bias=bglu.a[:, cb:cb + 1])
                B.tt("dve", ygl.a[:, cb, :], yg.a[:, cb, :], sg.a[:, cb, :], ALU.mult, [yg, sg], [ygl])
            S.dma("sp", YGLU.a[:, :, ti * T1:(ti + 1) * T1], ygl.a, reads=[ygl], writes=[YGLU], dreg=ygl)
            if "yglu" in dbo:
                S.dma("sp", dbo["yglu"].a[:, :, ti * T1:(ti + 1) * T1], ygl.a, reads=[ygl], writes=[dbo["yglu"]], dreg=ygl)
        B.pop()

    if "1b" in phases:
        phase_1b(B, W, x, YR, gpre, ident_f, ident_b, frontend, bc_load, col_load, dbo, ntiles * (T1 // 128))

    if "1c" in phases:
        phase_1c(B, W, x, x1s, YR, YGLU, gpre, frontend, bc_load, col_load, dbo, ntiles)

    B.pop()
    if "2" in phases:
        phase_2(B, W, x1s, out, frontend, bc_load, dbo, ntiles)

    S.final_wait("sp", list(B.dout.values()))
    B.Wshapes = shapes
    return B


def phase_1b(B, W, x, YR, gpre, ident_f, ident_b, frontend, bc_load, col_load, dbo, ntiles):
    nc, S = B.nc, B.S
    TB = 128
    C = 128
    NW = 3
    C0 = math.exp(-0.5)
    B.push()
    wrg = B.sb("wrg", [128, 8, NR], BF16)
    win = W["w_in"].a.rearrange("(c p) n -> p c n", p=128)
    for c in range(8):
        S.dma("pool", wrg.a[:, c, :], win[:, c, 0:NR], reads=[W["w_in"]], writes=[wrg])
    w2p = B.sb("w2p", [128, 512], BF16); a2p = B.sb("a2p", [128, 512], BF16); g2b = B.sb("g2b", [128, 512], BF16)
    B.memset("pool", w2p.a, 0.0, [w2p]); B.memset("pool", a2p.a, 0.0, [a2p])
    S.dma("pool", w2p.a[0:64, :], W["rwkv_w2"].a, reads=[W["rwkv_w2"]], writes=[w2p])
    S.dma("pool", a2p.a[64:128, :], W["rwkv_a2"].a, reads=[W["rwkv_a2"]], writes=[a2p])
    S.dma("pool", g2b.a, W["rwkv_g2"].a, reads=[W["rwkv_g2"]], writes=[g2b])
    mu = col_load("rwkv_shift_mu", 14); w0c = col_load("rwkv_w0", 4); a0c = col_load("rwkv_a0", 4)
    kkc = col_load("rwkv_k_k", 4); kac = col_load("rwkv_k_a", 4); rkc = col_load("rwkv_r_k", 4)
    lwc = col_load("rwkv_lnx_w", 4); lbc = col_load("rwkv_lnx_b", 4)
    omu = B.sb("omu", [128, 14]); oka = B.sb("oka", [128, 4])
    B.ts("dve", omu.a, mu.a, -1.0, ALU.mult, [mu], [omu], s2=1.0, op1=ALU.add)
    B.ts("dve", oka.a, kac.a, -1.0, ALU.mult, [kac], [oka], s2=1.0, op1=ALU.add)
    bones = B.sb("bones", [128, 128]); bavg = B.sb("bavg", [128, 128]); ones = B.sb("ones", [128, 128])
    B.memset("pool", ones.a, 1.0, [ones])
    B.memset("pool", bones.a, 0.0, [bones])
    B.memset("pool", bones.a[0:64, 0:64], 1.0, [bones]); B.memset("pool", bones.a[64:128, 64:128], 1.0, [bones])
    B.ts("pool", bavg.a, bones.a, 1.0 / 64, ALU.mult, [bones], [bavg])
    hm = B.sb("hm", [128, 2])
    B.memset("pool", hm.a, 0.0, [hm]); B.memset("pool", hm.a[0:64, 0:1], 1.0, [hm]); B.memset("pool", hm.a[64:128, 1:2], 1.0, [hm])
    mf = B.sb("mf", [128, 128])
    MU4 = B.sb("MU4", [128, 4, 128], BF16); MI4 = B.sb("MI4", [128, 4, 128], BF16)
    ML4 = B.sb("ML4", [128, 4, 128], BF16); I4 = B.sb("I4", [128, 4, 128], BF16)
    for (mt, pat, cm, cop) in ((MU4, 1, -1, ALU.is_gt), (MI4, 1, -1, ALU.is_ge), (ML4, -1, 1, ALU.is_gt)):
        B.memset("pool", mf.a, 1.0, [mf])
        B.op("pool", lambda e, pat=pat, cm=cm, cop=cop: e.affine_select(out=mf.a, in_=mf.a, pattern=[[pat, 128]], compare_op=cop,
                                                                      fill=0.0, base=0, channel_multiplier=cm), [mf], [mf])
        for h in range(4):
            B.cp("pool", mt.a[:, h, :], mf.a, [mf], [mt])
    for h in range(4):
        B.cp("pool", I4.a[:, h, :], ident_f.a, [ident_f], [I4])
    pc = B.sb("pc", [128, 14]); B.memset("pool", pc.a, 0.0, [pc])
    H32 = B.sb("H32", [128, 4, 64]); Hb = B.sb("Hb", [128, 4, 64], BF16); Ht = B.sb("Ht", [128, 4, 64])
    B.memset("pool", H32.a, 0.0, [H32]); B.memset("pool", Hb.a, 0.0, [Hb])

    def bc4(col):
        return col.a[:, :, None].broadcast_to([128, 4, TB])

    class BS:
        pass

    sets = []
    for w in range(NW):
        b = BS()
        f4 = lambda nm: B.sb(nm + str(w), [128, 4, TB])
        h4 = lambda nm: B.sb(nm + str(w), [128, 4, TB], BF16)
        b.xt = B.sb("xt%d" % w, [128, 1, D]); b.hb = B.sb("hb%d" % w, [128, 1, D], BF16); b.hT = B.sb("hT%d" % w, [128, 8, TB], BF16)
        b.st = B.sb("st%d" % w, [128, 6])
        b.PS = B.sb("PS%d" % w, [128, 14, TB]); b.t1 = B.sb("t1%d" % w, [128, 4, TB]); b.t2 = B.sb("t2%d" % w, [128, 4, TB])
        b.scr = B.view(b.PS, b.PS.a.rearrange("p a b -> p (a b)")[:, 0:D])
        b.twa = B.sb("twa%d" % w, [128, TB], BF16); b.sgd = B.sb("sgd%d" % w, [128, TB], BF16)
        b.sgw = f4("sgw"); b.asg = f4("asg"); b.gg = f4("gg"); b.kk = f4("kk"); b.tq = f4("tq"); b.kmod = f4("kmod")
        b.cum = f4("cum"); b.eg = b.t1; b.egx = f4("egx"); b.eng = b.t2; b.bon = f4("bon")
        b.am = B.sb("am%d" % w, [128, 4, 2, TB], BF16); b.rm = B.sb("rm%d" % w, [128, 4, 2, TB], BF16)
        b.bf = h4("bf"); b.kf = h4("kf"); b.vb = h4("vb")
        b.Btok = B.sb("Btok%d" % w, [128, 512], BF16); b.Ktok = B.sb("Ktok%d" % w, [128, 512], BF16); b.Vtok = B.sb("Vtok%d" % w, [128, 512], BF16)
        for nm, src in (("Pm", b.sgw), ("Qm", b.asg), ("Rm", b.kk), ("Nak", b.kmod), ("Nrb", b.egx), ("Nrk", b.eng)):
            setattr(b, nm, B.view(src, src.a.rearrange("p a b -> p (a b)").bitcast(BF16).rearrange("p (h t) -> p h t", h=8)))
        hTf = b.hT.a.rearrange("p a b -> p (a b)")
        b.Xb = B.view(b.hT, hTf[:, 0:512]); b.Ub = B.view(b.hT, hTf[:, 512:1024])
        b.Yf = B.view(b.xt, b.xt.a.rearrange("p a b -> p (a b)")[:, 0:4 * TB].rearrange("p (j t) -> p j t", j=4))
        b.dd = B.view(b.hb, b.hb.a.rearrange("p a b -> p (a b)").bitcast(F32).rearrange("p (j t) -> p j t", j=4))
        b.yrb = h4("yrb")
        sets.append(b)

    def chunk(ci):
        b = sets[ci % NW]
        PS, tq, cum, kk, kmod, eg, egx, eng, asg, sgw, gg, bon = b.PS, b.tq, b.cum, b.kk, b.kmod, b.eg, b.egx, b.eng, b.asg, b.sgw, b.gg, b.bon
        am, rm, Pm, Qm, Rm, Nak, Nrb, Nrk = b.am, b.rm, b.Pm, b.Qm, b.Rm, b.Nak, b.Nrb, b.Nrk
        frontend(x, ci, 1, gpre, b.xt, b.hb, b.hT, b.scr, b.st)
        yield
        for j0 in range(0, 14, 4):
            nj = min(4, 14 - j0)
            pb = B.ps()
            for jj in range(nj):
                for c in range(8):
                    B.mm(pb.a[:, jj * TB:(jj + 1) * TB], wrg.a[:, c, (j0 + jj) * 128:(j0 + jj + 1) * 128], b.hT.a[:, c, :],
                         c == 0, c == 7, [wrg, b.hT], [pb])
            pv = pb.a[:, 0:nj * TB].rearrange("p (j t) -> p j t", j=nj)
            om_b = omu.a[:, j0:j0 + nj, None].broadcast_to([128, nj, TB])
            mu_b = mu.a[:, j0:j0 + nj, None].broadcast_to([128, nj, TB - 1])
            B.tt("dve", b.t1.a[:, 0:nj, :], pv, om_b, ALU.mult, [pb, omu], [b.t1])
            B.tt("dve", b.t2.a[:, 0:nj, 1:TB], pv[:, :, 0:TB - 1], mu_b, ALU.mult, [pb, mu], [b.t2])
            B.tt("dve", b.t2.a[:, 0:nj, 0:1], pc.a[:, j0:j0 + nj, None], mu.a[:, j0:j0 + nj, None], ALU.mult, [pc, mu], [b.t2])
            B.cp("act", pc.a[:, j0:j0 + nj, None], pv[:, :, TB - 1:TB], [pb], [pc])
            B.tt("pool", PS.a[:, j0:j0 + nj, :], b.t1.a[:, 0:nj, :], b.t2.a[:, 0:nj, :], ALU.add, [b.t1, b.t2], [PS])
            yield
        if "pshift" in dbo:
            S.dma("sp", dbo["pshift"].a[:, :, ci * TB:(ci + 1) * TB], PS.a, reads=[PS], writes=[dbo["pshift"]], dreg=PS)
        r_ = PS.a[:, 0:4, :]; k_ = PS.a[:, 4:8, :]; v_ = PS.a[:, 8:12, :]
        B.act(b.twa.a[0:64, :], PS.a[0:64, 12, :], AF.Tanh, [PS], [b.twa])
        B.cp("act", b.twa.a[64:128, :], PS.a[64:128, 12, :], [PS], [b.twa])
        B.act(b.sgd.a, PS.a[:, 13, :], AF.Sigmoid, [PS], [b.sgd])
        pw_ = B.ps()
        for j in range(4):
            B.mm(pw_.a[:, j * TB:(j + 1) * TB], w2p.a[:, j * 128:(j + 1) * 128], b.twa.a, True, True, [w2p, b.twa], [pw_])
        for j in range(4):
            B.act(sgw.a[:, j, :], pw_.a[:, j * TB:(j + 1) * TB], AF.Sigmoid, [pw_, w0c], [sgw], bias=w0c.a[:, j:j + 1])
        pa_ = B.ps()
        for j in range(4):
            B.mm(pa_.a[:, j * TB:(j + 1) * TB], a2p.a[:, j * 128:(j + 1) * 128], b.twa.a, True, True, [a2p, b.twa], [pa_])
        for j in range(4):
            B.act(asg.a[:, j, :], pa_.a[:, j * TB:(j + 1) * TB], AF.Sigmoid, [pa_, a0c], [asg], bias=a0c.a[:, j:j + 1])
        pg_ = B.ps()
        for j in range(4):
            B.mm(pg_.a[:, j * TB:(j + 1) * TB], g2b.a[:, j * 128:(j + 1) * 128], b.sgd.a, True, True, [g2b, b.sgd], [pg_])
        B.cp("act", gg.a, pg_.a.rearrange("p (j t) -> p j t", j=4), [pg_], [gg])
        yield
        B.tt("dve", kk.a, k_, bc4(kkc), ALU.mult, [PS, kkc], [kk])
        B.tt("pool", tq.a, kk.a, kk.a, ALU.mult, [kk], [tq])
        pb = B.ps()
        for j in range(4):
            B.mm(pb.a[:, j * TB:(j + 1) * TB], bones.a, tq.a[:, j, :], True, True, [bones, tq], [pb])
        B.act(cum.a, pb.a.rearrange("p (j t) -> p j t", j=4), AF.Sqrt, [pb], [cum])
        B.ts("dve", cum.a, cum.a, 1e-12, ALU.max, [cum], [cum])
        B.op("dve", lambda e: e.reciprocal(cum.a, cum.a), [cum], [cum])
        B.tt("pool", kk.a, kk.a, cum.a, ALU.mult, [kk, cum], [kk])
        yield
        B.tt("dve", tq.a, asg.a, bc4(kac), ALU.mult, [asg, kac], [tq])
        B.tt("dve", tq.a, tq.a, bc4(oka), ALU.add, [tq, oka], [tq])
        B.tt("dve", kmod.a, k_, tq.a, ALU.mult, [PS, tq], [kmod])
        B.tt("pool", tq.a, r_, kmod.a, ALU.mult, [PS, kmod], [tq])
        B.tt("dve", tq.a, tq.a, bc4(rkc), ALU.mult, [tq, rkc], [tq])
        pbon = B.ps()
        for j in range(4):
            B.mm(pbon.a[:, j * TB:(j + 1) * TB], bones.a, tq.a[:, j, :], True, True, [bones, tq], [pbon])
        B.tt("dve", bon.a, pbon.a.rearrange("p (j t) -> p j t", j=4), v_, ALU.mult, [pbon, PS], [bon])
        yield
        for j in range(4):
            B.op("dve", lambda e, j=j: e.tensor_tensor_scan(out=cum.a[:, j, :], data0=ones.a, data1=sgw.a[:, j, :], initial=0.0,
                                                            op0=ALU.mult, op1=ALU.add), [ones, sgw], [cum])
        B.act(eg.a, cum.a, AF.Exp, [cum], [eg], scale=-C0)
        B.act(eng.a, cum.a, AF.Exp, [cum], [eng], scale=C0)
        B.tt("pool", tq.a, cum.a, sgw.a, ALU.subtract, [cum, sgw], [tq])
        B.act(egx.a, tq.a, AF.Exp, [tq], [egx], scale=-C0)
        yield
        B.stt(tq.a, kk.a, -1.0, egx.a, ALU.mult, ALU.mult, [kk, egx], [tq])
        for hh in range(2):
            B.act(am.a[:, :, hh, :], tq.a, AF.Copy, [tq, hm], [am], scale=hm.a[:, hh:hh + 1])
        B.tt("dve", egx.a, r_, eg.a, ALU.mult, [PS, eg], [egx])
        for hh in range(2):
            B.act(rm.a[:, :, hh, :], egx.a, AF.Copy, [egx, hm], [rm], scale=hm.a[:, hh:hh + 1])
        B.tt("pool", tq.a, kk.a, asg.a, ALU.mult, [kk, asg], [tq])
        B.tt("dve", b.bf.a, tq.a, eng.a, ALU.mult, [tq, eng], [b.bf])
        B.tt("dve", b.kf.a, kmod.a, eng.a, ALU.mult, [kmod, eng], [b.kf])
        B.cp("act", b.vb.a, v_, [PS], [b.vb])
        yield
        cs = slice(0, C)
        for n_, (src, dst) in enumerate(((b.bf, b.Btok), (b.kf, b.Ktok), (b.vb, b.Vtok))):
            pb = B.ps()
            pv = pb.a.bitcast(BF16)
            for j in range(4):
                B.tr(pv[:, j * 128:(j + 1) * 128], src.a[:, j, cs], ident_b.a, [src, ident_b], [pb])
            B.cp("act" if n_ != 1 else "dve", dst.a, pv[:, 0:512], [pb], [dst])
        yield
        for g in range(2):
            gs = slice(4 * g, 4 * g + 4)
            for kind in range(5):
                pbk = B.ps()
                for hq in range(4):
                    h = 4 * g + hq
                    j, hh = h // 2, h % 2
                    o = slice(hq * 128, (hq + 1) * 128)
                    bfs, kfs, ams, rms = b.bf.a[:, j, cs], b.kf.a[:, j, cs], am.a[:, j, hh, cs], rm.a[:, j, hh, cs]
                    lhsT, rhs, rd = ((bfs, ams, [b.bf, am]), (ams, bfs, [b.bf, am]), (kfs, ams, [b.kf, am]),
                                     (bfs, rms, [b.bf, rm]), (kfs, rms, [b.kf, rm]))[kind]
                    B.mm(pbk.a[:, o], lhsT, rhs, True, True, rd, [pbk])
                msk, dst = ((MU4, Pm), (ML4, Qm), (MU4, Nak), (MI4, Nrb), (MI4, Nrk))[kind]
                B.tt("dve", dst.a[:, gs, :], pbk.a.rearrange("p (h t) -> p h t", h=4), msk.a, ALU.mult, [pbk, msk], [dst])
            B.tt("pool", Rm.a[:, gs, :], Pm.a[:, gs, :], I4.a, ALU.add, [Pm, I4], [Rm])
            yield
        for lvl in range(1, 7):
            for g in range(2):
                gs = slice(4 * g, 4 * g + 4)
                pq = B.ps()
                pp = B.ps() if lvl < 6 else None
                for hq in range(4):
                    h = 4 * g + hq
                    o = slice(hq * 128, (hq + 1) * 128)
                    B.mm(pq.a[:, o], Pm.a[:, h, :], Qm.a[:, h, :], True, True, [Pm, Qm], [pq])
                    if pp is not None:
                        B.mm(pp.a[:, o], Qm.a[:, h, :], Pm.a[:, h, :], True, True, [Pm, Qm], [pp])
                B.cp("act", Qm.a[:, gs, :], pq.a.rearrange("p (h t) -> p h t", h=4), [pq], [Qm])
                if pp is not None:
                    B.cp("act", Pm.a[:, gs, :], pp.a.rearrange("p (h t) -> p h t", h=4), [pp], [Pm])
                pr = B.ps()
                for hq in range(4):
                    h = 4 * g + hq
                    o = slice(hq * 128, (hq + 1) * 128)
                    B.mm(pr.a[:, o], Qm.a[:, h, :], Rm.a[:, h, :], True, True, [Qm, Rm], [pr])
                B.tt("dve", Rm.a[:, gs, :], pr.a.rearrange("p (h t) -> p h t", h=4), Rm.a[:, gs, :], ALU.add, [pr, Rm], [Rm])
                yield
        px = B.ps()
        for h in range(8):
            j, hh = h // 2, h % 2
            o = slice(h * 64, (h + 1) * 64)
            B.mm(px.a[:, o], am.a[:, j, hh, cs], Hb.a[:, j, :], True, False, [am, Hb], [px])
            B.mm(px.a[:, o], Nak.a[:, h, :], b.Vtok.a[:, o], False, True, [Nak, b.Vtok], [px])
        B.cp("act", b.Xb.a, px.a, [px], [b.Xb])
        pu = B.ps()
        for h in range(8):
            o = slice(h * 64, (h + 1) * 64)
            B.mm(pu.a[:, o], Rm.a[:, h, :], b.Xb.a[:, o], True, True, [Rm, b.Xb], [pu])
        B.cp("act", b.Ub.a, pu.a, [pu], [b.Ub])
        ph = B.ps()
        for h in range(8):
            j, hh = h // 2, h % 2
            o = slice(h * 64, (h + 1) * 64)
            ho = ph.a[hh * 64:(hh + 1) * 64, j * 64:(j + 1) * 64]
            B.mm(ho, b.Btok.a[:, o], b.Ub.a[:, o], True, False, [b.Btok, b.Ub], [ph])
            B.mm(ho, b.Ktok.a[:, o], b.Vtok.a[:, o], False, True, [b.Ktok, b.Vtok], [ph])
        py = B.ps()
        for h in range(8):
            j, hh = h // 2, h % 2
            o = slice(h * 64, (h + 1) * 64)
            yo = py.a[hh * 64:(hh + 1) * 64, j * 128:(j + 1) * 128]
            B.mm(yo, Hb.a[:, j, :], rm.a[:, j, hh, cs], True, False, [Hb, rm], [py])
            B.mm(yo, b.Ub.a[:, o], Nrb.a[:, h, :], False, False, [b.Ub, Nrb], [py])
            B.mm(yo, b.Vtok.a[:, o], Nrk.a[:, h, :], False, True, [b.Vtok, Nrk], [py])
        B.tt("dve", Ht.a, ph.a[:, 0:256].rearrange("p (j i) -> p j i", j=4), H32.a, ALU.add, [ph, H32], [Ht])
        gC = eg.a[:, :, C - 1:C].broadcast_to([128, 4, 64])
        B.tt("dve", H32.a, Ht.a, gC, ALU.mult, [Ht, eg], [H32])
        B.cp("act", Hb.a, H32.a, [H32], [Hb])
        B.cp("act", b.Yf.a, py.a.rearrange("p (j t) -> p j t", j=4), [py], [b.Yf])
        yield
        if "wkv" in dbo:
            S.dma("sp", dbo["wkv"].a[:, :, ci * TB:(ci + 1) * TB], b.Yf.a, reads=[b.Yf], writes=[dbo["wkv"]], dreg=b.Yf)
        Yf, dd = b.Yf, b.dd
        pm_ = B.ps()
        for j in range(4):
            B.mm(pm_.a[:, j * TB:(j + 1) * TB], bavg.a, Yf.a[:, j, :], True, True, [bavg, Yf], [pm_])
        B.tt("dve", dd.a, Yf.a, pm_.a.rearrange("p (j t) -> p j t", j=4), ALU.subtract, [Yf, pm_], [dd])
        B.act(tq.a, dd.a, AF.Square, [dd], [tq])
        pv_ = B.ps()
        for j in range(4):
            B.mm(pv_.a[:, j * TB:(j + 1) * TB], bavg.a, tq.a[:, j, :], True, True, [bavg, tq], [pv_])
        B.ts("dve", cum.a, pv_.a.rearrange("p (j t) -> p j t", j=4), 64e-5, ALU.add, [pv_], [cum])
        B.act(cum.a, cum.a, AF.Sqrt, [cum], [cum])
        B.op("dve", lambda e: e.reciprocal(cum.a, cum.a), [cum], [cum])
        yield
        B.tt("pool", dd.a, dd.a, cum.a, ALU.mult, [dd, cum], [dd])
        B.tt("dve", dd.a, dd.a, bc4(lwc), ALU.mult, [dd, lwc], [dd])
        B.tt("dve", dd.a, dd.a, bc4(lbc), ALU.add, [dd, lbc], [dd])
        B.tt("pool", dd.a, dd.a, bon.a, ALU.add, [dd, bon], [dd])
        B.tt("dve", b.yrb.a, dd.a, gg.a, ALU.mult, [dd, gg], [b.yrb])
        if "rwkv_y" in dbo:
            S.dma("sp", dbo["rwkv_y"].a[:, :, ci * TB:(ci + 1) * TB], b.yrb.a, reads=[b.yrb], writes=[dbo["rwkv_y"]], dreg=b.yrb)
        S.dma("sp", YR.a[:, :, ci * TB:(ci + 1) * TB], b.yrb.a, reads=[b.yrb], writes=[YR], dreg=b.yrb)
        yield

    pools = [[0, 1, 2], [3, 4, 5], [6, 7]] if NW == 3 else ([[0, 1, 2, 3], [4, 5, 6, 7]] if NW == 2 else [list(range(8))])
    S.rec = []
    for ci in range(ntiles):
        B.ps_pool = pools[ci % NW]
        for _ in chunk(ci):
            pass
    items = S.rec
    S.rec = None
    B.ps_pool = None
    S.schedule_emit(items, window=int(os.environ.get("K_WIN", "1400")))
    B.pop()


def phase_1c(B, W, x, x1s, YR, YGLU, gpre, frontend, bc_load, col_load, dbo, ntiles):
    nc, S = B.nc, B.S
    TB = 512
    NS = TB // 128
    B.push()
    wg = B.sb("wg", [128, 8, 2048], BF16)
    win = W["w_in"].a.rearrange("(c p) n -> p c n", p=128)
    for c in range(8):
        S.dma("pool", wg.a[:, c, :], win[:, c, NR + 512:NIN], reads=[W["w_in"]], writes=[wg])
    wbr = B.sb("wbr", [128, 4, D], BF16); wbs = B.sb("wbs", [128, 4, D], BF16); wout = B.sb("wout", [128, 8, D], BF16)
    S.dma("pool", wbr.a, W["w_branch_rwkv"].a.rearrange("(c p) n -> p c n", p=128), reads=[W["w_branch_rwkv"]], writes=[wbr])
    S.dma("pool", wbs.a, W["w_branch_s5"].a.rearrange("(c p) n -> p c n", p=128), reads=[W["w_branch_s5"]], writes=[wbs])
    for c in range(8):
        S.dma("pool", wout.a[:, c, :], W["w_out"].a[c * 128:(c + 1) * 128, :], reads=[W["w_out"]], writes=[wout])
    gpost = bc_load("norm_mix_post", D)
    bgc = col_load("b_gate", 16)
    class BS:
        pass
    sets = []
    for w in range(2):
        q = BS()
        q.xt = B.sb("xt%d" % w, [128, NS, D]); q.hb = B.sb("hb%d" % w, [128, NS, D], BF16); q.hT = B.sb("hT%d" % w, [128, 8, TB], BF16)
        q.scr = B.sb("scr%d" % w, [128, D]); q.st = B.sb("st%d" % w, [128, 12])
        q.yrt = B.sb("yrt%d" % w, [128, 4, TB], BF16); q.ygt = B.sb("ygt%d" % w, [128, 4, TB], BF16)
        q.mixb = B.sb("mixb%d" % w, [128, 8, TB], BF16)
        sets.append(q)
    gA = B.sb("gA", [128, TB]); gB = B.sb("gB", [128, TB]); mt1 = B.sb("mt1", [128, TB]); mt2 = B.sb("mt2", [128, TB])

    def part_a(ti):
        q = sets[ti % 2]
        frontend(x, ti, NS, gpre, q.xt, q.hb, q.hT, q.scr, q.st)
        S.dma("sp", q.yrt.a, YR.a[:, :, ti * TB:(ti + 1) * TB], reads=[YR], writes=[q.yrt])
        S.dma("sp", q.ygt.a, YGLU.a[:, :, ti * TB:(ti + 1) * TB], reads=[YGLU], writes=[q.ygt])

    def part_b(ti):
        q = sets[ti % 2]
        xt, hb, hT, scr, st, yrt, ygt, mixb = q.xt, q.hb, q.hT, q.scr, q.st, q.yrt, q.ygt, q.mixb
        for cb in range(8):
            pa = B.ps(); pbb = B.ps(); po = B.ps(); ps_ = B.ps()
            for c in range(8):
                B.mm(pa.a, wg.a[:, c, cb * 128:(cb + 1) * 128], hT.a[:, c, :], c == 0, c == 7, [wg, hT], [pa])
            for c in range(8):
                B.mm(pbb.a, wg.a[:, c, (8 + cb) * 128:(9 + cb) * 128], hT.a[:, c, :], c == 0, c == 7, [wg, hT], [pbb])
            for j in range(4):
                B.mm(po.a, wbr.a[:, j, cb * 128:(cb + 1) * 128], yrt.a[:, j, :], j == 0, j == 3, [wbr, yrt], [po])
            for j in range(4):
                B.mm(ps_.a, wbs.a[:, j, cb * 128:(cb + 1) * 128], ygt.a[:, j, :], j == 0, j == 3, [wbs, ygt], [ps_])
            B.act(gA.a, pa.a, AF.Sigmoid, [pa, bgc], [gA], bias=bgc.a[:, cb:cb + 1])
            B.act(gB.a, pbb.a, AF.Sigmoid, [pbb, bgc], [gB], bias=bgc.a[:, 8 + cb:9 + cb])
            B.tt("dve", mt1.a, po.a, gA.a, ALU.mult, [po, gA], [mt1])
            B.tt("dve", mt2.a, ps_.a, gB.a, ALU.mult, [ps_, gB], [mt2])
            B.tt("pool", mixb.a[:, cb, :], mt1.a, mt2.a, ALU.add, [mt1, mt2], [mixb])

    def part_c(ti):
        q = sets[ti % 2]
        xt, hb, hT, scr, st, yrt, ygt, mixb = q.xt, q.hb, q.hT, q.scr, q.st, q.yrt, q.ygt, q.mixb
        for s_ in range(NS):
            pbs = [B.ps(), B.ps()]
            for half in range(2):
                for c8 in range(8):
                    B.mm(pbs[half].a, mixb.a[:, c8, s_ * 128:(s_ + 1) * 128], wout.a[:, c8, half * 512:(half + 1) * 512],
                         c8 == 0, c8 == 7, [mixb, wout], [pbs[half]])
            for half in range(2):
                B.act(scr.a[:, 0:512], pbs[half].a, AF.Square, [pbs[half]], [scr, st], accum_out=st.a[:, half:half + 1])
            B.tt("dve", st.a[:, 2:3], st.a[:, 0:1], st.a[:, 1:2], ALU.add, [st], [st])
            B.ts("dve", st.a[:, 2:3], st.a[:, 2:3], 1.0 / D, ALU.mult, [st], [st], s2=1e-6, op1=ALU.add)
            B.act(st.a[:, 2:3], st.a[:, 2:3], AF.Sqrt, [st], [st])
            B.op("dve", lambda e: e.reciprocal(st.a[:, 3:4], st.a[:, 2:3]), [st], [st])
            for half in range(2):
                hsl = slice(half * 512, (half + 1) * 512)
                B.stt(scr.a[:, hsl], pbs[half].a, st.a[:, 3:4], gpost.a[:, hsl], ALU.mult, ALU.mult, [pbs[half], st, gpost], [scr])
            B.tt("pool", xt.a[:, s_, :], xt.a[:, s_, :], scr.a, ALU.add, [xt, scr], [xt])
        S.dma("sp", x1s.a[ti * TB:(ti + 1) * TB, :].rearrange("(s p) d -> p s d", p=128), xt.a, reads=[xt], writes=[x1s], dreg=xt)
        if "x1" in dbo:
            S.dma("sp", dbo["x1"].a[ti * TB:(ti + 1) * TB, :].rearrange("(s p) d -> p s d", p=128), xt.a, reads=[xt],
                  writes=[dbo["x1"]], dreg=xt)

    part_a(0)
    for ti in range(ntiles):
        part_b(ti)
        if ti + 1 < ntiles:
            part_a(ti + 1)
        part_c(ti)
    B.pop()


def phase_2(B, W, x1s, out, frontend, bc_load, dbo, ntiles):
    nc, S = B.nc, B.S
    TB = 512
    NS = TB // 128
    ntiles = ntiles * (512 // TB)
    B.push()
    wup = B.sb("wup", [128, 8, 2 * FF], BF16)
    wsrc = W["ffn_w_up"].a.rearrange("(c p) n -> p c n", p=128)
    for c in range(8):
        for (a, b) in ((0, 2048), (2048, 4096), (4096, 2 * FF)):
            S.dma("pool", wup.a[:, c, a:b], wsrc[:, c, a:b], reads=[W["ffn_w_up"]], writes=[wup])
    wdn = B.sb("wdn", [128, 22, D], BF16)
    for i in range(22):
        S.dma("pool", wdn.a[:, i, :], W["ffn_w_down"].a[i * 128:(i + 1) * 128, :], reads=[W["ffn_w_down"]], writes=[wdn])
    g1 = bc_load("norm_ffn_pre", D)
    g2 = bc_load("norm_ffn_post", D)
    cw = B.sb("cw", [128, 3, 44]); cbias = B.sb("cbias", [128, 44])
    S.dma("sp", cw.a, W["ffn_conv_w"].a.rearrange("j (b p) -> p j b", p=128), reads=[W["ffn_conv_w"]], writes=[cw],
          allow_slow_non_contiguous=True)
    S.dma("sp", cbias.a, W["ffn_conv_b"].a.rearrange("(b p) -> p b", p=128), reads=[W["ffn_conv_b"]], writes=[cbias],
          allow_slow_non_contiguous=True)
    halo = B.sb("halo", [128, 44, 2]); B.memset("pool", halo.a, 0.0, [halo])
    class BS:
        pass
    sets = []
    for w in range(1):
        q = BS()
        q.xt = B.sb("xt2_%d" % w, [128, NS, D]); q.hT = B.sb("hT2_%d" % w, [128, 8, TB], BF16)
        q.actb = B.sb("actb%d" % w, [128, 22, TB], BF16)
        q.hb = B.view(q.actb, q.actb.a.rearrange("p a b -> p (a b)")[:, 0:NS * D].rearrange("p (s d) -> p s d", s=NS))
        q.st = B.sb("st2_%d" % w, [128, 12])
        sets.append(q)
    scrF_ = B.sb("scrF", [128, D])
    a0s_ = (B.sb("a0g", [128, TB]), B.sb("a0v", [128, TB]))
    a1s_ = (B.sb("a1g", [128, TB]), B.sb("a1v", [128, TB]))
    for q in sets:
        q.scrF, q.a0s, q.a1s = scrF_, a0s_, a1s_
    def tile(ti):
        q = sets[0]
        xt, hT, actb, hb, st, scrF, a0s, a1s = q.xt, q.hT, q.actb, q.hb, q.st, q.scrF, q.a0s, q.a1s
        accg, accv = a1s
        frontend(x1s, ti, NS, g1, xt, hb, hT, scrF, st)
        for i in range(22):
            accs = a1s
            for gv in range(2):
                b = i + 22 * gv
                pb = B.ps()
                for c in range(8):
                    B.mm(pb.a[:, 0:TB], wup.a[:, c, b * 128:(b + 1) * 128], hT.a[:, c, :], c == 0, c == 7, [wup, hT], [pb])
                acc = accs[gv]; a0 = a0s[gv]; a1 = a1s[gv]
                B.act(a0.a, pb.a[:, 0:TB], AF.Identity, [pb, cw, cbias], [a0], scale=cw.a[:, 2, b:b + 1], bias=cbias.a[:, b:b + 1])
                B.act(a1.a[:, 1:TB], pb.a[:, 0:TB - 1], AF.Copy, [pb, cw], [a1], scale=cw.a[:, 1, b:b + 1])
                B.stt(a0.a[:, 2:TB], pb.a[:, 0:TB - 2], cw.a[:, 0, b:b + 1], a0.a[:, 2:TB], ALU.mult, ALU.add, [pb, cw, a0], [a0])
                B.ts("dve", a1.a[:, 0:1], halo.a[:, b, 1:2], cw.a[:, 1, b:b + 1], ALU.mult, [halo, cw], [a1])
                B.stt(a0.a[:, 0:2], halo.a[:, b, 0:2], cw.a[:, 0, b:b + 1], a0.a[:, 0:2], ALU.mult, ALU.add, [halo, cw, a0], [a0])
                B.cp("dve", halo.a[:, b, :], pb.a[:, TB - 2:TB], [pb], [halo])
                B.tt("pool", acc.a, a0.a, a1.a, ALU.add, [a0, a1], [acc])
            if "zc" in dbo and ti == 0 and i == 0:
                S.dma("sp", dbo["zc"].a[:, 0:TB], accg.a, reads=[accg], writes=[dbo["zc"]], dreg=accg)
            B.act(accg.a, accg.a, AF.Gelu_apprx_tanh, [accg], [accg])
            B.tt("pool", actb.a[:, i, :], accg.a, accv.a, ALU.mult, [accg, accv], [actb])
        for s_ in range(NS):
            pbs = [B.ps(), B.ps()]
            for half in range(2):
                for i in range(22):
                    B.mm(pbs[half].a, actb.a[:, i, s_ * 128:(s_ + 1) * 128], wdn.a[:, i, half * 512:(half + 1) * 512],
                         i == 0, i == 21, [actb, wdn], [pbs[half]])
            for half in range(2):
                B.act(scrF.a[:, 0:512], pbs[half].a, AF.Square, [pbs[half]], [scrF, st], accum_out=st.a[:, half:half + 1])
            B.tt("dve", st.a[:, 2:3], st.a[:, 0:1], st.a[:, 1:2], ALU.add, [st], [st])
            B.ts("dve", st.a[:, 2:3], st.a[:, 2:3], 1.0 / D, ALU.mult, [st], [st], s2=1e-6, op1=ALU.add)
            B.act(st.a[:, 2:3], st.a[:, 2:3], AF.Sqrt, [st], [st])
            B.op("dve", lambda e: e.reciprocal(st.a[:, 3:4], st.a[:, 2:3]), [st], [st])
            for half in range(2):
                hsl = slice(half * 512, (half + 1) * 512)
                B.stt(scrF.a[:, hsl], pbs[half].a, st.a[:, 3:4], g2.a[:, hsl], ALU.mult, ALU.mult, [pbs[half], st, g2], [scrF])
            B.tt("pool", xt.a[:, s_, :], xt.a[:, s_, :], scrF.a, ALU.add, [xt, scrF], [xt])
        S.dma("sp", out.a[ti * TB:(ti + 1) * TB, :].rearrange("(s p) d -> p s d", p=128), xt.a, reads=[xt], writes=[out], dreg=xt)

    S.rec = []
    for ti in range(ntiles):
        tile(ti)
    items = S.rec
    S.rec = None
    if os.environ.get("K_P2SCHED", "0") == "1":
        S.schedule_emit(items, window=int(os.environ.get("K_WIN2", "1500")))
    else:
        for it in items:
            S.replay(it)
    B.pop()


_CACHE = {}


def kernel(**inputs):
    if "B" not in _CACHE:
        _CACHE["B"] = build()
    Bd = _CACHE["B"]
    x = np.ascontiguousarray(inputs["x"], dtype=np.float32)
    wmap = {k: np.ascontiguousarray(np.asarray(inputs[k], dtype=np.float32).reshape(shp)) for k, shp in Bd.Wshapes.items()}
    in_maps = []
    for c in range(8):
        m = dict(wmap)
        m["x"] = x[c]
        in_maps.append(m)
    res = run_bass_kernel_spmd(Bd.nc, in_maps, core_ids=list(range(8)))
    return np.stack([np.asarray(res.results[c]["out"], dtype=np.float32) for c in range(8)], axis=0)
```

```python
import contextlib
import math
import os
import numpy as np
import concourse.bass as bass
import concourse.mybir as mybir
from concourse.bass_utils import run_bass_kernel_spmd

F32 = mybir.dt.float32
BF16 = mybir.dt.bfloat16
I32 = mybir.dt.int32
ALU = mybir.AluOpType
AF = mybir.ActivationFunctionType

L = 4096
D = 1024
NR = 1792
NS5 = 512
NIN = 4352
FF = 2816
PI = math.pi


class Reg:
    __slots__ = ("name", "w", "r", "dsem", "dcnt", "last", "excl")

    def __init__(self, name=""):
        self.name = name
        self.w = None
        self.r = []
        self.dsem = None
        self.dcnt = 0
        self.last = 0
        self.excl = False


class Sched:
    def __init__(self, nc):
        self.nc = nc
        self.eng = {"pe": nc.tensor, "dve": nc.vector, "act": nc.scalar,
                    "pool": nc.gpsimd, "sp": nc.sync}
        self.sem = {k: nc.alloc_semaphore(name="sem_" + k) for k in self.eng}
        self.cnt = {k: 0 for k in self.eng}
        self.seen = {k: {} for k in self.eng}
        self.ninst = 0
        self.nds = 0
        self.dregs = []
        self.dmap = {}
        self.maxops = int(os.environ.get("K_MAXOPS", "100000000"))
        self.rec = None

    def _wait(self, e, tok):
        sem, val = tok
        key = sem.name
        if key in self.dmap:
            val = max(val, self.dmap[key].dcnt)
        if self.seen[e].get(key, 0) >= val:
            return
        if e == "pe" and sem is self.sem["pe"]:
            return
        self.eng[e].wait_ge(sem, val)
        self.seen[e][key] = val

    def _deps(self, e, reads, writes, skip=None):
        for r in reads:
            if r.w is not None:
                self._wait(e, r.w)
        for w in writes:
            if w.w is not None and w.w[0] is not skip:
                self._wait(e, w.w)
            for t in w.r:
                self._wait(e, t)

    def _commit(self, tok, reads, writes):
        for r in reads:
            r.last = self.ninst
            r.r.append(tok)
            if len(r.r) > 16:
                d = {}
                for s, v in r.r:
                    if d.get(s.name, (None, -1))[1] < v:
                        d[s.name] = (s, v)
                r.r = list(d.values())
        for w in writes:
            w.last = self.ninst
            w.w = tok
            w.r = []

    def op(self, e, fn, reads=(), writes=(), cost=None):
        if self.rec is not None:
            self.rec.append(("op", e, fn, list(reads), list(writes), None, cost))
            return None
        if self.ninst >= self.maxops:
            return None
        reads = [x.r if isinstance(x, Buf) else x for x in reads]
        writes = [x.r if isinstance(x, Buf) else x for x in writes]
        ex = [x for x in reads if x.excl and x not in writes]
        if ex:
            reads = [x for x in reads if not x.excl]
            writes = list(writes) + ex
        self._deps(e, reads, writes)
        ins = fn(self.eng[e])
        self.cnt[e] += 1
        ins.then_inc(self.sem[e], 1)
        tok = (self.sem[e], self.cnt[e])
        self._commit(tok, reads, writes)
        self.ninst += 1
        return tok

    def dma(self, e, out, in_, reads=(), writes=(), dreg=None, **kw):
        if self.rec is not None:
            self.rec.append(("dma", e, (out, in_), list(reads), list(writes), (dreg, kw), None))
            return None
        if self.ninst >= self.maxops and not kw.pop("force", False):
            return None
        kw.pop("force", None)
        reads = [x.r if isinstance(x, Buf) else x for x in reads]
        writes = [x.r if isinstance(x, Buf) else x for x in writes]
        if dreg is None:
            dreg = writes[0] if writes else reads[0]
        elif isinstance(dreg, Buf):
            dreg = dreg.r
        if dreg.dsem is None:
            self.nds += 1
            dreg.dsem = self.nc.alloc_semaphore(name="ds%d_%s" % (self.nds, dreg.name))
            self.dregs.append(dreg)
            self.dmap[dreg.dsem.name] = dreg
        self._deps(e, reads, writes, skip=dreg.dsem)
        ins = self.eng[e].dma_start(out=out, in_=in_, **kw)
        dreg.dcnt += 16
        ins.then_inc(dreg.dsem, 16)
        tok = (dreg.dsem, dreg.dcnt)
        self._commit(tok, reads, writes)
        self.ninst += 1
        return tok

    def replay(self, item):
        kind, e, a, reads, writes, extra = item[:6]
        if kind == "op":
            return self.op(e, a, reads, writes)
        dreg, kw = extra
        return self.dma(e, a[0], a[1], reads=reads, writes=writes, dreg=dreg, **kw)

    def schedule_emit(self, items, window=1500):
        import heapq
        n = len(items)
        norm = []
        for it in items:
            kind, e, a, reads, writes, extra, cost = it
            reads = [x.r if isinstance(x, Buf) else x for x in reads]
            writes = [x.r if isinstance(x, Buf) else x for x in writes]
            dreg = None
            if kind == "dma":
                dreg = extra[0]
                if dreg is None:
                    dreg = writes[0] if writes else reads[0]
                elif isinstance(dreg, Buf):
                    dreg = dreg.r
            ex = [x for x in reads if x.excl and x not in writes]
            if ex:
                reads = [x for x in reads if not x.excl]
                writes = list(writes) + ex
            norm.append((kind, e, a, reads, writes, extra, cost, dreg))
        lw = {}
        rd = {}
        first = [[] for _ in range(n)]
        preds = [set() for _ in range(n)]
        for i, (kind, e, a, reads, writes, extra, cost, dreg) in enumerate(norm):
            for r in reads:
                k = id(r)
                if k in lw:
                    preds[i].add(lw[k])
                else:
                    first[i].append((r, "r"))
            for w in writes:
                k = id(w)
                if k in lw:
                    if not (kind == "dma" and norm[lw[k]][0] == "dma" and norm[lw[k]][7] is dreg):
                        preds[i].add(lw[k])
                else:
                    first[i].append((w, "w"))
                for j in rd.get(k, ()):
                    preds[i].add(j)
            for r in reads:
                rd.setdefault(id(r), []).append(i)
            for w in writes:
                lw[id(w)] = i
                rd[id(w)] = []
            preds[i].discard(i)
        succs = [[] for _ in range(n)]
        npred = [len(p) for p in preds]
        for i, p in enumerate(preds):
            for j in p:
                succs[j].append(i)
        def dur(it):
            kind, e, a, reads, writes, extra, cost, dreg = it
            if cost is not None:
                return cost
            if kind == "op":
                return {"pe": 0.08, "dve": 0.4, "act": 0.4, "pool": 1.0, "sp": 2.0}[e]
            o = a[0]
            nbytes = 1
            for d in list(o.shape):
                nbytes *= int(d)
            nbytes *= mybir.dt.size(o.dtype)
            return 2.5 + nbytes / 140e3
        LAT = 0.15
        free = {e: 0.0 for e in self.eng}
        fin = [0.0] * n
        ready_t = [0.0] * n
        heaps = {e: [] for e in self.eng}
        low = 0
        done = [False] * n
        avail = [False] * n
        for i in range(n):
            if npred[i] == 0:
                heapq.heappush(heaps[norm[i][1]], (0.0, i)); avail[i] = True
        order = []
        deferred = {e: [] for e in self.eng}
        while len(order) < n:
            best = None
            for e, h in heaps.items():
                while h and h[0][1] >= low + window:
                    deferred[e].append(heapq.heappop(h))
                if not h:
                    continue
                rt, i = h[0]
                st = max(rt, free[e])
                if best is None or (st, i) < (best[0], best[1]):
                    best = (st, i, e)
            if best is None:
                for e in deferred:
                    for x in deferred[e]:
                        heapq.heappush(heaps[e], x)
                    deferred[e] = []
                window *= 2
                continue
            st, i, e = best
            heapq.heappop(heaps[e])
            order.append(i)
            done[i] = True
            f = st + dur(norm[i])
            fin[i] = f
            free[e] = f
            for j in succs[i]:
                npred[j] -= 1
                ready_t[j] = max(ready_t[j], f + LAT)
                if npred[j] == 0:
                    heapq.heappush(heaps[norm[j][1]], (ready_t[j], j)); avail[j] = True
            if i == low:
                while low < n and done[low]:
                    low += 1
                for e2 in deferred:
                    keep = []
                    for x in deferred[e2]:
                        if x[1] < low + window:
                            heapq.heappush(heaps[e2], x)
                        else:
                            keep.append(x)
                    deferred[e2] = keep
        self.sched_makespan = max(fin) if n else 0.0
        toks = [None] * n
        for i in order:
            kind, e, a, reads, writes, extra, cost, dreg = norm[i]
            for (reg, mode) in first[i]:
                if reg.w is not None and not (kind == "dma" and mode == "w" and reg.w[0] is (dreg.dsem if dreg is not None else None)):
                    self._wait(e, reg.w)
                if mode == "w":
                    for t in reg.r:
                        self._wait(e, t)
            for j in preds[i]:
                self._wait(e, toks[j])
            if kind == "op":
                ins = a(self.eng[e])
                self.cnt[e] += 1
                ins.then_inc(self.sem[e], 1)
                toks[i] = (self.sem[e], self.cnt[e])
            else:
                dg, kw = extra
                kw = dict(kw); kw.pop("force", None)
                if dreg.dsem is None:
                    self.nds += 1
                    dreg.dsem = self.nc.alloc_semaphore(name="ds%d_%s" % (self.nds, dreg.name))
                    self.dregs.append(dreg)
                    self.dmap[dreg.dsem.name] = dreg
                ins = self.eng[e].dma_start(out=a[0], in_=a[1], **kw)
                dreg.dcnt += 16
                ins.then_inc(dreg.dsem, 16)
                toks[i] = (dreg.dsem, dreg.dcnt)
            self.ninst += 1
        touched = {}
        for i, it in enumerate(norm):
            for r in it[3]:
                touched[id(r)] = r
            for w in it[4]:
                touched[id(w)] = w
        for k, reg in touched.items():
            if k in lw:
                reg.w = toks[lw[k]]
                reg.r = [toks[j] for j in rd.get(k, ())]
            else:
                reg.r = list(reg.r) + [toks[j] for j in rd.get(k, ())]
            reg.last = self.ninst

    def barrier(self):
        for e in self.eng:
            for f in self.eng:
                if f != e and self.cnt[f] > 0:
                    self._wait(e, (self.sem[f], self.cnt[f]))
            for d in self.dregs:
                if d.dcnt > 0:
                    self._wait(e, (d.dsem, d.dcnt))

    def final_wait(self, e, regs):
        for r in regs:
            r = r.r if isinstance(r, Buf) else r
            if r.w is not None:
                self._wait(e, r.w)
            for t in r.r:
                self._wait(e, t)


class Buf:
    def __init__(self, t, name):
        self.t = t
        self.a = t.ap()
        self.r = Reg(name)


class Builder:
    def __init__(self, dbg=None, ntiles=8):
        self.nc = bass.Bass("TRN2", target_bir_lowering=False)
        self.S = Sched(self.nc)
        self.dbg = dbg or {}
        self.ntiles = ntiles
        self.din = {}
        self.dout = {}
        self.nbuf = 0
        self.psb = None
        self.psi = 0
        self.ps_pool = None
        self.pclock = 0
        self.scopes = []

    def push(self):
        self.scopes.append(contextlib.ExitStack())

    def pop(self):
        self.S.barrier()
        self.scopes.pop().close()

    def inp(self, name, shape):
        b = Buf(self.nc.dram_tensor(name, list(shape), F32, kind="ExternalInput"), name)
        self.din[name] = b
        return b

    def outp(self, name, shape, dt=F32):
        b = Buf(self.nc.dram_tensor(name, list(shape), dt, kind="ExternalOutput"), name)
        self.dout[name] = b
        return b

    def sb(self, name, shape, dt=F32):
        self.nbuf += 1
        nm = "%s_%d" % (name, self.nbuf)
        if self.scopes:
            return Buf(self.scopes[-1].enter_context(self.nc.sbuf_tensor(nm, list(shape), dt)), name)
        return Buf(self.nc.alloc_sbuf_tensor(nm, list(shape), dt), name)

    def view(self, buf, ap):
        v = Buf.__new__(Buf)
        v.t = buf.t
        v.a = ap
        v.r = buf.r
        return v

    def init_psum(self):
        self.psb = []
        for i in range(8):
            t = self.nc.alloc_psum_tensor("psb%d" % i, [128, 512], F32)
            self.psb.append(Buf(t, "psb%d" % i))
            self.psb[-1].r.excl = True

    def ps(self):
        if self.ps_pool is not None:
            b = self.psb[self.ps_pool[self.psi % len(self.ps_pool)]]
            self.psi += 1
            return b
        b = min(self.psb, key=lambda t: t.r.last)
        self.pclock = max(self.pclock, self.S.ninst) + 1
        b.r.last = self.pclock
        return b

    def op(self, e, fn, reads=(), writes=(), cost=None):
        return self.S.op(e, fn, reads, writes, cost=cost)

    @staticmethod
    def fsz(ap):
        n = 1
        for d in list(ap.shape)[1:]:
            n *= int(d)
        return n

    def mm(self, out, lhsT, rhs, start, stop, reads, writes, **kw):
        return self.op("pe", lambda e: e.matmul(out, lhsT=lhsT, rhs=rhs, start=start, stop=stop, **kw), reads, writes,
                       cost=0.03 + max(self.fsz(rhs), 64) * 0.00052)

    def tr(self, out, in_, ident, reads, writes):
        return self.op("pe", lambda e: e.transpose(out, in_, ident), reads, writes, cost=0.1)

    def act(self, eng_out, in_, func, reads, writes, **kw):
        return self.op("act", lambda e: e.activation(out=eng_out, in_=in_, func=func, **kw), reads, writes,
                       cost=0.22 + self.fsz(in_) * 0.00075)

    def tt(self, e, out, in0, in1, op, reads, writes):
        c = (0.1 + self.fsz(in0) * 0.00115) if e == "dve" else (0.6 + self.fsz(in0) * 0.0013)
        return self.op(e, lambda g: g.tensor_tensor(out=out, in0=in0, in1=in1, op=op), reads, writes, cost=c)

    def ts(self, e, out, in0, s1, op0, reads, writes, s2=None, op1=None):
        c = (0.1 + self.fsz(in0) * 0.0008) if e == "dve" else (0.6 + self.fsz(in0) * 0.0013)
        if op1 is None:
            return self.op(e, lambda g: g.tensor_scalar(out=out, in0=in0, scalar1=s1, scalar2=None, op0=op0), reads, writes, cost=c)
        return self.op(e, lambda g: g.tensor_scalar(out=out, in0=in0, scalar1=s1, scalar2=s2, op0=op0, op1=op1), reads, writes, cost=c)

    def stt(self, out, in0, scalar, in1, op0, op1, reads, writes):
        return self.op("dve", lambda g: g.scalar_tensor_tensor(out=out, in0=in0, scalar=scalar, in1=in1, op0=op0, op1=op1), reads, writes,
                       cost=0.1 + self.fsz(in0) * 0.00115)

    def cp(self, e, out, in_, reads, writes):
        if e == "act":
            return self.op("act", lambda g: g.activation(out=out, in_=in_, func=AF.Copy), reads, writes, cost=0.22 + self.fsz(in_) * 0.00075)
        c = (0.1 + self.fsz(in_) * 0.0008) if e == "dve" else (0.6 + self.fsz(in_) * 0.0013)
        return self.op(e, lambda g: g.tensor_copy(out, in_), reads, writes, cost=c)

    def memset(self, e, out, val, writes):
        return self.op(e, lambda g: g.memset(out, val), (), writes)

    def cmul(self, e, o_re, o_im, a_re, a_im, b_re, b_im, t1, t2, reads, writes, tmp):
        R = list(reads)
        self.tt(e, t1, a_re, b_re, ALU.mult, R, [tmp])
        self.tt(e, t2, a_im, b_im, ALU.mult, R, [tmp])
        self.tt(e, o_re, t1, t2, ALU.subtract, [tmp], writes)
        self.tt(e, t1, a_re, b_im, ALU.mult, R, [tmp])
        self.tt(e, t2, a_im, b_re, ALU.mult, R, [tmp])
        self.tt(e, o_im, t1, t2, ALU.add, [tmp], writes)


def build(dbg=None, ntiles=8, phases=("1a", "1b", "1c", "2")):
    B = Builder(dbg, ntiles)
    nc, S = B.nc, B.S
    dbg = B.dbg

    x = B.inp("x", [L, D])
    shapes = {
        "norm_mix_pre": [D], "norm_mix_post": [D], "norm_ffn_pre": [D], "norm_ffn_post": [D],
        "w_in": [D, NIN], "b_gate": [2048], "rwkv_shift_mu": [NR], "rwkv_w0": [512],
        "rwkv_w2": [64, 512], "rwkv_a0": [512], "rwkv_a2": [64, 512], "rwkv_g2": [128, 512],
        "rwkv_k_k": [512], "rwkv_k_a": [512], "rwkv_r_k": [512], "rwkv_lnx_w": [512],
        "rwkv_lnx_b": [512], "s5_a_re": [32, 64], "s5_a_im": [32, 64], "s5_b_re": [32, 64, 16],
        "s5_b_im": [32, 64, 16], "s5_c_re": [32, 16, 64], "s5_c_im": [32, 16, 64], "s5_d": [512],
        "s5_log_step": [32], "s5_w_glu": [512, 512], "s5_b_glu": [512], "w_branch_rwkv": [512, D],
        "w_branch_s5": [512, D], "w_out": [D, D], "ffn_w_up": [D, 2 * FF], "ffn_conv_w": [3, 2 * FF],
        "ffn_conv_b": [2 * FF], "ffn_w_down": [FF, D],
    }
    W = {k: B.inp(k, v) for k, v in shapes.items()}
    out = B.outp("out", [L, D])
    dbo = {k: B.outp("dbg_" + k, shp, dt) for k, (shp, dt) in dbg.items()}

    B.init_psum()

    ident_f = B.sb("ident_f", [128, 128], F32)
    ident_b = B.sb("ident_b", [128, 128], BF16)
    B.memset("pool", ident_f.a, 1.0, [ident_f])
    B.op("pool", lambda e: e.affine_select(out=ident_f.a, in_=ident_f.a, pattern=[[-1, 128]],
                                           compare_op=ALU.is_equal, fill=0.0, base=0, channel_multiplier=1),
         [ident_f], [ident_f])
    B.cp("pool", ident_b.a, ident_f.a, [ident_f], [ident_b])
    epsc = B.sb("epsc", [128, 4])
    B.memset("pool", epsc.a[:, 0:1], 1e-6, [epsc]); B.memset("pool", epsc.a[:, 1:2], 1e-24, [epsc])
    B.memset("pool", epsc.a[:, 2:3], 64e-5, [epsc]); B.memset("pool", epsc.a[:, 3:4], 0.0, [epsc])

    def bc_load(name, n, q="sp"):
        t = B.sb(name + "_bc", [128, n], F32)
        S.dma(q, t.a, W[name].a.partition_broadcast(128), reads=[W[name]], writes=[t])
        return t

    def col_load(name, nt, q="sp"):
        t = B.sb(name + "_col", [128, nt], F32)
        S.dma(q, t.a, W[name].a.rearrange("(t p) -> p t", p=128), reads=[W[name]], writes=[t],
              allow_slow_non_contiguous=True)
        return t

    def frontend(src, ti, ntok_tiles, gbc, xt, hb, hT, scr, st):
        nt = ntok_tiles
        T = 128 * nt
        S.dma("sp", xt.a, src.a[ti * T:(ti + 1) * T, :].rearrange("(s p) d -> p s d", p=128),
              reads=[src], writes=[xt])
        for s in range(nt):
            B.act(scr.a, xt.a[:, s, :], AF.Square, [xt], [scr, st], accum_out=st.a[:, s:s + 1])
        B.act(st.a[:, nt:2 * nt], st.a[:, 0:nt], AF.Ln, [st, epsc], [st], scale=1.0 / D, bias=epsc.a[:, 0:1])
        B.act(st.a[:, 2 * nt:3 * nt], st.a[:, nt:2 * nt], AF.Exp, [st], [st], scale=-0.5)
        for s in range(nt):
            B.stt(hb.a[:, s, :], xt.a[:, s, :], st.a[:, 2 * nt + s:2 * nt + s + 1], gbc.a, ALU.mult, ALU.mult,
                  [xt, st, gbc], [hb])
        for c in range(8):
            pb = B.ps()
            pv = pb.a.bitcast(BF16)
            for s in range(nt):
                B.tr(pv[:, s * 128:(s + 1) * 128], hb.a[:, s, c * 128:(c + 1) * 128], ident_b.a,
                     [hb, ident_b], [pb])
            B.cp("act" if c % 2 == 0 else "dve", hT.a[:, c, :], pv[:, 0:T], [pb], [hT])

    T1 = 512
    YGLU = Buf(nc.dram_tensor("yglu_d", [128, 4, L], BF16, kind="Internal"), "yglu_d")

    B.push()
    gpre = bc_load("norm_mix_pre", D)
    x1s = Buf(nc.dram_tensor("x1s", [L, D], F32, kind="Internal"), "x1s")
    YR = Buf(nc.dram_tensor("yr_d", [128, 4, L], BF16, kind="Internal"), "yr_d")
    if "1a" in phases:
        B.push()
        ws5 = B.sb("ws5", [128, 8, 512], BF16)
        S.dma("pool", ws5.a, W["w_in"].a.rearrange("(c p) n -> p c n", p=128)[:, :, NR:NR + 512],
              reads=[W["w_in"]], writes=[ws5])
        wglu = B.sb("wglu", [128, 4, 512], BF16)
        S.dma("pool", wglu.a, W["s5_w_glu"].a.rearrange("(c p) n -> p c n", p=128), reads=[W["s5_w_glu"]], writes=[wglu])
        bglu = col_load("s5_b_glu", 4)
        dcol = col_load("s5_d", 4)

        MS = B.sb("ms", [128, 16, 2, 2])
        ER = B.sb("ER", [128, 16, 64]); EI = B.sb("EI", [128, 16, 64]); R8 = B.sb("R8", [128, 16])
        Wt = B.sb("Wt", [128, 4, 8, 2, 128], BF16)
        CAW = B.sb("CAW", [128, 16, 2, 9, 64], BF16)
        mk = B.sb("mk", [128, 2])
        B.memset("pool", mk.a[:, 0:1], 0.0, [mk]); B.memset("pool", mk.a[0:32, 0:1], 1.0, [mk]); B.memset("pool", mk.a[64:96, 0:1], 1.0, [mk])
        mk4 = B.sb("mk4", [128, 4])
        B.memset("pool", mk4.a, 0.0, [mk4])
        B.memset("pool", mk4.a[0:32, 0:1], 1.0, [mk4]); B.memset("pool", mk4.a[32:64, 1:2], 1.0, [mk4])
        B.memset("pool", mk4.a[64:96, 2:3], 1.0, [mk4]); B.memset("pool", mk4.a[64:128, 3:4], 1.0, [mk4]); B.memset("pool", mk4.a[64:96, 3:4], 0.0, [mk4])
        B.memset("pool", mk.a[:, 1:2], 1.0, [mk]); B.memset("pool", mk.a[0:32, 1:2], 0.0, [mk]); B.memset("pool", mk.a[64:96, 1:2], 0.0, [mk])
        Kbd = B.sb("Kbd", [128, 4, 8, 128], BF16)
        B.push()
        are = B.sb("are", [128, 16]); aim = B.sb("aim", [128, 16]); ls = B.sb("ls", [128, 16])
        for gl in range(2):
            S.dma("sp", are.a[gl * 64:(gl + 1) * 64, :], W["s5_a_re"].a.rearrange("(q gl) n -> gl n q", gl=2)[gl],
                  reads=[W["s5_a_re"]], writes=[are], allow_slow_non_contiguous=True)
            S.dma("sp", aim.a[gl * 64:(gl + 1) * 64, :], W["s5_a_im"].a.rearrange("(q gl) n -> gl n q", gl=2)[gl],
                  reads=[W["s5_a_im"]], writes=[aim], allow_slow_non_contiguous=True)
            S.dma("sp", ls.a[gl * 64:(gl + 1) * 64, :],
                  W["s5_log_step"].a.rearrange("(q gl) -> gl q", gl=2)[gl].partition_broadcast(64),
                  reads=[W["s5_log_step"]], writes=[ls], allow_slow_non_contiguous=True)
        braw = [B.sb("braw%d" % i, [128, 16, 16]) for i in range(2)]
        for i, nm in enumerate(("s5_b_re", "s5_b_im")):
            S.dma("sp", braw[i].a, W[nm].a.rearrange("(q gl) n c -> (gl n) q c", gl=2), reads=[W[nm]], writes=[braw[i]])
        craw = [B.sb("craw%d" % i, [128, 16, 16]) for i in range(2)]
        ctmp = B.sb("ctmp", [128, 128])
        for i, nm in enumerate(("s5_c_re", "s5_c_im")):
            for blk in range(2):
                src = W[nm].a.rearrange("(b qq gl) c n -> b qq c gl n", b=2, gl=2)[blk]
                for qq in range(8):
                    S.dma("sp", ctmp.a[qq * 16:(qq + 1) * 16, :].rearrange("p (gl n) -> p gl n", gl=2),
                          src[qq], reads=[W[nm]], writes=[ctmp])
                pb = B.ps()
                B.tr(pb.a[:, 0:128], ctmp.a, ident_f.a, [ctmp, ident_f], [pb])
                B.cp("dve", craw[i].a[:, blk * 8:(blk + 1) * 8, :],
                     pb.a[:, 0:128].rearrange("p (qq c) -> p qq c", c=16), [pb], [craw[i]])

        tm = B.sb("s5tmp", [128, 12, 32])
        tmr = tm.r

        def row(i, n=16):
            return tm.a[:, i, 0:n]

        dt_ = row(0)
        B.act(dt_, ls.a, AF.Exp, [ls], [tm])
        xr = row(1)
        B.tt("dve", xr, are.a, dt_, ALU.mult, [are, tm], [tm])
        rho = row(2)
        B.ts("dve", rho, xr, 1.0 / 720, ALU.mult, [tm], [tm], s2=1.0 / 120, op1=ALU.add)
        for cf in (1.0 / 24, 1.0 / 6, 0.5, 1.0, 1.0):
            B.tt("dve", rho, rho, xr, ALU.mult, [tm], [tm])
            B.ts("dve", rho, rho, cf, ALU.add, [tm], [tm])
        th2 = tm.a[:, 3, :]
        B.tt("dve", th2[:, 0:16], aim.a, dt_, ALU.mult, [aim, tm], [tm])
        B.ts("dve", th2[:, 16:32], th2[:, 0:16], PI / 2, ALU.add, [tm], [tm])
        kf = tm.a[:, 4, :]
        B.ts("dve", kf, th2, 1.0 / (2 * PI), ALU.mult, [tm], [tm])
        ki = B.sb("ki", [128, 32], I32)
        B.cp("dve", ki.a, kf, [tm], [ki])
        B.cp("dve", kf, ki.a, [ki], [tm])
        r1 = tm.a[:, 5, :]
        B.stt(r1, kf, -2 * PI, th2, ALU.mult, ALU.add, [tm], [tm])
        B.ts("dve", kf, r1, PI, ALU.is_gt, [tm], [tm], s2=-2 * PI, op1=ALU.mult)
        B.tt("dve", r1, r1, kf, ALU.add, [tm], [tm])
        B.ts("dve", kf, r1, -PI, ALU.is_lt, [tm], [tm], s2=2 * PI, op1=ALU.mult)
        B.tt("dve", r1, r1, kf, ALU.add, [tm], [tm])
        sc = tm.a[:, 6, :]
        B.act(sc, r1, AF.Sin, [tm], [tm])
        n2 = row(7)
        B.tt("dve", kf, sc, sc, ALU.mult, [tm], [tm])
        B.tt("dve", n2, kf[:, 0:16], kf[:, 16:32], ALU.add, [tm], [tm])
        B.ts("dve", n2, n2, -0.5, ALU.mult, [tm], [tm], s2=1.5, op1=ALU.add)
        B.tt("dve", n2, n2, rho, ALU.mult, [tm], [tm])
        PW = B.sb("pw", [128, 9, 2, 16])
        B.memset("dve", PW.a[:, 0, 0, :], 1.0, [PW])
        B.memset("dve", PW.a[:, 0, 1, :], 0.0, [PW])
        B.tt("dve", PW.a[:, 1, 0, :], sc[:, 16:32], n2, ALU.mult, [tm], [PW])
        B.tt("dve", PW.a[:, 1, 1, :], sc[:, 0:16], n2, ALU.mult, [tm], [PW])
        pt = B.sb("ptmp", [128, 2, 4, 16])
        for (lo, n, s) in ((2, 1, 1), (3, 2, 2), (5, 4, 4)):
            bre = PW.a[:, s:s + 1, 0, :].broadcast_to([128, n, 16])
            bim = PW.a[:, s:s + 1, 1, :].broadcast_to([128, n, 16])
            B.cmul("dve", PW.a[:, lo:lo + n, 0, :], PW.a[:, lo:lo + n, 1, :],
                   PW.a[:, lo - s:lo - s + n, 0, :], PW.a[:, lo - s:lo - s + n, 1, :], bre, bim,
                   pt.a[:, 0, 0:n, :], pt.a[:, 1, 0:n, :], [PW], [PW], pt)
        B.cp("dve", MS.a[:, :, 0, 0], PW.a[:, 8, 0, :], [PW], [MS])
        B.cp("dve", MS.a[:, :, 1, 1], PW.a[:, 8, 0, :], [PW], [MS])
        B.cp("dve", MS.a[:, :, 1, 0], PW.a[:, 8, 1, :], [PW], [MS])
        B.ts("dve", MS.a[:, :, 0, 1], PW.a[:, 8, 1, :], -1.0, ALU.mult, [PW], [MS])
        B.tt("dve", R8.a, rho, rho, ALU.mult, [tm], [R8])
        B.tt("dve", R8.a, R8.a, R8.a, ALU.mult, [R8], [R8])
        B.tt("dve", R8.a, R8.a, R8.a, ALU.mult, [R8], [R8])
        r8i = row(8)
        B.op("dve", lambda e: e.reciprocal(r8i, R8.a), [R8], [tm])
        B.tt("dve", ER.a[:, :, 0], PW.a[:, 8, 0, :], r8i, ALU.mult, [PW, tm], [ER])
        B.tt("dve", EI.a[:, :, 0], PW.a[:, 8, 1, :], r8i, ALU.mult, [PW, tm], [EI])
        et = B.sb("etmp", [128, 2, 16, 32])
        n_ = 1
        while n_ < 64:
            bre = ER.a[:, :, n_ - 1:n_].broadcast_to([128, 16, n_]); bim = EI.a[:, :, n_ - 1:n_].broadcast_to([128, 16, n_])
            B.cmul("dve", ER.a[:, :, n_:2 * n_], EI.a[:, :, n_:2 * n_], ER.a[:, :, 0:n_], EI.a[:, :, 0:n_], bre, bim,
                   et.a[:, 0, :, 0:n_], et.a[:, 1, :, 0:n_], [ER, EI], [ER, EI], et)
            n_ *= 2
        am1 = row(8); nre = row(9); nim = row(10); den = row(11); t0 = row(4); t1 = row(5)
        B.ts("dve", am1, PW.a[:, 1, 0, :], -1.0, ALU.add, [PW], [tm])
        B.tt("dve", nre, am1, are.a, ALU.mult, [tm, are], [tm])
        B.tt("dve", t0, PW.a[:, 1, 1, :], aim.a, ALU.mult, [PW, aim], [tm])
        B.tt("dve", nre, nre, t0, ALU.add, [tm], [tm])
        B.tt("dve", nim, PW.a[:, 1, 1, :], are.a, ALU.mult, [PW, are], [tm])
        B.tt("dve", t0, am1, aim.a, ALU.mult, [tm, aim], [tm])
        B.tt("dve", nim, nim, t0, ALU.subtract, [tm], [tm])
        B.tt("dve", den, are.a, are.a, ALU.mult, [are], [tm])
        B.tt("dve", t0, aim.a, aim.a, ALU.mult, [aim], [tm])
        B.tt("dve", den, den, t0, ALU.add, [tm], [tm])
        B.op("dve", lambda e: e.reciprocal(t1, den), [tm], [tm])
        B.tt("dve", nre, nre, t1, ALU.mult, [tm], [tm])
        B.tt("dve", nim, nim, t1, ALU.mult, [tm], [tm])
        bb = [B.sb("bb%d" % i, [128, 16, 16]) for i in range(2)]
        btmp = B.sb("btmp", [128, 2, 16, 16])
        cre = nre[:, :, None].broadcast_to([128, 16, 16]); cim = nim[:, :, None].broadcast_to([128, 16, 16])
        B.cmul("dve", bb[0].a, bb[1].a, cre, cim, braw[0].a, braw[1].a, btmp.a[:, 0], btmp.a[:, 1],
               [tm, braw[0], braw[1]], [bb[0], bb[1]], btmp)
        X = [B.sb("X%d" % i, [128, 16, 32]) for i in range(2)]
        Xb = [B.sb("Xb%d" % i, [128, 16, 64], BF16) for i in range(2)]
        for i in range(2):
            B.memset("pool", X[i].a, 0.0, [X[i]])
            for gl in range(2):
                B.cp("pool", X[i].a[gl * 64:(gl + 1) * 64, :, gl * 16:(gl + 1) * 16], bb[i].a[gl * 64:(gl + 1) * 64], [bb[i]], [X[i]])
            B.memset("pool", Xb[i].a, 0.0, [Xb[i]])
            for kk in range(2):
                B.cp("pool", Xb[i].a[:, kk::2, 32 * kk:32 * kk + 32], X[i].a[:, kk::2, :], [X[i]], [Xb[i]])
        B.push()
        WX = [B.sb("WX%d" % i, [128, 8, 16, 32]) for i in range(2)]
        wtmp = B.sb("wtmp", [128, 2, 8, 16, 32])
        pre = PW.a[:, 0:8, 0, :][:, :, :, None].broadcast_to([128, 8, 16, 32])
        pim = PW.a[:, 0:8, 1, :][:, :, :, None].broadcast_to([128, 8, 16, 32])
        xre = X[0].a[:, None, :, :].broadcast_to([128, 8, 16, 32])
        xim = X[1].a[:, None, :, :].broadcast_to([128, 8, 16, 32])
        B.cmul("dve", WX[0].a, WX[1].a, pre, pim, xre, xim, wtmp.a[:, 0], wtmp.a[:, 1], [PW, X[0], X[1]], [WX[0], WX[1]], wtmp)
        for tile in range(4):
            for e_ in range(8):
                pb = B.ps()
                for ri in range(2):
                    B.tr(pb.a[:, ri * 128:(ri + 1) * 128],
                         WX[ri].a[:, e_, 4 * tile:4 * tile + 4, :].rearrange("p k c -> p (k c)"), ident_f.a,
                         [WX[ri], ident_f], [pb])
                B.cp("act" if e_ % 2 else "dve", Wt.a[:, tile, e_, :, :],
                     pb.a[:, 0:256].rearrange("p (r n) -> p r n", r=2), [pb], [Wt])
        B.pop()
        B.push()
        CA = [B.sb("CA%d" % i, [128, 9, 16, 16]) for i in range(2)]
        catmp = B.sb("catmp", [128, 2, 9, 16, 16])
        pre9 = PW.a[:, :, 0, :][:, :, :, None].broadcast_to([128, 9, 16, 16])
        pim9 = PW.a[:, :, 1, :][:, :, :, None].broadcast_to([128, 9, 16, 16])
        cre9 = craw[0].a[:, None, :, :].broadcast_to([128, 9, 16, 16])
        cim9 = craw[1].a[:, None, :, :].broadcast_to([128, 9, 16, 16])
        B.cmul("dve", CA[0].a, CA[1].a, pre9, pim9, cre9, cim9, catmp.a[:, 0], catmp.a[:, 1],
               [PW, craw[0], craw[1]], [CA[0], CA[1]], catmp)
        B.memset("pool", CAW.a, 0.0, [CAW])
        for gl in range(2):
            hs = slice(gl * 64, (gl + 1) * 64)
            for tau in range(9):
                for kk in range(2):
                    o = 32 * kk + 16 * gl
                    B.cp("pool", CAW.a[hs, kk::2, 0, tau, o:o + 16], CA[0].a[hs, tau, kk::2], [CA[0]], [CAW])
                    B.ts("pool", CAW.a[hs, kk::2, 1, tau, o:o + 16], CA[1].a[hs, tau, kk::2], -1.0, ALU.mult, [CA[1]], [CAW])
        B.memset("pool", Kbd.a, 0.0, [Kbd])
        k0 = B.sb("k0", [128, 128])
        for tile in range(4):
            pb = B.ps()
            for h in range(2):
                for tau in range(8):
                    n = 0
                    for kk in range(2):
                        q = 4 * tile + 2 * h + kk
                        o = 32 * kk
                        for ri in range(2):
                            B.mm(pb.a[64 * h:64 * h + 64, tau * 32:(tau + 1) * 32], Xb[ri].a[:, q, :],
                                 CAW.a[:, q, ri, tau, o:o + 32], n == 0, n == 3, [Xb[ri], CAW], [pb])
                            n += 1
            for h in range(2):
                hs = slice(64 * h, 64 * h + 64)
                for kk in range(2):
                    cs = slice(64 * h + 32 * kk, 64 * h + 32 * kk + 32)
                    B.ts("dve", Kbd.a[hs, tile, 1:8, cs], pb.a[hs, 32:256].rearrange("p (t c) -> p t c", c=32),
                         mk.a[hs, kk:kk + 1], ALU.mult, [pb, mk], [Kbd])
            B.memset("dve", k0.a, 0.0, [k0])
            for h in range(2):
                hs = slice(64 * h, 64 * h + 64)
                for kk in range(2):
                    cs = slice(64 * h + 32 * kk, 64 * h + 32 * kk + 32)
                    B.ts("dve", k0.a[hs, cs], pb.a[hs, 0:32], mk.a[hs, kk:kk + 1], ALU.mult, [pb, mk], [k0])
            B.stt(k0.a, ident_f.a, dcol.a[:, tile:tile + 1], k0.a, ALU.mult, ALU.add, [ident_f, dcol, k0], [k0])
            B.cp("dve", Kbd.a[:, tile, 0, :], k0.a, [k0], [Kbd])
        B.pop()
        B.pop()
        for nm_, b_ in (("Wt", Wt), ("CAW", CAW), ("Kbd", Kbd), ("MS", MS)):
            if nm_ in dbo:
                S.dma("sp", dbo[nm_].a, b_.a, reads=[b_], writes=[dbo[nm_]], dreg=b_)
        xt = B.sb("xt", [128, 4, D]); hT = B.sb("hT", [128, 8, T1], BF16)
        st = B.sb("st", [128, 12])
        u = B.sb("u", [128, 4, 8, T1 // 8], BF16)
        um = B.sb("um", [128, 4, 4, 8, T1 // 8], BF16)
        hb = B.view(um, um.a.rearrange("p a b c d -> p (a b c d)")[:, 0:4 * D].rearrange("p (s d) -> p s d", s=4))
        NM = T1 // 8
        Dm = B.sb("Dm", [128, 16, 2, NM])
        St = B.sb("St", [128, 16, 2, NM + 1])
        Sb = B.sb("Sb", [128, 16, 2, NM], BF16)
        rt1 = B.sb("rt1", [128, 16, NM]); rt2 = B.sb("rt2", [128, 16, NM]); Dr = B.sb("Dr", [128, 16, 2, NM])
        Qs = B.sb("Qs", [128, 16, 2, NM]); Sfin = B.sb("Sfin", [128, 16, 2]); B.memset("pool", Sfin.a, 0.0, [Sfin])
        y2 = B.sb("y2", [128, 4, T1]); yg = B.sb("yg", [128, 4, T1])
        yy = B.view(Dm, Dm.a.rearrange("p a b c -> p (a b c)").rearrange("p (t n) -> p t n", t=4))
        scr = B.view(y2, y2.a.rearrange("p a b -> p (a b)")[:, 0:D])
        ygb = B.sb("ygb", [128, 4, T1], BF16); sg = y2
        ygl = B.sb("ygl", [128, 4, T1], BF16)
        B.memset("pool", St.a[:, :, :, 0], 0.0, [St])
        for ti in range(ntiles):
            frontend(x, ti, 4, gpre, xt, hb, hT, scr, st)
            for cb in range(4):
                pb = B.ps()
                for c in range(8):
                    B.mm(pb.a, ws5.a[:, c, cb * 128:(cb + 1) * 128], hT.a[:, c, :], c == 0, c == 7, [ws5, hT], [pb])
                pperm = pb.a.rearrange("p (m t) -> p t m", t=8)
                B.cp("act", u.a[:, cb, :, :], pperm, [pb], [u])
                for k in range(4):
                    B.act(um.a[:, k, cb, :, :], pperm, AF.Copy, [pb, mk4], [um], scale=mk4.a[:, k:k + 1])
            for qb in range(4):
                pb = B.ps()
                for qq in range(4):
                    q = 4 * qb + qq
                    tile, k = q // 4, q % 4
                    ks = slice(32 * k, 32 * k + 32)
                    for ri in range(2):
                        col = (qq * 2 + ri) * NM
                        for j0 in range(8):
                            B.mm(pb.a[:, col:col + NM], Wt.a[:, tile, 7 - j0, ri, :], um.a[:, k, tile, j0, :],
                                 j0 == 0, j0 == 7, [Wt, um], [pb])
                B.cp("dve", Dm.a[:, 4 * qb:4 * qb + 4, :, :],
                     pb.a.rearrange("p (q r m) -> p q r m", q=4, r=2), [pb], [Dm])
            erb = ER.a[:, :, :]; eib = EI.a[:, :, :]
            B.tt("dve", rt1.a, erb, Dm.a[:, :, 0, :], ALU.mult, [ER, Dm], [rt1])
            B.tt("dve", rt2.a, eib, Dm.a[:, :, 1, :], ALU.mult, [EI, Dm], [rt2])
            B.tt("dve", Dr.a[:, :, 0, :], rt1.a, rt2.a, ALU.add, [rt1, rt2], [Dr])
            B.tt("dve", rt1.a, erb, Dm.a[:, :, 1, :], ALU.mult, [ER, Dm], [rt1])
            B.tt("dve", rt2.a, eib, Dm.a[:, :, 0, :], ALU.mult, [EI, Dm], [rt2])
            B.tt("dve", Dr.a[:, :, 1, :], rt1.a, rt2.a, ALU.subtract, [rt1, rt2], [Dr])
            for q in range(16):
                for ri in range(2):
                    B.op("dve", lambda e, q=q, ri=ri: e.tensor_tensor_scan(
                        out=Qs.a[:, q, ri, :], data0=R8.a[:, q:q + 1].broadcast_to([128, NM]), data1=Dr.a[:, q, ri, :],
                        initial=Sfin.a[:, q, ri:ri + 1], op0=ALU.mult, op1=ALU.add), [R8, Dr, Sfin], [Qs])
            B.tt("dve", rt1.a, erb, Qs.a[:, :, 0, :], ALU.mult, [ER, Qs], [rt1])
            B.tt("dve", rt2.a, eib, Qs.a[:, :, 1, :], ALU.mult, [EI, Qs], [rt2])
            B.tt("dve", St.a[:, :, 0, 1:NM + 1], rt1.a, rt2.a, ALU.subtract, [rt1, rt2], [St])
            B.tt("dve", rt1.a, erb, Qs.a[:, :, 1, :], ALU.mult, [ER, Qs], [rt1])
            B.tt("dve", rt2.a, eib, Qs.a[:, :, 0, :], ALU.mult, [EI, Qs], [rt2])
            B.tt("dve", St.a[:, :, 1, 1:NM + 1], rt1.a, rt2.a, ALU.add, [rt1, rt2], [St])
            B.cp("act", Sb.a, St.a[:, :, :, 0:NM], [St], [Sb])
            B.cp("act", Sfin.a, St.a[:, :, :, NM], [St], [Sfin])
            B.cp("pool", St.a[:, :, :, 0], St.a[:, :, :, NM], [St], [St])
            for tile in range(4):
                pb = B.ps()
                for h in range(2):
                    hs = slice(64 * h, 64 * h + 64)
                    for t0_ in range(8):
                        n = 0
                        for kk in range(2):
                            q = 4 * tile + 2 * h + kk
                            for ri in range(2):
                                B.mm(pb.a[hs, t0_ * NM:(t0_ + 1) * NM], CAW.a[:, q, ri, t0_ + 1, :], Sb.a[:, q, ri, :], n == 0, n == 3,
                                     [CAW, Sb], [pb], skip_group_check=True)
                                n += 1
                B.cp("act", y2.a[:, tile, :], pb.a, [pb], [y2])
                pb = B.ps()
                for t0o in range(8):
                    for tau in range(t0o + 1):
                        B.mm(pb.a[:, t0o * NM:(t0o + 1) * NM], Kbd.a[:, tile, tau, :], u.a[:, tile, t0o - tau, :],
                             tau == 0, tau == t0o, [Kbd, u], [pb])
                B.tt("dve", yy.a[:, tile, :].rearrange("p (m t) -> p t m", t=8), pb.a.rearrange("p (t m) -> p t m", t=8),
                     y2.a[:, tile, :].rearrange("p (t m) -> p t m", t=8), ALU.add, [pb, y2], [yy])
            if "s5y" in dbo:
                S.dma("sp", dbo["s5y"].a[:, :, ti * T1:(ti + 1) * T1], yy.a, reads=[yy], writes=[dbo["s5y"]], dreg=yy)
            for tile in range(4):
                B.act(yg.a[:, tile, :], yy.a[:, tile, :], AF.Gelu_apprx_tanh, [yy], [yg])
            B.cp("act", ygb.a, yg.a, [yg], [ygb])
            for cb in range(4):
                pb = B.ps()
                for c in range(4):
                    B.mm(pb.a, wglu.a[:, c, cb * 128:(cb + 1) * 128], ygb.a[:, c, :], c == 0, c == 3, [wglu, ygb], [pb])
                B.act(sg.a[:, cb, :], pb.a, AF.Sigmoid, [pb, bglu], [sg], bias=bglu.a[:, cb:cb + 1])
                B.tt("dve", ygl.a[:, cb, :], yg.a[:, cb, :], sg.a[:, cb, :], ALU.mult, [yg, sg], [ygl])
            S.dma("sp", YGLU.a[:, :, ti * T1:(ti + 1) * T1], ygl.a, reads=[ygl], writes=[YGLU], dreg=ygl)
            if "yglu" in dbo:
                S.dma("sp", dbo["yglu"].a[:, :, ti * T1:(ti + 1) * T1], ygl.a, reads=[ygl], writes=[dbo["yglu"]], dreg=ygl)
        B.pop()

    if "1b" in phases:
        phase_1b(B, W, x, YR, epsc, gpre, ident_f, ident_b, frontend, bc_load, col_load, dbo, ntiles * (T1 // 128))

    if "1c" in phases:
        phase_1c(B, W, x, x1s, YR, YGLU, epsc, gpre, frontend, bc_load, col_load, dbo, ntiles)

    B.pop()
    if "2" in phases:
        phase_2(B, W, x1s, out, epsc, frontend, bc_load, dbo, ntiles)

    S.final_wait("sp", list(B.dout.values()))
    B.Wshapes = shapes
    return B


def phase_1b(B, W, x, YR, epsc, gpre, ident_f, ident_b, frontend, bc_load, col_load, dbo, ntiles):
    nc, S = B.nc, B.S
    TB = 128
    C = 128
    NW = 3
    C0 = math.exp(-0.5)
    B.push()
    wrg = B.sb("wrg", [128, 8, NR], BF16)
    win = W["w_in"].a.rearrange("(c p) n -> p c n", p=128)
    for c in range(8):
        S.dma("pool", wrg.a[:, c, :], win[:, c, 0:NR], reads=[W["w_in"]], writes=[wrg])
    w2p = B.sb("w2p", [128, 512], BF16); a2p = B.sb("a2p", [128, 512], BF16); g2b = B.sb("g2b", [128, 512], BF16)
    B.memset("pool", w2p.a, 0.0, [w2p]); B.memset("pool", a2p.a, 0.0, [a2p])
    S.dma("pool", w2p.a[0:64, :], W["rwkv_w2"].a, reads=[W["rwkv_w2"]], writes=[w2p])
    S.dma("pool", a2p.a[64:128, :], W["rwkv_a2"].a, reads=[W["rwkv_a2"]], writes=[a2p])
    S.dma("pool", g2b.a, W["rwkv_g2"].a, reads=[W["rwkv_g2"]], writes=[g2b])
    mu = col_load("rwkv_shift_mu", 14); w0c = col_load("rwkv_w0", 4); a0c = col_load("rwkv_a0", 4)
    kkc = col_load("rwkv_k_k", 4); kac = col_load("rwkv_k_a", 4); rkc = col_load("rwkv_r_k", 4)
    lwc = col_load("rwkv_lnx_w", 4); lbc = col_load("rwkv_lnx_b", 4)
    omu = B.sb("omu", [128, 14]); oka = B.sb("oka", [128, 4])
    B.ts("dve", omu.a, mu.a, -1.0, ALU.mult, [mu], [omu], s2=1.0, op1=ALU.add)
    B.ts("dve", oka.a, kac.a, -1.0, ALU.mult, [kac], [oka], s2=1.0, op1=ALU.add)
    bones = B.sb("bones", [128, 128]); bavg = B.sb("bavg", [128, 128]); ones = B.sb("ones", [128, 128])
    B.memset("pool", ones.a, 1.0, [ones])
    B.memset("pool", bones.a, 0.0, [bones])
    B.memset("pool", bones.a[0:64, 0:64], 1.0, [bones]); B.memset("pool", bones.a[64:128, 64:128], 1.0, [bones])
    B.ts("pool", bavg.a, bones.a, 1.0 / 64, ALU.mult, [bones], [bavg])
    hm = B.sb("hm", [128, 2])
    B.memset("pool", hm.a, 0.0, [hm]); B.memset("pool", hm.a[0:64, 0:1], 1.0, [hm]); B.memset("pool", hm.a[64:128, 1:2], 1.0, [hm])
    mf = B.sb("mf", [128, 128])
    MU4 = B.sb("MU4", [128, 4, 128], BF16); MI4 = B.sb("MI4", [128, 4, 128], BF16)
    ML4 = B.sb("ML4", [128, 4, 128], BF16); I4 = B.sb("I4", [128, 4, 128], BF16)
    for (mt, pat, cm, cop) in ((MU4, 1, -1, ALU.is_gt), (MI4, 1, -1, ALU.is_ge), (ML4, -1, 1, ALU.is_gt)):
        B.memset("pool", mf.a, 1.0, [mf])
        B.op("pool", lambda e, pat=pat, cm=cm, cop=cop: e.affine_select(out=mf.a, in_=mf.a, pattern=[[pat, 128]], compare_op=cop,
                                                                      fill=0.0, base=0, channel_multiplier=cm), [mf], [mf])
        for h in range(4):
            B.cp("pool", mt.a[:, h, :], mf.a, [mf], [mt])
    for h in range(4):
        B.cp("pool", I4.a[:, h, :], ident_f.a, [ident_f], [I4])
    pc = B.sb("pc", [128, 14]); B.memset("pool", pc.a, 0.0, [pc])
    H32 = B.sb("H32", [128, 4, 64]); Hb = B.sb("Hb", [128, 4, 64], BF16); Ht = B.sb("Ht", [128, 4, 64])
    B.memset("pool", H32.a, 0.0, [H32]); B.memset("pool", Hb.a, 0.0, [Hb])

    def bc4(col):
        return col.a[:, :, None].broadcast_to([128, 4, TB])

    class BS:
        pass

    sets = []
    for w in range(NW):
        b = BS()
        f4 = lambda nm: B.sb(nm + str(w), [128, 4, TB])
        h4 = lambda nm: B.sb(nm + str(w), [128, 4, TB], BF16)
        b.xt = B.sb("xt%d" % w, [128, 1, D]); b.hb = B.sb("hb%d" % w, [128, 1, D], BF16); b.hT = B.sb("hT%d" % w, [128, 8, TB], BF16)
        b.st = B.sb("st%d" % w, [128, 6])
        b.PS = B.sb("PS%d" % w, [128, 14, TB]); b.t1 = B.sb("t1%d" % w, [128, 4, TB]); b.t2 = B.sb("t2%d" % w, [128, 4, TB])
        b.scr = B.view(b.PS, b.PS.a.rearrange("p a b -> p (a b)")[:, 0:D])
        b.twa = B.sb("twa%d" % w, [128, TB], BF16); b.sgd = B.sb("sgd%d" % w, [128, TB], BF16)
        b.sgw = f4("sgw"); b.asg = f4("asg"); b.gg = f4("gg"); b.kk = f4("kk"); b.tq = f4("tq"); b.kmod = f4("kmod")
        b.cum = f4("cum"); b.eg = b.t1; b.egx = f4("egx"); b.eng = b.t2; b.bon = f4("bon")
        b.am = B.sb("am%d" % w, [128, 4, 2, TB], BF16); b.rm = B.sb("rm%d" % w, [128, 4, 2, TB], BF16)
        b.bf = h4("bf"); b.kf = h4("kf"); b.vb = h4("vb")
        b.Btok = B.sb("Btok%d" % w, [128, 512], BF16); b.Ktok = B.sb("Ktok%d" % w, [128, 512], BF16); b.Vtok = B.sb("Vtok%d" % w, [128, 512], BF16)
        for nm, src in (("Pm", b.sgw), ("Qm", b.asg), ("Rm", b.kk), ("Nak", b.kmod), ("Nrb", b.egx), ("Nrk", b.eng)):
            setattr(b, nm, B.view(src, src.a.rearrange("p a b -> p (a b)").bitcast(BF16).rearrange("p (h t) -> p h t", h=8)))
        hTf = b.hT.a.rearrange("p a b -> p (a b)")
        b.Xb = B.view(b.hT, hTf[:, 0:512]); b.Ub = B.view(b.hT, hTf[:, 512:1024])
        b.Yf = B.view(b.xt, b.xt.a.rearrange("p a b -> p (a b)")[:, 0:4 * TB].rearrange("p (j t) -> p j t", j=4))
        b.dd = B.view(b.hb, b.hb.a.rearrange("p a b -> p (a b)").bitcast(F32).rearrange("p (j t) -> p j t", j=4))
        b.yrb = h4("yrb")
        sets.append(b)

    def chunk(ci):
        b = sets[ci % NW]
        PS, tq, cum, kk, kmod, eg, egx, eng, asg, sgw, gg, bon = b.PS, b.tq, b.cum, b.kk, b.kmod, b.eg, b.egx, b.eng, b.asg, b.sgw, b.gg, b.bon
        am, rm, Pm, Qm, Rm, Nak, Nrb, Nrk = b.am, b.rm, b.Pm, b.Qm, b.Rm, b.Nak, b.Nrb, b.Nrk
        frontend(x, ci, 1, gpre, b.xt, b.hb, b.hT, b.scr, b.st)
        yield
        for j0 in range(0, 14, 4):
            nj = min(4, 14 - j0)
            pb = B.ps()
            for jj in range(nj):
                for c in range(8):
                    B.mm(pb.a[:, jj * TB:(jj + 1) * TB], wrg.a[:, c, (j0 + jj) * 128:(j0 + jj + 1) * 128], b.hT.a[:, c, :],
                         c == 0, c == 7, [wrg, b.hT], [pb])
            pv = pb.a[:, 0:nj * TB].rearrange("p (j t) -> p j t", j=nj)
            om_b = omu.a[:, j0:j0 + nj, None].broadcast_to([128, nj, TB])
            mu_b = mu.a[:, j0:j0 + nj, None].broadcast_to([128, nj, TB - 1])
            B.tt("dve", b.t1.a[:, 0:nj, :], pv, om_b, ALU.mult, [pb, omu], [b.t1])
            B.tt("dve", b.t2.a[:, 0:nj, 1:TB], pv[:, :, 0:TB - 1], mu_b, ALU.mult, [pb, mu], [b.t2])
            B.tt("dve", b.t2.a[:, 0:nj, 0:1], pc.a[:, j0:j0 + nj, None], mu.a[:, j0:j0 + nj, None], ALU.mult, [pc, mu], [b.t2])
            B.cp("act", pc.a[:, j0:j0 + nj, None], pv[:, :, TB - 1:TB], [pb], [pc])
            B.tt("pool", PS.a[:, j0:j0 + nj, :], b.t1.a[:, 0:nj, :], b.t2.a[:, 0:nj, :], ALU.add, [b.t1, b.t2], [PS])
            yield
        if "pshift" in dbo:
            S.dma("sp", dbo["pshift"].a[:, :, ci * TB:(ci + 1) * TB], PS.a, reads=[PS], writes=[dbo["pshift"]], dreg=PS)
        r_ = PS.a[:, 0:4, :]; k_ = PS.a[:, 4:8, :]; v_ = PS.a[:, 8:12, :]
        B.act(b.twa.a[0:64, :], PS.a[0:64, 12, :], AF.Tanh, [PS], [b.twa])
        B.cp("act", b.twa.a[64:128, :], PS.a[64:128, 12, :], [PS], [b.twa])
        B.act(b.sgd.a, PS.a[:, 13, :], AF.Sigmoid, [PS], [b.sgd])
        pw_ = B.ps()
        for j in range(4):
            B.mm(pw_.a[:, j * TB:(j + 1) * TB], w2p.a[:, j * 128:(j + 1) * 128], b.twa.a, True, True, [w2p, b.twa], [pw_])
        for j in range(4):
            B.act(sgw.a[:, j, :], pw_.a[:, j * TB:(j + 1) * TB], AF.Sigmoid, [pw_, w0c], [sgw], bias=w0c.a[:, j:j + 1])
        pa_ = B.ps()
        for j in range(4):
            B.mm(pa_.a[:, j * TB:(j + 1) * TB], a2p.a[:, j * 128:(j + 1) * 128], b.twa.a, True, True, [a2p, b.twa], [pa_])
        for j in range(4):
            B.act(asg.a[:, j, :], pa_.a[:, j * TB:(j + 1) * TB], AF.Sigmoid, [pa_, a0c], [asg], bias=a0c.a[:, j:j + 1])
        pg_ = B.ps()
        for j in range(4):
            B.mm(pg_.a[:, j * TB:(j + 1) * TB], g2b.a[:, j * 128:(j + 1) * 128], b.sgd.a, True, True, [g2b, b.sgd], [pg_])
        B.cp("act", gg.a, pg_.a.rearrange("p (j t) -> p j t", j=4), [pg_], [gg])
        yield
        B.tt("dve", kk.a, k_, bc4(kkc), ALU.mult, [PS, kkc], [kk])
        B.tt("pool", tq.a, kk.a, kk.a, ALU.mult, [kk], [tq])
        pb = B.ps()
        for j in range(4):
            B.mm(pb.a[:, j * TB:(j + 1) * TB], bones.a, tq.a[:, j, :], True, True, [bones, tq], [pb])
        B.act(cum.a, pb.a.rearrange("p (j t) -> p j t", j=4), AF.Ln, [pb, epsc], [cum], bias=epsc.a[:, 1:2])
        B.act(cum.a, cum.a, AF.Exp, [cum], [cum], scale=-0.5)
        B.tt("pool", kk.a, kk.a, cum.a, ALU.mult, [kk, cum], [kk])
        yield
        B.tt("dve", tq.a, asg.a, bc4(kac), ALU.mult, [asg, kac], [tq])
        B.tt("dve", tq.a, tq.a, bc4(oka), ALU.add, [tq, oka], [tq])
        B.tt("dve", kmod.a, k_, tq.a, ALU.mult, [PS, tq], [kmod])
        B.tt("pool", tq.a, r_, kmod.a, ALU.mult, [PS, kmod], [tq])
        B.tt("dve", tq.a, tq.a, bc4(rkc), ALU.mult, [tq, rkc], [tq])
        pbon = B.ps()
        for j in range(4):
            B.mm(pbon.a[:, j * TB:(j + 1) * TB], bones.a, tq.a[:, j, :], True, True, [bones, tq], [pbon])
        B.tt("dve", bon.a, pbon.a.rearrange("p (j t) -> p j t", j=4), v_, ALU.mult, [pbon, PS], [bon])
        yield
        for j in range(4):
            B.op("dve", lambda e, j=j: e.tensor_tensor_scan(out=cum.a[:, j, :], data0=ones.a, data1=sgw.a[:, j, :], initial=0.0,
                                                            op0=ALU.mult, op1=ALU.add), [ones, sgw], [cum])
        B.act(eg.a, cum.a, AF.Exp, [cum], [eg], scale=-C0)
        B.act(eng.a, cum.a, AF.Exp, [cum], [eng], scale=C0)
        B.tt("pool", tq.a, cum.a, sgw.a, ALU.subtract, [cum, sgw], [tq])
        B.act(egx.a, tq.a, AF.Exp, [tq], [egx], scale=-C0)
        yield
        B.stt(tq.a, kk.a, -1.0, egx.a, ALU.mult, ALU.mult, [kk, egx], [tq])
        for hh in range(2):
            B.act(am.a[:, :, hh, :], tq.a, AF.Copy, [tq, hm], [am], scale=hm.a[:, hh:hh + 1])
        B.tt("dve", egx.a, r_, eg.a, ALU.mult, [PS, eg], [egx])
        for hh in range(2):
            B.act(rm.a[:, :, hh, :], egx.a, AF.Copy, [egx, hm], [rm], scale=hm.a[:, hh:hh + 1])
        B.tt("pool", tq.a, kk.a, asg.a, ALU.mult, [kk, asg], [tq])
        B.tt("dve", b.bf.a, tq.a, eng.a, ALU.mult, [tq, eng], [b.bf])
        B.tt("dve", b.kf.a, kmod.a, eng.a, ALU.mult, [kmod, eng], [b.kf])
        B.cp("act", b.vb.a, v_, [PS], [b.vb])
        yield
        cs = slice(0, C)
        for n_, (src, dst) in enumerate(((b.bf, b.Btok), (b.kf, b.Ktok), (b.vb, b.Vtok))):
            pb = B.ps()
            pv = pb.a.bitcast(BF16)
            for j in range(4):
                B.tr(pv[:, j * 128:(j + 1) * 128], src.a[:, j, cs], ident_b.a, [src, ident_b], [pb])
            B.cp("act" if n_ != 1 else "dve", dst.a, pv[:, 0:512], [pb], [dst])
        yield
        for g in range(2):
            gs = slice(4 * g, 4 * g + 4)
            for kind in range(5):
                pbk = B.ps()
                for hq in range(4):
                    h = 4 * g + hq
                    j, hh = h // 2, h % 2
                    o = slice(hq * 128, (hq + 1) * 128)
                    bfs, kfs, ams, rms = b.bf.a[:, j, cs], b.kf.a[:, j, cs], am.a[:, j, hh, cs], rm.a[:, j, hh, cs]
                    lhsT, rhs, rd = ((bfs, ams, [b.bf, am]), (ams, bfs, [b.bf, am]), (kfs, ams, [b.kf, am]),
                                     (bfs, rms, [b.bf, rm]), (kfs, rms, [b.kf, rm]))[kind]
                    B.mm(pbk.a[:, o], lhsT, rhs, True, True, rd, [pbk])
                msk, dst = ((MU4, Pm), (ML4, Qm), (MU4, Nak), (MI4, Nrb), (MI4, Nrk))[kind]
                B.tt("dve", dst.a[:, gs, :], pbk.a.rearrange("p (h t) -> p h t", h=4), msk.a, ALU.mult, [pbk, msk], [dst])
            B.tt("pool", Rm.a[:, gs, :], Pm.a[:, gs, :], I4.a, ALU.add, [Pm, I4], [Rm])
            yield
        for lvl in range(1, 7):
            for g in range(2):
                gs = slice(4 * g, 4 * g + 4)
                pq = B.ps()
                pp = B.ps() if lvl < 6 else None
                for hq in range(4):
                    h = 4 * g + hq
                    o = slice(hq * 128, (hq + 1) * 128)
                    B.mm(pq.a[:, o], Pm.a[:, h, :], Qm.a[:, h, :], True, True, [Pm, Qm], [pq])
                    if pp is not None:
                        B.mm(pp.a[:, o], Qm.a[:, h, :], Pm.a[:, h, :], True, True, [Pm, Qm], [pp])
                B.cp("act", Qm.a[:, gs, :], pq.a.rearrange("p (h t) -> p h t", h=4), [pq], [Qm])
                if pp is not None:
                    B.cp("act", Pm.a[:, gs, :], pp.a.rearrange("p (h t) -> p h t", h=4), [pp], [Pm])
                pr = B.ps()
                for hq in range(4):
                    h = 4 * g + hq
                    o = slice(hq * 128, (hq + 1) * 128)
                    B.mm(pr.a[:, o], Qm.a[:, h, :], Rm.a[:, h, :], True, True, [Qm, Rm], [pr])
                B.tt("dve", Rm.a[:, gs, :], pr.a.rearrange("p (h t) -> p h t", h=4), Rm.a[:, gs, :], ALU.add, [pr, Rm], [Rm])
                yield
        px = B.ps()
        for h in range(8):
            j, hh = h // 2, h % 2
            o = slice(h * 64, (h + 1) * 64)
            B.mm(px.a[:, o], am.a[:, j, hh, cs], Hb.a[:, j, :], True, False, [am, Hb], [px])
            B.mm(px.a[:, o], Nak.a[:, h, :], b.Vtok.a[:, o], False, True, [Nak, b.Vtok], [px])
        B.cp("act", b.Xb.a, px.a, [px], [b.Xb])
        pu = B.ps()
        for h in range(8):
            o = slice(h * 64, (h + 1) * 64)
            B.mm(pu.a[:, o], Rm.a[:, h, :], b.Xb.a[:, o], True, True, [Rm, b.Xb], [pu])
        B.cp("act", b.Ub.a, pu.a, [pu], [b.Ub])
        ph = B.ps()
        for h in range(8):
            j, hh = h // 2, h % 2
            o = slice(h * 64, (h + 1) * 64)
            ho = ph.a[hh * 64:(hh + 1) * 64, j * 64:(j + 1) * 64]
            B.mm(ho, b.Btok.a[:, o], b.Ub.a[:, o], True, False, [b.Btok, b.Ub], [ph])
            B.mm(ho, b.Ktok.a[:, o], b.Vtok.a[:, o], False, True, [b.Ktok, b.Vtok], [ph])
        py = B.ps()
        for h in range(8):
            j, hh = h // 2, h % 2
            o = slice(h * 64, (h + 1) * 64)
            yo = py.a[hh * 64:(hh + 1) * 64, j * 128:(j + 1) * 128]
            B.mm(yo, Hb.a[:, j, :], rm.a[:, j, hh, cs], True, False, [Hb, rm], [py])
            B.mm(yo, b.Ub.a[:, o], Nrb.a[:, h, :], False, False, [b.Ub, Nrb], [py])
            B.mm(yo, b.Vtok.a[:, o], Nrk.a[:, h, :], False, True, [b.Vtok, Nrk], [py])
        B.tt("dve", Ht.a, ph.a[:, 0:256].rearrange("p (j i) -> p j i", j=4), H32.a, ALU.add, [ph, H32], [Ht])
        gC = eg.a[:, :, C - 1:C].broadcast_to([128, 4, 64])
        B.tt("dve", H32.a, Ht.a, gC, ALU.mult, [Ht, eg], [H32])
        B.cp("act", Hb.a, H32.a, [H32], [Hb])
        B.cp("act", b.Yf.a, py.a.rearrange("p (j t) -> p j t", j=4), [py], [b.Yf])
        yield
        if "wkv" in dbo:
            S.dma("sp", dbo["wkv"].a[:, :, ci * TB:(ci + 1) * TB], b.Yf.a, reads=[b.Yf], writes=[dbo["wkv"]], dreg=b.Yf)
        Yf, dd = b.Yf, b.dd
        pm_ = B.ps()
        for j in range(4):
            B.mm(pm_.a[:, j * TB:(j + 1) * TB], bavg.a, Yf.a[:, j, :], True, True, [bavg, Yf], [pm_])
        B.tt("dve", dd.a, Yf.a, pm_.a.rearrange("p (j t) -> p j t", j=4), ALU.subtract, [Yf, pm_], [dd])
        B.act(tq.a, dd.a, AF.Square, [dd], [tq])
        pv_ = B.ps()
        for j in range(4):
            B.mm(pv_.a[:, j * TB:(j + 1) * TB], bavg.a, tq.a[:, j, :], True, True, [bavg, tq], [pv_])
        B.act(cum.a, pv_.a.rearrange("p (j t) -> p j t", j=4), AF.Ln, [pv_, epsc], [cum], bias=epsc.a[:, 2:3])
        B.act(cum.a, cum.a, AF.Exp, [cum], [cum], scale=-0.5)
        yield
        B.tt("pool", dd.a, dd.a, cum.a, ALU.mult, [dd, cum], [dd])
        B.tt("dve", dd.a, dd.a, bc4(lwc), ALU.mult, [dd, lwc], [dd])
        B.tt("dve", dd.a, dd.a, bc4(lbc), ALU.add, [dd, lbc], [dd])
        B.tt("pool", dd.a, dd.a, bon.a, ALU.add, [dd, bon], [dd])
        B.tt("dve", b.yrb.a, dd.a, gg.a, ALU.mult, [dd, gg], [b.yrb])
        if "rwkv_y" in dbo:
            S.dma("sp", dbo["rwkv_y"].a[:, :, ci * TB:(ci + 1) * TB], b.yrb.a, reads=[b.yrb], writes=[dbo["rwkv_y"]], dreg=b.yrb)
        S.dma("sp", YR.a[:, :, ci * TB:(ci + 1) * TB], b.yrb.a, reads=[b.yrb], writes=[YR], dreg=b.yrb)
        yield

    pools = [[0, 1, 2], [3, 4, 5], [6, 7]] if NW == 3 else ([[0, 1, 2, 3], [4, 5, 6, 7]] if NW == 2 else [list(range(8))])
    S.rec = []
    for ci in range(ntiles):
        B.ps_pool = pools[ci % NW]
        for _ in chunk(ci):
            pass
    items = S.rec
    S.rec = None
    B.ps_pool = None
    S.schedule_emit(items, window=int(os.environ.get("K_WIN", "1400")))
    B.pop()


def phase_1c(B, W, x, x1s, YR, YGLU, epsc, gpre, frontend, bc_load, col_load, dbo, ntiles):
    nc, S = B.nc, B.S
    TB = 512
    NS = TB // 128
    B.push()
    wg = B.sb("wg", [128, 8, 2048], BF16)
    win = W["w_in"].a.rearrange("(c p) n -> p c n", p=128)
    for c in range(8):
        S.dma("pool", wg.a[:, c, :], win[:, c, NR + 512:NIN], reads=[W["w_in"]], writes=[wg])
    wbr = B.sb("wbr", [128, 4, D], BF16); wbs = B.sb("wbs", [128, 4, D], BF16); wout = B.sb("wout", [128, 8, D], BF16)
    S.dma("pool", wbr.a, W["w_branch_rwkv"].a.rearrange("(c p) n -> p c n", p=128), reads=[W["w_branch_rwkv"]], writes=[wbr])
    S.dma("pool", wbs.a, W["w_branch_s5"].a.rearrange("(c p) n -> p c n", p=128), reads=[W["w_branch_s5"]], writes=[wbs])
    for c in range(8):
        S.dma("pool", wout.a[:, c, :], W["w_out"].a[c * 128:(c + 1) * 128, :], reads=[W["w_out"]], writes=[wout])
    gpost = bc_load("norm_mix_post", D)
    bgc = col_load("b_gate", 16)
    class BS:
        pass
    sets = []
    for w in range(2):
        q = BS()
        q.xt = B.sb("xt%d" % w, [128, NS, D]); q.hb = B.sb("hb%d" % w, [128, NS, D], BF16); q.hT = B.sb("hT%d" % w, [128, 8, TB], BF16)
        q.scr = B.sb("scr%d" % w, [128, D]); q.st = B.sb("st%d" % w, [128, 12])
        q.yrt = B.sb("yrt%d" % w, [128, 4, TB], BF16); q.ygt = B.sb("ygt%d" % w, [128, 4, TB], BF16)
        q.mixb = B.sb("mixb%d" % w, [128, 8, TB], BF16)
        sets.append(q)
    gA = B.sb("gA", [128, TB]); gB = B.sb("gB", [128, TB]); mt1 = B.sb("mt1", [128, TB]); mt2 = B.sb("mt2", [128, TB])

    def part_a(ti):
        q = sets[ti % 2]
        frontend(x, ti, NS, gpre, q.xt, q.hb, q.hT, q.scr, q.st)
        S.dma("sp", q.yrt.a, YR.a[:, :, ti * TB:(ti + 1) * TB], reads=[YR], writes=[q.yrt])
        S.dma("sp", q.ygt.a, YGLU.a[:, :, ti * TB:(ti + 1) * TB], reads=[YGLU], writes=[q.ygt])

    def part_b(ti):
        q = sets[ti % 2]
        xt, hb, hT, scr, st, yrt, ygt, mixb = q.xt, q.hb, q.hT, q.scr, q.st, q.yrt, q.ygt, q.mixb
        for cb in range(8):
            pa = B.ps(); pbb = B.ps(); po = B.ps(); ps_ = B.ps()
            for c in range(8):
                B.mm(pa.a, wg.a[:, c, cb * 128:(cb + 1) * 128], hT.a[:, c, :], c == 0, c == 7, [wg, hT], [pa])
            for c in range(8):
                B.mm(pbb.a, wg.a[:, c, (8 + cb) * 128:(9 + cb) * 128], hT.a[:, c, :], c == 0, c == 7, [wg, hT], [pbb])
            for j in range(4):
                B.mm(po.a, wbr.a[:, j, cb * 128:(cb + 1) * 128], yrt.a[:, j, :], j == 0, j == 3, [wbr, yrt], [po])
            for j in range(4):
                B.mm(ps_.a, wbs.a[:, j, cb * 128:(cb + 1) * 128], ygt.a[:, j, :], j == 0, j == 3, [wbs, ygt], [ps_])
            B.act(gA.a, pa.a, AF.Sigmoid, [pa, bgc], [gA], bias=bgc.a[:, cb:cb + 1])
            B.act(gB.a, pbb.a, AF.Sigmoid, [pbb, bgc], [gB], bias=bgc.a[:, 8 + cb:9 + cb])
            B.tt("dve", mt1.a, po.a, gA.a, ALU.mult, [po, gA], [mt1])
            B.tt("dve", mt2.a, ps_.a, gB.a, ALU.mult, [ps_, gB], [mt2])
            B.tt("pool", mixb.a[:, cb, :], mt1.a, mt2.a, ALU.add, [mt1, mt2], [mixb])

    def part_c(ti):
        q = sets[ti % 2]
        xt, hb, hT, scr, st, yrt, ygt, mixb = q.xt, q.hb, q.hT, q.scr, q.st, q.yrt, q.ygt, q.mixb
        for s_ in range(NS):
            pbs = [B.ps(), B.ps()]
            for half in range(2):
                for c8 in range(8):
                    B.mm(pbs[half].a, mixb.a[:, c8, s_ * 128:(s_ + 1) * 128], wout.a[:, c8, half * 512:(half + 1) * 512],
                         c8 == 0, c8 == 7, [mixb, wout], [pbs[half]])
            for half in range(2):
                B.act(scr.a[:, 0:512], pbs[half].a, AF.Square, [pbs[half]], [scr, st], accum_out=st.a[:, half:half + 1])
            B.tt("dve", st.a[:, 2:3], st.a[:, 0:1], st.a[:, 1:2], ALU.add, [st], [st])
            B.act(st.a[:, 2:3], st.a[:, 2:3], AF.Ln, [st, epsc], [st], scale=1.0 / D, bias=epsc.a[:, 0:1])
            B.act(st.a[:, 3:4], st.a[:, 2:3], AF.Exp, [st], [st], scale=-0.5)
            for half in range(2):
                hsl = slice(half * 512, (half + 1) * 512)
                B.stt(scr.a[:, hsl], pbs[half].a, st.a[:, 3:4], gpost.a[:, hsl], ALU.mult, ALU.mult, [pbs[half], st, gpost], [scr])
            B.tt("pool", xt.a[:, s_, :], xt.a[:, s_, :], scr.a, ALU.add, [xt, scr], [xt])
        S.dma("sp", x1s.a[ti * TB:(ti + 1) * TB, :].rearrange("(s p) d -> p s d", p=128), xt.a, reads=[xt], writes=[x1s], dreg=xt)
        if "x1" in dbo:
            S.dma("sp", dbo["x1"].a[ti * TB:(ti + 1) * TB, :].rearrange("(s p) d -> p s d", p=128), xt.a, reads=[xt],
                  writes=[dbo["x1"]], dreg=xt)

    part_a(0)
    for ti in range(ntiles):
        part_b(ti)
        if ti + 1 < ntiles:
            part_a(ti + 1)
        part_c(ti)
    B.pop()


def phase_2(B, W, x1s, out, epsc, frontend, bc_load, dbo, ntiles):
    nc, S = B.nc, B.S
    TB = 512
    NS = TB // 128
    ntiles = ntiles * (512 // TB)
    B.push()
    wup = B.sb("wup", [128, 8, 2 * FF], BF16)
    wsrc = W["ffn_w_up"].a.rearrange("(c p) n -> p c n", p=128)
    for c in range(8):
        for (a, b) in ((0, 2048), (2048, 4096), (4096, 2 * FF)):
            S.dma("pool", wup.a[:, c, a:b], wsrc[:, c, a:b], reads=[W["ffn_w_up"]], writes=[wup])
    wdn = B.sb("wdn", [128, 22, D], BF16)
    for i in range(22):
        S.dma("pool", wdn.a[:, i, :], W["ffn_w_down"].a[i * 128:(i + 1) * 128, :], reads=[W["ffn_w_down"]], writes=[wdn])
    g1 = bc_load("norm_ffn_pre", D)
    g2 = bc_load("norm_ffn_post", D)
    cw = B.sb("cw", [128, 3, 44]); cbias = B.sb("cbias", [128, 44])
    S.dma("sp", cw.a, W["ffn_conv_w"].a.rearrange("j (b p) -> p j b", p=128), reads=[W["ffn_conv_w"]], writes=[cw],
          allow_slow_non_contiguous=True)
    S.dma("sp", cbias.a, W["ffn_conv_b"].a.rearrange("(b p) -> p b", p=128), reads=[W["ffn_conv_b"]], writes=[cbias],
          allow_slow_non_contiguous=True)
    halo = B.sb("halo", [128, 44, 2]); B.memset("pool", halo.a, 0.0, [halo])
    class BS:
        pass
    sets = []
    for w in range(1):
        q = BS()
        q.xt = B.sb("xt2_%d" % w, [128, NS, D]); q.hT = B.sb("hT2_%d" % w, [128, 8, TB], BF16)
        q.actb = B.sb("actb%d" % w, [128, 22, TB], BF16)
        q.hb = B.view(q.actb, q.actb.a.rearrange("p a b -> p (a b)")[:, 0:NS * D].rearrange("p (s d) -> p s d", s=NS))
        q.st = B.sb("st2_%d" % w, [128, 12])
        sets.append(q)
    scrF_ = B.sb("scrF", [128, D])
    a0s_ = (B.sb("a0g", [128, TB]), B.sb("a0v", [128, TB]))
    a1s_ = (B.sb("a1g", [128, TB]), B.sb("a1v", [128, TB]))
    for q in sets:
        q.scrF, q.a0s, q.a1s = scrF_, a0s_, a1s_
    def tile(ti):
        q = sets[0]
        xt, hT, actb, hb, st, scrF, a0s, a1s = q.xt, q.hT, q.actb, q.hb, q.st, q.scrF, q.a0s, q.a1s
        accg, accv = a1s
        frontend(x1s, ti, NS, g1, xt, hb, hT, scrF, st)
        for i in range(22):
            accs = a1s
            for gv in range(2):
                b = i + 22 * gv
                pb = B.ps()
                for c in range(8):
                    B.mm(pb.a[:, 0:TB], wup.a[:, c, b * 128:(b + 1) * 128], hT.a[:, c, :], c == 0, c == 7, [wup, hT], [pb])
                acc = accs[gv]; a0 = a0s[gv]; a1 = a1s[gv]
                B.act(a0.a, pb.a[:, 0:TB], AF.Identity, [pb, cw, cbias], [a0], scale=cw.a[:, 2, b:b + 1], bias=cbias.a[:, b:b + 1])
                B.act(a1.a[:, 1:TB], pb.a[:, 0:TB - 1], AF.Copy, [pb, cw], [a1], scale=cw.a[:, 1, b:b + 1])
                B.stt(a0.a[:, 2:TB], pb.a[:, 0:TB - 2], cw.a[:, 0, b:b + 1], a0.a[:, 2:TB], ALU.mult, ALU.add, [pb, cw, a0], [a0])
                B.ts("dve", a1.a[:, 0:1], halo.a[:, b, 1:2], cw.a[:, 1, b:b + 1], ALU.mult, [halo, cw], [a1])
                B.stt(a0.a[:, 0:2], halo.a[:, b, 0:2], cw.a[:, 0, b:b + 1], a0.a[:, 0:2], ALU.mult, ALU.add, [halo, cw, a0], [a0])
                B.cp("dve", halo.a[:, b, :], pb.a[:, TB - 2:TB], [pb], [halo])
                B.tt("pool", acc.a, a0.a, a1.a, ALU.add, [a0, a1], [acc])
            if "zc" in dbo and ti == 0 and i == 0:
                S.dma("sp", dbo["zc"].a[:, 0:TB], accg.a, reads=[accg], writes=[dbo["zc"]], dreg=accg)
            B.act(accg.a, accg.a, AF.Gelu_apprx_tanh, [accg], [accg])
            B.tt("pool", actb.a[:, i, :], accg.a, accv.a, ALU.mult, [accg, accv], [actb])
        for s_ in range(NS):
            pbs = [B.ps(), B.ps()]
            for half in range(2):
                for i in range(22):
                    B.mm(pbs[half].a, actb.a[:, i, s_ * 128:(s_ + 1) * 128], wdn.a[:, i, half * 512:(half + 1) * 512],
                         i == 0, i == 21, [actb, wdn], [pbs[half]])
            for half in range(2):
                B.act(scrF.a[:, 0:512], pbs[half].a, AF.Square, [pbs[half]], [scrF, st], accum_out=st.a[:, half:half + 1])
            B.tt("dve", st.a[:, 2:3], st.a[:, 0:1], st.a[:, 1:2], ALU.add, [st], [st])
            B.act(st.a[:, 2:3], st.a[:, 2:3], AF.Ln, [st, epsc], [st], scale=1.0 / D, bias=epsc.a[:, 0:1])
            B.act(st.a[:, 3:4], st.a[:, 2:3], AF.Exp, [st], [st], scale=-0.5)
            for half in range(2):
                hsl = slice(half * 512, (half + 1) * 512)
                B.stt(scrF.a[:, hsl], pbs[half].a, st.a[:, 3:4], g2.a[:, hsl], ALU.mult, ALU.mult, [pbs[half], st, g2], [scrF])
            B.tt("pool", xt.a[:, s_, :], xt.a[:, s_, :], scrF.a, ALU.add, [xt, scrF], [xt])
        S.dma("sp", out.a[ti * TB:(ti + 1) * TB, :].rearrange("(s p) d -> p s d", p=128), xt.a, reads=[xt], writes=[out], dreg=xt)

    S.rec = []
    for ti in range(ntiles):
        tile(ti)
    items = S.rec
    S.rec = None
    if os.environ.get("K_P2SCHED", "0") == "1":
        S.schedule_emit(items, window=int(os.environ.get("K_WIN2", "1500")))
    else:
        for it in items:
            S.replay(it)
    B.pop()


_CACHE = {}


def kernel(**inputs):
    if "B" not in _CACHE:
        _CACHE["B"] = build()
    Bd = _CACHE["B"]
    x = np.ascontiguousarray(inputs["x"], dtype=np.float32)
    wmap = {k: np.ascontiguousarray(np.asarray(inputs[k], dtype=np.float32).reshape(shp)) for k, shp in Bd.Wshapes.items()}
    in_maps = []
    for c in range(8):
        m = dict(wmap)
        m["x"] = x[c]
        in_maps.append(m)
    res = run_bass_kernel_spmd(Bd.nc, in_maps, core_ids=list(range(8)))
    return np.stack([np.asarray(res.results[c]["out"], dtype=np.float32) for c in range(8)], axis=0)
```

```python
import contextlib
import math
import os
import numpy as np
import concourse.bass as bass
import concourse.mybir as mybir
from concourse.bass_utils import run_bass_kernel_spmd

F32 = mybir.dt.float32
BF16 = mybir.dt.bfloat16
I32 = mybir.dt.int32
ALU = mybir.AluOpType
AF = mybir.ActivationFunctionType

L = 4096
D = 1024
NR = 1792
NS5 = 512
NIN = 4352
FF = 2816
PI = math.pi


class Reg:
    __slots__ = ("name", "w", "r", "dsem", "dcnt", "last", "excl")

    def __init__(self, name=""):
        self.name = name
        self.w = None
        self.r = []
        self.dsem = None
        self.dcnt = 0
        self.last = 0
        self.excl = False


class Sched:
    def __init__(self, nc):
        self.nc = nc
        self.eng = {"pe": nc.tensor, "dve": nc.vector, "act": nc.scalar,
                    "pool": nc.gpsimd, "sp": nc.sync}
        self.sem = {k: nc.alloc_semaphore(name="sem_" + k) for k in self.eng}
        self.cnt = {k: 0 for k in self.eng}
        self.seen = {k: {} for k in self.eng}
        self.ninst = 0
        self.nds = 0
        self.dregs = []
        self.dmap = {}
        self.maxops = int(os.environ.get("K_MAXOPS", "100000000"))
        self.rec = None

    def _wait(self, e, tok):
        sem, val = tok
        key = sem.name
        if key in self.dmap:
            val = max(val, self.dmap[key].dcnt)
        if self.seen[e].get(key, 0) >= val:
            return
        if e == "pe" and sem is self.sem["pe"]:
            return
        self.eng[e].wait_ge(sem, val)
        self.seen[e][key] = val

    def _deps(self, e, reads, writes, skip=None):
        for r in reads:
            if r.w is not None:
                self._wait(e, r.w)
        for w in writes:
            if w.w is not None and w.w[0] is not skip:
                self._wait(e, w.w)
            for t in w.r:
                self._wait(e, t)

    def _commit(self, tok, reads, writes):
        for r in reads:
            r.last = self.ninst
            r.r.append(tok)
            if len(r.r) > 16:
                d = {}
                for s, v in r.r:
                    if d.get(s.name, (None, -1))[1] < v:
                        d[s.name] = (s, v)
                r.r = list(d.values())
        for w in writes:
            w.last = self.ninst
            w.w = tok
            w.r = []

    def op(self, e, fn, reads=(), writes=(), cost=None):
        if self.rec is not None:
            self.rec.append(("op", e, fn, list(reads), list(writes), None, cost))
            return None
        if self.ninst >= self.maxops:
            return None
        reads = [x.r if isinstance(x, Buf) else x for x in reads]
        writes = [x.r if isinstance(x, Buf) else x for x in writes]
        ex = [x for x in reads if x.excl and x not in writes]
        if ex:
            reads = [x for x in reads if not x.excl]
            writes = list(writes) + ex
        self._deps(e, reads, writes)
        ins = fn(self.eng[e])
        self.cnt[e] += 1
        ins.then_inc(self.sem[e], 1)
        tok = (self.sem[e], self.cnt[e])
        self._commit(tok, reads, writes)
        self.ninst += 1
        return tok

    def dma(self, e, out, in_, reads=(), writes=(), dreg=None, **kw):
        if self.rec is not None:
            self.rec.append(("dma", e, (out, in_), list(reads), list(writes), (dreg, kw), None))
            return None
        if self.ninst >= self.maxops and not kw.pop("force", False):
            return None
        kw.pop("force", None)
        reads = [x.r if isinstance(x, Buf) else x for x in reads]
        writes = [x.r if isinstance(x, Buf) else x for x in writes]
        if dreg is None:
            dreg = writes[0] if writes else reads[0]
        elif isinstance(dreg, Buf):
            dreg = dreg.r
        if dreg.dsem is None:
            self.nds += 1
            dreg.dsem = self.nc.alloc_semaphore(name="ds%d_%s" % (self.nds, dreg.name))
            self.dregs.append(dreg)
            self.dmap[dreg.dsem.name] = dreg
        self._deps(e, reads, writes, skip=dreg.dsem)
        ins = self.eng[e].dma_start(out=out, in_=in_, **kw)
        dreg.dcnt += 16
        ins.then_inc(dreg.dsem, 16)
        tok = (dreg.dsem, dreg.dcnt)
        self._commit(tok, reads, writes)
        self.ninst += 1
        return tok

    def replay(self, item):
        kind, e, a, reads, writes, extra = item[:6]
        if kind == "op":
            return self.op(e, a, reads, writes)
        dreg, kw = extra
        return self.dma(e, a[0], a[1], reads=reads, writes=writes, dreg=dreg, **kw)

    def schedule_emit(self, items, window=1500):
        import heapq
        n = len(items)
        norm = []
        for it in items:
            kind, e, a, reads, writes, extra, cost = it
            reads = [x.r if isinstance(x, Buf) else x for x in reads]
            writes = [x.r if isinstance(x, Buf) else x for x in writes]
            dreg = None
            if kind == "dma":
                dreg = extra[0]
                if dreg is None:
                    dreg = writes[0] if writes else reads[0]
                elif isinstance(dreg, Buf):
                    dreg = dreg.r
            ex = [x for x in reads if x.excl and x not in writes]
            if ex:
                reads = [x for x in reads if not x.excl]
                writes = list(writes) + ex
            norm.append((kind, e, a, reads, writes, extra, cost, dreg))
        lw = {}
        rd = {}
        first = [[] for _ in range(n)]
        preds = [set() for _ in range(n)]
        for i, (kind, e, a, reads, writes, extra, cost, dreg) in enumerate(norm):
            for r in reads:
                k = id(r)
                if k in lw:
                    preds[i].add(lw[k])
                else:
                    first[i].append((r, "r"))
            for w in writes:
                k = id(w)
                if k in lw:
                    if not (kind == "dma" and norm[lw[k]][0] == "dma" and norm[lw[k]][7] is dreg):
                        preds[i].add(lw[k])
                else:
                    first[i].append((w, "w"))
                for j in rd.get(k, ()):
                    preds[i].add(j)
            for r in reads:
                rd.setdefault(id(r), []).append(i)
            for w in writes:
                lw[id(w)] = i
                rd[id(w)] = []
            preds[i].discard(i)
        succs = [[] for _ in range(n)]
        npred = [len(p) for p in preds]
        for i, p in enumerate(preds):
            for j in p:
                succs[j].append(i)
        def dur(it):
            kind, e, a, reads, writes, extra, cost, dreg = it
            if cost is not None:
                return cost
            if kind == "op":
                return {"pe": 0.08, "dve": 0.4, "act": 0.4, "pool": 1.0, "sp": 2.0}[e]
            o = a[0]
            nbytes = 1
            for d in list(o.shape):
                nbytes *= int(d)
            nbytes *= mybir.dt.size(o.dtype)
            return 2.5 + nbytes / 140e3
        LAT = 0.15
        free = {e: 0.0 for e in self.eng}
        fin = [0.0] * n
        ready_t = [0.0] * n
        heaps = {e: [] for e in self.eng}
        low = 0
        done = [False] * n
        avail = [False] * n
        for i in range(n):
            if npred[i] == 0:
                heapq.heappush(heaps[norm[i][1]], (0.0, i)); avail[i] = True
        order = []
        deferred = {e: [] for e in self.eng}
        while len(order) < n:
            best = None
            for e, h in heaps.items():
                while h and h[0][1] >= low + window:
                    deferred[e].append(heapq.heappop(h))
                if not h:
                    continue
                rt, i = h[0]
                st = max(rt, free[e])
                if best is None or (st, i) < (best[0], best[1]):
                    best = (st, i, e)
            if best is None:
                for e in deferred:
                    for x in deferred[e]:
                        heapq.heappush(heaps[e], x)
                    deferred[e] = []
                window *= 2
                continue
            st, i, e = best
            heapq.heappop(heaps[e])
            order.append(i)
            done[i] = True
            f = st + dur(norm[i])
            fin[i] = f
            free[e] = f
            for j in succs[i]:
                npred[j] -= 1
                ready_t[j] = max(ready_t[j], f + LAT)
                if npred[j] == 0:
                    heapq.heappush(heaps[norm[j][1]], (ready_t[j], j)); avail[j] = True
            if i == low:
                while low < n and done[low]:
                    low += 1
                for e2 in deferred:
                    keep = []
                    for x in deferred[e2]:
                        if x[1] < low + window:
                            heapq.heappush(heaps[e2], x)
                        else:
                            keep.append(x)
                    deferred[e2] = keep
        self.sched_makespan = max(fin) if n else 0.0
        toks = [None] * n
        for i in order:
            kind, e, a, reads, writes, extra, cost, dreg = norm[i]
            for (reg, mode) in first[i]:
                if reg.w is not None and not (kind == "dma" and mode == "w" and reg.w[0] is (dreg.dsem if dreg is not None else None)):
                    self._wait(e, reg.w)
                if mode == "w":
                    for t in reg.r:
                        self._wait(e, t)
            for j in preds[i]:
                self._wait(e, toks[j])
            if kind == "op":
                ins = a(self.eng[e])
                self.cnt[e] += 1
                ins.then_inc(self.sem[e], 1)
                toks[i] = (self.sem[e], self.cnt[e])
            else:
                dg, kw = extra
                kw = dict(kw); kw.pop("force", None)
                if dreg.dsem is None:
                    self.nds += 1
                    dreg.dsem = self.nc.alloc_semaphore(name="ds%d_%s" % (self.nds, dreg.name))
                    self.dregs.append(dreg)
                    self.dmap[dreg.dsem.name] = dreg
                ins = self.eng[e].dma_start(out=a[0], in_=a[1], **kw)
                dreg.dcnt += 16
                ins.then_inc(dreg.dsem, 16)
                toks[i] = (dreg.dsem, dreg.dcnt)
            self.ninst += 1
        touched = {}
        for i, it in enumerate(norm):
            for r in it[3]:
                touched[id(r)] = r
            for w in it[4]:
                touched[id(w)] = w
        for k, reg in touched.items():
            if k in lw:
                reg.w = toks[lw[k]]
                reg.r = [toks[j] for j in rd.get(k, ())]
            else:
                reg.r = list(reg.r) + [toks[j] for j in rd.get(k, ())]
            reg.last = self.ninst

    def barrier(self):
        for e in self.eng:
            for f in self.eng:
                if f != e and self.cnt[f] > 0:
                    self._wait(e, (self.sem[f], self.cnt[f]))
            for d in self.dregs:
                if d.dcnt > 0:
                    self._wait(e, (d.dsem, d.dcnt))

    def final_wait(self, e, regs):
        for r in regs:
            r = r.r if isinstance(r, Buf) else r
            if r.w is not None:
                self._wait(e, r.w)
            for t in r.r:
                self._wait(e, t)


class Buf:
    def __init__(self, t, name):
        self.t = t
        self.a = t.ap()
        self.r = Reg(name)


class Builder:
    def __init__(self, dbg=None, ntiles=8):
        self.nc = bass.Bass("TRN2", target_bir_lowering=False)
        self.S = Sched(self.nc)
        self.dbg = dbg or {}
        self.ntiles = ntiles
        self.din = {}
        self.dout = {}
        self.nbuf = 0
        self.psb = None
        self.psi = 0
        self.ps_pool = None
        self.pclock = 0
        self.scopes = []

    def push(self):
        self.scopes.append(contextlib.ExitStack())

    def pop(self):
        self.S.barrier()
        self.scopes.pop().close()

    def inp(self, name, shape):
        b = Buf(self.nc.dram_tensor(name, list(shape), F32, kind="ExternalInput"), name)
        self.din[name] = b
        return b

    def outp(self, name, shape, dt=F32):
        b = Buf(self.nc.dram_tensor(name, list(shape), dt, kind="ExternalOutput"), name)
        self.dout[name] = b
        return b

    def sb(self, name, shape, dt=F32):
        self.nbuf += 1
        nm = "%s_%d" % (name, self.nbuf)
        if self.scopes:
            return Buf(self.scopes[-1].enter_context(self.nc.sbuf_tensor(nm, list(shape), dt)), name)
        return Buf(self.nc.alloc_sbuf_tensor(nm, list(shape), dt), name)

    def view(self, buf, ap):
        v = Buf.__new__(Buf)
        v.t = buf.t
        v.a = ap
        v.r = buf.r
        return v

    def init_psum(self):
        self.psb = []
        for i in range(8):
            t = self.nc.alloc_psum_tensor("psb%d" % i, [128, 512], F32)
            self.psb.append(Buf(t, "psb%d" % i))
            self.psb[-1].r.excl = True

    def ps(self):
        if self.ps_pool is not None:
            b = self.psb[self.ps_pool[self.psi % len(self.ps_pool)]]
            self.psi += 1
            return b
        b = min(self.psb, key=lambda t: t.r.last)
        self.pclock = max(self.pclock, self.S.ninst) + 1
        b.r.last = self.pclock
        return b

    def op(self, e, fn, reads=(), writes=(), cost=None):
        return self.S.op(e, fn, reads, writes, cost=cost)

    @staticmethod
    def fsz(ap):
        n = 1
        for d in list(ap.shape)[1:]:
            n *= int(d)
        return n

    def mm(self, out, lhsT, rhs, start, stop, reads, writes, **kw):
        return self.op("pe", lambda e: e.matmul(out, lhsT=lhsT, rhs=rhs, start=start, stop=stop, **kw), reads, writes,
                       cost=0.03 + max(self.fsz(rhs), 64) * 0.00052)

    def tr(self, out, in_, ident, reads, writes):
        return self.op("pe", lambda e: e.transpose(out, in_, ident), reads, writes, cost=0.1)

    def act(self, eng_out, in_, func, reads, writes, **kw):
        return self.op("act", lambda e: e.activation(out=eng_out, in_=in_, func=func, **kw), reads, writes,
                       cost=0.22 + self.fsz(in_) * 0.00075)

    def tt(self, e, out, in0, in1, op, reads, writes):
        c = (0.1 + self.fsz(in0) * 0.00115) if e == "dve" else (0.6 + self.fsz(in0) * 0.0013)
        return self.op(e, lambda g: g.tensor_tensor(out=out, in0=in0, in1=in1, op=op), reads, writes, cost=c)

    def ts(self, e, out, in0, s1, op0, reads, writes, s2=None, op1=None):
        c = (0.1 + self.fsz(in0) * 0.0008) if e == "dve" else (0.6 + self.fsz(in0) * 0.0013)
        if op1 is None:
            return self.op(e, lambda g: g.tensor_scalar(out=out, in0=in0, scalar1=s1, scalar2=None, op0=op0), reads, writes, cost=c)
        return self.op(e, lambda g: g.tensor_scalar(out=out, in0=in0, scalar1=s1, scalar2=s2, op0=op0, op1=op1), reads, writes, cost=c)

    def stt(self, out, in0, scalar, in1, op0, op1, reads, writes):
        return self.op("dve", lambda g: g.scalar_tensor_tensor(out=out, in0=in0, scalar=scalar, in1=in1, op0=op0, op1=op1), reads, writes,
                       cost=0.1 + self.fsz(in0) * 0.00115)

    def cp(self, e, out, in_, reads, writes):
        if e == "act":
            return self.op("act", lambda g: g.activation(out=out, in_=in_, func=AF.Copy), reads, writes, cost=0.22 + self.fsz(in_) * 0.00075)
        c = (0.1 + self.fsz(in_) * 0.0008) if e == "dve" else (0.6 + self.fsz(in_) * 0.0013)
        return self.op(e, lambda g: g.tensor_copy(out, in_), reads, writes, cost=c)

    def memset(self, e, out, val, writes):
        return self.op(e, lambda g: g.memset(out, val), (), writes)

    def cmul(self, e, o_re, o_im, a_re, a_im, b_re, b_im, t1, t2, reads, writes, tmp):
        R = list(reads)
        self.tt(e, t1, a_re, b_re, ALU.mult, R, [tmp])
        self.tt(e, t2, a_im, b_im, ALU.mult, R, [tmp])
        self.tt(e, o_re, t1, t2, ALU.subtract, [tmp], writes)
        self.tt(e, t1, a_re, b_im, ALU.mult, R, [tmp])
        self.tt(e, t2, a_im, b_re, ALU.mult, R, [tmp])
        self.tt(e, o_im, t1, t2, ALU.add, [tmp], writes)


def build(dbg=None, ntiles=8, phases=("1a", "1b", "1c", "2")):
    B = Builder(dbg, ntiles)
    nc, S = B.nc, B.S
    dbg = B.dbg

    x = B.inp("x", [L, D])
    shapes = {
        "norm_mix_pre": [D], "norm_mix_post": [D], "norm_ffn_pre": [D], "norm_ffn_post": [D],
        "w_in": [D, NIN], "b_gate": [2048], "rwkv_shift_mu": [NR], "rwkv_w0": [512],
        "rwkv_w2": [64, 512], "rwkv_a0": [512], "rwkv_a2": [64, 512], "rwkv_g2": [128, 512],
        "rwkv_k_k": [512], "rwkv_k_a": [512], "rwkv_r_k": [512], "rwkv_lnx_w": [512],
        "rwkv_lnx_b": [512], "s5_a_re": [32, 64], "s5_a_im": [32, 64], "s5_b_re": [32, 64, 16],
        "s5_b_im": [32, 64, 16], "s5_c_re": [32, 16, 64], "s5_c_im": [32, 16, 64], "s5_d": [512],
        "s5_log_step": [32], "s5_w_glu": [512, 512], "s5_b_glu": [512], "w_branch_rwkv": [512, D],
        "w_branch_s5": [512, D], "w_out": [D, D], "ffn_w_up": [D, 2 * FF], "ffn_conv_w": [3, 2 * FF],
        "ffn_conv_b": [2 * FF], "ffn_w_down": [FF, D],
    }
    W = {k: B.inp(k, v) for k, v in shapes.items()}
    out = B.outp("out", [L, D])
    dbo = {k: B.outp("dbg_" + k, shp, dt) for k, (shp, dt) in dbg.items()}

    B.init_psum()

    B.push()
    ident_f = B.sb("ident_f", [128, 128], F32)
    ident_b = B.sb("ident_b", [128, 128], BF16)
    B.memset("pool", ident_f.a, 1.0, [ident_f])
    B.op("pool", lambda e: e.affine_select(out=ident_f.a, in_=ident_f.a, pattern=[[-1, 128]],
                                           compare_op=ALU.is_equal, fill=0.0, base=0, channel_multiplier=1),
         [ident_f], [ident_f])
    B.cp("pool", ident_b.a, ident_f.a, [ident_f], [ident_b])
    epsc = B.sb("epsc", [128, 4])
    B.memset("pool", epsc.a[:, 0:1], 1e-6, [epsc]); B.memset("pool", epsc.a[:, 1:2], 1e-24, [epsc])
    B.memset("pool", epsc.a[:, 2:3], 64e-5, [epsc]); B.memset("pool", epsc.a[:, 3:4], 0.0, [epsc])

    def bc_load(name, n, q="sp"):
        t = B.sb(name + "_bc", [128, n], F32)
        S.dma(q, t.a, W[name].a.partition_broadcast(128), reads=[W[name]], writes=[t])
        return t

    def col_load(name, nt, q="sp"):
        t = B.sb(name + "_col", [128, nt], F32)
        S.dma(q, t.a, W[name].a.rearrange("(t p) -> p t", p=128), reads=[W[name]], writes=[t],
              allow_slow_non_contiguous=True)
        return t

    def frontend(src, ti, ntok_tiles, gbc, xt, hb, hT, scr, st, load=True):
        nt = ntok_tiles
        T = 128 * nt
        if load:
            S.dma("sp", xt.a, src.a[ti * T:(ti + 1) * T, :].rearrange("(s p) d -> p s d", p=128),
                  reads=[src], writes=[xt])
        for s in range(nt):
            B.act(scr.a, xt.a[:, s, :], AF.Square, [xt], [scr, st], accum_out=st.a[:, s:s + 1])
        B.act(st.a[:, nt:2 * nt], st.a[:, 0:nt], AF.Ln, [st, epsc], [st], scale=1.0 / D, bias=epsc.a[:, 0:1])
        B.act(st.a[:, 2 * nt:3 * nt], st.a[:, nt:2 * nt], AF.Exp, [st], [st], scale=-0.5)
        for s in range(nt):
            B.stt(hb.a[:, s, :], xt.a[:, s, :], st.a[:, 2 * nt + s:2 * nt + s + 1], gbc.a, ALU.mult, ALU.mult,
                  [xt, st, gbc], [hb])
        for c in range(8):
            pb = B.ps()
            pv = pb.a.bitcast(BF16)
            for s in range(nt):
                B.tr(pv[:, s * 128:(s + 1) * 128], hb.a[:, s, c * 128:(c + 1) * 128], ident_b.a,
                     [hb, ident_b], [pb])
            B.cp("act" if c % 2 == 0 else "dve", hT.a[:, c, :], pv[:, 0:T], [pb], [hT])

    T1 = 512
    YGLU = Buf(nc.dram_tensor("yglu_d", [128, 4, L], BF16, kind="Internal"), "yglu_d")

    gpre = bc_load("norm_mix_pre", D)
    x1s = Buf(nc.dram_tensor("x1s", [L, D], F32, kind="Internal"), "x1s")
    YR = Buf(nc.dram_tensor("yr_d", [128, 4, L], BF16, kind="Internal"), "yr_d")
    H2T = Buf(nc.dram_tensor("h2t_d", [128, 8, L], BF16, kind="Internal"), "h2t_d")
    if "1a" in phases:
        B.push()
        ws5 = B.sb("ws5", [128, 8, 512], BF16)
        S.dma("pool", ws5.a, W["w_in"].a.rearrange("(c p) n -> p c n", p=128)[:, :, NR:NR + 512],
              reads=[W["w_in"]], writes=[ws5])
        wglu = B.sb("wglu", [128, 4, 512], BF16)
        S.dma("pool", wglu.a, W["s5_w_glu"].a.rearrange("(c p) n -> p c n", p=128), reads=[W["s5_w_glu"]], writes=[wglu])
        bglu = col_load("s5_b_glu", 4)
        dcol = col_load("s5_d", 4)

        MS = B.sb("ms", [128, 16, 2, 2])
        ER = B.sb("ER", [128, 16, 64]); EI = B.sb("EI", [128, 16, 64]); R8 = B.sb("R8", [128, 16])
        Wt = B.sb("Wt", [128, 4, 8, 2, 128], BF16)
        CAW = B.sb("CAW", [128, 16, 2, 9, 64], BF16)
        mk = B.sb("mk", [128, 2])
        B.memset("pool", mk.a[:, 0:1], 0.0, [mk]); B.memset("pool", mk.a[0:32, 0:1], 1.0, [mk]); B.memset("pool", mk.a[64:96, 0:1], 1.0, [mk])
        mk4 = B.sb("mk4", [128, 4])
        B.memset("pool", mk4.a, 0.0, [mk4])
        B.memset("pool", mk4.a[0:32, 0:1], 1.0, [mk4]); B.memset("pool", mk4.a[32:64, 1:2], 1.0, [mk4])
        B.memset("pool", mk4.a[64:96, 2:3], 1.0, [mk4]); B.memset("pool", mk4.a[64:128, 3:4], 1.0, [mk4]); B.memset("pool", mk4.a[64:96, 3:4], 0.0, [mk4])
        B.memset("pool", mk.a[:, 1:2], 1.0, [mk]); B.memset("pool", mk.a[0:32, 1:2], 0.0, [mk]); B.memset("pool", mk.a[64:96, 1:2], 0.0, [mk])
        Kbd = B.sb("Kbd", [128, 4, 8, 128], BF16)
        B.push()
        are = B.sb("are", [128, 16]); aim = B.sb("aim", [128, 16]); ls = B.sb("ls", [128, 16])
        for gl in range(2):
            S.dma("sp", are.a[gl * 64:(gl + 1) * 64, :], W["s5_a_re"].a.rearrange("(q gl) n -> gl n q", gl=2)[gl],
                  reads=[W["s5_a_re"]], writes=[are], allow_slow_non_contiguous=True)
            S.dma("sp", aim.a[gl * 64:(gl + 1) * 64, :], W["s5_a_im"].a.rearrange("(q gl) n -> gl n q", gl=2)[gl],
                  reads=[W["s5_a_im"]], writes=[aim], allow_slow_non_contiguous=True)
            S.dma("sp", ls.a[gl * 64:(gl + 1) * 64, :],
                  W["s5_log_step"].a.rearrange("(q gl) -> gl q", gl=2)[gl].partition_broadcast(64),
                  reads=[W["s5_log_step"]], writes=[ls], allow_slow_non_contiguous=True)
        braw = [B.sb("braw%d" % i, [128, 16, 16]) for i in range(2)]
        for i, nm in enumerate(("s5_b_re", "s5_b_im")):
            S.dma("sp", braw[i].a, W[nm].a.rearrange("(q gl) n c -> (gl n) q c", gl=2), reads=[W[nm]], writes=[braw[i]])
        craw = [B.sb("craw%d" % i, [128, 16, 16]) for i in range(2)]
        ctmp = B.sb("ctmp", [128, 128])
        for i, nm in enumerate(("s5_c_re", "s5_c_im")):
            for blk in range(2):
                src = W[nm].a.rearrange("(b qq gl) c n -> b qq c gl n", b=2, gl=2)[blk]
                for qq in range(8):
                    S.dma("sp", ctmp.a[qq * 16:(qq + 1) * 16, :].rearrange("p (gl n) -> p gl n", gl=2),
                          src[qq], reads=[W[nm]], writes=[ctmp])
                pb = B.ps()
                B.tr(pb.a[:, 0:128], ctmp.a, ident_f.a, [ctmp, ident_f], [pb])
                B.cp("dve", craw[i].a[:, blk * 8:(blk + 1) * 8, :],
                     pb.a[:, 0:128].rearrange("p (qq c) -> p qq c", c=16), [pb], [craw[i]])

        tm = B.sb("s5tmp", [128, 12, 32])
        tmr = tm.r

        def row(i, n=16):
            return tm.a[:, i, 0:n]

        dt_ = row(0)
        B.act(dt_, ls.a, AF.Exp, [ls], [tm])
        xr = row(1)
        B.tt("dve", xr, are.a, dt_, ALU.mult, [are, tm], [tm])
        rho = row(2)
        B.ts("dve", rho, xr, 1.0 / 720, ALU.mult, [tm], [tm], s2=1.0 / 120, op1=ALU.add)
        for cf in (1.0 / 24, 1.0 / 6, 0.5, 1.0, 1.0):
            B.tt("dve", rho, rho, xr, ALU.mult, [tm], [tm])
            B.ts("dve", rho, rho, cf, ALU.add, [tm], [tm])
        th2 = tm.a[:, 3, :]
        B.tt("dve", th2[:, 0:16], aim.a, dt_, ALU.mult, [aim, tm], [tm])
        B.ts("dve", th2[:, 16:32], th2[:, 0:16], PI / 2, ALU.add, [tm], [tm])
        kf = tm.a[:, 4, :]
        B.ts("dve", kf, th2, 1.0 / (2 * PI), ALU.mult, [tm], [tm])
        ki = B.sb("ki", [128, 32], I32)
        B.cp("dve", ki.a, kf, [tm], [ki])
        B.cp("dve", kf, ki.a, [ki], [tm])
        r1 = tm.a[:, 5, :]
        B.stt(r1, kf, -2 * PI, th2, ALU.mult, ALU.add, [tm], [tm])
        B.ts("dve", kf, r1, PI, ALU.is_gt, [tm], [tm], s2=-2 * PI, op1=ALU.mult)
        B.tt("dve", r1, r1, kf, ALU.add, [tm], [tm])
        B.ts("dve", kf, r1, -PI, ALU.is_lt, [tm], [tm], s2=2 * PI, op1=ALU.mult)
        B.tt("dve", r1, r1, kf, ALU.add, [tm], [tm])
        sc = tm.a[:, 6, :]
        B.act(sc, r1, AF.Sin, [tm], [tm])
        n2 = row(7)
        B.tt("dve", kf, sc, sc, ALU.mult, [tm], [tm])
        B.tt("dve", n2, kf[:, 0:16], kf[:, 16:32], ALU.add, [tm], [tm])
        B.ts("dve", n2, n2, -0.5, ALU.mult, [tm], [tm], s2=1.5, op1=ALU.add)
        B.tt("dve", n2, n2, rho, ALU.mult, [tm], [tm])
        PW = B.sb("pw", [128, 9, 2, 16])
        B.memset("dve", PW.a[:, 0, 0, :], 1.0, [PW])
        B.memset("dve", PW.a[:, 0, 1, :], 0.0, [PW])
        B.tt("dve", PW.a[:, 1, 0, :], sc[:, 16:32], n2, ALU.mult, [tm], [PW])
        B.tt("dve", PW.a[:, 1, 1, :], sc[:, 0:16], n2, ALU.mult, [tm], [PW])
        pt = B.sb("ptmp", [128, 2, 4, 16])
        for (lo, n, s) in ((2, 1, 1), (3, 2, 2), (5, 4, 4)):
            bre = PW.a[:, s:s + 1, 0, :].broadcast_to([128, n, 16])
            bim = PW.a[:, s:s + 1, 1, :].broadcast_to([128, n, 16])
            B.cmul("dve", PW.a[:, lo:lo + n, 0, :], PW.a[:, lo:lo + n, 1, :],
                   PW.a[:, lo - s:lo - s + n, 0, :], PW.a[:, lo - s:lo - s + n, 1, :], bre, bim,
                   pt.a[:, 0, 0:n, :], pt.a[:, 1, 0:n, :], [PW], [PW], pt)
        B.cp("dve", MS.a[:, :, 0, 0], PW.a[:, 8, 0, :], [PW], [MS])
        B.cp("dve", MS.a[:, :, 1, 1], PW.a[:, 8, 0, :], [PW], [MS])
        B.cp("dve", MS.a[:, :, 1, 0], PW.a[:, 8, 1, :], [PW], [MS])
        B.ts("dve", MS.a[:, :, 0, 1], PW.a[:, 8, 1, :], -1.0, ALU.mult, [PW], [MS])
        B.tt("dve", R8.a, rho, rho, ALU.mult, [tm], [R8])
        B.tt("dve", R8.a, R8.a, R8.a, ALU.mult, [R8], [R8])
        B.tt("dve", R8.a, R8.a, R8.a, ALU.mult, [R8], [R8])
        r8i = row(8)
        B.op("dve", lambda e: e.reciprocal(r8i, R8.a), [R8], [tm])
        B.tt("dve", ER.a[:, :, 0], PW.a[:, 8, 0, :], r8i, ALU.mult, [PW, tm], [ER])
        B.tt("dve", EI.a[:, :, 0], PW.a[:, 8, 1, :], r8i, ALU.mult, [PW, tm], [EI])
        et = B.sb("etmp", [128, 2, 16, 32])
        n_ = 1
        while n_ < 64:
            bre = ER.a[:, :, n_ - 1:n_].broadcast_to([128, 16, n_]); bim = EI.a[:, :, n_ - 1:n_].broadcast_to([128, 16, n_])
            B.cmul("dve", ER.a[:, :, n_:2 * n_], EI.a[:, :, n_:2 * n_], ER.a[:, :, 0:n_], EI.a[:, :, 0:n_], bre, bim,
                   et.a[:, 0, :, 0:n_], et.a[:, 1, :, 0:n_], [ER, EI], [ER, EI], et)
            n_ *= 2
        am1 = row(8); nre = row(9); nim = row(10); den = row(11); t0 = row(4); t1 = row(5)
        B.ts("dve", am1, PW.a[:, 1, 0, :], -1.0, ALU.add, [PW], [tm])
        B.tt("dve", nre, am1, are.a, ALU.mult, [tm, are], [tm])
        B.tt("dve", t0, PW.a[:, 1, 1, :], aim.a, ALU.mult, [PW, aim], [tm])
        B.tt("dve", nre, nre, t0, ALU.add, [tm], [tm])
        B.tt("dve", nim, PW.a[:, 1, 1, :], are.a, ALU.mult, [PW, are], [tm])
        B.tt("dve", t0, am1, aim.a, ALU.mult, [tm, aim], [tm])
        B.tt("dve", nim, nim, t0, ALU.subtract, [tm], [tm])
        B.tt("dve", den, are.a, are.a, ALU.mult, [are], [tm])
        B.tt("dve", t0, aim.a, aim.a, ALU.mult, [aim], [tm])
        B.tt("dve", den, den, t0, ALU.add, [tm], [tm])
        B.op("dve", lambda e: e.reciprocal(t1, den), [tm], [tm])
        B.tt("dve", nre, nre, t1, ALU.mult, [tm], [tm])
        B.tt("dve", nim, nim, t1, ALU.mult, [tm], [tm])
        bb = [B.sb("bb%d" % i, [128, 16, 16]) for i in range(2)]
        btmp = B.sb("btmp", [128, 2, 16, 16])
        cre = nre[:, :, None].broadcast_to([128, 16, 16]); cim = nim[:, :, None].broadcast_to([128, 16, 16])
        B.cmul("dve", bb[0].a, bb[1].a, cre, cim, braw[0].a, braw[1].a, btmp.a[:, 0], btmp.a[:, 1],
               [tm, braw[0], braw[1]], [bb[0], bb[1]], btmp)
        X = [B.sb("X%d" % i, [128, 16, 32]) for i in range(2)]
        Xb = [B.sb("Xb%d" % i, [128, 16, 64], BF16) for i in range(2)]
        for i in range(2):
            B.memset("pool", X[i].a, 0.0, [X[i]])
            for gl in range(2):
                B.cp("pool", X[i].a[gl * 64:(gl + 1) * 64, :, gl * 16:(gl + 1) * 16], bb[i].a[gl * 64:(gl + 1) * 64], [bb[i]], [X[i]])
            B.memset("pool", Xb[i].a, 0.0, [Xb[i]])
            for kk in range(2):
                B.cp("pool", Xb[i].a[:, kk::2, 32 * kk:32 * kk + 32], X[i].a[:, kk::2, :], [X[i]], [Xb[i]])
        B.push()
        WX = [B.sb("WX%d" % i, [128, 8, 16, 32]) for i in range(2)]
        wtmp = B.sb("wtmp", [128, 2, 8, 16, 32])
        pre = PW.a[:, 0:8, 0, :][:, :, :, None].broadcast_to([128, 8, 16, 32])
        pim = PW.a[:, 0:8, 1, :][:, :, :, None].broadcast_to([128, 8, 16, 32])
        xre = X[0].a[:, None, :, :].broadcast_to([128, 8, 16, 32])
        xim = X[1].a[:, None, :, :].broadcast_to([128, 8, 16, 32])
        B.cmul("dve", WX[0].a, WX[1].a, pre, pim, xre, xim, wtmp.a[:, 0], wtmp.a[:, 1], [PW, X[0], X[1]], [WX[0], WX[1]], wtmp)
        for tile in range(4):
            for e_ in range(8):
                pb = B.ps()
                for ri in range(2):
                    B.tr(pb.a[:, ri * 128:(ri + 1) * 128],
                         WX[ri].a[:, e_, 4 * tile:4 * tile + 4, :].rearrange("p k c -> p (k c)"), ident_f.a,
                         [WX[ri], ident_f], [pb])
                B.cp("act" if e_ % 2 else "dve", Wt.a[:, tile, e_, :, :],
                     pb.a[:, 0:256].rearrange("p (r n) -> p r n", r=2), [pb], [Wt])
        B.pop()
        B.push()
        CA = [B.sb("CA%d" % i, [128, 9, 16, 16]) for i in range(2)]
        catmp = B.sb("catmp", [128, 2, 9, 16, 16])
        pre9 = PW.a[:, :, 0, :][:, :, :, None].broadcast_to([128, 9, 16, 16])
        pim9 = PW.a[:, :, 1, :][:, :, :, None].broadcast_to([128, 9, 16, 16])
        cre9 = craw[0].a[:, None, :, :].broadcast_to([128, 9, 16, 16])
        cim9 = craw[1].a[:, None, :, :].broadcast_to([128, 9, 16, 16])
        B.cmul("dve", CA[0].a, CA[1].a, pre9, pim9, cre9, cim9, catmp.a[:, 0], catmp.a[:, 1],
               [PW, craw[0], craw[1]], [CA[0], CA[1]], catmp)
        B.memset("pool", CAW.a, 0.0, [CAW])
        for gl in range(2):
            hs = slice(gl * 64, (gl + 1) * 64)
            for tau in range(9):
                for kk in range(2):
                    o = 32 * kk + 16 * gl
                    B.cp("pool", CAW.a[hs, kk::2, 0, tau, o:o + 16], CA[0].a[hs, tau, kk::2], [CA[0]], [CAW])
                    B.ts("pool", CAW.a[hs, kk::2, 1, tau, o:o + 16], CA[1].a[hs, tau, kk::2], -1.0, ALU.mult, [CA[1]], [CAW])
        B.memset("pool", Kbd.a, 0.0, [Kbd])
        k0 = B.sb("k0", [128, 128])
        for tile in range(4):
            pb = B.ps()
            for h in range(2):
                for tau in range(8):
                    n = 0
                    for kk in range(2):
                        q = 4 * tile + 2 * h + kk
                        o = 32 * kk
                        for ri in range(2):
                            B.mm(pb.a[64 * h:64 * h + 64, tau * 32:(tau + 1) * 32], Xb[ri].a[:, q, :],
                                 CAW.a[:, q, ri, tau, o:o + 32], n == 0, n == 3, [Xb[ri], CAW], [pb])
                            n += 1
            for h in range(2):
                hs = slice(64 * h, 64 * h + 64)
                for kk in range(2):
                    cs = slice(64 * h + 32 * kk, 64 * h + 32 * kk + 32)
                    B.ts("dve", Kbd.a[hs, tile, 1:8, cs], pb.a[hs, 32:256].rearrange("p (t c) -> p t c", c=32),
                         mk.a[hs, kk:kk + 1], ALU.mult, [pb, mk], [Kbd])
            B.memset("dve", k0.a, 0.0, [k0])
            for h in range(2):
                hs = slice(64 * h, 64 * h + 64)
                for kk in range(2):
                    cs = slice(64 * h + 32 * kk, 64 * h + 32 * kk + 32)
                    B.ts("dve", k0.a[hs, cs], pb.a[hs, 0:32], mk.a[hs, kk:kk + 1], ALU.mult, [pb, mk], [k0])
            B.stt(k0.a, ident_f.a, dcol.a[:, tile:tile + 1], k0.a, ALU.mult, ALU.add, [ident_f, dcol, k0], [k0])
            B.cp("dve", Kbd.a[:, tile, 0, :], k0.a, [k0], [Kbd])
        B.pop()
        B.pop()
        for nm_, b_ in (("Wt", Wt), ("CAW", CAW), ("Kbd", Kbd), ("MS", MS)):
            if nm_ in dbo:
                S.dma("sp", dbo[nm_].a, b_.a, reads=[b_], writes=[dbo[nm_]], dreg=b_)
        xt = B.sb("xt", [128, 4, D]); hT = B.sb("hT", [128, 8, T1], BF16)
        st = B.sb("st", [128, 12])
        u = B.sb("u", [128, 4, 8, T1 // 8], BF16)
        um = B.sb("um", [128, 4, 4, 8, T1 // 8], BF16)
        hb = B.view(um, um.a.rearrange("p a b c d -> p (a b c d)")[:, 0:4 * D].rearrange("p (s d) -> p s d", s=4))
        NM = T1 // 8
        Dm = B.sb("Dm", [128, 16, 2, NM])
        St = B.sb("St", [128, 16, 2, NM + 1])
        Sb = B.sb("Sb", [128, 16, 2, NM], BF16)
        rt1 = B.sb("rt1", [128, 16, NM]); rt2 = B.sb("rt2", [128, 16, NM]); Dr = B.sb("Dr", [128, 16, 2, NM])
        Qs = B.sb("Qs", [128, 16, 2, NM]); Sfin = B.sb("Sfin", [128, 16, 2]); B.memset("pool", Sfin.a, 0.0, [Sfin])
        y2 = B.sb("y2", [128, 4, T1]); yg = B.sb("yg", [128, 4, T1])
        yy = B.view(Dm, Dm.a.rearrange("p a b c -> p (a b c)").rearrange("p (t n) -> p t n", t=4))
        scr = B.view(y2, y2.a.rearrange("p a b -> p (a b)")[:, 0:D])
        ygb = B.sb("ygb", [128, 4, T1], BF16); sg = y2
        ygl = B.sb("ygl", [128, 4, T1], BF16)
        B.memset("pool", St.a[:, :, :, 0], 0.0, [St])
        for ti in range(ntiles):
            frontend(x, ti, 4, gpre, xt, hb, hT, scr, st)
            for cb in range(4):
                pb = B.ps()
                for c in range(8):
                    B.mm(pb.a, ws5.a[:, c, cb * 128:(cb + 1) * 128], hT.a[:, c, :], c == 0, c == 7, [ws5, hT], [pb])
                pperm = pb.a.rearrange("p (m t) -> p t m", t=8)
                B.cp("act", u.a[:, cb, :, :], pperm, [pb], [u])
                for k in range(4):
                    B.act(um.a[:, k, cb, :, :], pperm, AF.Copy, [pb, mk4], [um], scale=mk4.a[:, k:k + 1])
            for qb in range(4):
                pb = B.ps()
                for qq in range(4):
                    q = 4 * qb + qq
                    tile, k = q // 4, q % 4
                    ks = slice(32 * k, 32 * k + 32)
                    for ri in range(2):
                        col = (qq * 2 + ri) * NM
                        for j0 in range(8):
                            B.mm(pb.a[:, col:col + NM], Wt.a[:, tile, 7 - j0, ri, :], um.a[:, k, tile, j0, :],
                                 j0 == 0, j0 == 7, [Wt, um], [pb])
                B.cp("dve", Dm.a[:, 4 * qb:4 * qb + 4, :, :],
                     pb.a.rearrange("p (q r m) -> p q r m", q=4, r=2), [pb], [Dm])
            erb = ER.a[:, :, :]; eib = EI.a[:, :, :]
            B.tt("dve", rt1.a, erb, Dm.a[:, :, 0, :], ALU.mult, [ER, Dm], [rt1])
            B.tt("dve", rt2.a, eib, Dm.a[:, :, 1, :], ALU.mult, [EI, Dm], [rt2])
            B.tt("dve", Dr.a[:, :, 0, :], rt1.a, rt2.a, ALU.add, [rt1, rt2], [Dr])
            B.tt("dve", rt1.a, erb, Dm.a[:, :, 1, :], ALU.mult, [ER, Dm], [rt1])
            B.tt("dve", rt2.a, eib, Dm.a[:, :, 0, :], ALU.mult, [EI, Dm], [rt2])
            B.tt("dve", Dr.a[:, :, 1, :], rt1.a, rt2.a, ALU.subtract, [rt1, rt2], [Dr])
            for q in range(16):
                for ri in range(2):
                    B.op("dve", lambda e, q=q, ri=ri: e.tensor_tensor_scan(
                        out=Qs.a[:, q, ri, :], data0=R8.a[:, q:q + 1].broadcast_to([128, NM]), data1=Dr.a[:, q, ri, :],
                        initial=Sfin.a[:, q, ri:ri + 1], op0=ALU.mult, op1=ALU.add), [R8, Dr, Sfin], [Qs])
            B.tt("dve", rt1.a, erb, Qs.a[:, :, 0, :], ALU.mult, [ER, Qs], [rt1])
            B.tt("dve", rt2.a, eib, Qs.a[:, :, 1, :], ALU.mult, [EI, Qs], [rt2])
            B.tt("dve", St.a[:, :, 0, 1:NM + 1], rt1.a, rt2.a, ALU.subtract, [rt1, rt2], [St])
            B.tt("dve", rt1.a, erb, Qs.a[:, :, 1, :], ALU.mult, [ER, Qs], [rt1])
            B.tt("dve", rt2.a, eib, Qs.a[:, :, 0, :], ALU.mult, [EI, Qs], [rt2])
            B.tt("dve", St.a[:, :, 1, 1:NM + 1], rt1.a, rt2.a, ALU.add, [rt1, rt2], [St])
            B.cp("act", Sb.a, St.a[:, :, :, 0:NM], [St], [Sb])
            B.cp("act", Sfin.a, St.a[:, :, :, NM], [St], [Sfin])
            B.cp("pool", St.a[:, :, :, 0], St.a[:, :, :, NM], [St], [St])
            for tile in range(4):
                pb = B.ps()
                for h in range(2):
                    hs = slice(64 * h, 64 * h + 64)
                    for t0_ in range(8):
                        n = 0
                        for kk in range(2):
                            q = 4 * tile + 2 * h + kk
                            for ri in range(2):
                                B.mm(pb.a[hs, t0_ * NM:(t0_ + 1) * NM], CAW.a[:, q, ri, t0_ + 1, :], Sb.a[:, q, ri, :], n == 0, n == 3,
                                     [CAW, Sb], [pb], skip_group_check=True)
                                n += 1
                B.cp("act", y2.a[:, tile, :], pb.a, [pb], [y2])
                pb = B.ps()
                for t0o in range(8):
                    for tau in range(t0o + 1):
                        B.mm(pb.a[:, t0o * NM:(t0o + 1) * NM], Kbd.a[:, tile, tau, :], u.a[:, tile, t0o - tau, :],
                             tau == 0, tau == t0o, [Kbd, u], [pb])
                B.tt("dve", yy.a[:, tile, :].rearrange("p (m t) -> p t m", t=8), pb.a.rearrange("p (t m) -> p t m", t=8),
                     y2.a[:, tile, :].rearrange("p (t m) -> p t m", t=8), ALU.add, [pb, y2], [yy])
            if "s5y" in dbo:
                S.dma("sp", dbo["s5y"].a[:, :, ti * T1:(ti + 1) * T1], yy.a, reads=[yy], writes=[dbo["s5y"]], dreg=yy)
            for tile in range(4):
                B.act(yg.a[:, tile, :], yy.a[:, tile, :], AF.Gelu_apprx_tanh, [yy], [yg])
            B.cp("act", ygb.a, yg.a, [yg], [ygb])
            for cb in range(4):
                pb = B.ps()
                for c in range(4):
                    B.mm(pb.a, wglu.a[:, c, cb * 128:(cb + 1) * 128], ygb.a[:, c, :], c == 0, c == 3, [wglu, ygb], [pb])
                B.act(sg.a[:, cb, :], pb.a, AF.Sigmoid, [pb, bglu], [sg], bias=bglu.a[:, cb:cb + 1])
                B.tt("dve", ygl.a[:, cb, :], yg.a[:, cb, :], sg.a[:, cb, :], ALU.mult, [yg, sg], [ygl])
            S.dma("sp", YGLU.a[:, :, ti * T1:(ti + 1) * T1], ygl.a, reads=[ygl], writes=[YGLU], dreg=ygl)
            if "yglu" in dbo:
                S.dma("sp", dbo["yglu"].a[:, :, ti * T1:(ti + 1) * T1], ygl.a, reads=[ygl], writes=[dbo["yglu"]], dreg=ygl)
        B.pop()

    if "1b" in phases:
        phase_1b(B, W, x, YR, epsc, gpre, ident_f, ident_b, frontend, bc_load, col_load, dbo, ntiles * (T1 // 128))

    if "1c" in phases:
        phase_1c(B, W, x, x1s, H2T, YR, YGLU, epsc, gpre, frontend, bc_load, col_load, dbo, ntiles)

    B.pop()
    if "2" in phases:
        phase_2(B, W, x1s, H2T, out, epsc, frontend, bc_load, dbo, ntiles)

    S.final_wait("sp", list(B.dout.values()))
    B.Wshapes = shapes
    return B


def phase_1b(B, W, x, YR, epsc, gpre, ident_f, ident_b, frontend, bc_load, col_load, dbo, ntiles):
    nc, S = B.nc, B.S
    TB = 128
    C = 128
    NW = 3
    C0 = math.exp(-0.5)
    B.push()
    wrg = B.sb("wrg", [128, 8, NR], BF16)
    win = W["w_in"].a.rearrange("(c p) n -> p c n", p=128)
    for c in range(8):
        S.dma("pool", wrg.a[:, c, :], win[:, c, 0:NR], reads=[W["w_in"]], writes=[wrg])
    w2p = B.sb("w2p", [128, 512], BF16); a2p = B.sb("a2p", [128, 512], BF16); g2b = B.sb("g2b", [128, 512], BF16)
    B.memset("pool", w2p.a, 0.0, [w2p]); B.memset("pool", a2p.a, 0.0, [a2p])
    S.dma("pool", w2p.a[0:64, :], W["rwkv_w2"].a, reads=[W["rwkv_w2"]], writes=[w2p])
    S.dma("pool", a2p.a[64:128, :], W["rwkv_a2"].a, reads=[W["rwkv_a2"]], writes=[a2p])
    S.dma("pool", g2b.a, W["rwkv_g2"].a, reads=[W["rwkv_g2"]], writes=[g2b])
    mu = col_load("rwkv_shift_mu", 14); w0c = col_load("rwkv_w0", 4); a0c = col_load("rwkv_a0", 4)
    kkc = col_load("rwkv_k_k", 4); kac = col_load("rwkv_k_a", 4); rkc = col_load("rwkv_r_k", 4)
    lwc = col_load("rwkv_lnx_w", 4); lbc = col_load("rwkv_lnx_b", 4)
    omu = B.sb("omu", [128, 14]); oka = B.sb("oka", [128, 4])
    B.ts("dve", omu.a, mu.a, -1.0, ALU.mult, [mu], [omu], s2=1.0, op1=ALU.add)
    B.ts("dve", oka.a, kac.a, -1.0, ALU.mult, [kac], [oka], s2=1.0, op1=ALU.add)
    bones = B.sb("bones", [128, 128]); bavg = B.sb("bavg", [128, 128]); ones = B.sb("ones", [128, 128])
    B.memset("pool", ones.a, 1.0, [ones])
    B.memset("pool", bones.a, 0.0, [bones])
    B.memset("pool", bones.a[0:64, 0:64], 1.0, [bones]); B.memset("pool", bones.a[64:128, 64:128], 1.0, [bones])
    B.ts("pool", bavg.a, bones.a, 1.0 / 64, ALU.mult, [bones], [bavg])
    hm = B.sb("hm", [128, 2])
    B.memset("pool", hm.a, 0.0, [hm]); B.memset("pool", hm.a[0:64, 0:1], 1.0, [hm]); B.memset("pool", hm.a[64:128, 1:2], 1.0, [hm])
    mf = B.sb("mf", [128, 128])
    MU4 = B.sb("MU4", [128, 4, 128], BF16); MI4 = B.sb("MI4", [128, 4, 128], BF16)
    ML4 = B.sb("ML4", [128, 4, 128], BF16); I4 = B.sb("I4", [128, 4, 128], BF16)
    for (mt, pat, cm, cop) in ((MU4, 1, -1, ALU.is_gt), (MI4, 1, -1, ALU.is_ge), (ML4, -1, 1, ALU.is_gt)):
        B.memset("pool", mf.a, 1.0, [mf])
        B.op("pool", lambda e, pat=pat, cm=cm, cop=cop: e.affine_select(out=mf.a, in_=mf.a, pattern=[[pat, 128]], compare_op=cop,
                                                                      fill=0.0, base=0, channel_multiplier=cm), [mf], [mf])
        for h in range(4):
            B.cp("pool", mt.a[:, h, :], mf.a, [mf], [mt])
    for h in range(4):
        B.cp("pool", I4.a[:, h, :], ident_f.a, [ident_f], [I4])
    pc = B.sb("pc", [128, 14]); B.memset("pool", pc.a, 0.0, [pc])
    H32 = B.sb("H32", [128, 4, 64]); Hb = B.sb("Hb", [128, 4, 64], BF16); Ht = B.sb("Ht", [128, 4, 64])
    B.memset("pool", H32.a, 0.0, [H32]); B.memset("pool", Hb.a, 0.0, [Hb])

    def bc4(col):
        return col.a[:, :, None].broadcast_to([128, 4, TB])

    class BS:
        pass

    sets = []
    for w in range(NW):
        b = BS()
        f4 = lambda nm: B.sb(nm + str(w), [128, 4, TB])
        h4 = lambda nm: B.sb(nm + str(w), [128, 4, TB], BF16)
        b.xt = B.sb("xt%d" % w, [128, 1, D]); b.hb = B.sb("hb%d" % w, [128, 1, D], BF16); b.hT = B.sb("hT%d" % w, [128, 8, TB], BF16)
        b.st = B.sb("st%d" % w, [128, 6])
        b.PS = B.sb("PS%d" % w, [128, 14, TB]); b.t1 = B.sb("t1%d" % w, [128, 4, TB]); b.t2 = B.sb("t2%d" % w, [128, 4, TB])
        b.scr = B.view(b.PS, b.PS.a.rearrange("p a b -> p (a b)")[:, 0:D])
        b.twa = B.sb("twa%d" % w, [128, TB], BF16); b.sgd = B.sb("sgd%d" % w, [128, TB], BF16)
        b.sgw = f4("sgw"); b.asg = f4("asg"); b.gg = f4("gg"); b.kk = f4("kk"); b.tq = f4("tq"); b.kmod = f4("kmod")
        b.cum = f4("cum"); b.eg = b.t1; b.egx = f4("egx"); b.eng = b.t2; b.bon = f4("bon")
        b.am = B.sb("am%d" % w, [128, 4, 2, TB], BF16); b.rm = B.sb("rm%d" % w, [128, 4, 2, TB], BF16)
        b.bf = h4("bf"); b.kf = h4("kf"); b.vb = h4("vb")
        b.Btok = B.sb("Btok%d" % w, [128, 512], BF16); b.Ktok = B.sb("Ktok%d" % w, [128, 512], BF16); b.Vtok = B.sb("Vtok%d" % w, [128, 512], BF16)
        for nm, src in (("Pm", b.sgw), ("Qm", b.asg), ("Rm", b.kk), ("Nak", b.kmod), ("Nrb", b.egx), ("Nrk", b.eng)):
            setattr(b, nm, B.view(src, src.a.rearrange("p a b -> p (a b)").bitcast(BF16).rearrange("p (h t) -> p h t", h=8)))
        hTf = b.hT.a.rearrange("p a b -> p (a b)")
        b.Xb = B.view(b.hT, hTf[:, 0:512]); b.Ub = B.view(b.hT, hTf[:, 512:1024])
        b.Yf = B.view(b.xt, b.xt.a.rearrange("p a b -> p (a b)")[:, 0:4 * TB].rearrange("p (j t) -> p j t", j=4))
        b.dd = B.view(b.hb, b.hb.a.rearrange("p a b -> p (a b)").bitcast(F32).rearrange("p (j t) -> p j t", j=4))
        b.yrb = h4("yrb")
        sets.append(b)

    def chunk(ci):
        b = sets[ci % NW]
        PS, tq, cum, kk, kmod, eg, egx, eng, asg, sgw, gg, bon = b.PS, b.tq, b.cum, b.kk, b.kmod, b.eg, b.egx, b.eng, b.asg, b.sgw, b.gg, b.bon
        am, rm, Pm, Qm, Rm, Nak, Nrb, Nrk = b.am, b.rm, b.Pm, b.Qm, b.Rm, b.Nak, b.Nrb, b.Nrk
        frontend(x, ci, 1, gpre, b.xt, b.hb, b.hT, b.scr, b.st)
        yield
        for j0 in range(0, 14, 4):
            nj = min(4, 14 - j0)
            pb = B.ps()
            for jj in range(nj):
                for c in range(8):
                    B.mm(pb.a[:, jj * TB:(jj + 1) * TB], wrg.a[:, c, (j0 + jj) * 128:(j0 + jj + 1) * 128], b.hT.a[:, c, :],
                         c == 0, c == 7, [wrg, b.hT], [pb])
            pv = pb.a[:, 0:nj * TB].rearrange("p (j t) -> p j t", j=nj)
            om_b = omu.a[:, j0:j0 + nj, None].broadcast_to([128, nj, TB])
            mu_b = mu.a[:, j0:j0 + nj, None].broadcast_to([128, nj, TB - 1])
            B.tt("dve", b.t1.a[:, 0:nj, :], pv, om_b, ALU.mult, [pb, omu], [b.t1])
            B.tt("dve", b.t2.a[:, 0:nj, 1:TB], pv[:, :, 0:TB - 1], mu_b, ALU.mult, [pb, mu], [b.t2])
            B.tt("dve", b.t2.a[:, 0:nj, 0:1], pc.a[:, j0:j0 + nj, None], mu.a[:, j0:j0 + nj, None], ALU.mult, [pc, mu], [b.t2])
            B.cp("act", pc.a[:, j0:j0 + nj, None], pv[:, :, TB - 1:TB], [pb], [pc])
            B.tt("pool", PS.a[:, j0:j0 + nj, :], b.t1.a[:, 0:nj, :], b.t2.a[:, 0:nj, :], ALU.add, [b.t1, b.t2], [PS])
            yield
        if "pshift" in dbo:
            S.dma("sp", dbo["pshift"].a[:, :, ci * TB:(ci + 1) * TB], PS.a, reads=[PS], writes=[dbo["pshift"]], dreg=PS)
        r_ = PS.a[:, 0:4, :]; k_ = PS.a[:, 4:8, :]; v_ = PS.a[:, 8:12, :]
        B.act(b.twa.a[0:64, :], PS.a[0:64, 12, :], AF.Tanh, [PS], [b.twa])
        B.cp("act", b.twa.a[64:128, :], PS.a[64:128, 12, :], [PS], [b.twa])
        B.act(b.sgd.a, PS.a[:, 13, :], AF.Sigmoid, [PS], [b.sgd])
        pw_ = B.ps()
        for j in range(4):
            B.mm(pw_.a[:, j * TB:(j + 1) * TB], w2p.a[:, j * 128:(j + 1) * 128], b.twa.a, True, True, [w2p, b.twa], [pw_])
        for j in range(4):
            B.act(sgw.a[:, j, :], pw_.a[:, j * TB:(j + 1) * TB], AF.Sigmoid, [pw_, w0c], [sgw], bias=w0c.a[:, j:j + 1])
        pa_ = B.ps()
        for j in range(4):
            B.mm(pa_.a[:, j * TB:(j + 1) * TB], a2p.a[:, j * 128:(j + 1) * 128], b.twa.a, True, True, [a2p, b.twa], [pa_])
        for j in range(4):
            B.act(asg.a[:, j, :], pa_.a[:, j * TB:(j + 1) * TB], AF.Sigmoid, [pa_, a0c], [asg], bias=a0c.a[:, j:j + 1])
        pg_ = B.ps()
        for j in range(4):
            B.mm(pg_.a[:, j * TB:(j + 1) * TB], g2b.a[:, j * 128:(j + 1) * 128], b.sgd.a, True, True, [g2b, b.sgd], [pg_])
        B.cp("act", gg.a, pg_.a.rearrange("p (j t) -> p j t", j=4), [pg_], [gg])
        yield
        B.tt("dve", kk.a, k_, bc4(kkc), ALU.mult, [PS, kkc], [kk])
        B.tt("pool", tq.a, kk.a, kk.a, ALU.mult, [kk], [tq])
        pb = B.ps()
        for j in range(4):
            B.mm(pb.a[:, j * TB:(j + 1) * TB], bones.a, tq.a[:, j, :], True, True, [bones, tq], [pb])
        B.act(cum.a, pb.a.rearrange("p (j t) -> p j t", j=4), AF.Ln, [pb, epsc], [cum], bias=epsc.a[:, 1:2])
        B.act(cum.a, cum.a, AF.Exp, [cum], [cum], scale=-0.5)
        B.tt("pool", kk.a, kk.a, cum.a, ALU.mult, [kk, cum], [kk])
        yield
        B.tt("dve", tq.a, asg.a, bc4(kac), ALU.mult, [asg, kac], [tq])
        B.tt("dve", tq.a, tq.a, bc4(oka), ALU.add, [tq, oka], [tq])
        B.tt("dve", kmod.a, k_, tq.a, ALU.mult, [PS, tq], [kmod])
        B.tt("pool", tq.a, r_, kmod.a, ALU.mult, [PS, kmod], [tq])
        B.tt("dve", tq.a, tq.a, bc4(rkc), ALU.mult, [tq, rkc], [tq])
        pbon = B.ps()
        for j in range(4):
            B.mm(pbon.a[:, j * TB:(j + 1) * TB], bones.a, tq.a[:, j, :], True, True, [bones, tq], [pbon])
        B.tt("dve", bon.a, pbon.a.rearrange("p (j t) -> p j t", j=4), v_, ALU.mult, [pbon, PS], [bon])
        yield
        for j in range(4):
            B.op("dve", lambda e, j=j: e.tensor_tensor_scan(out=cum.a[:, j, :], data0=ones.a, data1=sgw.a[:, j, :], initial=0.0,
                                                            op0=ALU.mult, op1=ALU.add), [ones, sgw], [cum])
        B.act(eg.a, cum.a, AF.Exp, [cum], [eg], scale=-C0)
        B.act(eng.a, cum.a, AF.Exp, [cum], [eng], scale=C0)
        B.tt("pool", tq.a, cum.a, sgw.a, ALU.subtract, [cum, sgw], [tq])
        B.act(egx.a, tq.a, AF.Exp, [tq], [egx], scale=-C0)
        yield
        B.stt(tq.a, kk.a, -1.0, egx.a, ALU.mult, ALU.mult, [kk, egx], [tq])
        for hh in range(2):
            B.act(am.a[:, :, hh, :], tq.a, AF.Copy, [tq, hm], [am], scale=hm.a[:, hh:hh + 1])
        B.tt("dve", egx.a, r_, eg.a, ALU.mult, [PS, eg], [egx])
        for hh in range(2):
            B.act(rm.a[:, :, hh, :], egx.a, AF.Copy, [egx, hm], [rm], scale=hm.a[:, hh:hh + 1])
        B.tt("pool", tq.a, kk.a, asg.a, ALU.mult, [kk, asg], [tq])
        B.tt("dve", b.bf.a, tq.a, eng.a, ALU.mult, [tq, eng], [b.bf])
        B.tt("dve", b.kf.a, kmod.a, eng.a, ALU.mult, [kmod, eng], [b.kf])
        B.cp("act", b.vb.a, v_, [PS], [b.vb])
        yield
        cs = slice(0, C)
        for n_, (src, dst) in enumerate(((b.bf, b.Btok), (b.kf, b.Ktok), (b.vb, b.Vtok))):
            pb = B.ps()
            pv = pb.a.bitcast(BF16)
            for j in range(4):
                B.tr(pv[:, j * 128:(j + 1) * 128], src.a[:, j, cs], ident_b.a, [src, ident_b], [pb])
            B.cp("act" if n_ != 1 else "dve", dst.a, pv[:, 0:512], [pb], [dst])
        yield
        for g in range(2):
            gs = slice(4 * g, 4 * g + 4)
            for kind in range(5):
                pbk = B.ps()
                for hq in range(4):
                    h = 4 * g + hq
                    j, hh = h // 2, h % 2
                    o = slice(hq * 128, (hq + 1) * 128)
                    bfs, kfs, ams, rms = b.bf.a[:, j, cs], b.kf.a[:, j, cs], am.a[:, j, hh, cs], rm.a[:, j, hh, cs]
                    lhsT, rhs, rd = ((bfs, ams, [b.bf, am]), (ams, bfs, [b.bf, am]), (kfs, ams, [b.kf, am]),
                                     (bfs, rms, [b.bf, rm]), (kfs, rms, [b.kf, rm]))[kind]
                    B.mm(pbk.a[:, o], lhsT, rhs, True, True, rd, [pbk])
                msk, dst = ((MU4, Pm), (ML4, Qm), (MU4, Nak), (MI4, Nrb), (MI4, Nrk))[kind]
                B.tt("dve", dst.a[:, gs, :], pbk.a.rearrange("p (h t) -> p h t", h=4), msk.a, ALU.mult, [pbk, msk], [dst])
            B.tt("pool", Rm.a[:, gs, :], Pm.a[:, gs, :], I4.a, ALU.add, [Pm, I4], [Rm])
            yield
        for lvl in range(1, 7):
            for g in range(2):
                gs = slice(4 * g, 4 * g + 4)
                pq = B.ps()
                pp = B.ps() if lvl < 6 else None
                for hq in range(4):
                    h = 4 * g + hq
                    o = slice(hq * 128, (hq + 1) * 128)
                    B.mm(pq.a[:, o], Pm.a[:, h, :], Qm.a[:, h, :], True, True, [Pm, Qm], [pq])
                    if pp is not None:
                        B.mm(pp.a[:, o], Qm.a[:, h, :], Pm.a[:, h, :], True, True, [Pm, Qm], [pp])
                B.cp("act", Qm.a[:, gs, :], pq.a.rearrange("p (h t) -> p h t", h=4), [pq], [Qm])
                if pp is not None:
                    B.cp("act", Pm.a[:, gs, :], pp.a.rearrange("p (h t) -> p h t", h=4), [pp], [Pm])
                pr = B.ps()
                for hq in range(4):
                    h = 4 * g + hq
                    o = slice(hq * 128, (hq + 1) * 128)
                    B.mm(pr.a[:, o], Qm.a[:, h, :], Rm.a[:, h, :], True, True, [Qm, Rm], [pr])
                B.tt("dve", Rm.a[:, gs, :], pr.a.rearrange("p (h t) -> p h t", h=4), Rm.a[:, gs, :], ALU.add, [pr, Rm], [Rm])
                yield
        px = B.ps()
        for h in range(8):
            j, hh = h // 2, h % 2
            o = slice(h * 64, (h + 1) * 64)
            B.mm(px.a[:, o], am.a[:, j, hh, cs], Hb.a[:, j, :], True, False, [am, Hb], [px])
            B.mm(px.a[:, o], Nak.a[:, h, :], b.Vtok.a[:, o], False, True, [Nak, b.Vtok], [px])
        B.cp("act", b.Xb.a, px.a, [px], [b.Xb])
        pu = B.ps()
        for h in range(8):
            o = slice(h * 64, (h + 1) * 64)
            B.mm(pu.a[:, o], Rm.a[:, h, :], b.Xb.a[:, o], True, True, [Rm, b.Xb], [pu])
        B.cp("act", b.Ub.a, pu.a, [pu], [b.Ub])
        ph = B.ps()
        for h in range(8):
            j, hh = h // 2, h % 2
            o = slice(h * 64, (h + 1) * 64)
            ho = ph.a[hh * 64:(hh + 1) * 64, j * 64:(j + 1) * 64]
            B.mm(ho, b.Btok.a[:, o], b.Ub.a[:, o], True, False, [b.Btok, b.Ub], [ph])
            B.mm(ho, b.Ktok.a[:, o], b.Vtok.a[:, o], False, True, [b.Ktok, b.Vtok], [ph])
        py = B.ps()
        for h in range(8):
            j, hh = h // 2, h % 2
            o = slice(h * 64, (h + 1) * 64)
            yo = py.a[hh * 64:(hh + 1) * 64, j * 128:(j + 1) * 128]
            B.mm(yo, Hb.a[:, j, :], rm.a[:, j, hh, cs], True, False, [Hb, rm], [py])
            B.mm(yo, b.Ub.a[:, o], Nrb.a[:, h, :], False, False, [b.Ub, Nrb], [py])
            B.mm(yo, b.Vtok.a[:, o], Nrk.a[:, h, :], False, True, [b.Vtok, Nrk], [py])
        B.tt("dve", Ht.a, ph.a[:, 0:256].rearrange("p (j i) -> p j i", j=4), H32.a, ALU.add, [ph, H32], [Ht])
        gC = eg.a[:, :, C - 1:C].broadcast_to([128, 4, 64])
        B.tt("dve", H32.a, Ht.a, gC, ALU.mult, [Ht, eg], [H32])
        B.cp("act", Hb.a, H32.a, [H32], [Hb])
        B.cp("act", b.Yf.a, py.a.rearrange("p (j t) -> p j t", j=4), [py], [b.Yf])
        yield
        if "wkv" in dbo:
            S.dma("sp", dbo["wkv"].a[:, :, ci * TB:(ci + 1) * TB], b.Yf.a, reads=[b.Yf], writes=[dbo["wkv"]], dreg=b.Yf)
        Yf, dd = b.Yf, b.dd
        pm_ = B.ps()
        for j in range(4):
            B.mm(pm_.a[:, j * TB:(j + 1) * TB], bavg.a, Yf.a[:, j, :], True, True, [bavg, Yf], [pm_])
        B.tt("dve", dd.a, Yf.a, pm_.a.rearrange("p (j t) -> p j t", j=4), ALU.subtract, [Yf, pm_], [dd])
        B.act(tq.a, dd.a, AF.Square, [dd], [tq])
        pv_ = B.ps()
        for j in range(4):
            B.mm(pv_.a[:, j * TB:(j + 1) * TB], bavg.a, tq.a[:, j, :], True, True, [bavg, tq], [pv_])
        B.act(cum.a, pv_.a.rearrange("p (j t) -> p j t", j=4), AF.Ln, [pv_, epsc], [cum], bias=epsc.a[:, 2:3])
        B.act(cum.a, cum.a, AF.Exp, [cum], [cum], scale=-0.5)
        yield
        B.tt("pool", dd.a, dd.a, cum.a, ALU.mult, [dd, cum], [dd])
        B.tt("dve", dd.a, dd.a, bc4(lwc), ALU.mult, [dd, lwc], [dd])
        B.tt("dve", dd.a, dd.a, bc4(lbc), ALU.add, [dd, lbc], [dd])
        B.tt("pool", dd.a, dd.a, bon.a, ALU.add, [dd, bon], [dd])
        B.tt("dve", b.yrb.a, dd.a, gg.a, ALU.mult, [dd, gg], [b.yrb])
        if "rwkv_y" in dbo:
            S.dma("sp", dbo["rwkv_y"].a[:, :, ci * TB:(ci + 1) * TB], b.yrb.a, reads=[b.yrb], writes=[dbo["rwkv_y"]], dreg=b.yrb)
        S.dma("sp", YR.a[:, :, ci * TB:(ci + 1) * TB], b.yrb.a, reads=[b.yrb], writes=[YR], dreg=b.yrb)
        yield

    pools = [[0, 1, 2], [3, 4, 5], [6, 7]] if NW == 3 else ([[0, 1, 2, 3], [4, 5, 6, 7]] if NW == 2 else [list(range(8))])
    S.rec = []
    for ci in range(ntiles):
        B.ps_pool = pools[ci % NW]
        for _ in chunk(ci):
            pass
    items = S.rec
    S.rec = None
    B.ps_pool = None
    S.schedule_emit(items, window=int(os.environ.get("K_WIN", "1400")))
    B.pop()


def phase_1c(B, W, x, x1s, H2T, YR, YGLU, epsc, gpre, frontend, bc_load, col_load, dbo, ntiles):
    nc, S = B.nc, B.S
    TB = 512
    NS = TB // 128
    B.push()
    wg = B.sb("wg", [128, 8, 2048], BF16)
    win = W["w_in"].a.rearrange("(c p) n -> p c n", p=128)
    for c in range(8):
        S.dma("pool", wg.a[:, c, :], win[:, c, NR + 512:NIN], reads=[W["w_in"]], writes=[wg])
    wbr = B.sb("wbr", [128, 4, D], BF16); wbs = B.sb("wbs", [128, 4, D], BF16); wout = B.sb("wout", [128, 8, D], BF16)
    S.dma("pool", wbr.a, W["w_branch_rwkv"].a.rearrange("(c p) n -> p c n", p=128), reads=[W["w_branch_rwkv"]], writes=[wbr])
    S.dma("pool", wbs.a, W["w_branch_s5"].a.rearrange("(c p) n -> p c n", p=128), reads=[W["w_branch_s5"]], writes=[wbs])
    for c in range(8):
        S.dma("pool", wout.a[:, c, :], W["w_out"].a[c * 128:(c + 1) * 128, :], reads=[W["w_out"]], writes=[wout])
    gpost = bc_load("norm_mix_post", D)
    gffn = bc_load("norm_ffn_pre", D)
    bgc = col_load("b_gate", 16)
    class BS:
        pass
    sets = []
    for w in range(2):
        q = BS()
        q.xt = B.sb("xt%d" % w, [128, NS, D]); q.hb = B.sb("hb%d" % w, [128, NS, D], BF16); q.hT = B.sb("hT%d" % w, [128, 8, TB], BF16)
        q.scr = B.sb("scr%d" % w, [128, D]); q.st = B.sb("st%d" % w, [128, 12])
        q.yrt = B.sb("yrt%d" % w, [128, 4, TB], BF16); q.ygt = B.sb("ygt%d" % w, [128, 4, TB], BF16)
        q.mixb = B.sb("mixb%d" % w, [128, 8, TB], BF16)
        sets.append(q)
    gA = B.sb("gA", [128, TB]); gB = B.sb("gB", [128, TB]); mt1 = B.sb("mt1", [128, TB]); mt2 = B.sb("mt2", [128, TB])

    def part_a(ti):
        q = sets[ti % 2]
        frontend(x, ti, NS, gpre, q.xt, q.hb, q.hT, q.scr, q.st)
        S.dma("sp", q.yrt.a, YR.a[:, :, ti * TB:(ti + 1) * TB], reads=[YR], writes=[q.yrt])
        S.dma("sp", q.ygt.a, YGLU.a[:, :, ti * TB:(ti + 1) * TB], reads=[YGLU], writes=[q.ygt])

    def part_b(ti):
        q = sets[ti % 2]
        xt, hb, hT, scr, st, yrt, ygt, mixb = q.xt, q.hb, q.hT, q.scr, q.st, q.yrt, q.ygt, q.mixb
        for cb in range(8):
            pa = B.ps(); pbb = B.ps(); po = B.ps(); ps_ = B.ps()
            for c in range(8):
                B.mm(pa.a, wg.a[:, c, cb * 128:(cb + 1) * 128], hT.a[:, c, :], c == 0, c == 7, [wg, hT], [pa])
            for c in range(8):
                B.mm(pbb.a, wg.a[:, c, (8 + cb) * 128:(9 + cb) * 128], hT.a[:, c, :], c == 0, c == 7, [wg, hT], [pbb])
            for j in range(4):
                B.mm(po.a, wbr.a[:, j, cb * 128:(cb + 1) * 128], yrt.a[:, j, :], j == 0, j == 3, [wbr, yrt], [po])
            for j in range(4):
                B.mm(ps_.a, wbs.a[:, j, cb * 128:(cb + 1) * 128], ygt.a[:, j, :], j == 0, j == 3, [wbs, ygt], [ps_])
            B.act(gA.a, pa.a, AF.Sigmoid, [pa, bgc], [gA], bias=bgc.a[:, cb:cb + 1])
            B.act(gB.a, pbb.a, AF.Sigmoid, [pbb, bgc], [gB], bias=bgc.a[:, 8 + cb:9 + cb])
            B.tt("dve", mt1.a, po.a, gA.a, ALU.mult, [po, gA], [mt1])
            B.tt("dve", mt2.a, ps_.a, gB.a, ALU.mult, [ps_, gB], [mt2])
            B.tt("pool", mixb.a[:, cb, :], mt1.a, mt2.a, ALU.add, [mt1, mt2], [mixb])

    def part_c(ti):
        q = sets[ti % 2]
        xt, hb, hT, scr, st, yrt, ygt, mixb = q.xt, q.hb, q.hT, q.scr, q.st, q.yrt, q.ygt, q.mixb
        for s_ in range(NS):
            pbs = [B.ps(), B.ps()]
            for half in range(2):
                for c8 in range(8):
                    B.mm(pbs[half].a, mixb.a[:, c8, s_ * 128:(s_ + 1) * 128], wout.a[:, c8, half * 512:(half + 1) * 512],
                         c8 == 0, c8 == 7, [mixb, wout], [pbs[half]])
            for half in range(2):
                B.act(scr.a[:, 0:512], pbs[half].a, AF.Square, [pbs[half]], [scr, st], accum_out=st.a[:, half:half + 1])
            B.tt("dve", st.a[:, 2:3], st.a[:, 0:1], st.a[:, 1:2], ALU.add, [st], [st])
            B.act(st.a[:, 2:3], st.a[:, 2:3], AF.Ln, [st, epsc], [st], scale=1.0 / D, bias=epsc.a[:, 0:1])
            B.act(st.a[:, 3:4], st.a[:, 2:3], AF.Exp, [st], [st], scale=-0.5)
            for half in range(2):
                hsl = slice(half * 512, (half + 1) * 512)
                B.stt(scr.a[:, hsl], pbs[half].a, st.a[:, 3:4], gpost.a[:, hsl], ALU.mult, ALU.mult, [pbs[half], st, gpost], [scr])
            B.tt("pool", xt.a[:, s_, :], xt.a[:, s_, :], scr.a, ALU.add, [xt, scr], [xt])
        S.dma("sp", x1s.a[ti * TB:(ti + 1) * TB, :].rearrange("(s p) d -> p s d", p=128), xt.a, reads=[xt], writes=[x1s], dreg=xt)
        if "x1" in dbo:
            S.dma("sp", dbo["x1"].a[ti * TB:(ti + 1) * TB, :].rearrange("(s p) d -> p s d", p=128), xt.a, reads=[xt],
                  writes=[dbo["x1"]], dreg=xt)
        frontend(None, ti, NS, gffn, xt, hb, hT, scr, st, load=False)
        S.dma("sp", H2T.a[:, :, ti * TB:(ti + 1) * TB], hT.a, reads=[hT], writes=[H2T], dreg=hT)

    part_a(0)
    for ti in range(ntiles):
        part_b(ti)
        if ti + 1 < ntiles:
            part_a(ti + 1)
        part_c(ti)
    B.pop()


def phase_2(B, W, x1s, H2T, out, epsc, frontend, bc_load, dbo, ntiles):
    nc, S = B.nc, B.S
    TB = 512
    NS = TB // 128
    ntiles = ntiles * (512 // TB)
    B.push()
    wup = B.sb("wup", [128, 8, 2 * FF], BF16)
    wsrc = W["ffn_w_up"].a.rearrange("(c p) n -> p c n", p=128)
    for c in range(8):
        for (a, b) in ((0, 2048), (2048, 4096), (4096, 2 * FF)):
            S.dma("pool", wup.a[:, c, a:b], wsrc[:, c, a:b], reads=[W["ffn_w_up"]], writes=[wup])
    wdn = B.sb("wdn", [128, 22, D], BF16)
    for i in range(22):
        S.dma("pool", wdn.a[:, i, :], W["ffn_w_down"].a[i * 128:(i + 1) * 128, :], reads=[W["ffn_w_down"]], writes=[wdn])
    g2 = bc_load("norm_ffn_post", D)
    cw = B.sb("cw", [128, 3, 44]); cbias = B.sb("cbias", [128, 44])
    S.dma("sp", cw.a, W["ffn_conv_w"].a.rearrange("j (b p) -> p j b", p=128), reads=[W["ffn_conv_w"]], writes=[cw],
          allow_slow_non_contiguous=True)
    S.dma("sp", cbias.a, W["ffn_conv_b"].a.rearrange("(b p) -> p b", p=128), reads=[W["ffn_conv_b"]], writes=[cbias],
          allow_slow_non_contiguous=True)
    halo = B.sb("halo", [128, 44, 2]); B.memset("pool", halo.a, 0.0, [halo])
    epsc = B.sb("epsc2", [128, 1]); B.memset("pool", epsc.a, 1e-6, [epsc])
    class BS:
        pass
    sets = []
    hTs = [B.sb("hT2_%d" % w, [128, 8, TB], BF16) for w in range(2)]
    for w in range(1):
        q = BS()
        q.xt = B.sb("xt2_%d" % w, [128, NS, D])
        q.actb = B.sb("actb%d" % w, [128, 22, TB], BF16)
        q.st = B.sb("st2_%d" % w, [128, 12])
        sets.append(q)
    scrF_ = B.sb("scrF", [128, D])
    a0s_ = (B.sb("a0g", [128, TB]), B.sb("a0v", [128, TB]))
    a1s_ = ((B.sb("a1g0", [128, TB]), B.sb("a1v0", [128, TB])), (B.sb("a1g1", [128, TB]), B.sb("a1v1", [128, TB])))
    for q in sets:
        q.scrF, q.a0s, q.a1s = scrF_, a0s_, a1s_
    def tile(ti):
        q = sets[0]
        xt, actb, st, scrF, a0s, a1s = q.xt, q.actb, q.st, q.scrF, q.a0s, q.a1s
        hT = hTs[ti % 2]
        if ti == 0:
            S.dma("sp", hT.a, H2T.a[:, :, 0:TB], reads=[H2T], writes=[hT])
        if ti + 1 < ntiles:
            S.dma("sp", hTs[(ti + 1) % 2].a, H2T.a[:, :, (ti + 1) * TB:(ti + 2) * TB], reads=[H2T], writes=[hTs[(ti + 1) % 2]])
        S.dma("sp", xt.a, x1s.a[ti * TB:(ti + 1) * TB, :].rearrange("(s p) d -> p s d", p=128), reads=[x1s], writes=[xt])
        def finish(i):
            accg, accv = a1s[i % 2]
            B.act(accg.a, accg.a, AF.Gelu_apprx_tanh, [accg], [accg])
            B.tt("pool", actb.a[:, i, :], accg.a, accv.a, ALU.mult, [accg, accv], [actb])

        for i in range(22):
            accs = a1s[i % 2]
            for gv in range(2):
                b = i + 22 * gv
                pb = B.ps()
                for c in range(8):
                    B.mm(pb.a[:, 0:TB], wup.a[:, c, b * 128:(b + 1) * 128], hT.a[:, c, :], c == 0, c == 7, [wup, hT], [pb])
                acc = accs[gv]; a0 = a0s[gv]; a1 = accs[gv]
                B.act(a0.a, pb.a[:, 0:TB], AF.Identity, [pb, cw, cbias], [a0], scale=cw.a[:, 2, b:b + 1], bias=cbias.a[:, b:b + 1])
                B.act(a1.a[:, 1:TB], pb.a[:, 0:TB - 1], AF.Copy, [pb, cw], [a1], scale=cw.a[:, 1, b:b + 1])
                B.stt(a0.a[:, 2:TB], pb.a[:, 0:TB - 2], cw.a[:, 0, b:b + 1], a0.a[:, 2:TB], ALU.mult, ALU.add, [pb, cw, a0], [a0])
                B.ts("dve", a1.a[:, 0:1], halo.a[:, b, 1:2], cw.a[:, 1, b:b + 1], ALU.mult, [halo, cw], [a1])
                B.stt(a0.a[:, 0:2], halo.a[:, b, 0:2], cw.a[:, 0, b:b + 1], a0.a[:, 0:2], ALU.mult, ALU.add, [halo, cw, a0], [a0])
                B.cp("dve", halo.a[:, b, :], pb.a[:, TB - 2:TB], [pb], [halo])
                B.tt("pool", acc.a, a0.a, a1.a, ALU.add, [a0, a1], [acc])
            if "zc" in dbo and ti == 0 and i == 0:
                S.dma("sp", dbo["zc"].a[:, 0:TB], accs[0].a, reads=[accs[0]], writes=[dbo["zc"]], dreg=accs[0])
            if i > 0:
                finish(i - 1)
        finish(21)
        for s_ in range(NS):
            pbs = [B.ps(), B.ps()]
            for half in range(2):
                for i in range(22):
                    B.mm(pbs[half].a, actb.a[:, i, s_ * 128:(s_ + 1) * 128], wdn.a[:, i, half * 512:(half + 1) * 512],
                         i == 0, i == 21, [actb, wdn], [pbs[half]])
            for half in range(2):
                B.act(scrF.a[:, 0:512], pbs[half].a, AF.Square, [pbs[half]], [scrF, st], accum_out=st.a[:, half:half + 1])
            B.tt("dve", st.a[:, 2:3], st.a[:, 0:1], st.a[:, 1:2], ALU.add, [st], [st])
            B.act(st.a[:, 2:3], st.a[:, 2:3], AF.Ln, [st, epsc], [st], scale=1.0 / D, bias=epsc.a[:, 0:1])
            B.act(st.a[:, 3:4], st.a[:, 2:3], AF.Exp, [st], [st], scale=-0.5)
            for half in range(2):
                hsl = slice(half * 512, (half + 1) * 512)
                B.stt(scrF.a[:, hsl], pbs[half].a, st.a[:, 3:4], g2.a[:, hsl], ALU.mult, ALU.mult, [pbs[half], st, g2], [scrF])
            B.tt("pool", xt.a[:, s_, :], xt.a[:, s_, :], scrF.a, ALU.add, [xt, scrF], [xt])
        S.dma("sp", out.a[ti * TB:(ti + 1) * TB, :].rearrange("(s p) d -> p s d", p=128), xt.a, reads=[xt], writes=[out], dreg=xt)

    S.rec = []
    for ti in range(ntiles):
        tile(ti)
    items = S.rec
    S.rec = None
    if os.environ.get("K_P2SCHED", "0") == "1":
        S.schedule_emit(items, window=int(os.environ.get("K_WIN2", "1500")))
    else:
        for it in items:
            S.replay(it)
    B.pop()


_CACHE = {}


def kernel(**inputs):
    if "B" not in _CACHE:
        _CACHE["B"] = build()
    Bd = _CACHE["B"]
    x = np.ascontiguousarray(inputs["x"], dtype=np.float32)
    wmap = {k: np.ascontiguousarray(np.asarray(inputs[k], dtype=np.float32).reshape(shp)) for k, shp in Bd.Wshapes.items()}
    in_maps = []
    for c in range(8):
        m = dict(wmap)
        m["x"] = x[c]
        in_maps.append(m)
    res = run_bass_kernel_spmd(Bd.nc, in_maps, core_ids=list(range(8)))
    return np.stack([np.asarray(res.results[c]["out"], dtype=np.float32) for c in range(8)], axis=0)
```

```python
import contextlib
import math
import os
import numpy as np
import concourse.bass as bass
import concourse.mybir as mybir
from concourse.bass_utils import run_bass_kernel_spmd

F32 = mybir.dt.float32
BF16 = mybir.dt.bfloat16
I32 = mybir.dt.int32
ALU = mybir.AluOpType
AF = mybir.ActivationFunctionType

L = 4096
D = 1024
NR = 1792
NS5 = 512
NIN = 4352
FF = 2816
PI = math.pi


class Reg:
    __slots__ = ("name", "w", "r", "dsem", "dcnt", "last", "excl")

    def __init__(self, name=""):
        self.name = name
        self.w = None
        self.r = []
        self.dsem = None
        self.dcnt = 0
        self.last = 0
        self.excl = False


class Sched:
    def __init__(self, nc):
        self.nc = nc
        self.eng = {"pe": nc.tensor, "dve": nc.vector, "act": nc.scalar,
                    "pool": nc.gpsimd, "sp": nc.sync}
        self.sem = {k: nc.alloc_semaphore(name="sem_" + k) for k in self.eng}
        self.cnt = {k: 0 for k in self.eng}
        self.seen = {k: {} for k in self.eng}
        self.ninst = 0
        self.nds = 0
        self.dregs = []
        self.dmap = {}
        self.maxops = int(os.environ.get("K_MAXOPS", "100000000"))
        self.rec = None

    def _wait(self, e, tok):
        sem, val = tok
        key = sem.name
        if key in self.dmap:
            val = max(val, self.dmap[key].dcnt)
        if self.seen[e].get(key, 0) >= val:
            return
        if e == "pe" and sem is self.sem["pe"]:
            return
        self.eng[e].wait_ge(sem, val)
        self.seen[e][key] = val

    def _deps(self, e, reads, writes, skip=None):
        for r in reads:
            if r.w is not None:
                self._wait(e, r.w)
        for w in writes:
            if w.w is not None and w.w[0] is not skip:
                self._wait(e, w.w)
            for t in w.r:
                self._wait(e, t)

    def _commit(self, tok, reads, writes):
        for r in reads:
            r.last = self.ninst
            r.r.append(tok)
            if len(r.r) > 16:
                d = {}
                for s, v in r.r:
                    if d.get(s.name, (None, -1))[1] < v:
                        d[s.name] = (s, v)
                r.r = list(d.values())
        for w in writes:
            w.last = self.ninst
            w.w = tok
            w.r = []

    def op(self, e, fn, reads=(), writes=(), cost=None):
        if self.rec is not None:
            self.rec.append(("op", e, fn, list(reads), list(writes), None, cost))
            return None
        if self.ninst >= self.maxops:
            return None
        reads = [x.r if isinstance(x, Buf) else x for x in reads]
        writes = [x.r if isinstance(x, Buf) else x for x in writes]
        ex = [x for x in reads if x.excl and x not in writes]
        if ex:
            reads = [x for x in reads if not x.excl]
            writes = list(writes) + ex
        self._deps(e, reads, writes)
        ins = fn(self.eng[e])
        self.cnt[e] += 1
        ins.then_inc(self.sem[e], 1)
        tok = (self.sem[e], self.cnt[e])
        self._commit(tok, reads, writes)
        self.ninst += 1
        return tok

    def dma(self, e, out, in_, reads=(), writes=(), dreg=None, **kw):
        if self.rec is not None:
            self.rec.append(("dma", e, (out, in_), list(reads), list(writes), (dreg, kw), None))
            return None
        if self.ninst >= self.maxops and not kw.pop("force", False):
            return None
        kw.pop("force", None)
        reads = [x.r if isinstance(x, Buf) else x for x in reads]
        writes = [x.r if isinstance(x, Buf) else x for x in writes]
        if dreg is None:
            dreg = writes[0] if writes else reads[0]
        elif isinstance(dreg, Buf):
            dreg = dreg.r
        if dreg.dsem is None:
            self.nds += 1
            dreg.dsem = self.nc.alloc_semaphore(name="ds%d_%s" % (self.nds, dreg.name))
            self.dregs.append(dreg)
            self.dmap[dreg.dsem.name] = dreg
        self._deps(e, reads, writes, skip=dreg.dsem)
        ins = self.eng[e].dma_start(out=out, in_=in_, **kw)
        dreg.dcnt += 16
        ins.then_inc(dreg.dsem, 16)
        tok = (dreg.dsem, dreg.dcnt)
        self._commit(tok, reads, writes)
        self.ninst += 1
        return tok

    def replay(self, item):
        kind, e, a, reads, writes, extra = item[:6]
        if kind == "op":
            return self.op(e, a, reads, writes)
        dreg, kw = extra
        return self.dma(e, a[0], a[1], reads=reads, writes=writes, dreg=dreg, **kw)

    def schedule_emit(self, items, window=1500):
        import heapq
        n = len(items)
        norm = []
        for it in items:
            kind, e, a, reads, writes, extra, cost = it
            reads = [x.r if isinstance(x, Buf) else x for x in reads]
            writes = [x.r if isinstance(x, Buf) else x for x in writes]
            dreg = None
            if kind == "dma":
                dreg = extra[0]
                if dreg is None:
                    dreg = writes[0] if writes else reads[0]
                elif isinstance(dreg, Buf):
                    dreg = dreg.r
            ex = [x for x in reads if x.excl and x not in writes]
            if ex:
                reads = [x for x in reads if not x.excl]
                writes = list(writes) + ex
            norm.append((kind, e, a, reads, writes, extra, cost, dreg))
        lw = {}
        rd = {}
        first = [[] for _ in range(n)]
        preds = [set() for _ in range(n)]
        for i, (kind, e, a, reads, writes, extra, cost, dreg) in enumerate(norm):
            for r in reads:
                k = id(r)
                if k in lw:
                    preds[i].add(lw[k])
                else:
                    first[i].append((r, "r"))
            for w in writes:
                k = id(w)
                if k in lw:
                    if not (kind == "dma" and norm[lw[k]][0] == "dma" and norm[lw[k]][7] is dreg):
                        preds[i].add(lw[k])
                else:
                    first[i].append((w, "w"))
                for j in rd.get(k, ()):
                    preds[i].add(j)
            for r in reads:
                rd.setdefault(id(r), []).append(i)
            for w in writes:
                lw[id(w)] = i
                rd[id(w)] = []
            preds[i].discard(i)
        succs = [[] for _ in range(n)]
        npred = [len(p) for p in preds]
        for i, p in enumerate(preds):
            for j in p:
                succs[j].append(i)
        def dur(it):
            kind, e, a, reads, writes, extra, cost, dreg = it
            if cost is not None:
                return cost
            if kind == "op":
                return {"pe": 0.08, "dve": 0.4, "act": 0.4, "pool": 1.0, "sp": 2.0}[e]
            o = a[0]
            nbytes = 1
            for d in list(o.shape):
                nbytes *= int(d)
            nbytes *= mybir.dt.size(o.dtype)
            return 2.5 + nbytes / 140e3
        LAT = 0.15
        free = {e: 0.0 for e in self.eng}
        fin = [0.0] * n
        ready_t = [0.0] * n
        heaps = {e: [] for e in self.eng}
        low = 0
        done = [False] * n
        avail = [False] * n
        for i in range(n):
            if npred[i] == 0:
                heapq.heappush(heaps[norm[i][1]], (0.0, i)); avail[i] = True
        order = []
        deferred = {e: [] for e in self.eng}
        while len(order) < n:
            best = None
            for e, h in heaps.items():
                while h and h[0][1] >= low + window:
                    deferred[e].append(heapq.heappop(h))
                if not h:
                    continue
                rt, i = h[0]
                st = max(rt, free[e])
                if best is None or (st, i) < (best[0], best[1]):
                    best = (st, i, e)
            if best is None:
                for e in deferred:
                    for x in deferred[e]:
                        heapq.heappush(heaps[e], x)
                    deferred[e] = []
                window *= 2
                continue
            st, i, e = best
            heapq.heappop(heaps[e])
            order.append(i)
            done[i] = True
            f = st + dur(norm[i])
            fin[i] = f
            free[e] = f
            for j in succs[i]:
                npred[j] -= 1
                ready_t[j] = max(ready_t[j], f + LAT)
                if npred[j] == 0:
                    heapq.heappush(heaps[norm[j][1]], (ready_t[j], j)); avail[j] = True
            if i == low:
                while low < n and done[low]:
                    low += 1
                for e2 in deferred:
                    keep = []
                    for x in deferred[e2]:
                        if x[1] < low + window:
                            heapq.heappush(heaps[e2], x)
                        else:
                            keep.append(x)
                    deferred[e2] = keep
        self.sched_makespan = max(fin) if n else 0.0
        toks = [None] * n
        for i in order:
            kind, e, a, reads, writes, extra, cost, dreg = norm[i]
            for (reg, mode) in first[i]:
                if reg.w is not None and not (kind == "dma" and mode == "w" and reg.w[0] is (dreg.dsem if dreg is not None else None)):
                    self._wait(e, reg.w)
                if mode == "w":
                    for t in reg.r:
                        self._wait(e, t)
            for j in preds[i]:
                self._wait(e, toks[j])
            if kind == "op":
                ins = a(self.eng[e])
                self.cnt[e] += 1
                ins.then_inc(self.sem[e], 1)
                toks[i] = (self.sem[e], self.cnt[e])
            else:
                dg, kw = extra
                kw = dict(kw); kw.pop("force", None)
                if dreg.dsem is None:
                    self.nds += 1
                    dreg.dsem = self.nc.alloc_semaphore(name="ds%d_%s" % (self.nds, dreg.name))
                    self.dregs.append(dreg)
                    self.dmap[dreg.dsem.name] = dreg
                ins = self.eng[e].dma_start(out=a[0], in_=a[1], **kw)
                dreg.dcnt += 16
                ins.then_inc(dreg.dsem, 16)
                toks[i] = (dreg.dsem, dreg.dcnt)
            self.ninst += 1
        touched = {}
        for i, it in enumerate(norm):
            for r in it[3]:
                touched[id(r)] = r
            for w in it[4]:
                touched[id(w)] = w
        for k, reg in touched.items():
            if k in lw:
                reg.w = toks[lw[k]]
                reg.r = [toks[j] for j in rd.get(k, ())]
            else:
                reg.r = list(reg.r) + [toks[j] for j in rd.get(k, ())]
            reg.last = self.ninst

    def barrier(self):
        for e in self.eng:
            for f in self.eng:
                if f != e and self.cnt[f] > 0:
                    self._wait(e, (self.sem[f], self.cnt[f]))
            for d in self.dregs:
                if d.dcnt > 0:
                    self._wait(e, (d.dsem, d.dcnt))

    def final_wait(self, e, regs):
        for r in regs:
            r = r.r if isinstance(r, Buf) else r
            if r.w is not None:
                self._wait(e, r.w)
            for t in r.r:
                self._wait(e, t)


class Buf:
    def __init__(self, t, name):
        self.t = t
        self.a = t.ap()
        self.r = Reg(name)


class Builder:
    def __init__(self, dbg=None, ntiles=8):
        self.nc = bass.Bass("TRN2", target_bir_lowering=False)
        self.S = Sched(self.nc)
        self.dbg = dbg or {}
        self.ntiles = ntiles
        self.din = {}
        self.dout = {}
        self.nbuf = 0
        self.psb = None
        self.psi = 0
        self.ps_pool = None
        self.pclock = 0
        self.scopes = []

    def push(self):
        self.scopes.append(contextlib.ExitStack())

    def pop(self):
        self.S.barrier()
        self.scopes.pop().close()

    def inp(self, name, shape):
        b = Buf(self.nc.dram_tensor(name, list(shape), F32, kind="ExternalInput"), name)
        self.din[name] = b
        return b

    def outp(self, name, shape, dt=F32):
        b = Buf(self.nc.dram_tensor(name, list(shape), dt, kind="ExternalOutput"), name)
        self.dout[name] = b
        return b

    def sb(self, name, shape, dt=F32):
        self.nbuf += 1
        nm = "%s_%d" % (name, self.nbuf)
        if self.scopes:
            return Buf(self.scopes[-1].enter_context(self.nc.sbuf_tensor(nm, list(shape), dt)), name)
        return Buf(self.nc.alloc_sbuf_tensor(nm, list(shape), dt), name)

    def view(self, buf, ap):
        v = Buf.__new__(Buf)
        v.t = buf.t
        v.a = ap
        v.r = buf.r
        return v

    def init_psum(self):
        self.psb = []
        for i in range(8):
            t = self.nc.alloc_psum_tensor("psb%d" % i, [128, 512], F32)
            self.psb.append(Buf(t, "psb%d" % i))
            self.psb[-1].r.excl = True

    def ps(self):
        if self.ps_pool is not None:
            b = self.psb[self.ps_pool[self.psi % len(self.ps_pool)]]
            self.psi += 1
            return b
        b = min(self.psb, key=lambda t: t.r.last)
        self.pclock = max(self.pclock, self.S.ninst) + 1
        b.r.last = self.pclock
        return b

    def op(self, e, fn, reads=(), writes=(), cost=None):
        return self.S.op(e, fn, reads, writes, cost=cost)

    @staticmethod
    def fsz(ap):
        n = 1
        for d in list(ap.shape)[1:]:
            n *= int(d)
        return n

    def mm(self, out, lhsT, rhs, start, stop, reads, writes, **kw):
        return self.op("pe", lambda e: e.matmul(out, lhsT=lhsT, rhs=rhs, start=start, stop=stop, **kw), reads, writes,
                       cost=0.03 + max(self.fsz(rhs), 64) * 0.00052)

    def tr(self, out, in_, ident, reads, writes):
        return self.op("pe", lambda e: e.transpose(out, in_, ident), reads, writes, cost=0.1)

    def act(self, eng_out, in_, func, reads, writes, **kw):
        return self.op("act", lambda e: e.activation(out=eng_out, in_=in_, func=func, **kw), reads, writes,
                       cost=0.22 + self.fsz(in_) * 0.00075)

    def tt(self, e, out, in0, in1, op, reads, writes):
        c = (0.1 + self.fsz(in0) * 0.00115) if e == "dve" else (0.6 + self.fsz(in0) * 0.0013)
        return self.op(e, lambda g: g.tensor_tensor(out=out, in0=in0, in1=in1, op=op), reads, writes, cost=c)

    def ts(self, e, out, in0, s1, op0, reads, writes, s2=None, op1=None):
        c = (0.1 + self.fsz(in0) * 0.0008) if e == "dve" else (0.6 + self.fsz(in0) * 0.0013)
        if op1 is None:
            return self.op(e, lambda g: g.tensor_scalar(out=out, in0=in0, scalar1=s1, scalar2=None, op0=op0), reads, writes, cost=c)
        return self.op(e, lambda g: g.tensor_scalar(out=out, in0=in0, scalar1=s1, scalar2=s2, op0=op0, op1=op1), reads, writes, cost=c)

    def stt(self, out, in0, scalar, in1, op0, op1, reads, writes):
        return self.op("dve", lambda g: g.scalar_tensor_tensor(out=out, in0=in0, scalar=scalar, in1=in1, op0=op0, op1=op1), reads, writes,
                       cost=0.1 + self.fsz(in0) * 0.00115)

    def cp(self, e, out, in_, reads, writes):
        if e == "act":
            return self.op("act", lambda g: g.activation(out=out, in_=in_, func=AF.Copy), reads, writes, cost=0.22 + self.fsz(in_) * 0.00075)
        c = (0.1 + self.fsz(in_) * 0.0008) if e == "dve" else (0.6 + self.fsz(in_) * 0.0013)
        return self.op(e, lambda g: g.tensor_copy(out, in_), reads, writes, cost=c)

    def memset(self, e, out, val, writes):
        return self.op(e, lambda g: g.memset(out, val), (), writes)

    def cmul(self, e, o_re, o_im, a_re, a_im, b_re, b_im, t1, t2, reads, writes, tmp):
        R = list(reads)
        self.tt(e, t1, a_re, b_re, ALU.mult, R, [tmp])
        self.tt(e, t2, a_im, b_im, ALU.mult, R, [tmp])
        self.tt(e, o_re, t1, t2, ALU.subtract, [tmp], writes)
        self.tt(e, t1, a_re, b_im, ALU.mult, R, [tmp])
        self.tt(e, t2, a_im, b_re, ALU.mult, R, [tmp])
        self.tt(e, o_im, t1, t2, ALU.add, [tmp], writes)


def build(dbg=None, ntiles=8, phases=("1a", "1b", "1c", "2")):
    B = Builder(dbg, ntiles)
    nc, S = B.nc, B.S
    dbg = B.dbg

    x = B.inp("x", [L, D])
    shapes = {
        "norm_mix_pre": [D], "norm_mix_post": [D], "norm_ffn_pre": [D], "norm_ffn_post": [D],
        "w_in": [D, NIN], "b_gate": [2048], "rwkv_shift_mu": [NR], "rwkv_w0": [512],
        "rwkv_w2": [64, 512], "rwkv_a0": [512], "rwkv_a2": [64, 512], "rwkv_g2": [128, 512],
        "rwkv_k_k": [512], "rwkv_k_a": [512], "rwkv_r_k": [512], "rwkv_lnx_w": [512],
        "rwkv_lnx_b": [512], "s5_a_re": [32, 64], "s5_a_im": [32, 64], "s5_b_re": [32, 64, 16],
        "s5_b_im": [32, 64, 16], "s5_c_re": [32, 16, 64], "s5_c_im": [32, 16, 64], "s5_d": [512],
        "s5_log_step": [32], "s5_w_glu": [512, 512], "s5_b_glu": [512], "w_branch_rwkv": [512, D],
        "w_branch_s5": [512, D], "w_out": [D, D], "ffn_w_up": [D, 2 * FF], "ffn_conv_w": [3, 2 * FF],
        "ffn_conv_b": [2 * FF], "ffn_w_down": [FF, D],
    }
    W = {k: B.inp(k, v) for k, v in shapes.items()}
    out = B.outp("out", [L, D])
    dbo = {k: B.outp("dbg_" + k, shp, dt) for k, (shp, dt) in dbg.items()}

    B.init_psum()

    B.push()
    ident_f = B.sb("ident_f", [128, 128], F32)
    ident_b = B.sb("ident_b", [128, 128], BF16)
    B.memset("pool", ident_f.a, 1.0, [ident_f])
    B.op("pool", lambda e: e.affine_select(out=ident_f.a, in_=ident_f.a, pattern=[[-1, 128]],
                                           compare_op=ALU.is_equal, fill=0.0, base=0, channel_multiplier=1),
         [ident_f], [ident_f])
    B.cp("pool", ident_b.a, ident_f.a, [ident_f], [ident_b])
    epsc = B.sb("epsc", [128, 4])
    B.memset("pool", epsc.a[:, 0:1], 1e-6, [epsc]); B.memset("pool", epsc.a[:, 1:2], 1e-24, [epsc])
    B.memset("pool", epsc.a[:, 2:3], 64e-5, [epsc]); B.memset("pool", epsc.a[:, 3:4], 0.0, [epsc])

    def bc_load(name, n, q="sp"):
        t = B.sb(name + "_bc", [128, n], F32)
        S.dma(q, t.a, W[name].a.partition_broadcast(128), reads=[W[name]], writes=[t])
        return t

    def col_load(name, nt, q="sp"):
        t = B.sb(name + "_col", [128, nt], F32)
        S.dma(q, t.a, W[name].a.rearrange("(t p) -> p t", p=128), reads=[W[name]], writes=[t],
              allow_slow_non_contiguous=True)
        return t

    def frontend(src, ti, ntok_tiles, gbc, xt, hb, hT, scr, st, load=True):
        nt = ntok_tiles
        T = 128 * nt
        if load:
            S.dma("sp", xt.a, src.a[ti * T:(ti + 1) * T, :].rearrange("(s p) d -> p s d", p=128),
                  reads=[src], writes=[xt])
        for s in range(nt):
            B.act(scr.a, xt.a[:, s, :], AF.Square, [xt], [scr, st], accum_out=st.a[:, s:s + 1])
        B.act(st.a[:, nt:2 * nt], st.a[:, 0:nt], AF.Ln, [st, epsc], [st], scale=1.0 / D, bias=epsc.a[:, 0:1])
        B.act(st.a[:, 2 * nt:3 * nt], st.a[:, nt:2 * nt], AF.Exp, [st], [st], scale=-0.5)
        for s in range(nt):
            B.stt(hb.a[:, s, :], xt.a[:, s, :], st.a[:, 2 * nt + s:2 * nt + s + 1], gbc.a, ALU.mult, ALU.mult,
                  [xt, st, gbc], [hb])
        for c in range(8):
            pb = B.ps()
            pv = pb.a.bitcast(BF16)
            for s in range(nt):
                B.tr(pv[:, s * 128:(s + 1) * 128], hb.a[:, s, c * 128:(c + 1) * 128], ident_b.a,
                     [hb, ident_b], [pb])
            B.cp("act" if c % 2 == 0 else "dve", hT.a[:, c, :], pv[:, 0:T], [pb], [hT])

    T1 = 512
    YGLU = Buf(nc.dram_tensor("yglu_d", [128, 4, L], BF16, kind="Internal"), "yglu_d")

    gpre = bc_load("norm_mix_pre", D)
    x1s = Buf(nc.dram_tensor("x1s", [L, D], F32, kind="Internal"), "x1s")
    YR = Buf(nc.dram_tensor("yr_d", [128, 4, L], BF16, kind="Internal"), "yr_d")
    H2T = Buf(nc.dram_tensor("h2t_d", [128, 8, L], BF16, kind="Internal"), "h2t_d")
    if "1a" in phases:
        B.push()
        ws5 = B.sb("ws5", [128, 8, 512], BF16)
        S.dma("pool", ws5.a, W["w_in"].a.rearrange("(c p) n -> p c n", p=128)[:, :, NR:NR + 512],
              reads=[W["w_in"]], writes=[ws5])
        wglu = B.sb("wglu", [128, 4, 512], BF16)
        S.dma("pool", wglu.a, W["s5_w_glu"].a.rearrange("(c p) n -> p c n", p=128), reads=[W["s5_w_glu"]], writes=[wglu])
        bglu = col_load("s5_b_glu", 4)
        dcol = col_load("s5_d", 4)

        MS = B.sb("ms", [128, 16, 2, 2])
        ER = B.sb("ER", [128, 16, 64]); EI = B.sb("EI", [128, 16, 64]); R8 = B.sb("R8", [128, 16])
        Wt = B.sb("Wt", [128, 4, 8, 2, 128], BF16)
        CAW = B.sb("CAW", [128, 16, 2, 9, 64], BF16)
        mk = B.sb("mk", [128, 2])
        B.memset("pool", mk.a[:, 0:1], 0.0, [mk]); B.memset("pool", mk.a[0:32, 0:1], 1.0, [mk]); B.memset("pool", mk.a[64:96, 0:1], 1.0, [mk])
        mk4 = B.sb("mk4", [128, 4])
        B.memset("pool", mk4.a, 0.0, [mk4])
        B.memset("pool", mk4.a[0:32, 0:1], 1.0, [mk4]); B.memset("pool", mk4.a[32:64, 1:2], 1.0, [mk4])
        B.memset("pool", mk4.a[64:96, 2:3], 1.0, [mk4]); B.memset("pool", mk4.a[64:128, 3:4], 1.0, [mk4]); B.memset("pool", mk4.a[64:96, 3:4], 0.0, [mk4])
        B.memset("pool", mk.a[:, 1:2], 1.0, [mk]); B.memset("pool", mk.a[0:32, 1:2], 0.0, [mk]); B.memset("pool", mk.a[64:96, 1:2], 0.0, [mk])
        Kbd = B.sb("Kbd", [128, 4, 8, 128], BF16)
        B.push()
        are = B.sb("are", [128, 16]); aim = B.sb("aim", [128, 16]); ls = B.sb("ls", [128, 16])
        for gl in range(2):
            S.dma("sp", are.a[gl * 64:(gl + 1) * 64, :], W["s5_a_re"].a.rearrange("(q gl) n -> gl n q", gl=2)[gl],
                  reads=[W["s5_a_re"]], writes=[are], allow_slow_non_contiguous=True)
            S.dma("sp", aim.a[gl * 64:(gl + 1) * 64, :], W["s5_a_im"].a.rearrange("(q gl) n -> gl n q", gl=2)[gl],
                  reads=[W["s5_a_im"]], writes=[aim], allow_slow_non_contiguous=True)
            S.dma("sp", ls.a[gl * 64:(gl + 1) * 64, :],
                  W["s5_log_step"].a.rearrange("(q gl) -> gl q", gl=2)[gl].partition_broadcast(64),
                  reads=[W["s5_log_step"]], writes=[ls], allow_slow_non_contiguous=True)
        braw = [B.sb("braw%d" % i, [128, 16, 16]) for i in range(2)]
        for i, nm in enumerate(("s5_b_re", "s5_b_im")):
            S.dma("sp", braw[i].a, W[nm].a.rearrange("(q gl) n c -> (gl n) q c", gl=2), reads=[W[nm]], writes=[braw[i]])
        craw = [B.sb("craw%d" % i, [128, 16, 16]) for i in range(2)]
        ctmp = B.sb("ctmp", [128, 128])
        for i, nm in enumerate(("s5_c_re", "s5_c_im")):
            for blk in range(2):
                src = W[nm].a.rearrange("(b qq gl) c n -> b qq c gl n", b=2, gl=2)[blk]
                for qq in range(8):
                    S.dma("sp", ctmp.a[qq * 16:(qq + 1) * 16, :].rearrange("p (gl n) -> p gl n", gl=2),
                          src[qq], reads=[W[nm]], writes=[ctmp])
                pb = B.ps()
                B.tr(pb.a[:, 0:128], ctmp.a, ident_f.a, [ctmp, ident_f], [pb])
                B.cp("dve", craw[i].a[:, blk * 8:(blk + 1) * 8, :],
                     pb.a[:, 0:128].rearrange("p (qq c) -> p qq c", c=16), [pb], [craw[i]])

        tm = B.sb("s5tmp", [128, 12, 32])
        tmr = tm.r

        def row(i, n=16):
            return tm.a[:, i, 0:n]

        dt_ = row(0)
        B.act(dt_, ls.a, AF.Exp, [ls], [tm])
        xr = row(1)
        B.tt("dve", xr, are.a, dt_, ALU.mult, [are, tm], [tm])
        rho = row(2)
        B.ts("dve", rho, xr, 1.0 / 720, ALU.mult, [tm], [tm], s2=1.0 / 120, op1=ALU.add)
        for cf in (1.0 / 24, 1.0 / 6, 0.5, 1.0, 1.0):
            B.tt("dve", rho, rho, xr, ALU.mult, [tm], [tm])
            B.ts("dve", rho, rho, cf, ALU.add, [tm], [tm])
        th2 = tm.a[:, 3, :]
        B.tt("dve", th2[:, 0:16], aim.a, dt_, ALU.mult, [aim, tm], [tm])
        B.ts("dve", th2[:, 16:32], th2[:, 0:16], PI / 2, ALU.add, [tm], [tm])
        kf = tm.a[:, 4, :]
        B.ts("dve", kf, th2, 1.0 / (2 * PI), ALU.mult, [tm], [tm])
        ki = B.sb("ki", [128, 32], I32)
        B.cp("dve", ki.a, kf, [tm], [ki])
        B.cp("dve", kf, ki.a, [ki], [tm])
        r1 = tm.a[:, 5, :]
        B.stt(r1, kf, -2 * PI, th2, ALU.mult, ALU.add, [tm], [tm])
        B.ts("dve", kf, r1, PI, ALU.is_gt, [tm], [tm], s2=-2 * PI, op1=ALU.mult)
        B.tt("dve", r1, r1, kf, ALU.add, [tm], [tm])
        B.ts("dve", kf, r1, -PI, ALU.is_lt, [tm], [tm], s2=2 * PI, op1=ALU.mult)
        B.tt("dve", r1, r1, kf, ALU.add, [tm], [tm])
        sc = tm.a[:, 6, :]
        B.act(sc, r1, AF.Sin, [tm], [tm])
        n2 = row(7)
        B.tt("dve", kf, sc, sc, ALU.mult, [tm], [tm])
        B.tt("dve", n2, kf[:, 0:16], kf[:, 16:32], ALU.add, [tm], [tm])
        B.ts("dve", n2, n2, -0.5, ALU.mult, [tm], [tm], s2=1.5, op1=ALU.add)
        B.tt("dve", n2, n2, rho, ALU.mult, [tm], [tm])
        PW = B.sb("pw", [128, 9, 2, 16])
        B.memset("dve", PW.a[:, 0, 0, :], 1.0, [PW])
        B.memset("dve", PW.a[:, 0, 1, :], 0.0, [PW])
        B.tt("dve", PW.a[:, 1, 0, :], sc[:, 16:32], n2, ALU.mult, [tm], [PW])
        B.tt("dve", PW.a[:, 1, 1, :], sc[:, 0:16], n2, ALU.mult, [tm], [PW])
        pt = B.sb("ptmp", [128, 2, 4, 16])
        for (lo, n, s) in ((2, 1, 1), (3, 2, 2), (5, 4, 4)):
            bre = PW.a[:, s:s + 1, 0, :].broadcast_to([128, n, 16])
            bim = PW.a[:, s:s + 1, 1, :].broadcast_to([128, n, 16])
            B.cmul("dve", PW.a[:, lo:lo + n, 0, :], PW.a[:, lo:lo + n, 1, :],
                   PW.a[:, lo - s:lo - s + n, 0, :], PW.a[:, lo - s:lo - s + n, 1, :], bre, bim,
                   pt.a[:, 0, 0:n, :], pt.a[:, 1, 0:n, :], [PW], [PW], pt)
        B.cp("dve", MS.a[:, :, 0, 0], PW.a[:, 8, 0, :], [PW], [MS])
        B.cp("dve", MS.a[:, :, 1, 1], PW.a[:, 8, 0, :], [PW], [MS])
        B.cp("dve", MS.a[:, :, 1, 0], PW.a[:, 8, 1, :], [PW], [MS])
        B.ts("dve", MS.a[:, :, 0, 1], PW.a[:, 8, 1, :], -1.0, ALU.mult, [PW], [MS])
        B.tt("dve", R8.a, rho, rho, ALU.mult, [tm], [R8])
        B.tt("dve", R8.a, R8.a, R8.a, ALU.mult, [R8], [R8])
        B.tt("dve", R8.a, R8.a, R8.a, ALU.mult, [R8], [R8])
        r8i = row(8)
        B.op("dve", lambda e: e.reciprocal(r8i, R8.a), [R8], [tm])
        B.tt("dve", ER.a[:, :, 0], PW.a[:, 8, 0, :], r8i, ALU.mult, [PW, tm], [ER])
        B.tt("dve", EI.a[:, :, 0], PW.a[:, 8, 1, :], r8i, ALU.mult, [PW, tm], [EI])
        et = B.sb("etmp", [128, 2, 16, 32])
        n_ = 1
        while n_ < 64:
            bre = ER.a[:, :, n_ - 1:n_].broadcast_to([128, 16, n_]); bim = EI.a[:, :, n_ - 1:n_].broadcast_to([128, 16, n_])
            B.cmul("dve", ER.a[:, :, n_:2 * n_], EI.a[:, :, n_:2 * n_], ER.a[:, :, 0:n_], EI.a[:, :, 0:n_], bre, bim,
                   et.a[:, 0, :, 0:n_], et.a[:, 1, :, 0:n_], [ER, EI], [ER, EI], et)
            n_ *= 2
        am1 = row(8); nre = row(9); nim = row(10); den = row(11); t0 = row(4); t1 = row(5)
        B.ts("dve", am1, PW.a[:, 1, 0, :], -1.0, ALU.add, [PW], [tm])
        B.tt("dve", nre, am1, are.a, ALU.mult, [tm, are], [tm])
        B.tt("dve", t0, PW.a[:, 1, 1, :], aim.a, ALU.mult, [PW, aim], [tm])
        B.tt("dve", nre, nre, t0, ALU.add, [tm], [tm])
        B.tt("dve", nim, PW.a[:, 1, 1, :], are.a, ALU.mult, [PW, are], [tm])
        B.tt("dve", t0, am1, aim.a, ALU.mult, [tm, aim], [tm])
        B.tt("dve", nim, nim, t0, ALU.subtract, [tm], [tm])
        B.tt("dve", den, are.a, are.a, ALU.mult, [are], [tm])
        B.tt("dve", t0, aim.a, aim.a, ALU.mult, [aim], [tm])
        B.tt("dve", den, den, t0, ALU.add, [tm], [tm])
        B.op("dve", lambda e: e.reciprocal(t1, den), [tm], [tm])
        B.tt("dve", nre, nre, t1, ALU.mult, [tm], [tm])
        B.tt("dve", nim, nim, t1, ALU.mult, [tm], [tm])
        bb = [B.sb("bb%d" % i, [128, 16, 16]) for i in range(2)]
        btmp = B.sb("btmp", [128, 2, 16, 16])
        cre = nre[:, :, None].broadcast_to([128, 16, 16]); cim = nim[:, :, None].broadcast_to([128, 16, 16])
        B.cmul("dve", bb[0].a, bb[1].a, cre, cim, braw[0].a, braw[1].a, btmp.a[:, 0], btmp.a[:, 1],
               [tm, braw[0], braw[1]], [bb[0], bb[1]], btmp)
        X = [B.sb("X%d" % i, [128, 16, 32]) for i in range(2)]
        Xb = [B.sb("Xb%d" % i, [128, 16, 64], BF16) for i in range(2)]
        for i in range(2):
            B.memset("pool", X[i].a, 0.0, [X[i]])
            for gl in range(2):
                B.cp("pool", X[i].a[gl * 64:(gl + 1) * 64, :, gl * 16:(gl + 1) * 16], bb[i].a[gl * 64:(gl + 1) * 64], [bb[i]], [X[i]])
            B.memset("pool", Xb[i].a, 0.0, [Xb[i]])
            for kk in range(2):
                B.cp("pool", Xb[i].a[:, kk::2, 32 * kk:32 * kk + 32], X[i].a[:, kk::2, :], [X[i]], [Xb[i]])
        B.push()
        WX = [B.sb("WX%d" % i, [128, 8, 16, 32]) for i in range(2)]
        wtmp = B.sb("wtmp", [128, 2, 8, 16, 32])
        pre = PW.a[:, 0:8, 0, :][:, :, :, None].broadcast_to([128, 8, 16, 32])
        pim = PW.a[:, 0:8, 1, :][:, :, :, None].broadcast_to([128, 8, 16, 32])
        xre = X[0].a[:, None, :, :].broadcast_to([128, 8, 16, 32])
        xim = X[1].a[:, None, :, :].broadcast_to([128, 8, 16, 32])
        B.cmul("dve", WX[0].a, WX[1].a, pre, pim, xre, xim, wtmp.a[:, 0], wtmp.a[:, 1], [PW, X[0], X[1]], [WX[0], WX[1]], wtmp)
        for tile in range(4):
            for e_ in range(8):
                pb = B.ps()
                for ri in range(2):
                    B.tr(pb.a[:, ri * 128:(ri + 1) * 128],
                         WX[ri].a[:, e_, 4 * tile:4 * tile + 4, :].rearrange("p k c -> p (k c)"), ident_f.a,
                         [WX[ri], ident_f], [pb])
                B.cp("act" if e_ % 2 else "dve", Wt.a[:, tile, e_, :, :],
                     pb.a[:, 0:256].rearrange("p (r n) -> p r n", r=2), [pb], [Wt])
        B.pop()
        B.push()
        CA = [B.sb("CA%d" % i, [128, 9, 16, 16]) for i in range(2)]
        catmp = B.sb("catmp", [128, 2, 9, 16, 16])
        pre9 = PW.a[:, :, 0, :][:, :, :, None].broadcast_to([128, 9, 16, 16])
        pim9 = PW.a[:, :, 1, :][:, :, :, None].broadcast_to([128, 9, 16, 16])
        cre9 = craw[0].a[:, None, :, :].broadcast_to([128, 9, 16, 16])
        cim9 = craw[1].a[:, None, :, :].broadcast_to([128, 9, 16, 16])
        B.cmul("dve", CA[0].a, CA[1].a, pre9, pim9, cre9, cim9, catmp.a[:, 0], catmp.a[:, 1],
               [PW, craw[0], craw[1]], [CA[0], CA[1]], catmp)
        B.memset("pool", CAW.a, 0.0, [CAW])
        for gl in range(2):
            hs = slice(gl * 64, (gl + 1) * 64)
            for tau in range(9):
                for kk in range(2):
                    o = 32 * kk + 16 * gl
                    B.cp("pool", CAW.a[hs, kk::2, 0, tau, o:o + 16], CA[0].a[hs, tau, kk::2], [CA[0]], [CAW])
                    B.ts("pool", CAW.a[hs, kk::2, 1, tau, o:o + 16], CA[1].a[hs, tau, kk::2], -1.0, ALU.mult, [CA[1]], [CAW])
        B.memset("pool", Kbd.a, 0.0, [Kbd])
        k0 = B.sb("k0", [128, 128])
        for tile in range(4):
            pb = B.ps()
            for h in range(2):
                for tau in range(8):
                    n = 0
                    for kk in range(2):
                        q = 4 * tile + 2 * h + kk
                        o = 32 * kk
                        for ri in range(2):
                            B.mm(pb.a[64 * h:64 * h + 64, tau * 32:(tau + 1) * 32], Xb[ri].a[:, q, :],
                                 CAW.a[:, q, ri, tau, o:o + 32], n == 0, n == 3, [Xb[ri], CAW], [pb])
                            n += 1
            for h in range(2):
                hs = slice(64 * h, 64 * h + 64)
                for kk in range(2):
                    cs = slice(64 * h + 32 * kk, 64 * h + 32 * kk + 32)
                    B.ts("dve", Kbd.a[hs, tile, 1:8, cs], pb.a[hs, 32:256].rearrange("p (t c) -> p t c", c=32),
                         mk.a[hs, kk:kk + 1], ALU.mult, [pb, mk], [Kbd])
            B.memset("dve", k0.a, 0.0, [k0])
            for h in range(2):
                hs = slice(64 * h, 64 * h + 64)
                for kk in range(2):
                    cs = slice(64 * h + 32 * kk, 64 * h + 32 * kk + 32)
                    B.ts("dve", k0.a[hs, cs], pb.a[hs, 0:32], mk.a[hs, kk:kk + 1], ALU.mult, [pb, mk], [k0])
            B.stt(k0.a, ident_f.a, dcol.a[:, tile:tile + 1], k0.a, ALU.mult, ALU.add, [ident_f, dcol, k0], [k0])
            B.cp("dve", Kbd.a[:, tile, 0, :], k0.a, [k0], [Kbd])
        B.pop()
        B.pop()
        for nm_, b_ in (("Wt", Wt), ("CAW", CAW), ("Kbd", Kbd), ("MS", MS)):
            if nm_ in dbo:
                S.dma("sp", dbo[nm_].a, b_.a, reads=[b_], writes=[dbo[nm_]], dreg=b_)
        NM = T1 // 8
        xt = B.sb("xt", [128, 4, D]); hT = B.sb("hT", [128, 8, T1], BF16); st = B.sb("st", [128, 12])
        hb = B.sb("hb", [128, 4, D], BF16)
        Sfin = [B.sb("Sfin%d" % h, [128, 8, 2]) for h in range(2)]
        for h in range(2):
            B.memset("pool", Sfin[h].a, 0.0, [Sfin[h]])

        class BS:
            pass
        hv = []
        for h in range(2):
            q = BS()
            q.u = B.sb("u%d" % h, [128, 2, 8, NM], BF16)
            q.um = B.sb("um%d" % h, [128, 4, 2, 8, NM], BF16)
            q.Dm = B.sb("Dm%d" % h, [128, 8, 2, NM]); q.St = B.sb("St%d" % h, [128, 8, 2, NM + 1])
            q.Sb = B.sb("Sb%d" % h, [128, 8, 2, NM], BF16)
            q.rt1 = B.sb("rt1%d" % h, [128, 8, NM]); q.rt2 = B.sb("rt2%d" % h, [128, 8, NM])
            q.Dr = B.sb("Dr%d" % h, [128, 8, 2, NM]); q.Qs = B.sb("Qs%d" % h, [128, 8, 2, NM])
            q.y2 = B.sb("y2%d" % h, [128, 2, T1]); q.yg = B.sb("yg%d" % h, [128, 2, T1])
            q.yy = B.view(q.Dm, q.Dm.a.rearrange("p a b c -> p (a b c)").rearrange("p (t n) -> p t n", t=2))
            q.ygb = B.sb("ygb%d" % h, [128, 2, T1], BF16)
            hv.append(q)
        sgb = B.sb("sgb", [128, T1]); ygl = B.sb("ygl", [128, 4, T1], BF16)
        scr = B.view(hb, hb.a.rearrange("p a b -> p (a b)")[:, 0:D])

        def tile1a(ti):
            frontend(x, ti, 4, gpre, xt, hb, hT, scr, st)
            for cb in range(4):
                q = hv[cb // 2]; tl = cb % 2
                pb = B.ps()
                for c in range(8):
                    B.mm(pb.a, ws5.a[:, c, cb * 128:(cb + 1) * 128], hT.a[:, c, :], c == 0, c == 7, [ws5, hT], [pb])
                pperm = pb.a.rearrange("p (m t) -> p t m", t=8)
                B.cp("act", q.u.a[:, tl, :, :], pperm, [pb], [q.u])
                for k in range(4):
                    B.act(q.um.a[:, k, tl, :, :], pperm, AF.Copy, [pb, mk4], [q.um], scale=mk4.a[:, k:k + 1])
            for h in range(2):
                q = hv[h]
                u, um, Dm, St, Sb, rt1, rt2, Dr, Qs, y2, yg, yy, ygb = (q.u, q.um, q.Dm, q.St, q.Sb, q.rt1, q.rt2, q.Dr, q.Qs,
                                                                        q.y2, q.yg, q.yy, q.ygb)
                sf = Sfin[h]
                for qb in range(2):
                    pb = B.ps()
                    for qq in range(4):
                        ql = 4 * qb + qq
                        tl, k = ql // 4, ql % 4
                        gt = 2 * h + tl
                        for ri in range(2):
                            col = (qq * 2 + ri) * NM
                            for j0 in range(8):
                                B.mm(pb.a[:, col:col + NM], Wt.a[:, gt, 7 - j0, ri, :], um.a[:, k, tl, j0, :],
                                     j0 == 0, j0 == 7, [Wt, um], [pb])
                    B.cp("dve", Dm.a[:, 4 * qb:4 * qb + 4, :, :],
                         pb.a.rearrange("p (q r m) -> p q r m", q=4, r=2), [pb], [Dm])
                erb = ER.a[:, 8 * h:8 * h + 8, :]; eib = EI.a[:, 8 * h:8 * h + 8, :]
                B.cp("pool", St.a[:, :, :, 0], sf.a, [sf], [St])
                B.tt("dve", rt1.a, erb, Dm.a[:, :, 0, :], ALU.mult, [ER, Dm], [rt1])
                B.tt("dve", rt2.a, eib, Dm.a[:, :, 1, :], ALU.mult, [EI, Dm], [rt2])
                B.tt("dve", Dr.a[:, :, 0, :], rt1.a, rt2.a, ALU.add, [rt1, rt2], [Dr])
                B.tt("dve", rt1.a, erb, Dm.a[:, :, 1, :], ALU.mult, [ER, Dm], [rt1])
                B.tt("dve", rt2.a, eib, Dm.a[:, :, 0, :], ALU.mult, [EI, Dm], [rt2])
                B.tt("dve", Dr.a[:, :, 1, :], rt1.a, rt2.a, ALU.subtract, [rt1, rt2], [Dr])
                for ql in range(8):
                    qi = 8 * h + ql
                    for ri in range(2):
                        B.op("dve", lambda e, ql=ql, qi=qi, ri=ri, Qs=Qs, Dr=Dr, sf=sf: e.tensor_tensor_scan(
                            out=Qs.a[:, ql, ri, :], data0=R8.a[:, qi:qi + 1].broadcast_to([128, NM]), data1=Dr.a[:, ql, ri, :],
                            initial=sf.a[:, ql, ri:ri + 1], op0=ALU.mult, op1=ALU.add), [R8, Dr, sf], [Qs], cost=0.4)
                B.tt("dve", rt1.a, erb, Qs.a[:, :, 0, :], ALU.mult, [ER, Qs], [rt1])
                B.tt("dve", rt2.a, eib, Qs.a[:, :, 1, :], ALU.mult, [EI, Qs], [rt2])
                B.tt("dve", St.a[:, :, 0, 1:NM + 1], rt1.a, rt2.a, ALU.subtract, [rt1, rt2], [St])
                B.tt("dve", rt1.a, erb, Qs.a[:, :, 1, :], ALU.mult, [ER, Qs], [rt1])
                B.tt("dve", rt2.a, eib, Qs.a[:, :, 0, :], ALU.mult, [EI, Qs], [rt2])
                B.tt("dve", St.a[:, :, 1, 1:NM + 1], rt1.a, rt2.a, ALU.add, [rt1, rt2], [St])
                B.cp("act", Sb.a, St.a[:, :, :, 0:NM], [St], [Sb])
                B.cp("act", sf.a, St.a[:, :, :, NM], [St], [sf])
                for tl in range(2):
                    gt = 2 * h + tl
                    pb = B.ps()
                    for hh in range(2):
                        hs = slice(64 * hh, 64 * hh + 64)
                        for t0_ in range(8):
                            n = 0
                            for kk in range(2):
                                ql = 4 * tl + 2 * hh + kk
                                qi = 8 * h + ql
                                for ri in range(2):
                                    B.mm(pb.a[hs, t0_ * NM:(t0_ + 1) * NM], CAW.a[:, qi, ri, t0_ + 1, :], Sb.a[:, ql, ri, :], n == 0, n == 3,
                                         [CAW, Sb], [pb], skip_group_check=True)
                                    n += 1
                    B.cp("act", y2.a[:, tl, :], pb.a, [pb], [y2])
                    pb = B.ps()
                    for t0o in range(8):
                        for tau in range(t0o + 1):
                            B.mm(pb.a[:, t0o * NM:(t0o + 1) * NM], Kbd.a[:, gt, tau, :], u.a[:, tl, t0o - tau, :],
                                 tau == 0, tau == t0o, [Kbd, u], [pb])
                    B.tt("dve", yy.a[:, tl, :].rearrange("p (m t) -> p t m", t=8), pb.a.rearrange("p (t m) -> p t m", t=8),
                         y2.a[:, tl, :].rearrange("p (t m) -> p t m", t=8), ALU.add, [pb, y2], [yy])
                if "s5y" in dbo:
                    S.dma("sp", dbo["s5y"].a[:, 2 * h:2 * h + 2, ti * T1:(ti + 1) * T1], yy.a, reads=[yy], writes=[dbo["s5y"]], dreg=yy)
                B.act(yg.a, yy.a, AF.Gelu_apprx_tanh, [yy], [yg])
                B.cp("act", ygb.a, yg.a, [yg], [ygb])
            for cb in range(4):
                pb = B.ps()
                for c in range(4):
                    B.mm(pb.a, wglu.a[:, c, cb * 128:(cb + 1) * 128], hv[c // 2].ygb.a[:, c % 2, :], c == 0, c == 3,
                         [wglu, hv[0].ygb, hv[1].ygb], [pb])
                B.act(sgb.a, pb.a, AF.Sigmoid, [pb, bglu], [sgb], bias=bglu.a[:, cb:cb + 1])
                B.tt("dve", ygl.a[:, cb, :], hv[cb // 2].yg.a[:, cb % 2, :], sgb.a, ALU.mult, [hv[cb // 2].yg, sgb], [ygl])
            S.dma("sp", YGLU.a[:, :, ti * T1:(ti + 1) * T1], ygl.a, reads=[ygl], writes=[YGLU], dreg=ygl)
            if "yglu" in dbo:
                S.dma("sp", dbo["yglu"].a[:, :, ti * T1:(ti + 1) * T1], ygl.a, reads=[ygl], writes=[dbo["yglu"]], dreg=ygl)

        S.rec = []
        for ti in range(ntiles):
            tile1a(ti)
        items = S.rec
        S.rec = None
        if os.environ.get("K_P1ASCHED", "1") == "1":
            S.schedule_emit(items, window=int(os.environ.get("K_WIN1A", "2500")))
        else:
            for it in items:
                S.replay(it)
        B.pop()

    if "1b" in phases:
        phase_1b(B, W, x, YR, epsc, gpre, ident_f, ident_b, frontend, bc_load, col_load, dbo, ntiles * (T1 // 128))

    if "1c" in phases:
        phase_1c(B, W, x, x1s, H2T, YR, YGLU, epsc, gpre, frontend, bc_load, col_load, dbo, ntiles)

    B.pop()
    if "2" in phases:
        phase_2(B, W, x1s, H2T, out, epsc, frontend, bc_load, dbo, ntiles)

    S.final_wait("sp", list(B.dout.values()))
    B.Wshapes = shapes
    return B


def phase_1b(B, W, x, YR, epsc, gpre, ident_f, ident_b, frontend, bc_load, col_load, dbo, ntiles):
    nc, S = B.nc, B.S
    TB = 128
    C = 128
    NW = 3
    C0 = math.exp(-0.5)
    B.push()
    wrg = B.sb("wrg", [128, 8, NR], BF16)
    win = W["w_in"].a.rearrange("(c p) n -> p c n", p=128)
    for c in range(8):
        S.dma("pool", wrg.a[:, c, :], win[:, c, 0:NR], reads=[W["w_in"]], writes=[wrg])
    w2p = B.sb("w2p", [128, 512], BF16); a2p = B.sb("a2p", [128, 512], BF16); g2b = B.sb("g2b", [128, 512], BF16)
    B.memset("pool", w2p.a, 0.0, [w2p]); B.memset("pool", a2p.a, 0.0, [a2p])
    S.dma("pool", w2p.a[0:64, :], W["rwkv_w2"].a, reads=[W["rwkv_w2"]], writes=[w2p])
    S.dma("pool", a2p.a[64:128, :], W["rwkv_a2"].a, reads=[W["rwkv_a2"]], writes=[a2p])
    S.dma("pool", g2b.a, W["rwkv_g2"].a, reads=[W["rwkv_g2"]], writes=[g2b])
    mu = col_load("rwkv_shift_mu", 14); w0c = col_load("rwkv_w0", 4); a0c = col_load("rwkv_a0", 4)
    kkc = col_load("rwkv_k_k", 4); kac = col_load("rwkv_k_a", 4); rkc = col_load("rwkv_r_k", 4)
    lwc = col_load("rwkv_lnx_w", 4); lbc = col_load("rwkv_lnx_b", 4)
    omu = B.sb("omu", [128, 14]); oka = B.sb("oka", [128, 4])
    B.ts("dve", omu.a, mu.a, -1.0, ALU.mult, [mu], [omu], s2=1.0, op1=ALU.add)
    B.ts("dve", oka.a, kac.a, -1.0, ALU.mult, [kac], [oka], s2=1.0, op1=ALU.add)
    bones = B.sb("bones", [128, 128]); bavg = B.sb("bavg", [128, 128]); ones = B.sb("ones", [128, 128])
    B.memset("pool", ones.a, 1.0, [ones])
    B.memset("pool", bones.a, 0.0, [bones])
    B.memset("pool", bones.a[0:64, 0:64], 1.0, [bones]); B.memset("pool", bones.a[64:128, 64:128], 1.0, [bones])
    B.ts("pool", bavg.a, bones.a, 1.0 / 64, ALU.mult, [bones], [bavg])
    hm = B.sb("hm", [128, 2])
    B.memset("pool", hm.a, 0.0, [hm]); B.memset("pool", hm.a[0:64, 0:1], 1.0, [hm]); B.memset("pool", hm.a[64:128, 1:2], 1.0, [hm])
    mf = B.sb("mf", [128, 128])
    MU4 = B.sb("MU4", [128, 4, 128], BF16); MI4 = B.sb("MI4", [128, 4, 128], BF16)
    ML4 = B.sb("ML4", [128, 4, 128], BF16); I4 = B.sb("I4", [128, 4, 128], BF16)
    for (mt, pat, cm, cop) in ((MU4, 1, -1, ALU.is_gt), (MI4, 1, -1, ALU.is_ge), (ML4, -1, 1, ALU.is_gt)):
        B.memset("pool", mf.a, 1.0, [mf])
        B.op("pool", lambda e, pat=pat, cm=cm, cop=cop: e.affine_select(out=mf.a, in_=mf.a, pattern=[[pat, 128]], compare_op=cop,
                                                                      fill=0.0, base=0, channel_multiplier=cm), [mf], [mf])
        for h in range(4):
            B.cp("pool", mt.a[:, h, :], mf.a, [mf], [mt])
    for h in range(4):
        B.cp("pool", I4.a[:, h, :], ident_f.a, [ident_f], [I4])
    pc = B.sb("pc", [128, 14]); B.memset("pool", pc.a, 0.0, [pc])
    H32 = B.sb("H32", [128, 4, 64]); Hb = B.sb("Hb", [128, 4, 64], BF16); Ht = B.sb("Ht", [128, 4, 64])
    B.memset("pool", H32.a, 0.0, [H32]); B.memset("pool", Hb.a, 0.0, [Hb])

    def bc4(col):
        return col.a[:, :, None].broadcast_to([128, 4, TB])

    class BS:
        pass

    sets = []
    for w in range(NW):
        b = BS()
        f4 = lambda nm: B.sb(nm + str(w), [128, 4, TB])
        h4 = lambda nm: B.sb(nm + str(w), [128, 4, TB], BF16)
        b.xt = B.sb("xt%d" % w, [128, 1, D]); b.hb = B.sb("hb%d" % w, [128, 1, D], BF16); b.hT = B.sb("hT%d" % w, [128, 8, TB], BF16)
        b.st = B.sb("st%d" % w, [128, 6])
        b.PS = B.sb("PS%d" % w, [128, 14, TB]); b.t1 = B.sb("t1%d" % w, [128, 4, TB]); b.t2 = B.sb("t2%d" % w, [128, 4, TB])
        b.scr = B.view(b.PS, b.PS.a.rearrange("p a b -> p (a b)")[:, 0:D])
        b.twa = B.sb("twa%d" % w, [128, TB], BF16); b.sgd = B.sb("sgd%d" % w, [128, TB], BF16)
        b.sgw = f4("sgw"); b.asg = f4("asg"); b.gg = f4("gg"); b.kk = f4("kk"); b.tq = f4("tq"); b.kmod = f4("kmod")
        b.cum = f4("cum"); b.eg = b.t1; b.egx = f4("egx"); b.eng = b.t2; b.bon = f4("bon")
        b.am = B.sb("am%d" % w, [128, 4, 2, TB], BF16); b.rm = B.sb("rm%d" % w, [128, 4, 2, TB], BF16)
        b.bf = h4("bf"); b.kf = h4("kf"); b.vb = h4("vb")
        b.Btok = B.sb("Btok%d" % w, [128, 512], BF16); b.Ktok = B.sb("Ktok%d" % w, [128, 512], BF16); b.Vtok = B.sb("Vtok%d" % w, [128, 512], BF16)
        for nm, src in (("Pm", b.sgw), ("Qm", b.asg), ("Rm", b.kk), ("Nak", b.kmod), ("Nrb", b.egx), ("Nrk", b.eng)):
            setattr(b, nm, B.view(src, src.a.rearrange("p a b -> p (a b)").bitcast(BF16).rearrange("p (h t) -> p h t", h=8)))
        hTf = b.hT.a.rearrange("p a b -> p (a b)")
        b.Xb = B.view(b.hT, hTf[:, 0:512]); b.Ub = B.view(b.hT, hTf[:, 512:1024])
        b.Yf = B.view(b.xt, b.xt.a.rearrange("p a b -> p (a b)")[:, 0:4 * TB].rearrange("p (j t) -> p j t", j=4))
        b.dd = B.view(b.hb, b.hb.a.rearrange("p a b -> p (a b)").bitcast(F32).rearrange("p (j t) -> p j t", j=4))
        b.yrb = h4("yrb")
        sets.append(b)

    def chunk(ci):
        b = sets[ci % NW]
        PS, tq, cum, kk, kmod, eg, egx, eng, asg, sgw, gg, bon = b.PS, b.tq, b.cum, b.kk, b.kmod, b.eg, b.egx, b.eng, b.asg, b.sgw, b.gg, b.bon
        am, rm, Pm, Qm, Rm, Nak, Nrb, Nrk = b.am, b.rm, b.Pm, b.Qm, b.Rm, b.Nak, b.Nrb, b.Nrk
        frontend(x, ci, 1, gpre, b.xt, b.hb, b.hT, b.scr, b.st)
        yield
        for j0 in range(0, 14, 4):
            nj = min(4, 14 - j0)
            pb = B.ps()
            for jj in range(nj):
                for c in range(8):
                    B.mm(pb.a[:, jj * TB:(jj + 1) * TB], wrg.a[:, c, (j0 + jj) * 128:(j0 + jj + 1) * 128], b.hT.a[:, c, :],
                         c == 0, c == 7, [wrg, b.hT], [pb])
            pv = pb.a[:, 0:nj * TB].rearrange("p (j t) -> p j t", j=nj)
            om_b = omu.a[:, j0:j0 + nj, None].broadcast_to([128, nj, TB])
            mu_b = mu.a[:, j0:j0 + nj, None].broadcast_to([128, nj, TB - 1])
            B.tt("dve", b.t1.a[:, 0:nj, :], pv, om_b, ALU.mult, [pb, omu], [b.t1])
            B.tt("dve", b.t2.a[:, 0:nj, 1:TB], pv[:, :, 0:TB - 1], mu_b, ALU.mult, [pb, mu], [b.t2])
            B.tt("dve", b.t2.a[:, 0:nj, 0:1], pc.a[:, j0:j0 + nj, None], mu.a[:, j0:j0 + nj, None], ALU.mult, [pc, mu], [b.t2])
            B.cp("act", pc.a[:, j0:j0 + nj, None], pv[:, :, TB - 1:TB], [pb], [pc])
            B.tt("pool", PS.a[:, j0:j0 + nj, :], b.t1.a[:, 0:nj, :], b.t2.a[:, 0:nj, :], ALU.add, [b.t1, b.t2], [PS])
            yield
        if "pshift" in dbo:
            S.dma("sp", dbo["pshift"].a[:, :, ci * TB:(ci + 1) * TB], PS.a, reads=[PS], writes=[dbo["pshift"]], dreg=PS)
        r_ = PS.a[:, 0:4, :]; k_ = PS.a[:, 4:8, :]; v_ = PS.a[:, 8:12, :]
        B.act(b.twa.a[0:64, :], PS.a[0:64, 12, :], AF.Tanh, [PS], [b.twa])
        B.cp("act", b.twa.a[64:128, :], PS.a[64:128, 12, :], [PS], [b.twa])
        B.act(b.sgd.a, PS.a[:, 13, :], AF.Sigmoid, [PS], [b.sgd])
        pw_ = B.ps()
        for j in range(4):
            B.mm(pw_.a[:, j * TB:(j + 1) * TB], w2p.a[:, j * 128:(j + 1) * 128], b.twa.a, True, True, [w2p, b.twa], [pw_])
        for j in range(4):
            B.act(sgw.a[:, j, :], pw_.a[:, j * TB:(j + 1) * TB], AF.Sigmoid, [pw_, w0c], [sgw], bias=w0c.a[:, j:j + 1])
        pa_ = B.ps()
        for j in range(4):
            B.mm(pa_.a[:, j * TB:(j + 1) * TB], a2p.a[:, j * 128:(j + 1) * 128], b.twa.a, True, True, [a2p, b.twa], [pa_])
        for j in range(4):
            B.act(asg.a[:, j, :], pa_.a[:, j * TB:(j + 1) * TB], AF.Sigmoid, [pa_, a0c], [asg], bias=a0c.a[:, j:j + 1])
        pg_ = B.ps()
        for j in range(4):
            B.mm(pg_.a[:, j * TB:(j + 1) * TB], g2b.a[:, j * 128:(j + 1) * 128], b.sgd.a, True, True, [g2b, b.sgd], [pg_])
        B.cp("act", gg.a, pg_.a.rearrange("p (j t) -> p j t", j=4), [pg_], [gg])
        yield
        B.tt("dve", kk.a, k_, bc4(kkc), ALU.mult, [PS, kkc], [kk])
        B.tt("pool", tq.a, kk.a, kk.a, ALU.mult, [kk], [tq])
        pb = B.ps()
        for j in range(4):
            B.mm(pb.a[:, j * TB:(j + 1) * TB], bones.a, tq.a[:, j, :], True, True, [bones, tq], [pb])
        B.act(cum.a, pb.a.rearrange("p (j t) -> p j t", j=4), AF.Ln, [pb, epsc], [cum], bias=epsc.a[:, 1:2])
        B.act(cum.a, cum.a, AF.Exp, [cum], [cum], scale=-0.5)
        B.tt("pool", kk.a, kk.a, cum.a, ALU.mult, [kk, cum], [kk])
        yield
        B.tt("dve", tq.a, asg.a, bc4(kac), ALU.mult, [asg, kac], [tq])
        B.tt("dve", tq.a, tq.a, bc4(oka), ALU.add, [tq, oka], [tq])
        B.tt("dve", kmod.a, k_, tq.a, ALU.mult, [PS, tq], [kmod])
        B.tt("pool", tq.a, r_, kmod.a, ALU.mult, [PS, kmod], [tq])
        B.tt("dve", tq.a, tq.a, bc4(rkc), ALU.mult, [tq, rkc], [tq])
        pbon = B.ps()
        for j in range(4):
            B.mm(pbon.a[:, j * TB:(j + 1) * TB], bones.a, tq.a[:, j, :], True, True, [bones, tq], [pbon])
        B.tt("dve", bon.a, pbon.a.rearrange("p (j t) -> p j t", j=4), v_, ALU.mult, [pbon, PS], [bon])
        yield
        for j in range(4):
            B.op("dve", lambda e, j=j: e.tensor_tensor_scan(out=cum.a[:, j, :], data0=ones.a, data1=sgw.a[:, j, :], initial=0.0,
                                                            op0=ALU.mult, op1=ALU.add), [ones, sgw], [cum])
        B.act(eg.a, cum.a, AF.Exp, [cum], [eg], scale=-C0)
        B.act(eng.a, cum.a, AF.Exp, [cum], [eng], scale=C0)
        B.tt("pool", tq.a, cum.a, sgw.a, ALU.subtract, [cum, sgw], [tq])
        B.act(egx.a, tq.a, AF.Exp, [tq], [egx], scale=-C0)
        yield
        B.stt(tq.a, kk.a, -1.0, egx.a, ALU.mult, ALU.mult, [kk, egx], [tq])
        for hh in range(2):
            B.act(am.a[:, :, hh, :], tq.a, AF.Copy, [tq, hm], [am], scale=hm.a[:, hh:hh + 1])
        B.tt("dve", egx.a, r_, eg.a, ALU.mult, [PS, eg], [egx])
        for hh in range(2):
            B.act(rm.a[:, :, hh, :], egx.a, AF.Copy, [egx, hm], [rm], scale=hm.a[:, hh:hh + 1])
        B.tt("pool", tq.a, kk.a, asg.a, ALU.mult, [kk, asg], [tq])
        B.tt("dve", b.bf.a, tq.a, eng.a, ALU.mult, [tq, eng], [b.bf])
        B.tt("dve", b.kf.a, kmod.a, eng.a, ALU.mult, [kmod, eng], [b.kf])
        B.cp("act", b.vb.a, v_, [PS], [b.vb])
        yield
        cs = slice(0, C)
        for n_, (src, dst) in enumerate(((b.bf, b.Btok), (b.kf, b.Ktok), (b.vb, b.Vtok))):
            pb = B.ps()
            pv = pb.a.bitcast(BF16)
            for j in range(4):
                B.tr(pv[:, j * 128:(j + 1) * 128], src.a[:, j, cs], ident_b.a, [src, ident_b], [pb])
            B.cp("act" if n_ != 1 else "dve", dst.a, pv[:, 0:512], [pb], [dst])
        yield
        for g in range(2):
            gs = slice(4 * g, 4 * g + 4)
            for kind in range(5):
                pbk = B.ps()
                for hq in range(4):
                    h = 4 * g + hq
                    j, hh = h // 2, h % 2
                    o = slice(hq * 128, (hq + 1) * 128)
                    bfs, kfs, ams, rms = b.bf.a[:, j, cs], b.kf.a[:, j, cs], am.a[:, j, hh, cs], rm.a[:, j, hh, cs]
                    lhsT, rhs, rd = ((bfs, ams, [b.bf, am]), (ams, bfs, [b.bf, am]), (kfs, ams, [b.kf, am]),
                                     (bfs, rms, [b.bf, rm]), (kfs, rms, [b.kf, rm]))[kind]
                    B.mm(pbk.a[:, o], lhsT, rhs, True, True, rd, [pbk])
                msk, dst = ((MU4, Pm), (ML4, Qm), (MU4, Nak), (MI4, Nrb), (MI4, Nrk))[kind]
                B.tt("dve", dst.a[:, gs, :], pbk.a.rearrange("p (h t) -> p h t", h=4), msk.a, ALU.mult, [pbk, msk], [dst])
            B.tt("pool", Rm.a[:, gs, :], Pm.a[:, gs, :], I4.a, ALU.add, [Pm, I4], [Rm])
            yield
        for lvl in range(1, 7):
            for g in range(2):
                gs = slice(4 * g, 4 * g + 4)
                pq = B.ps()
                pp = B.ps() if lvl < 6 else None
                for hq in range(4):
                    h = 4 * g + hq
                    o = slice(hq * 128, (hq + 1) * 128)
                    B.mm(pq.a[:, o], Pm.a[:, h, :], Qm.a[:, h, :], True, True, [Pm, Qm], [pq])
                    if pp is not None:
                        B.mm(pp.a[:, o], Qm.a[:, h, :], Pm.a[:, h, :], True, True, [Pm, Qm], [pp])
                B.cp("act", Qm.a[:, gs, :], pq.a.rearrange("p (h t) -> p h t", h=4), [pq], [Qm])
                if pp is not None:
                    B.cp("act", Pm.a[:, gs, :], pp.a.rearrange("p (h t) -> p h t", h=4), [pp], [Pm])
                pr = B.ps()
                for hq in range(4):
                    h = 4 * g + hq
                    o = slice(hq * 128, (hq + 1) * 128)
                    B.mm(pr.a[:, o], Qm.a[:, h, :], Rm.a[:, h, :], True, True, [Qm, Rm], [pr])
                B.tt("dve", Rm.a[:, gs, :], pr.a.rearrange("p (h t) -> p h t", h=4), Rm.a[:, gs, :], ALU.add, [pr, Rm], [Rm])
                yield
        px = B.ps()
        for h in range(8):
            j, hh = h // 2, h % 2
            o = slice(h * 64, (h + 1) * 64)
            B.mm(px.a[:, o], am.a[:, j, hh, cs], Hb.a[:, j, :], True, False, [am, Hb], [px])
            B.mm(px.a[:, o], Nak.a[:, h, :], b.Vtok.a[:, o], False, True, [Nak, b.Vtok], [px])
        B.cp("act", b.Xb.a, px.a, [px], [b.Xb])
        pu = B.ps()
        for h in range(8):
            o = slice(h * 64, (h + 1) * 64)
            B.mm(pu.a[:, o], Rm.a[:, h, :], b.Xb.a[:, o], True, True, [Rm, b.Xb], [pu])
        B.cp("act", b.Ub.a, pu.a, [pu], [b.Ub])
        ph = B.ps()
        for h in range(8):
            j, hh = h // 2, h % 2
            o = slice(h * 64, (h + 1) * 64)
            ho = ph.a[hh * 64:(hh + 1) * 64, j * 64:(j + 1) * 64]
            B.mm(ho, b.Btok.a[:, o], b.Ub.a[:, o], True, False, [b.Btok, b.Ub], [ph])
            B.mm(ho, b.Ktok.a[:, o], b.Vtok.a[:, o], False, True, [b.Ktok, b.Vtok], [ph])
        py = B.ps()
        for h in range(8):
            j, hh = h // 2, h % 2
            o = slice(h * 64, (h + 1) * 64)
            yo = py.a[hh * 64:(hh + 1) * 64, j * 128:(j + 1) * 128]
            B.mm(yo, Hb.a[:, j, :], rm.a[:, j, hh, cs], True, False, [Hb, rm], [py])
            B.mm(yo, b.Ub.a[:, o], Nrb.a[:, h, :], False, False, [b.Ub, Nrb], [py])
            B.mm(yo, b.Vtok.a[:, o], Nrk.a[:, h, :], False, True, [b.Vtok, Nrk], [py])
        B.tt("dve", Ht.a, ph.a[:, 0:256].rearrange("p (j i) -> p j i", j=4), H32.a, ALU.add, [ph, H32], [Ht])
        gC = eg.a[:, :, C - 1:C].broadcast_to([128, 4, 64])
        B.tt("dve", H32.a, Ht.a, gC, ALU.mult, [Ht, eg], [H32])
        B.cp("act", Hb.a, H32.a, [H32], [Hb])
        B.cp("act", b.Yf.a, py.a.rearrange("p (j t) -> p j t", j=4), [py], [b.Yf])
        yield
        if "wkv" in dbo:
            S.dma("sp", dbo["wkv"].a[:, :, ci * TB:(ci + 1) * TB], b.Yf.a, reads=[b.Yf], writes=[dbo["wkv"]], dreg=b.Yf)
        Yf, dd = b.Yf, b.dd
        pm_ = B.ps()
        for j in range(4):
            B.mm(pm_.a[:, j * TB:(j + 1) * TB], bavg.a, Yf.a[:, j, :], True, True, [bavg, Yf], [pm_])
        B.tt("dve", dd.a, Yf.a, pm_.a.rearrange("p (j t) -> p j t", j=4), ALU.subtract, [Yf, pm_], [dd])
        B.act(tq.a, dd.a, AF.Square, [dd], [tq])
        pv_ = B.ps()
        for j in range(4):
            B.mm(pv_.a[:, j * TB:(j + 1) * TB], bavg.a, tq.a[:, j, :], True, True, [bavg, tq], [pv_])
        B.act(cum.a, pv_.a.rearrange("p (j t) -> p j t", j=4), AF.Ln, [pv_, epsc], [cum], bias=epsc.a[:, 2:3])
        B.act(cum.a, cum.a, AF.Exp, [cum], [cum], scale=-0.5)
        yield
        B.tt("pool", dd.a, dd.a, cum.a, ALU.mult, [dd, cum], [dd])
        B.tt("dve", dd.a, dd.a, bc4(lwc), ALU.mult, [dd, lwc], [dd])
        B.tt("dve", dd.a, dd.a, bc4(lbc), ALU.add, [dd, lbc], [dd])
        B.tt("pool", dd.a, dd.a, bon.a, ALU.add, [dd, bon], [dd])
        B.tt("dve", b.yrb.a, dd.a, gg.a, ALU.mult, [dd, gg], [b.yrb])
        if "rwkv_y" in dbo:
            S.dma("sp", dbo["rwkv_y"].a[:, :, ci * TB:(ci + 1) * TB], b.yrb.a, reads=[b.yrb], writes=[dbo["rwkv_y"]], dreg=b.yrb)
        S.dma("sp", YR.a[:, :, ci * TB:(ci + 1) * TB], b.yrb.a, reads=[b.yrb], writes=[YR], dreg=b.yrb)
        yield

    pools = [[0, 1, 2], [3, 4, 5], [6, 7]] if NW == 3 else ([[0, 1, 2, 3], [4, 5, 6, 7]] if NW == 2 else [list(range(8))])
    S.rec = []
    for ci in range(ntiles):
        B.ps_pool = pools[ci % NW]
        for _ in chunk(ci):
            pass
    items = S.rec
    S.rec = None
    B.ps_pool = None
    S.schedule_emit(items, window=int(os.environ.get("K_WIN", "1400")))
    B.pop()


def phase_1c(B, W, x, x1s, H2T, YR, YGLU, epsc, gpre, frontend, bc_load, col_load, dbo, ntiles):
    nc, S = B.nc, B.S
    TB = 512
    NS = TB // 128
    B.push()
    wg = B.sb("wg", [128, 8, 2048], BF16)
    win = W["w_in"].a.rearrange("(c p) n -> p c n", p=128)
    for c in range(8):
        S.dma("pool", wg.a[:, c, :], win[:, c, NR + 512:NIN], reads=[W["w_in"]], writes=[wg])
    wbr = B.sb("wbr", [128, 4, D], BF16); wbs = B.sb("wbs", [128, 4, D], BF16); wout = B.sb("wout", [128, 8, D], BF16)
    S.dma("pool", wbr.a, W["w_branch_rwkv"].a.rearrange("(c p) n -> p c n", p=128), reads=[W["w_branch_rwkv"]], writes=[wbr])
    S.dma("pool", wbs.a, W["w_branch_s5"].a.rearrange("(c p) n -> p c n", p=128), reads=[W["w_branch_s5"]], writes=[wbs])
    for c in range(8):
        S.dma("pool", wout.a[:, c, :], W["w_out"].a[c * 128:(c + 1) * 128, :], reads=[W["w_out"]], writes=[wout])
    gpost = bc_load("norm_mix_post", D)
    gffn = bc_load("norm_ffn_pre", D)
    bgc = col_load("b_gate", 16)
    class BS:
        pass
    sets = []
    for w in range(2):
        q = BS()
        q.xt = B.sb("xt%d" % w, [128, NS, D]); q.hb = B.sb("hb%d" % w, [128, NS, D], BF16); q.hT = B.sb("hT%d" % w, [128, 8, TB], BF16)
        q.scr = B.sb("scr%d" % w, [128, D]); q.st = B.sb("st%d" % w, [128, 12])
        q.yrt = B.sb("yrt%d" % w, [128, 4, TB], BF16); q.ygt = B.sb("ygt%d" % w, [128, 4, TB], BF16)
        q.mixb = B.sb("mixb%d" % w, [128, 8, TB], BF16)
        sets.append(q)
    gA = B.sb("gA", [128, TB]); gB = B.sb("gB", [128, TB]); mt1 = B.sb("mt1", [128, TB]); mt2 = B.sb("mt2", [128, TB])

    def part_a(ti):
        q = sets[ti % 2]
        frontend(x, ti, NS, gpre, q.xt, q.hb, q.hT, q.scr, q.st)
        S.dma("sp", q.yrt.a, YR.a[:, :, ti * TB:(ti + 1) * TB], reads=[YR], writes=[q.yrt])
        S.dma("sp", q.ygt.a, YGLU.a[:, :, ti * TB:(ti + 1) * TB], reads=[YGLU], writes=[q.ygt])

    def part_b(ti):
        q = sets[ti % 2]
        xt, hb, hT, scr, st, yrt, ygt, mixb = q.xt, q.hb, q.hT, q.scr, q.st, q.yrt, q.ygt, q.mixb
        for cb in range(8):
            pa = B.ps(); pbb = B.ps(); po = B.ps(); ps_ = B.ps()
            for c in range(8):
                B.mm(pa.a, wg.a[:, c, cb * 128:(cb + 1) * 128], hT.a[:, c, :], c == 0, c == 7, [wg, hT], [pa])
            for c in range(8):
                B.mm(pbb.a, wg.a[:, c, (8 + cb) * 128:(9 + cb) * 128], hT.a[:, c, :], c == 0, c == 7, [wg, hT], [pbb])
            for j in range(4):
                B.mm(po.a, wbr.a[:, j, cb * 128:(cb + 1) * 128], yrt.a[:, j, :], j == 0, j == 3, [wbr, yrt], [po])
            for j in range(4):
                B.mm(ps_.a, wbs.a[:, j, cb * 128:(cb + 1) * 128], ygt.a[:, j, :], j == 0, j == 3, [wbs, ygt], [ps_])
            B.act(gA.a, pa.a, AF.Sigmoid, [pa, bgc], [gA], bias=bgc.a[:, cb:cb + 1])
            B.act(gB.a, pbb.a, AF.Sigmoid, [pbb, bgc], [gB], bias=bgc.a[:, 8 + cb:9 + cb])
            B.tt("dve", mt1.a, po.a, gA.a, ALU.mult, [po, gA], [mt1])
            B.tt("dve", mt2.a, ps_.a, gB.a, ALU.mult, [ps_, gB], [mt2])
            B.tt("pool", mixb.a[:, cb, :], mt1.a, mt2.a, ALU.add, [mt1, mt2], [mixb])

    def part_c(ti):
        q = sets[ti % 2]
        xt, hb, hT, scr, st, yrt, ygt, mixb = q.xt, q.hb, q.hT, q.scr, q.st, q.yrt, q.ygt, q.mixb
        for s_ in range(NS):
            pbs = [B.ps(), B.ps()]
            for half in range(2):
                for c8 in range(8):
                    B.mm(pbs[half].a, mixb.a[:, c8, s_ * 128:(s_ + 1) * 128], wout.a[:, c8, half * 512:(half + 1) * 512],
                         c8 == 0, c8 == 7, [mixb, wout], [pbs[half]])
            for half in range(2):
                B.act(scr.a[:, 0:512], pbs[half].a, AF.Square, [pbs[half]], [scr, st], accum_out=st.a[:, half:half + 1])
            B.tt("dve", st.a[:, 2:3], st.a[:, 0:1], st.a[:, 1:2], ALU.add, [st], [st])
            B.act(st.a[:, 2:3], st.a[:, 2:3], AF.Ln, [st, epsc], [st], scale=1.0 / D, bias=epsc.a[:, 0:1])
            B.act(st.a[:, 3:4], st.a[:, 2:3], AF.Exp, [st], [st], scale=-0.5)
            for half in range(2):
                hsl = slice(half * 512, (half + 1) * 512)
                B.stt(scr.a[:, hsl], pbs[half].a, st.a[:, 3:4], gpost.a[:, hsl], ALU.mult, ALU.mult, [pbs[half], st, gpost], [scr])
            B.tt("pool", xt.a[:, s_, :], xt.a[:, s_, :], scr.a, ALU.add, [xt, scr], [xt])
        S.dma("sp", x1s.a[ti * TB:(ti + 1) * TB, :].rearrange("(s p) d -> p s d", p=128), xt.a, reads=[xt], writes=[x1s], dreg=xt)
        if "x1" in dbo:
            S.dma("sp", dbo["x1"].a[ti * TB:(ti + 1) * TB, :].rearrange("(s p) d -> p s d", p=128), xt.a, reads=[xt],
                  writes=[dbo["x1"]], dreg=xt)
        frontend(None, ti, NS, gffn, xt, hb, hT, scr, st, load=False)
        S.dma("sp", H2T.a[:, :, ti * TB:(ti + 1) * TB], hT.a, reads=[hT], writes=[H2T], dreg=hT)

    part_a(0)
    for ti in range(ntiles):
        part_b(ti)
        if ti + 1 < ntiles:
            part_a(ti + 1)
        part_c(ti)
    B.pop()


def phase_2(B, W, x1s, H2T, out, epsc, frontend, bc_load, dbo, ntiles):
    nc, S = B.nc, B.S
    TB = 512
    NS = TB // 128
    ntiles = ntiles * (512 // TB)
    B.push()
    wup = B.sb("wup", [128, 8, 2 * FF], BF16)
    wsrc = W["ffn_w_up"].a.rearrange("(c p) n -> p c n", p=128)
    for c in range(8):
        for (a, b) in ((0, 2048), (2048, 4096), (4096, 2 * FF)):
            S.dma("pool", wup.a[:, c, a:b], wsrc[:, c, a:b], reads=[W["ffn_w_up"]], writes=[wup])
    wdn = B.sb("wdn", [128, 22, D], BF16)
    for i in range(22):
        S.dma("pool", wdn.a[:, i, :], W["ffn_w_down"].a[i * 128:(i + 1) * 128, :], reads=[W["ffn_w_down"]], writes=[wdn])
    g2 = bc_load("norm_ffn_post", D)
    cw = B.sb("cw", [128, 3, 44]); cbias = B.sb("cbias", [128, 44])
    S.dma("sp", cw.a, W["ffn_conv_w"].a.rearrange("j (b p) -> p j b", p=128), reads=[W["ffn_conv_w"]], writes=[cw],
          allow_slow_non_contiguous=True)
    S.dma("sp", cbias.a, W["ffn_conv_b"].a.rearrange("(b p) -> p b", p=128), reads=[W["ffn_conv_b"]], writes=[cbias],
          allow_slow_non_contiguous=True)
    halo = B.sb("halo", [128, 44, 2]); B.memset("pool", halo.a, 0.0, [halo])
    epsc = B.sb("epsc2", [128, 1]); B.memset("pool", epsc.a, 1e-6, [epsc])
    class BS:
        pass
    sets = []
    hTs = [B.sb("hT2_%d" % w, [128, 8, TB], BF16) for w in range(2)]
    for w in range(1):
        q = BS()
        q.xt = B.sb("xt2_%d" % w, [128, NS, D])
        q.actb = B.sb("actb%d" % w, [128, 22, TB], BF16)
        q.st = B.sb("st2_%d" % w, [128, 12])
        sets.append(q)
    scrF_ = B.sb("scrF", [128, D])
    a0s_ = (B.sb("a0g", [128, TB]), B.sb("a0v", [128, TB]))
    a1s_ = ((B.sb("a1g0", [128, TB]), B.sb("a1v0", [128, TB])), (B.sb("a1g1", [128, TB]), B.sb("a1v1", [128, TB])))
    for q in sets:
        q.scrF, q.a0s, q.a1s = scrF_, a0s_, a1s_
    def tile(ti):
        q = sets[0]
        xt, actb, st, scrF, a0s, a1s = q.xt, q.actb, q.st, q.scrF, q.a0s, q.a1s
        hT = hTs[ti % 2]
        if ti == 0:
            S.dma("sp", hT.a, H2T.a[:, :, 0:TB], reads=[H2T], writes=[hT])
        if ti + 1 < ntiles:
            S.dma("sp", hTs[(ti + 1) % 2].a, H2T.a[:, :, (ti + 1) * TB:(ti + 2) * TB], reads=[H2T], writes=[hTs[(ti + 1) % 2]])
        S.dma("sp", xt.a, x1s.a[ti * TB:(ti + 1) * TB, :].rearrange("(s p) d -> p s d", p=128), reads=[x1s], writes=[xt])
        def finish(i):
            accg, accv = a1s[i % 2]
            B.act(accg.a, accg.a, AF.Gelu_apprx_tanh, [accg], [accg])
            B.tt("pool", actb.a[:, i, :], accg.a, accv.a, ALU.mult, [accg, accv], [actb])

        for i in range(22):
            accs = a1s[i % 2]
            for gv in range(2):
                b = i + 22 * gv
                pb = B.ps()
                for c in range(8):
                    B.mm(pb.a[:, 0:TB], wup.a[:, c, b * 128:(b + 1) * 128], hT.a[:, c, :], c == 0, c == 7, [wup, hT], [pb])
                acc = accs[gv]; a0 = a0s[gv]; a1 = accs[gv]
                B.act(a0.a, pb.a[:, 0:TB], AF.Identity, [pb, cw, cbias], [a0], scale=cw.a[:, 2, b:b + 1], bias=cbias.a[:, b:b + 1])
                B.act(a1.a[:, 1:TB], pb.a[:, 0:TB - 1], AF.Copy, [pb, cw], [a1], scale=cw.a[:, 1, b:b + 1])
                B.stt(a0.a[:, 2:TB], pb.a[:, 0:TB - 2], cw.a[:, 0, b:b + 1], a0.a[:, 2:TB], ALU.mult, ALU.add, [pb, cw, a0], [a0])
                B.ts("dve", a1.a[:, 0:1], halo.a[:, b, 1:2], cw.a[:, 1, b:b + 1], ALU.mult, [halo, cw], [a1])
                B.stt(a0.a[:, 0:2], halo.a[:, b, 0:2], cw.a[:, 0, b:b + 1], a0.a[:, 0:2], ALU.mult, ALU.add, [halo, cw, a0], [a0])
                B.cp("dve", halo.a[:, b, :], pb.a[:, TB - 2:TB], [pb], [halo])
                B.tt("pool", acc.a, a0.a, a1.a, ALU.add, [a0, a1], [acc])
            if "zc" in dbo and ti == 0 and i == 0:
                S.dma("sp", dbo["zc"].a[:, 0:TB], accs[0].a, reads=[accs[0]], writes=[dbo["zc"]], dreg=accs[0])
            if i > 0:
                finish(i - 1)
        finish(21)
        for s_ in range(NS):
            pbs = [B.ps(), B.ps()]
            for half in range(2):
                for i in range(22):
                    B.mm(pbs[half].a, actb.a[:, i, s_ * 128:(s_ + 1) * 128], wdn.a[:, i, half * 512:(half + 1) * 512],
                         i == 0, i == 21, [actb, wdn], [pbs[half]])
            for half in range(2):
                B.act(scrF.a[:, 0:512], pbs[half].a, AF.Square, [pbs[half]], [scrF, st], accum_out=st.a[:, half:half + 1])
            B.tt("dve", st.a[:, 2:3], st.a[:, 0:1], st.a[:, 1:2], ALU.add, [st], [st])
            B.act(st.a[:, 2:3], st.a[:, 2:3], AF.Ln, [st, epsc], [st], scale=1.0 / D, bias=epsc.a[:, 0:1])
            B.act(st.a[:, 3:4], st.a[:, 2:3], AF.Exp, [st], [st], scale=-0.5)
            for half in range(2):
                hsl = slice(half * 512, (half + 1) * 512)
                B.stt(scrF.a[:, hsl], pbs[half].a, st.a[:, 3:4], g2.a[:, hsl], ALU.mult, ALU.mult, [pbs[half], st, g2], [scrF])
            B.tt("pool", xt.a[:, s_, :], xt.a[:, s_, :], scrF.a, ALU.add, [xt, scrF], [xt])
        S.dma("sp", out.a[ti * TB:(ti + 1) * TB, :].rearrange("(s p) d -> p s d", p=128), xt.a, reads=[xt], writes=[out], dreg=xt)

    S.rec = []
    for ti in range(ntiles):
        tile(ti)
    items = S.rec
    S.rec = None
    if os.environ.get("K_P2SCHED", "0") == "1":
        S.schedule_emit(items, window=int(os.environ.get("K_WIN2", "1500")))
    else:
        for it in items:
            S.replay(it)
    B.pop()


_CACHE = {}


def kernel(**inputs):
    if "B" not in _CACHE:
        _CACHE["B"] = build()
    Bd = _CACHE["B"]
    x = np.ascontiguousarray(inputs["x"], dtype=np.float32)
    wmap = {k: np.ascontiguousarray(np.asarray(inputs[k], dtype=np.float32).reshape(shp)) for k, shp in Bd.Wshapes.items()}
    in_maps = []
    for c in range(8):
        m = dict(wmap)
        m["x"] = x[c]
        in_maps.append(m)
    res = run_bass_kernel_spmd(Bd.nc, in_maps, core_ids=list(range(8)))
    return np.stack([np.asarray(res.results[c]["out"], dtype=np.float32) for c in range(8)], axis=0)
```

```python
import contextlib
import math
import os
import numpy as np
import concourse.bass as bass
import concourse.mybir as mybir
from concourse.bass_utils import run_bass_kernel_spmd

F32 = mybir.dt.float32
BF16 = mybir.dt.bfloat16
I32 = mybir.dt.int32
ALU = mybir.AluOpType
AF = mybir.ActivationFunctionType

L = 4096
D = 1024
NR = 1792
NS5 = 512
NIN = 4352
FF = 2816
PI = math.pi


class Reg:
    __slots__ = ("name", "w", "r", "dsem", "dcnt", "last", "excl")

    def __init__(self, name=""):
        self.name = name
        self.w = None
        self.r = []
        self.dsem = None
        self.dcnt = 0
        self.last = 0
        self.excl = False


class Sched:
    def __init__(self, nc):
        self.nc = nc
        self.eng = {"pe": nc.tensor, "dve": nc.vector, "act": nc.scalar,
                    "pool": nc.gpsimd, "sp": nc.sync}
        self.sem = {k: nc.alloc_semaphore(name="sem_" + k) for k in self.eng}
        self.cnt = {k: 0 for k in self.eng}
        self.seen = {k: {} for k in self.eng}
        self.ninst = 0
        self.nds = 0
        self.dregs = []
        self.dmap = {}
        self.maxops = int(os.environ.get("K_MAXOPS", "100000000"))
        self.rec = None

    def _wait(self, e, tok):
        sem, val = tok
        key = sem.name
        if key in self.dmap:
            val = max(val, self.dmap[key].dcnt)
        if self.seen[e].get(key, 0) >= val:
            return
        if e == "pe" and sem is self.sem["pe"]:
            return
        self.eng[e].wait_ge(sem, val)
        self.seen[e][key] = val

    def _deps(self, e, reads, writes, skip=None):
        for r in reads:
            if r.w is not None:
                self._wait(e, r.w)
        for w in writes:
            if w.w is not None and w.w[0] is not skip:
                self._wait(e, w.w)
            for t in w.r:
                self._wait(e, t)

    def _commit(self, tok, reads, writes):
        for r in reads:
            r.last = self.ninst
            r.r.append(tok)
            if len(r.r) > 16:
                d = {}
                for s, v in r.r:
                    if d.get(s.name, (None, -1))[1] < v:
                        d[s.name] = (s, v)
                r.r = list(d.values())
        for w in writes:
            w.last = self.ninst
            w.w = tok
            w.r = []

    def op(self, e, fn, reads=(), writes=(), cost=None):
        if self.rec is not None:
            self.rec.append(("op", e, fn, list(reads), list(writes), None, cost))
            return None
        if self.ninst >= self.maxops:
            return None
        reads = [x.r if isinstance(x, Buf) else x for x in reads]
        writes = [x.r if isinstance(x, Buf) else x for x in writes]
        ex = [x for x in reads if x.excl and x not in writes]
        if ex:
            reads = [x for x in reads if not x.excl]
            writes = list(writes) + ex
        self._deps(e, reads, writes)
        ins = fn(self.eng[e])
        self.cnt[e] += 1
        ins.then_inc(self.sem[e], 1)
        tok = (self.sem[e], self.cnt[e])
        self._commit(tok, reads, writes)
        self.ninst += 1
        return tok

    def dma(self, e, out, in_, reads=(), writes=(), dreg=None, **kw):
        if self.rec is not None:
            self.rec.append(("dma", e, (out, in_), list(reads), list(writes), (dreg, kw), None))
            return None
        if self.ninst >= self.maxops and not kw.pop("force", False):
            return None
        kw.pop("force", None)
        reads = [x.r if isinstance(x, Buf) else x for x in reads]
        writes = [x.r if isinstance(x, Buf) else x for x in writes]
        if dreg is None:
            dreg = writes[0] if writes else reads[0]
        elif isinstance(dreg, Buf):
            dreg = dreg.r
        if dreg.dsem is None:
            self.nds += 1
            dreg.dsem = self.nc.alloc_semaphore(name="ds%d_%s" % (self.nds, dreg.name))
            self.dregs.append(dreg)
            self.dmap[dreg.dsem.name] = dreg
        self._deps(e, reads, writes, skip=dreg.dsem)
        ins = self.eng[e].dma_start(out=out, in_=in_, **kw)
        dreg.dcnt += 16
        ins.then_inc(dreg.dsem, 16)
        tok = (dreg.dsem, dreg.dcnt)
        self._commit(tok, reads, writes)
        self.ninst += 1
        return tok

    def replay(self, item):
        kind, e, a, reads, writes, extra = item[:6]
        if kind == "op":
            return self.op(e, a, reads, writes)
        dreg, kw = extra
        return self.dma(e, a[0], a[1], reads=reads, writes=writes, dreg=dreg, **kw)

    def schedule_emit(self, items, window=1500):
        import heapq
        n = len(items)
        norm = []
        for it in items:
            kind, e, a, reads, writes, extra, cost = it
            reads = [x.r if isinstance(x, Buf) else x for x in reads]
            writes = [x.r if isinstance(x, Buf) else x for x in writes]
            dreg = None
            if kind == "dma":
                dreg = extra[0]
                if dreg is None:
                    dreg = writes[0] if writes else reads[0]
                elif isinstance(dreg, Buf):
                    dreg = dreg.r
            ex = [x for x in reads if x.excl and x not in writes]
            if ex:
                reads = [x for x in reads if not x.excl]
                writes = list(writes) + ex
            norm.append((kind, e, a, reads, writes, extra, cost, dreg))
        lw = {}
        rd = {}
        first = [[] for _ in range(n)]
        preds = [set() for _ in range(n)]
        for i, (kind, e, a, reads, writes, extra, cost, dreg) in enumerate(norm):
            for r in reads:
                k = id(r)
                if k in lw:
                    preds[i].add(lw[k])
                else:
                    first[i].append((r, "r"))
            for w in writes:
                k = id(w)
                if k in lw:
                    if not (kind == "dma" and norm[lw[k]][0] == "dma" and norm[lw[k]][7] is dreg):
                        preds[i].add(lw[k])
                else:
                    first[i].append((w, "w"))
                for j in rd.get(k, ()):
                    preds[i].add(j)
            for r in reads:
                rd.setdefault(id(r), []).append(i)
            for w in writes:
                lw[id(w)] = i
                rd[id(w)] = []
            preds[i].discard(i)
        succs = [[] for _ in range(n)]
        npred = [len(p) for p in preds]
        for i, p in enumerate(preds):
            for j in p:
                succs[j].append(i)
        def dur(it):
            kind, e, a, reads, writes, extra, cost, dreg = it
            if cost is not None:
                return cost
            if kind == "op":
                return {"pe": 0.08, "dve": 0.4, "act": 0.4, "pool": 1.0, "sp": 2.0}[e]
            o = a[0]
            nbytes = 1
            for d in list(o.shape):
                nbytes *= int(d)
            nbytes *= mybir.dt.size(o.dtype)
            return 2.5 + nbytes / 140e3
        LAT = float(os.environ.get("K_LAT", "0.15"))
        free = {e: 0.0 for e in self.eng}
        fin = [0.0] * n
        ready_t = [0.0] * n
        heaps = {e: [] for e in self.eng}
        low = 0
        done = [False] * n
        avail = [False] * n
        for i in range(n):
            if npred[i] == 0:
                heapq.heappush(heaps[norm[i][1]], (0.0, i)); avail[i] = True
        order = []
        deferred = {e: [] for e in self.eng}
        while len(order) < n:
            best = None
            for e, h in heaps.items():
                while h and h[0][1] >= low + window:
                    deferred[e].append(heapq.heappop(h))
                if not h:
                    continue
                rt, i = h[0]
                st = max(rt, free[e])
                if best is None or (st, i) < (best[0], best[1]):
                    best = (st, i, e)
            if best is None:
                for e in deferred:
                    for x in deferred[e]:
                        heapq.heappush(heaps[e], x)
                    deferred[e] = []
                window *= 2
                continue
            st, i, e = best
            heapq.heappop(heaps[e])
            order.append(i)
            done[i] = True
            f = st + dur(norm[i])
            fin[i] = f
            free[e] = f
            for j in succs[i]:
                npred[j] -= 1
                ready_t[j] = max(ready_t[j], f + LAT)
                if npred[j] == 0:
                    heapq.heappush(heaps[norm[j][1]], (ready_t[j], j)); avail[j] = True
            if i == low:
                while low < n and done[low]:
                    low += 1
                for e2 in deferred:
                    keep = []
                    for x in deferred[e2]:
                        if x[1] < low + window:
                            heapq.heappush(heaps[e2], x)
                        else:
                            keep.append(x)
                    deferred[e2] = keep
        self.sched_makespan = max(fin) if n else 0.0
        toks = [None] * n
        for i in order:
            kind, e, a, reads, writes, extra, cost, dreg = norm[i]
            for (reg, mode) in first[i]:
                if reg.w is not None and not (kind == "dma" and mode == "w" and reg.w[0] is (dreg.dsem if dreg is not None else None)):
                    self._wait(e, reg.w)
                if mode == "w":
                    for t in reg.r:
                        self._wait(e, t)
            for j in preds[i]:
                self._wait(e, toks[j])
            if kind == "op":
                ins = a(self.eng[e])
                self.cnt[e] += 1
                ins.then_inc(self.sem[e], 1)
                toks[i] = (self.sem[e], self.cnt[e])
            else:
                dg, kw = extra
                kw = dict(kw); kw.pop("force", None)
                if dreg.dsem is None:
                    self.nds += 1
                    dreg.dsem = self.nc.alloc_semaphore(name="ds%d_%s" % (self.nds, dreg.name))
                    self.dregs.append(dreg)
                    self.dmap[dreg.dsem.name] = dreg
                ins = self.eng[e].dma_start(out=a[0], in_=a[1], **kw)
                dreg.dcnt += 16
                ins.then_inc(dreg.dsem, 16)
                toks[i] = (dreg.dsem, dreg.dcnt)
            self.ninst += 1
        touched = {}
        for i, it in enumerate(norm):
            for r in it[3]:
                touched[id(r)] = r
            for w in it[4]:
                touched[id(w)] = w
        for k, reg in touched.items():
            if k in lw:
                reg.w = toks[lw[k]]
                reg.r = [toks[j] for j in rd.get(k, ())]
            else:
                reg.r = list(reg.r) + [toks[j] for j in rd.get(k, ())]
            reg.last = self.ninst

    def barrier(self):
        for e in self.eng:
            for f in self.eng:
                if f != e and self.cnt[f] > 0:
                    self._wait(e, (self.sem[f], self.cnt[f]))
            for d in self.dregs:
                if d.dcnt > 0:
                    self._wait(e, (d.dsem, d.dcnt))

    def final_wait(self, e, regs):
        for r in regs:
            r = r.r if isinstance(r, Buf) else r
            if r.w is not None:
                self._wait(e, r.w)
            for t in r.r:
                self._wait(e, t)


class Buf:
    def __init__(self, t, name):
        self.t = t
        self.a = t.ap()
        self.r = Reg(name)


class Builder:
    def __init__(self, dbg=None, ntiles=8):
        self.nc = bass.Bass("TRN2", target_bir_lowering=False)
        self.S = Sched(self.nc)
        self.dbg = dbg or {}
        self.ntiles = ntiles
        self.din = {}
        self.dout = {}
        self.nbuf = 0
        self.psb = None
        self.psi = 0
        self.ps_pool = None
        self.pclock = 0
        self.scopes = []

    def push(self):
        self.scopes.append(contextlib.ExitStack())

    def pop(self):
        self.S.barrier()
        self.scopes.pop().close()

    def inp(self, name, shape):
        b = Buf(self.nc.dram_tensor(name, list(shape), F32, kind="ExternalInput"), name)
        self.din[name] = b
        return b

    def outp(self, name, shape, dt=F32):
        b = Buf(self.nc.dram_tensor(name, list(shape), dt, kind="ExternalOutput"), name)
        self.dout[name] = b
        return b

    def sb(self, name, shape, dt=F32):
        self.nbuf += 1
        nm = "%s_%d" % (name, self.nbuf)
        if self.scopes:
            return Buf(self.scopes[-1].enter_context(self.nc.sbuf_tensor(nm, list(shape), dt)), name)
        return Buf(self.nc.alloc_sbuf_tensor(nm, list(shape), dt), name)

    def view(self, buf, ap):
        v = Buf.__new__(Buf)
        v.t = buf.t
        v.a = ap
        v.r = buf.r
        return v

    def init_psum(self):
        self.psb = []
        for i in range(8):
            t = self.nc.alloc_psum_tensor("psb%d" % i, [128, 512], F32)
            self.psb.append(Buf(t, "psb%d" % i))
            self.psb[-1].r.excl = True

    def ps(self):
        if self.ps_pool is not None:
            b = self.psb[self.ps_pool[self.psi % len(self.ps_pool)]]
            self.psi += 1
            return b
        b = min(self.psb, key=lambda t: t.r.last)
        self.pclock = max(self.pclock, self.S.ninst) + 1
        b.r.last = self.pclock
        return b

    def op(self, e, fn, reads=(), writes=(), cost=None):
        return self.S.op(e, fn, reads, writes, cost=cost)

    @staticmethod
    def fsz(ap):
        n = 1
        for d in list(ap.shape)[1:]:
            n *= int(d)
        return n

    def mm(self, out, lhsT, rhs, start, stop, reads, writes, **kw):
        return self.op("pe", lambda e: e.matmul(out, lhsT=lhsT, rhs=rhs, start=start, stop=stop, **kw), reads, writes,
                       cost=0.03 + max(self.fsz(rhs), 64) * 0.00052)

    def tr(self, out, in_, ident, reads, writes):
        return self.op("pe", lambda e: e.transpose(out, in_, ident), reads, writes, cost=0.1)

    def act(self, eng_out, in_, func, reads, writes, **kw):
        return self.op("act", lambda e: e.activation(out=eng_out, in_=in_, func=func, **kw), reads, writes,
                       cost=0.22 + self.fsz(in_) * 0.00075)

    def tt(self, e, out, in0, in1, op, reads, writes):
        c = (0.1 + self.fsz(in0) * 0.00115) if e == "dve" else (0.6 + self.fsz(in0) * 0.0013)
        return self.op(e, lambda g: g.tensor_tensor(out=out, in0=in0, in1=in1, op=op), reads, writes, cost=c)

    def ts(self, e, out, in0, s1, op0, reads, writes, s2=None, op1=None):
        c = (0.1 + self.fsz(in0) * 0.0008) if e == "dve" else (0.6 + self.fsz(in0) * 0.0013)
        if op1 is None:
            return self.op(e, lambda g: g.tensor_scalar(out=out, in0=in0, scalar1=s1, scalar2=None, op0=op0), reads, writes, cost=c)
        return self.op(e, lambda g: g.tensor_scalar(out=out, in0=in0, scalar1=s1, scalar2=s2, op0=op0, op1=op1), reads, writes, cost=c)

    def stt(self, out, in0, scalar, in1, op0, op1, reads, writes):
        return self.op("dve", lambda g: g.scalar_tensor_tensor(out=out, in0=in0, scalar=scalar, in1=in1, op0=op0, op1=op1), reads, writes,
                       cost=0.1 + self.fsz(in0) * 0.00115)

    def cp(self, e, out, in_, reads, writes):
        if e == "act":
            return self.op("act", lambda g: g.activation(out=out, in_=in_, func=AF.Copy), reads, writes, cost=0.22 + self.fsz(in_) * 0.00075)
        c = (0.1 + self.fsz(in_) * 0.0008) if e == "dve" else (0.6 + self.fsz(in_) * 0.0013)
        return self.op(e, lambda g: g.tensor_copy(out, in_), reads, writes, cost=c)

    def memset(self, e, out, val, writes):
        return self.op(e, lambda g: g.memset(out, val), (), writes)

    def cmul(self, e, o_re, o_im, a_re, a_im, b_re, b_im, t1, t2, reads, writes, tmp):
        R = list(reads)
        self.tt(e, t1, a_re, b_re, ALU.mult, R, [tmp])
        self.tt(e, t2, a_im, b_im, ALU.mult, R, [tmp])
        self.tt(e, o_re, t1, t2, ALU.subtract, [tmp], writes)
        self.tt(e, t1, a_re, b_im, ALU.mult, R, [tmp])
        self.tt(e, t2, a_im, b_re, ALU.mult, R, [tmp])
        self.tt(e, o_im, t1, t2, ALU.add, [tmp], writes)


def build(dbg=None, ntiles=8, phases=("1a", "1b", "1c", "2")):
    B = Builder(dbg, ntiles)
    nc, S = B.nc, B.S
    dbg = B.dbg

    x = B.inp("x", [L, D])
    shapes = {
        "norm_mix_pre": [D], "norm_mix_post": [D], "norm_ffn_pre": [D], "norm_ffn_post": [D],
        "w_in": [D, NIN], "b_gate": [2048], "rwkv_shift_mu": [NR], "rwkv_w0": [512],
        "rwkv_w2": [64, 512], "rwkv_a0": [512], "rwkv_a2": [64, 512], "rwkv_g2": [128, 512],
        "rwkv_k_k": [512], "rwkv_k_a": [512], "rwkv_r_k": [512], "rwkv_lnx_w": [512],
        "rwkv_lnx_b": [512], "s5_a_re": [32, 64], "s5_a_im": [32, 64], "s5_b_re": [32, 64, 16],
        "s5_b_im": [32, 64, 16], "s5_c_re": [32, 16, 64], "s5_c_im": [32, 16, 64], "s5_d": [512],
        "s5_log_step": [32], "s5_w_glu": [512, 512], "s5_b_glu": [512], "w_branch_rwkv": [512, D],
        "w_branch_s5": [512, D], "w_out": [D, D], "ffn_w_up": [D, 2 * FF], "ffn_conv_w": [3, 2 * FF],
        "ffn_conv_b": [2 * FF], "ffn_w_down": [FF, D],
    }
    W = {k: B.inp(k, v) for k, v in shapes.items()}
    out = B.outp("out", [L, D])
    dbo = {k: B.outp("dbg_" + k, shp, dt) for k, (shp, dt) in dbg.items()}

    B.init_psum()

    B.push()
    ident_f = B.sb("ident_f", [128, 128], F32)
    ident_b = B.sb("ident_b", [128, 128], BF16)
    B.memset("pool", ident_f.a, 1.0, [ident_f])
    B.op("pool", lambda e: e.affine_select(out=ident_f.a, in_=ident_f.a, pattern=[[-1, 128]],
                                           compare_op=ALU.is_equal, fill=0.0, base=0, channel_multiplier=1),
         [ident_f], [ident_f])
    B.cp("pool", ident_b.a, ident_f.a, [ident_f], [ident_b])
    epsc = B.sb("epsc", [128, 4])
    B.memset("pool", epsc.a[:, 0:1], 1e-6, [epsc]); B.memset("pool", epsc.a[:, 1:2], 1e-24, [epsc])
    B.memset("pool", epsc.a[:, 2:3], 64e-5, [epsc]); B.memset("pool", epsc.a[:, 3:4], 0.0, [epsc])

    def bc_load(name, n, q="sp"):
        t = B.sb(name + "_bc", [128, n], F32)
        S.dma(q, t.a, W[name].a.partition_broadcast(128), reads=[W[name]], writes=[t])
        return t

    def col_load(name, nt, q="sp"):
        t = B.sb(name + "_col", [128, nt], F32)
        S.dma(q, t.a, W[name].a.rearrange("(t p) -> p t", p=128), reads=[W[name]], writes=[t],
              allow_slow_non_contiguous=True)
        return t

    def frontend(src, ti, ntok_tiles, gbc, xt, hb, hT, scr, st, load=True):
        nt = ntok_tiles
        T = 128 * nt
        if load:
            S.dma("sp", xt.a, src.a[ti * T:(ti + 1) * T, :].rearrange("(s p) d -> p s d", p=128),
                  reads=[src], writes=[xt])
        for s in range(nt):
            B.act(scr.a, xt.a[:, s, :], AF.Square, [xt], [scr, st], accum_out=st.a[:, s:s + 1])
        B.act(st.a[:, nt:2 * nt], st.a[:, 0:nt], AF.Ln, [st, epsc], [st], scale=1.0 / D, bias=epsc.a[:, 0:1])
        B.act(st.a[:, 2 * nt:3 * nt], st.a[:, nt:2 * nt], AF.Exp, [st], [st], scale=-0.5)
        for s in range(nt):
            B.stt(hb.a[:, s, :], xt.a[:, s, :], st.a[:, 2 * nt + s:2 * nt + s + 1], gbc.a, ALU.mult, ALU.mult,
                  [xt, st, gbc], [hb])
        for c in range(8):
            pb = B.ps()
            pv = pb.a.bitcast(BF16)
            for s in range(nt):
                B.tr(pv[:, s * 128:(s + 1) * 128], hb.a[:, s, c * 128:(c + 1) * 128], ident_b.a,
                     [hb, ident_b], [pb])
            B.cp("act" if c % 2 == 0 else "dve", hT.a[:, c, :], pv[:, 0:T], [pb], [hT])

    T1 = 512
    YGLU = Buf(nc.dram_tensor("yglu_d", [128, 4, L], BF16, kind="Internal"), "yglu_d")

    gpre = bc_load("norm_mix_pre", D)
    x1s = Buf(nc.dram_tensor("x1s", [L, D], F32, kind="Internal"), "x1s")
    YR = Buf(nc.dram_tensor("yr_d", [128, 4, L], BF16, kind="Internal"), "yr_d")
    H2T = Buf(nc.dram_tensor("h2t_d", [128, 8, L], BF16, kind="Internal"), "h2t_d")
    if "1a" in phases:
        B.push()
        ws5 = B.sb("ws5", [128, 8, 512], BF16)
        S.dma("pool", ws5.a, W["w_in"].a.rearrange("(c p) n -> p c n", p=128)[:, :, NR:NR + 512],
              reads=[W["w_in"]], writes=[ws5])
        wglu = B.sb("wglu", [128, 4, 512], BF16)
        S.dma("pool", wglu.a, W["s5_w_glu"].a.rearrange("(c p) n -> p c n", p=128), reads=[W["s5_w_glu"]], writes=[wglu])
        bglu = col_load("s5_b_glu", 4)
        dcol = col_load("s5_d", 4)

        MS = B.sb("ms", [128, 16, 2, 2])
        ER = B.sb("ER", [128, 16, 64]); EI = B.sb("EI", [128, 16, 64]); R8 = B.sb("R8", [128, 16])
        Wt = B.sb("Wt", [128, 4, 8, 2, 128], BF16)
        CAW = B.sb("CAW", [128, 16, 2, 9, 64], BF16)
        mk = B.sb("mk", [128, 2])
        B.memset("pool", mk.a[:, 0:1], 0.0, [mk]); B.memset("pool", mk.a[0:32, 0:1], 1.0, [mk]); B.memset("pool", mk.a[64:96, 0:1], 1.0, [mk])
        mk4 = B.sb("mk4", [128, 4])
        B.memset("pool", mk4.a, 0.0, [mk4])
        B.memset("pool", mk4.a[0:32, 0:1], 1.0, [mk4]); B.memset("pool", mk4.a[32:64, 1:2], 1.0, [mk4])
        B.memset("pool", mk4.a[64:96, 2:3], 1.0, [mk4]); B.memset("pool", mk4.a[64:128, 3:4], 1.0, [mk4]); B.memset("pool", mk4.a[64:96, 3:4], 0.0, [mk4])
        B.memset("pool", mk.a[:, 1:2], 1.0, [mk]); B.memset("pool", mk.a[0:32, 1:2], 0.0, [mk]); B.memset("pool", mk.a[64:96, 1:2], 0.0, [mk])
        Kbd = B.sb("Kbd", [128, 4, 8, 128], BF16)
        B.push()
        are = B.sb("are", [128, 16]); aim = B.sb("aim", [128, 16]); ls = B.sb("ls", [128, 16])
        for gl in range(2):
            S.dma("sp", are.a[gl * 64:(gl + 1) * 64, :], W["s5_a_re"].a.rearrange("(q gl) n -> gl n q", gl=2)[gl],
                  reads=[W["s5_a_re"]], writes=[are], allow_slow_non_contiguous=True)
            S.dma("sp", aim.a[gl * 64:(gl + 1) * 64, :], W["s5_a_im"].a.rearrange("(q gl) n -> gl n q", gl=2)[gl],
                  reads=[W["s5_a_im"]], writes=[aim], allow_slow_non_contiguous=True)
            S.dma("sp", ls.a[gl * 64:(gl + 1) * 64, :],
                  W["s5_log_step"].a.rearrange("(q gl) -> gl q", gl=2)[gl].partition_broadcast(64),
                  reads=[W["s5_log_step"]], writes=[ls], allow_slow_non_contiguous=True)
        braw = [B.sb("braw%d" % i, [128, 16, 16]) for i in range(2)]
        for i, nm in enumerate(("s5_b_re", "s5_b_im")):
            S.dma("sp", braw[i].a, W[nm].a.rearrange("(q gl) n c -> (gl n) q c", gl=2), reads=[W[nm]], writes=[braw[i]])
        craw = [B.sb("craw%d" % i, [128, 16, 16]) for i in range(2)]
        ctmp = B.sb("ctmp", [128, 128])
        for i, nm in enumerate(("s5_c_re", "s5_c_im")):
            for blk in range(2):
                src = W[nm].a.rearrange("(b qq gl) c n -> b qq c gl n", b=2, gl=2)[blk]
                for qq in range(8):
                    S.dma("sp", ctmp.a[qq * 16:(qq + 1) * 16, :].rearrange("p (gl n) -> p gl n", gl=2),
                          src[qq], reads=[W[nm]], writes=[ctmp])
                pb = B.ps()
                B.tr(pb.a[:, 0:128], ctmp.a, ident_f.a, [ctmp, ident_f], [pb])
                B.cp("dve", craw[i].a[:, blk * 8:(blk + 1) * 8, :],
                     pb.a[:, 0:128].rearrange("p (qq c) -> p qq c", c=16), [pb], [craw[i]])

        tm = B.sb("s5tmp", [128, 12, 32])
        tmr = tm.r

        def row(i, n=16):
            return tm.a[:, i, 0:n]

        dt_ = row(0)
        B.act(dt_, ls.a, AF.Exp, [ls], [tm])
        xr = row(1)
        B.tt("dve", xr, are.a, dt_, ALU.mult, [are, tm], [tm])
        rho = row(2)
        B.ts("dve", rho, xr, 1.0 / 720, ALU.mult, [tm], [tm], s2=1.0 / 120, op1=ALU.add)
        for cf in (1.0 / 24, 1.0 / 6, 0.5, 1.0, 1.0):
            B.tt("dve", rho, rho, xr, ALU.mult, [tm], [tm])
            B.ts("dve", rho, rho, cf, ALU.add, [tm], [tm])
        th2 = tm.a[:, 3, :]
        B.tt("dve", th2[:, 0:16], aim.a, dt_, ALU.mult, [aim, tm], [tm])
        B.ts("dve", th2[:, 16:32], th2[:, 0:16], PI / 2, ALU.add, [tm], [tm])
        kf = tm.a[:, 4, :]
        B.ts("dve", kf, th2, 1.0 / (2 * PI), ALU.mult, [tm], [tm])
        ki = B.sb("ki", [128, 32], I32)
        B.cp("dve", ki.a, kf, [tm], [ki])
        B.cp("dve", kf, ki.a, [ki], [tm])
        r1 = tm.a[:, 5, :]
        B.stt(r1, kf, -2 * PI, th2, ALU.mult, ALU.add, [tm], [tm])
        B.ts("dve", kf, r1, PI, ALU.is_gt, [tm], [tm], s2=-2 * PI, op1=ALU.mult)
        B.tt("dve", r1, r1, kf, ALU.add, [tm], [tm])
        B.ts("dve", kf, r1, -PI, ALU.is_lt, [tm], [tm], s2=2 * PI, op1=ALU.mult)
        B.tt("dve", r1, r1, kf, ALU.add, [tm], [tm])
        sc = tm.a[:, 6, :]
        B.act(sc, r1, AF.Sin, [tm], [tm])
        n2 = row(7)
        B.tt("dve", kf, sc, sc, ALU.mult, [tm], [tm])
        B.tt("dve", n2, kf[:, 0:16], kf[:, 16:32], ALU.add, [tm], [tm])
        B.ts("dve", n2, n2, -0.5, ALU.mult, [tm], [tm], s2=1.5, op1=ALU.add)
        B.tt("dve", n2, n2, rho, ALU.mult, [tm], [tm])
        PW = B.sb("pw", [128, 9, 2, 16])
        B.memset("dve", PW.a[:, 0, 0, :], 1.0, [PW])
        B.memset("dve", PW.a[:, 0, 1, :], 0.0, [PW])
        B.tt("dve", PW.a[:, 1, 0, :], sc[:, 16:32], n2, ALU.mult, [tm], [PW])
        B.tt("dve", PW.a[:, 1, 1, :], sc[:, 0:16], n2, ALU.mult, [tm], [PW])
        pt = B.sb("ptmp", [128, 2, 4, 16])
        for (lo, n, s) in ((2, 1, 1), (3, 2, 2), (5, 4, 4)):
            bre = PW.a[:, s:s + 1, 0, :].broadcast_to([128, n, 16])
            bim = PW.a[:, s:s + 1, 1, :].broadcast_to([128, n, 16])
            B.cmul("dve", PW.a[:, lo:lo + n, 0, :], PW.a[:, lo:lo + n, 1, :],
                   PW.a[:, lo - s:lo - s + n, 0, :], PW.a[:, lo - s:lo - s + n, 1, :], bre, bim,
                   pt.a[:, 0, 0:n, :], pt.a[:, 1, 0:n, :], [PW], [PW], pt)
        B.cp("dve", MS.a[:, :, 0, 0], PW.a[:, 8, 0, :], [PW], [MS])
        B.cp("dve", MS.a[:, :, 1, 1], PW.a[:, 8, 0, :], [PW], [MS])
        B.cp("dve", MS.a[:, :, 1, 0], PW.a[:, 8, 1, :], [PW], [MS])
        B.ts("dve", MS.a[:, :, 0, 1], PW.a[:, 8, 1, :], -1.0, ALU.mult, [PW], [MS])
        B.tt("dve", R8.a, rho, rho, ALU.mult, [tm], [R8])
        B.tt("dve", R8.a, R8.a, R8.a, ALU.mult, [R8], [R8])
        B.tt("dve", R8.a, R8.a, R8.a, ALU.mult, [R8], [R8])
        r8i = row(8)
        B.op("dve", lambda e: e.reciprocal(r8i, R8.a), [R8], [tm])
        B.tt("dve", ER.a[:, :, 0], PW.a[:, 8, 0, :], r8i, ALU.mult, [PW, tm], [ER])
        B.tt("dve", EI.a[:, :, 0], PW.a[:, 8, 1, :], r8i, ALU.mult, [PW, tm], [EI])
        et = B.sb("etmp", [128, 2, 16, 32])
        n_ = 1
        while n_ < 64:
            bre = ER.a[:, :, n_ - 1:n_].broadcast_to([128, 16, n_]); bim = EI.a[:, :, n_ - 1:n_].broadcast_to([128, 16, n_])
            B.cmul("dve", ER.a[:, :, n_:2 * n_], EI.a[:, :, n_:2 * n_], ER.a[:, :, 0:n_], EI.a[:, :, 0:n_], bre, bim,
                   et.a[:, 0, :, 0:n_], et.a[:, 1, :, 0:n_], [ER, EI], [ER, EI], et)
            n_ *= 2
        am1 = row(8); nre = row(9); nim = row(10); den = row(11); t0 = row(4); t1 = row(5)
        B.ts("dve", am1, PW.a[:, 1, 0, :], -1.0, ALU.add, [PW], [tm])
        B.tt("dve", nre, am1, are.a, ALU.mult, [tm, are], [tm])
        B.tt("dve", t0, PW.a[:, 1, 1, :], aim.a, ALU.mult, [PW, aim], [tm])
        B.tt("dve", nre, nre, t0, ALU.add, [tm], [tm])
        B.tt("dve", nim, PW.a[:, 1, 1, :], are.a, ALU.mult, [PW, are], [tm])
        B.tt("dve", t0, am1, aim.a, ALU.mult, [tm, aim], [tm])
        B.tt("dve", nim, nim, t0, ALU.subtract, [tm], [tm])
        B.tt("dve", den, are.a, are.a, ALU.mult, [are], [tm])
        B.tt("dve", t0, aim.a, aim.a, ALU.mult, [aim], [tm])
        B.tt("dve", den, den, t0, ALU.add, [tm], [tm])
        B.op("dve", lambda e: e.reciprocal(t1, den), [tm], [tm])
        B.tt("dve", nre, nre, t1, ALU.mult, [tm], [tm])
        B.tt("dve", nim, nim, t1, ALU.mult, [tm], [tm])
        bb = [B.sb("bb%d" % i, [128, 16, 16]) for i in range(2)]
        btmp = B.sb("btmp", [128, 2, 16, 16])
        cre = nre[:, :, None].broadcast_to([128, 16, 16]); cim = nim[:, :, None].broadcast_to([128, 16, 16])
        B.cmul("dve", bb[0].a, bb[1].a, cre, cim, braw[0].a, braw[1].a, btmp.a[:, 0], btmp.a[:, 1],
               [tm, braw[0], braw[1]], [bb[0], bb[1]], btmp)
        X = [B.sb("X%d" % i, [128, 16, 32]) for i in range(2)]
        Xb = [B.sb("Xb%d" % i, [128, 16, 64], BF16) for i in range(2)]
        for i in range(2):
            B.memset("dve", X[i].a, 0.0, [X[i]])
            for gl in range(2):
                B.cp("dve", X[i].a[gl * 64:(gl + 1) * 64, :, gl * 16:(gl + 1) * 16], bb[i].a[gl * 64:(gl + 1) * 64], [bb[i]], [X[i]])
            B.memset("dve", Xb[i].a, 0.0, [Xb[i]])
            for kk in range(2):
                B.cp("act", Xb[i].a[:, kk::2, 32 * kk:32 * kk + 32], X[i].a[:, kk::2, :], [X[i]], [Xb[i]])
        B.push()
        WX = [B.sb("WX%d" % i, [128, 8, 16, 32]) for i in range(2)]
        wtmp = B.sb("wtmp", [128, 2, 8, 16, 32])
        pre = PW.a[:, 0:8, 0, :][:, :, :, None].broadcast_to([128, 8, 16, 32])
        pim = PW.a[:, 0:8, 1, :][:, :, :, None].broadcast_to([128, 8, 16, 32])
        xre = X[0].a[:, None, :, :].broadcast_to([128, 8, 16, 32])
        xim = X[1].a[:, None, :, :].broadcast_to([128, 8, 16, 32])
        B.cmul("dve", WX[0].a, WX[1].a, pre, pim, xre, xim, wtmp.a[:, 0], wtmp.a[:, 1], [PW, X[0], X[1]], [WX[0], WX[1]], wtmp)
        for tile in range(4):
            for e_ in range(8):
                pb = B.ps()
                for ri in range(2):
                    B.tr(pb.a[:, ri * 128:(ri + 1) * 128],
                         WX[ri].a[:, e_, 4 * tile:4 * tile + 4, :].rearrange("p k c -> p (k c)"), ident_f.a,
                         [WX[ri], ident_f], [pb])
                B.cp("act" if e_ % 2 else "dve", Wt.a[:, tile, e_, :, :],
                     pb.a[:, 0:256].rearrange("p (r n) -> p r n", r=2), [pb], [Wt])
        B.pop()
        B.push()
        CA = [B.sb("CA%d" % i, [128, 9, 16, 16]) for i in range(2)]
        catmp = B.sb("catmp", [128, 2, 9, 16, 16])
        pre9 = PW.a[:, :, 0, :][:, :, :, None].broadcast_to([128, 9, 16, 16])
        pim9 = PW.a[:, :, 1, :][:, :, :, None].broadcast_to([128, 9, 16, 16])
        cre9 = craw[0].a[:, None, :, :].broadcast_to([128, 9, 16, 16])
        cim9 = craw[1].a[:, None, :, :].broadcast_to([128, 9, 16, 16])
        B.cmul("dve", CA[0].a, CA[1].a, pre9, pim9, cre9, cim9, catmp.a[:, 0], catmp.a[:, 1],
               [PW, craw[0], craw[1]], [CA[0], CA[1]], catmp)
        B.memset("dve", CAW.a, 0.0, [CAW])
        for gl in range(2):
            hs = slice(gl * 64, (gl + 1) * 64)
            for kk in range(2):
                o = 32 * kk + 16 * gl
                B.cp("dve", CAW.a[hs, kk::2, 0, :, o:o + 16], CA[0].a[hs, :, kk::2, :].rearrange("p t q c -> p q t c"), [CA[0]], [CAW])
                B.ts("dve", CAW.a[hs, kk::2, 1, :, o:o + 16], CA[1].a[hs, :, kk::2, :].rearrange("p t q c -> p q t c"), -1.0, ALU.mult,
                     [CA[1]], [CAW])
        B.memset("dve", Kbd.a, 0.0, [Kbd])
        k0 = B.sb("k0", [128, 128])
        for tile in range(4):
            pb = B.ps()
            for h in range(2):
                for tau in range(8):
                    n = 0
                    for kk in range(2):
                        q = 4 * tile + 2 * h + kk
                        o = 32 * kk
                        for ri in range(2):
                            B.mm(pb.a[64 * h:64 * h + 64, tau * 32:(tau + 1) * 32], Xb[ri].a[:, q, :],
                                 CAW.a[:, q, ri, tau, o:o + 32], n == 0, n == 3, [Xb[ri], CAW], [pb])
                            n += 1
            for h in range(2):
                hs = slice(64 * h, 64 * h + 64)
                for kk in range(2):
                    cs = slice(64 * h + 32 * kk, 64 * h + 32 * kk + 32)
                    B.ts("dve", Kbd.a[hs, tile, 1:8, cs], pb.a[hs, 32:256].rearrange("p (t c) -> p t c", c=32),
                         mk.a[hs, kk:kk + 1], ALU.mult, [pb, mk], [Kbd])
            B.memset("dve", k0.a, 0.0, [k0])
            for h in range(2):
                hs = slice(64 * h, 64 * h + 64)
                for kk in range(2):
                    cs = slice(64 * h + 32 * kk, 64 * h + 32 * kk + 32)
                    B.ts("dve", k0.a[hs, cs], pb.a[hs, 0:32], mk.a[hs, kk:kk + 1], ALU.mult, [pb, mk], [k0])
            B.stt(k0.a, ident_f.a, dcol.a[:, tile:tile + 1], k0.a, ALU.mult, ALU.add, [ident_f, dcol, k0], [k0])
            B.cp("dve", Kbd.a[:, tile, 0, :], k0.a, [k0], [Kbd])
        B.pop()
        B.pop()
        for nm_, b_ in (("Wt", Wt), ("CAW", CAW), ("Kbd", Kbd), ("MS", MS)):
            if nm_ in dbo:
                S.dma("sp", dbo[nm_].a, b_.a, reads=[b_], writes=[dbo[nm_]], dreg=b_)
        NM = T1 // 8
        xt = B.sb("xt", [128, 4, D]); hT = B.sb("hT", [128, 8, T1], BF16); st = B.sb("st", [128, 12])
        hb = B.sb("hb", [128, 4, D], BF16)
        Sfin = [B.sb("Sfin%d" % h, [128, 8, 2]) for h in range(2)]
        for h in range(2):
            B.memset("pool", Sfin[h].a, 0.0, [Sfin[h]])

        class BS:
            pass
        hv = []
        for h in range(2):
            q = BS()
            q.u = B.sb("u%d" % h, [128, 2, 8, NM], BF16)
            q.um = B.sb("um%d" % h, [128, 4, 2, 8, NM], BF16)
            q.Dm = B.sb("Dm%d" % h, [128, 8, 2, NM]); q.St = B.sb("St%d" % h, [128, 8, 2, NM + 1])
            q.Sb = B.sb("Sb%d" % h, [128, 8, 2, NM], BF16)
            q.rt1 = B.sb("rt1%d" % h, [128, 8, NM]); q.rt2 = B.sb("rt2%d" % h, [128, 8, NM])
            q.Dr = B.sb("Dr%d" % h, [128, 8, 2, NM]); q.Qs = B.sb("Qs%d" % h, [128, 8, 2, NM])
            q.y2 = B.sb("y2%d" % h, [128, 2, T1]); q.yg = B.sb("yg%d" % h, [128, 2, T1])
            q.yy = B.view(q.Dm, q.Dm.a.rearrange("p a b c -> p (a b c)").rearrange("p (t n) -> p t n", t=2))
            q.ygb = B.sb("ygb%d" % h, [128, 2, T1], BF16)
            hv.append(q)
        sgb = B.sb("sgb", [128, T1]); ygl = B.sb("ygl", [128, 4, T1], BF16)
        scr = B.view(hb, hb.a.rearrange("p a b -> p (a b)")[:, 0:D])

        def tile1a(ti):
            frontend(x, ti, 4, gpre, xt, hb, hT, scr, st)
            for cb in range(4):
                q = hv[cb // 2]; tl = cb % 2
                pb = B.ps()
                for c in range(8):
                    B.mm(pb.a, ws5.a[:, c, cb * 128:(cb + 1) * 128], hT.a[:, c, :], c == 0, c == 7, [ws5, hT], [pb])
                pperm = pb.a.rearrange("p (m t) -> p t m", t=8)
                B.cp("act", q.u.a[:, tl, :, :], pperm, [pb], [q.u])
                for k in range(4):
                    B.act(q.um.a[:, k, tl, :, :], pperm, AF.Copy, [pb, mk4], [q.um], scale=mk4.a[:, k:k + 1])
            for h in range(2):
                q = hv[h]
                u, um, Dm, St, Sb, rt1, rt2, Dr, Qs, y2, yg, yy, ygb = (q.u, q.um, q.Dm, q.St, q.Sb, q.rt1, q.rt2, q.Dr, q.Qs,
                                                                        q.y2, q.yg, q.yy, q.ygb)
                sf = Sfin[h]
                for qb in range(2):
                    pb = B.ps()
                    for qq in range(4):
                        ql = 4 * qb + qq
                        tl, k = ql // 4, ql % 4
                        gt = 2 * h + tl
                        for ri in range(2):
                            col = (qq * 2 + ri) * NM
                            for j0 in range(8):
                                B.mm(pb.a[:, col:col + NM], Wt.a[:, gt, 7 - j0, ri, :], um.a[:, k, tl, j0, :],
                                     j0 == 0, j0 == 7, [Wt, um], [pb])
                    B.cp("dve", Dm.a[:, 4 * qb:4 * qb + 4, :, :],
                         pb.a.rearrange("p (q r m) -> p q r m", q=4, r=2), [pb], [Dm])
                erb = ER.a[:, 8 * h:8 * h + 8, :]; eib = EI.a[:, 8 * h:8 * h + 8, :]
                B.cp("pool", St.a[:, :, :, 0], sf.a, [sf], [St])
                B.tt("dve", rt1.a, erb, Dm.a[:, :, 0, :], ALU.mult, [ER, Dm], [rt1])
                B.tt("dve", rt2.a, eib, Dm.a[:, :, 1, :], ALU.mult, [EI, Dm], [rt2])
                B.tt("dve", Dr.a[:, :, 0, :], rt1.a, rt2.a, ALU.add, [rt1, rt2], [Dr])
                B.tt("dve", rt1.a, erb, Dm.a[:, :, 1, :], ALU.mult, [ER, Dm], [rt1])
                B.tt("dve", rt2.a, eib, Dm.a[:, :, 0, :], ALU.mult, [EI, Dm], [rt2])
                B.tt("dve", Dr.a[:, :, 1, :], rt1.a, rt2.a, ALU.subtract, [rt1, rt2], [Dr])
                for ql in range(8):
                    qi = 8 * h + ql
                    for ri in range(2):
                        B.op("dve", lambda e, ql=ql, qi=qi, ri=ri, Qs=Qs, Dr=Dr, sf=sf: e.tensor_tensor_scan(
                            out=Qs.a[:, ql, ri, :], data0=R8.a[:, qi:qi + 1].broadcast_to([128, NM]), data1=Dr.a[:, ql, ri, :],
                            initial=sf.a[:, ql, ri:ri + 1], op0=ALU.mult, op1=ALU.add), [R8, Dr, sf], [Qs], cost=0.4)
                B.tt("dve", rt1.a, erb, Qs.a[:, :, 0, :], ALU.mult, [ER, Qs], [rt1])
                B.tt("dve", rt2.a, eib, Qs.a[:, :, 1, :], ALU.mult, [EI, Qs], [rt2])
                B.tt("dve", St.a[:, :, 0, 1:NM + 1], rt1.a, rt2.a, ALU.subtract, [rt1, rt2], [St])
                B.tt("dve", rt1.a, erb, Qs.a[:, :, 1, :], ALU.mult, [ER, Qs], [rt1])
                B.tt("dve", rt2.a, eib, Qs.a[:, :, 0, :], ALU.mult, [EI, Qs], [rt2])
                B.tt("dve", St.a[:, :, 1, 1:NM + 1], rt1.a, rt2.a, ALU.add, [rt1, rt2], [St])
                B.cp("act", Sb.a, St.a[:, :, :, 0:NM], [St], [Sb])
                B.cp("act", sf.a, St.a[:, :, :, NM], [St], [sf])
                for tl in range(2):
                    gt = 2 * h + tl
                    pb = B.ps()
                    for hh in range(2):
                        hs = slice(64 * hh, 64 * hh + 64)
                        for t0_ in range(8):
                            n = 0
                            for kk in range(2):
                                ql = 4 * tl + 2 * hh + kk
                                qi = 8 * h + ql
                                for ri in range(2):
                                    B.mm(pb.a[hs, t0_ * NM:(t0_ + 1) * NM], CAW.a[:, qi, ri, t0_ + 1, :], Sb.a[:, ql, ri, :], n == 0, n == 3,
                                         [CAW, Sb], [pb], skip_group_check=True)
                                    n += 1
                    B.cp("act", y2.a[:, tl, :], pb.a, [pb], [y2])
                    pb = B.ps()
                    for t0o in range(8):
                        for tau in range(t0o + 1):
                            B.mm(pb.a[:, t0o * NM:(t0o + 1) * NM], Kbd.a[:, gt, tau, :], u.a[:, tl, t0o - tau, :],
                                 tau == 0, tau == t0o, [Kbd, u], [pb])
                    B.tt("dve", yy.a[:, tl, :].rearrange("p (m t) -> p t m", t=8), pb.a.rearrange("p (t m) -> p t m", t=8),
                         y2.a[:, tl, :].rearrange("p (t m) -> p t m", t=8), ALU.add, [pb, y2], [yy])
                if "s5y" in dbo:
                    S.dma("sp", dbo["s5y"].a[:, 2 * h:2 * h + 2, ti * T1:(ti + 1) * T1], yy.a, reads=[yy], writes=[dbo["s5y"]], dreg=yy)
                B.act(yg.a, yy.a, AF.Gelu_apprx_tanh, [yy], [yg])
                B.cp("act", ygb.a, yg.a, [yg], [ygb])
            for cb in range(4):
                pb = B.ps()
                for c in range(4):
                    B.mm(pb.a, wglu.a[:, c, cb * 128:(cb + 1) * 128], hv[c // 2].ygb.a[:, c % 2, :], c == 0, c == 3,
                         [wglu, hv[0].ygb, hv[1].ygb], [pb])
                B.act(sgb.a, pb.a, AF.Sigmoid, [pb, bglu], [sgb], bias=bglu.a[:, cb:cb + 1])
                B.tt("dve", ygl.a[:, cb, :], hv[cb // 2].yg.a[:, cb % 2, :], sgb.a, ALU.mult, [hv[cb // 2].yg, sgb], [ygl])
            S.dma("sp", YGLU.a[:, :, ti * T1:(ti + 1) * T1], ygl.a, reads=[ygl], writes=[YGLU], dreg=ygl)
            if "yglu" in dbo:
                S.dma("sp", dbo["yglu"].a[:, :, ti * T1:(ti + 1) * T1], ygl.a, reads=[ygl], writes=[dbo["yglu"]], dreg=ygl)

        S.rec = []
        for ti in range(ntiles):
            tile1a(ti)
        items = S.rec
        S.rec = None
        if os.environ.get("K_P1ASCHED", "1") == "1":
            S.schedule_emit(items, window=int(os.environ.get("K_WIN1A", "2500")))
        else:
            for it in items:
                S.replay(it)
        B.pop()

    if "1b" in phases:
        phase_1b(B, W, x, YR, epsc, gpre, ident_f, ident_b, frontend, bc_load, col_load, dbo, ntiles * (T1 // 128))

    if "1c" in phases:
        phase_1c(B, W, x, x1s, H2T, YR, YGLU, epsc, gpre, frontend, bc_load, col_load, dbo, ntiles)

    B.pop()
    if "2" in phases:
        phase_2(B, W, x1s, H2T, out, epsc, frontend, bc_load, dbo, ntiles)

    S.final_wait("sp", list(B.dout.values()))
    B.Wshapes = shapes
    return B


def phase_1b(B, W, x, YR, epsc, gpre, ident_f, ident_b, frontend, bc_load, col_load, dbo, ntiles):
    nc, S = B.nc, B.S
    TB = 128
    C = 128
    NW = 3
    C0 = math.exp(-0.5)
    B.push()
    wrg = B.sb("wrg", [128, 8, NR], BF16)
    win = W["w_in"].a.rearrange("(c p) n -> p c n", p=128)
    for c in range(8):
        S.dma("pool", wrg.a[:, c, :], win[:, c, 0:NR], reads=[W["w_in"]], writes=[wrg])
    w2p = B.sb("w2p", [128, 512], BF16); a2p = B.sb("a2p", [128, 512], BF16); g2b = B.sb("g2b", [128, 512], BF16)
    B.memset("pool", w2p.a, 0.0, [w2p]); B.memset("pool", a2p.a, 0.0, [a2p])
    S.dma("pool", w2p.a[0:64, :], W["rwkv_w2"].a, reads=[W["rwkv_w2"]], writes=[w2p])
    S.dma("pool", a2p.a[64:128, :], W["rwkv_a2"].a, reads=[W["rwkv_a2"]], writes=[a2p])
    S.dma("pool", g2b.a, W["rwkv_g2"].a, reads=[W["rwkv_g2"]], writes=[g2b])
    mu = col_load("rwkv_shift_mu", 14); w0c = col_load("rwkv_w0", 4); a0c = col_load("rwkv_a0", 4)
    kkc = col_load("rwkv_k_k", 4); kac = col_load("rwkv_k_a", 4); rkc = col_load("rwkv_r_k", 4)
    lwc = col_load("rwkv_lnx_w", 4); lbc = col_load("rwkv_lnx_b", 4)
    omu = B.sb("omu", [128, 14]); oka = B.sb("oka", [128, 4])
    B.ts("dve", omu.a, mu.a, -1.0, ALU.mult, [mu], [omu], s2=1.0, op1=ALU.add)
    B.ts("dve", oka.a, kac.a, -1.0, ALU.mult, [kac], [oka], s2=1.0, op1=ALU.add)
    bones = B.sb("bones", [128, 128]); bavg = B.sb("bavg", [128, 128]); ones = B.sb("ones", [128, 128])
    B.memset("pool", ones.a, 1.0, [ones])
    B.memset("pool", bones.a, 0.0, [bones])
    B.memset("pool", bones.a[0:64, 0:64], 1.0, [bones]); B.memset("pool", bones.a[64:128, 64:128], 1.0, [bones])
    B.ts("pool", bavg.a, bones.a, 1.0 / 64, ALU.mult, [bones], [bavg])
    hm = B.sb("hm", [128, 2])
    B.memset("pool", hm.a, 0.0, [hm]); B.memset("pool", hm.a[0:64, 0:1], 1.0, [hm]); B.memset("pool", hm.a[64:128, 1:2], 1.0, [hm])
    mf = B.sb("mf", [128, 128])
    MU4 = B.sb("MU4", [128, 4, 128], BF16); MI4 = B.sb("MI4", [128, 4, 128], BF16)
    ML4 = B.sb("ML4", [128, 4, 128], BF16); I4 = B.sb("I4", [128, 4, 128], BF16)
    for (mt, pat, cm, cop) in ((MU4, 1, -1, ALU.is_gt), (MI4, 1, -1, ALU.is_ge), (ML4, -1, 1, ALU.is_gt)):
        B.memset("pool", mf.a, 1.0, [mf])
        B.op("pool", lambda e, pat=pat, cm=cm, cop=cop: e.affine_select(out=mf.a, in_=mf.a, pattern=[[pat, 128]], compare_op=cop,
                                                                      fill=0.0, base=0, channel_multiplier=cm), [mf], [mf])
        for h in range(4):
            B.cp("pool", mt.a[:, h, :], mf.a, [mf], [mt])
    for h in range(4):
        B.cp("pool", I4.a[:, h, :], ident_f.a, [ident_f], [I4])
    pc = B.sb("pc", [128, 14]); B.memset("pool", pc.a, 0.0, [pc])
    H32 = B.sb("H32", [128, 4, 64]); Hb = B.sb("Hb", [128, 4, 64], BF16); Ht = B.sb("Ht", [128, 4, 64])
    B.memset("pool", H32.a, 0.0, [H32]); B.memset("pool", Hb.a, 0.0, [Hb])

    def bc4(col):
        return col.a[:, :, None].broadcast_to([128, 4, TB])

    class BS:
        pass

    sets = []
    for w in range(NW):
        b = BS()
        f4 = lambda nm: B.sb(nm + str(w), [128, 4, TB])
        h4 = lambda nm: B.sb(nm + str(w), [128, 4, TB], BF16)
        b.xt = B.sb("xt%d" % w, [128, 1, D]); b.hb = B.sb("hb%d" % w, [128, 1, D], BF16); b.hT = B.sb("hT%d" % w, [128, 8, TB], BF16)
        b.st = B.sb("st%d" % w, [128, 6])
        b.PS = B.sb("PS%d" % w, [128, 14, TB]); b.t1 = B.sb("t1%d" % w, [128, 4, TB]); b.t2 = B.sb("t2%d" % w, [128, 4, TB])
        b.scr = B.view(b.PS, b.PS.a.rearrange("p a b -> p (a b)")[:, 0:D])
        b.twa = B.sb("twa%d" % w, [128, TB], BF16); b.sgd = B.sb("sgd%d" % w, [128, TB], BF16)
        b.sgw = f4("sgw"); b.asg = f4("asg"); b.gg = f4("gg"); b.kk = f4("kk"); b.tq = f4("tq"); b.kmod = f4("kmod")
        b.cum = f4("cum"); b.eg = b.t1; b.egx = f4("egx"); b.eng = b.t2; b.bon = f4("bon")
        b.am = B.sb("am%d" % w, [128, 4, 2, TB], BF16); b.rm = B.sb("rm%d" % w, [128, 4, 2, TB], BF16)
        b.bf = h4("bf"); b.kf = h4("kf"); b.vb = h4("vb")
        b.Btok = B.sb("Btok%d" % w, [128, 512], BF16); b.Ktok = B.sb("Ktok%d" % w, [128, 512], BF16); b.Vtok = B.sb("Vtok%d" % w, [128, 512], BF16)
        for nm, src in (("Pm", b.sgw), ("Qm", b.asg), ("Rm", b.kk), ("Nak", b.kmod), ("Nrb", b.egx), ("Nrk", b.eng)):
            setattr(b, nm, B.view(src, src.a.rearrange("p a b -> p (a b)").bitcast(BF16).rearrange("p (h t) -> p h t", h=8)))
        hTf = b.hT.a.rearrange("p a b -> p (a b)")
        b.Xb = B.view(b.hT, hTf[:, 0:512]); b.Ub = B.view(b.hT, hTf[:, 512:1024])
        b.Yf = B.view(b.xt, b.xt.a.rearrange("p a b -> p (a b)")[:, 0:4 * TB].rearrange("p (j t) -> p j t", j=4))
        b.dd = B.view(b.hb, b.hb.a.rearrange("p a b -> p (a b)").bitcast(F32).rearrange("p (j t) -> p j t", j=4))
        b.yrb = h4("yrb")
        sets.append(b)

    def chunk(ci):
        b = sets[ci % NW]
        PS, tq, cum, kk, kmod, eg, egx, eng, asg, sgw, gg, bon = b.PS, b.tq, b.cum, b.kk, b.kmod, b.eg, b.egx, b.eng, b.asg, b.sgw, b.gg, b.bon
        am, rm, Pm, Qm, Rm, Nak, Nrb, Nrk = b.am, b.rm, b.Pm, b.Qm, b.Rm, b.Nak, b.Nrb, b.Nrk
        frontend(x, ci, 1, gpre, b.xt, b.hb, b.hT, b.scr, b.st)
        yield
        for j0 in range(0, 14, 4):
            nj = min(4, 14 - j0)
            pb = B.ps()
            for jj in range(nj):
                for c in range(8):
                    B.mm(pb.a[:, jj * TB:(jj + 1) * TB], wrg.a[:, c, (j0 + jj) * 128:(j0 + jj + 1) * 128], b.hT.a[:, c, :],
                         c == 0, c == 7, [wrg, b.hT], [pb])
            pv = pb.a[:, 0:nj * TB].rearrange("p (j t) -> p j t", j=nj)
            om_b = omu.a[:, j0:j0 + nj, None].broadcast_to([128, nj, TB])
            mu_b = mu.a[:, j0:j0 + nj, None].broadcast_to([128, nj, TB - 1])
            B.tt("dve", b.t1.a[:, 0:nj, :], pv, om_b, ALU.mult, [pb, omu], [b.t1])
            B.tt("dve", b.t2.a[:, 0:nj, 1:TB], pv[:, :, 0:TB - 1], mu_b, ALU.mult, [pb, mu], [b.t2])
            B.tt("dve", b.t2.a[:, 0:nj, 0:1], pc.a[:, j0:j0 + nj, None], mu.a[:, j0:j0 + nj, None], ALU.mult, [pc, mu], [b.t2])
            B.cp("act", pc.a[:, j0:j0 + nj, None], pv[:, :, TB - 1:TB], [pb], [pc])
            B.tt("pool", PS.a[:, j0:j0 + nj, :], b.t1.a[:, 0:nj, :], b.t2.a[:, 0:nj, :], ALU.add, [b.t1, b.t2], [PS])
            yield
        if "pshift" in dbo:
            S.dma("sp", dbo["pshift"].a[:, :, ci * TB:(ci + 1) * TB], PS.a, reads=[PS], writes=[dbo["pshift"]], dreg=PS)
        r_ = PS.a[:, 0:4, :]; k_ = PS.a[:, 4:8, :]; v_ = PS.a[:, 8:12, :]
        B.act(b.twa.a[0:64, :], PS.a[0:64, 12, :], AF.Tanh, [PS], [b.twa])
        B.cp("act", b.twa.a[64:128, :], PS.a[64:128, 12, :], [PS], [b.twa])
        B.act(b.sgd.a, PS.a[:, 13, :], AF.Sigmoid, [PS], [b.sgd])
        pw_ = B.ps()
        for j in range(4):
            B.mm(pw_.a[:, j * TB:(j + 1) * TB], w2p.a[:, j * 128:(j + 1) * 128], b.twa.a, True, True, [w2p, b.twa], [pw_])
        for j in range(4):
            B.act(sgw.a[:, j, :], pw_.a[:, j * TB:(j + 1) * TB], AF.Sigmoid, [pw_, w0c], [sgw], bias=w0c.a[:, j:j + 1])
        pa_ = B.ps()
        for j in range(4):
            B.mm(pa_.a[:, j * TB:(j + 1) * TB], a2p.a[:, j * 128:(j + 1) * 128], b.twa.a, True, True, [a2p, b.twa], [pa_])
        for j in range(4):
            B.act(asg.a[:, j, :], pa_.a[:, j * TB:(j + 1) * TB], AF.Sigmoid, [pa_, a0c], [asg], bias=a0c.a[:, j:j + 1])
        pg_ = B.ps()
        for j in range(4):
            B.mm(pg_.a[:, j * TB:(j + 1) * TB], g2b.a[:, j * 128:(j + 1) * 128], b.sgd.a, True, True, [g2b, b.sgd], [pg_])
        B.cp("act", gg.a, pg_.a.rearrange("p (j t) -> p j t", j=4), [pg_], [gg])
        yield
        B.tt("dve", kk.a, k_, bc4(kkc), ALU.mult, [PS, kkc], [kk])
        B.tt("pool", tq.a, kk.a, kk.a, ALU.mult, [kk], [tq])
        pb = B.ps()
        for j in range(4):
            B.mm(pb.a[:, j * TB:(j + 1) * TB], bones.a, tq.a[:, j, :], True, True, [bones, tq], [pb])
        B.act(cum.a, pb.a.rearrange("p (j t) -> p j t", j=4), AF.Ln, [pb, epsc], [cum], bias=epsc.a[:, 1:2])
        B.act(cum.a, cum.a, AF.Exp, [cum], [cum], scale=-0.5)
        B.tt("pool", kk.a, kk.a, cum.a, ALU.mult, [kk, cum], [kk])
        yield
        B.tt("dve", tq.a, asg.a, bc4(kac), ALU.mult, [asg, kac], [tq])
        B.tt("dve", tq.a, tq.a, bc4(oka), ALU.add, [tq, oka], [tq])
        B.tt("dve", kmod.a, k_, tq.a, ALU.mult, [PS, tq], [kmod])
        B.tt("pool", tq.a, r_, kmod.a, ALU.mult, [PS, kmod], [tq])
        B.tt("dve", tq.a, tq.a, bc4(rkc), ALU.mult, [tq, rkc], [tq])
        pbon = B.ps()
        for j in range(4):
            B.mm(pbon.a[:, j * TB:(j + 1) * TB], bones.a, tq.a[:, j, :], True, True, [bones, tq], [pbon])
        B.tt("dve", bon.a, pbon.a.rearrange("p (j t) -> p j t", j=4), v_, ALU.mult, [pbon, PS], [bon])
        yield
        for j in range(4):
            B.op("dve", lambda e, j=j: e.tensor_tensor_scan(out=cum.a[:, j, :], data0=ones.a, data1=sgw.a[:, j, :], initial=0.0,
                                                            op0=ALU.mult, op1=ALU.add), [ones, sgw], [cum])
        B.act(eg.a, cum.a, AF.Exp, [cum], [eg], scale=-C0)
        B.act(eng.a, cum.a, AF.Exp, [cum], [eng], scale=C0)
        B.tt("pool", tq.a, cum.a, sgw.a, ALU.subtract, [cum, sgw], [tq])
        B.act(egx.a, tq.a, AF.Exp, [tq], [egx], scale=-C0)
        yield
        B.stt(tq.a, kk.a, -1.0, egx.a, ALU.mult, ALU.mult, [kk, egx], [tq])
        for hh in range(2):
            B.act(am.a[:, :, hh, :], tq.a, AF.Copy, [tq, hm], [am], scale=hm.a[:, hh:hh + 1])
        B.tt("dve", egx.a, r_, eg.a, ALU.mult, [PS, eg], [egx])
        for hh in range(2):
            B.act(rm.a[:, :, hh, :], egx.a, AF.Copy, [egx, hm], [rm], scale=hm.a[:, hh:hh + 1])
        B.tt("pool", tq.a, kk.a, asg.a, ALU.mult, [kk, asg], [tq])
        B.tt("dve", b.bf.a, tq.a, eng.a, ALU.mult, [tq, eng], [b.bf])
        B.tt("dve", b.kf.a, kmod.a, eng.a, ALU.mult, [kmod, eng], [b.kf])
        B.cp("act", b.vb.a, v_, [PS], [b.vb])
        yield
        cs = slice(0, C)
        for n_, (src, dst) in enumerate(((b.bf, b.Btok), (b.kf, b.Ktok), (b.vb, b.Vtok))):
            pb = B.ps()
            pv = pb.a.bitcast(BF16)
            for j in range(4):
                B.tr(pv[:, j * 128:(j + 1) * 128], src.a[:, j, cs], ident_b.a, [src, ident_b], [pb])
            B.cp("act" if n_ != 1 else "dve", dst.a, pv[:, 0:512], [pb], [dst])
        yield
        for g in range(2):
            gs = slice(4 * g, 4 * g + 4)
            for kind in range(5):
                pbk = B.ps()
                for hq in range(4):
                    h = 4 * g + hq
                    j, hh = h // 2, h % 2
                    o = slice(hq * 128, (hq + 1) * 128)
                    bfs, kfs, ams, rms = b.bf.a[:, j, cs], b.kf.a[:, j, cs], am.a[:, j, hh, cs], rm.a[:, j, hh, cs]
                    lhsT, rhs, rd = ((bfs, ams, [b.bf, am]), (ams, bfs, [b.bf, am]), (kfs, ams, [b.kf, am]),
                                     (bfs, rms, [b.bf, rm]), (kfs, rms, [b.kf, rm]))[kind]
                    B.mm(pbk.a[:, o], lhsT, rhs, True, True, rd, [pbk])
                msk, dst = ((MU4, Pm), (ML4, Qm), (MU4, Nak), (MI4, Nrb), (MI4, Nrk))[kind]
                B.tt("dve", dst.a[:, gs, :], pbk.a.rearrange("p (h t) -> p h t", h=4), msk.a, ALU.mult, [pbk, msk], [dst])
            B.tt("pool", Rm.a[:, gs, :], Pm.a[:, gs, :], I4.a, ALU.add, [Pm, I4], [Rm])
            yield
        for lvl in range(1, 7):
            for g in range(2):
                gs = slice(4 * g, 4 * g + 4)
                pq = B.ps()
                pp = B.ps() if lvl < 6 else None
                for hq in range(4):
                    h = 4 * g + hq
                    o = slice(hq * 128, (hq + 1) * 128)
                    B.mm(pq.a[:, o], Pm.a[:, h, :], Qm.a[:, h, :], True, True, [Pm, Qm], [pq])
                    if pp is not None:
                        B.mm(pp.a[:, o], Qm.a[:, h, :], Pm.a[:, h, :], True, True, [Pm, Qm], [pp])
                B.cp("act", Qm.a[:, gs, :], pq.a.rearrange("p (h t) -> p h t", h=4), [pq], [Qm])
                if pp is not None:
                    B.cp("act", Pm.a[:, gs, :], pp.a.rearrange("p (h t) -> p h t", h=4), [pp], [Pm])
                pr = B.ps()
                for hq in range(4):
                    h = 4 * g + hq
                    o = slice(hq * 128, (hq + 1) * 128)
                    B.mm(pr.a[:, o], Qm.a[:, h, :], Rm.a[:, h, :], True, True, [Qm, Rm], [pr])
                B.tt("dve", Rm.a[:, gs, :], pr.a.rearrange("p (h t) -> p h t", h=4), Rm.a[:, gs, :], ALU.add, [pr, Rm], [Rm])
                yield
        px = B.ps()
        for h in range(8):
            j, hh = h // 2, h % 2
            o = slice(h * 64, (h + 1) * 64)
            B.mm(px.a[:, o], am.a[:, j, hh, cs], Hb.a[:, j, :], True, False, [am, Hb], [px])
            B.mm(px.a[:, o], Nak.a[:, h, :], b.Vtok.a[:, o], False, True, [Nak, b.Vtok], [px])
        B.cp("act", b.Xb.a, px.a, [px], [b.Xb])
        pu = B.ps()
        for h in range(8):
            o = slice(h * 64, (h + 1) * 64)
            B.mm(pu.a[:, o], Rm.a[:, h, :], b.Xb.a[:, o], True, True, [Rm, b.Xb], [pu])
        B.cp("act", b.Ub.a, pu.a, [pu], [b.Ub])
        ph = B.ps()
        for h in range(8):
            j, hh = h // 2, h % 2
            o = slice(h * 64, (h + 1) * 64)
            ho = ph.a[hh * 64:(hh + 1) * 64, j * 64:(j + 1) * 64]
            B.mm(ho, b.Btok.a[:, o], b.Ub.a[:, o], True, False, [b.Btok, b.Ub], [ph])
            B.mm(ho, b.Ktok.a[:, o], b.Vtok.a[:, o], False, True, [b.Ktok, b.Vtok], [ph])
        py = B.ps()
        for h in range(8):
            j, hh = h // 2, h % 2
            o = slice(h * 64, (h + 1) * 64)
            yo = py.a[hh * 64:(hh + 1) * 64, j * 128:(j + 1) * 128]
            B.mm(yo, Hb.a[:, j, :], rm.a[:, j, hh, cs], True, False, [Hb, rm], [py])
            B.mm(yo, b.Ub.a[:, o], Nrb.a[:, h, :], False, False, [b.Ub, Nrb], [py])
            B.mm(yo, b.Vtok.a[:, o], Nrk.a[:, h, :], False, True, [b.Vtok, Nrk], [py])
        B.tt("dve", Ht.a, ph.a[:, 0:256].rearrange("p (j i) -> p j i", j=4), H32.a, ALU.add, [ph, H32], [Ht])
        gC = eg.a[:, :, C - 1:C].broadcast_to([128, 4, 64])
        B.tt("dve", H32.a, Ht.a, gC, ALU.mult, [Ht, eg], [H32])
        B.cp("act", Hb.a, H32.a, [H32], [Hb])
        B.cp("act", b.Yf.a, py.a.rearrange("p (j t) -> p j t", j=4), [py], [b.Yf])
        yield
        if "wkv" in dbo:
            S.dma("sp", dbo["wkv"].a[:, :, ci * TB:(ci + 1) * TB], b.Yf.a, reads=[b.Yf], writes=[dbo["wkv"]], dreg=b.Yf)
        Yf, dd = b.Yf, b.dd
        pm_ = B.ps()
        for j in range(4):
            B.mm(pm_.a[:, j * TB:(j + 1) * TB], bavg.a, Yf.a[:, j, :], True, True, [bavg, Yf], [pm_])
        B.tt("dve", dd.a, Yf.a, pm_.a.rearrange("p (j t) -> p j t", j=4), ALU.subtract, [Yf, pm_], [dd])
        B.act(tq.a, dd.a, AF.Square, [dd], [tq])
        pv_ = B.ps()
        for j in range(4):
            B.mm(pv_.a[:, j * TB:(j + 1) * TB], bavg.a, tq.a[:, j, :], True, True, [bavg, tq], [pv_])
        B.act(cum.a, pv_.a.rearrange("p (j t) -> p j t", j=4), AF.Ln, [pv_, epsc], [cum], bias=epsc.a[:, 2:3])
        B.act(cum.a, cum.a, AF.Exp, [cum], [cum], scale=-0.5)
        yield
        B.tt("pool", dd.a, dd.a, cum.a, ALU.mult, [dd, cum], [dd])
        B.tt("dve", dd.a, dd.a, bc4(lwc), ALU.mult, [dd, lwc], [dd])
        B.tt("dve", dd.a, dd.a, bc4(lbc), ALU.add, [dd, lbc], [dd])
        B.tt("pool", dd.a, dd.a, bon.a, ALU.add, [dd, bon], [dd])
        B.tt("dve", b.yrb.a, dd.a, gg.a, ALU.mult, [dd, gg], [b.yrb])
        if "rwkv_y" in dbo:
            S.dma("sp", dbo["rwkv_y"].a[:, :, ci * TB:(ci + 1) * TB], b.yrb.a, reads=[b.yrb], writes=[dbo["rwkv_y"]], dreg=b.yrb)
        S.dma("sp", YR.a[:, :, ci * TB:(ci + 1) * TB], b.yrb.a, reads=[b.yrb], writes=[YR], dreg=b.yrb)
        yield

    pools = [[0, 1, 2], [3, 4, 5], [6, 7]] if NW == 3 else ([[0, 1, 2, 3], [4, 5, 6, 7]] if NW == 2 else [list(range(8))])
    S.rec = []
    for ci in range(ntiles):
        B.ps_pool = pools[ci % NW]
        for _ in chunk(ci):
            pass
    items = S.rec
    S.rec = None
    B.ps_pool = None
    S.schedule_emit(items, window=int(os.environ.get("K_WIN", "1400")))
    B.pop()


def phase_1c(B, W, x, x1s, H2T, YR, YGLU, epsc, gpre, frontend, bc_load, col_load, dbo, ntiles):
    nc, S = B.nc, B.S
    TB = 512
    NS = TB // 128
    B.push()
    wg = B.sb("wg", [128, 8, 2048], BF16)
    win = W["w_in"].a.rearrange("(c p) n -> p c n", p=128)
    for c in range(8):
        S.dma("pool", wg.a[:, c, :], win[:, c, NR + 512:NIN], reads=[W["w_in"]], writes=[wg])
    wbr = B.sb("wbr", [128, 4, D], BF16); wbs = B.sb("wbs", [128, 4, D], BF16); wout = B.sb("wout", [128, 8, D], BF16)
    S.dma("pool", wbr.a, W["w_branch_rwkv"].a.rearrange("(c p) n -> p c n", p=128), reads=[W["w_branch_rwkv"]], writes=[wbr])
    S.dma("pool", wbs.a, W["w_branch_s5"].a.rearrange("(c p) n -> p c n", p=128), reads=[W["w_branch_s5"]], writes=[wbs])
    for c in range(8):
        S.dma("pool", wout.a[:, c, :], W["w_out"].a[c * 128:(c + 1) * 128, :], reads=[W["w_out"]], writes=[wout])
    gpost = bc_load("norm_mix_post", D)
    gffn = bc_load("norm_ffn_pre", D)
    bgc = col_load("b_gate", 16)
    class BS:
        pass
    sets = []
    for w in range(2):
        q = BS()
        q.xt = B.sb("xt%d" % w, [128, NS, D]); q.hb = B.sb("hb%d" % w, [128, NS, D], BF16); q.hT = B.sb("hT%d" % w, [128, 8, TB], BF16)
        q.scr = B.sb("scr%d" % w, [128, D]); q.st = B.sb("st%d" % w, [128, 12])
        q.yrt = B.sb("yrt%d" % w, [128, 4, TB], BF16); q.ygt = B.sb("ygt%d" % w, [128, 4, TB], BF16)
        q.mixb = B.sb("mixb%d" % w, [128, 8, TB], BF16)
        sets.append(q)
    gA = B.sb("gA", [128, TB]); gB = B.sb("gB", [128, TB]); mt1 = B.sb("mt1", [128, TB]); mt2 = B.sb("mt2", [128, TB])

    def part_a(ti):
        q = sets[ti % 2]
        frontend(x, ti, NS, gpre, q.xt, q.hb, q.hT, q.scr, q.st)
        S.dma("sp", q.yrt.a, YR.a[:, :, ti * TB:(ti + 1) * TB], reads=[YR], writes=[q.yrt])
        S.dma("sp", q.ygt.a, YGLU.a[:, :, ti * TB:(ti + 1) * TB], reads=[YGLU], writes=[q.ygt])

    def part_b(ti):
        q = sets[ti % 2]
        xt, hb, hT, scr, st, yrt, ygt, mixb = q.xt, q.hb, q.hT, q.scr, q.st, q.yrt, q.ygt, q.mixb
        for cb in range(8):
            pa = B.ps(); pbb = B.ps(); po = B.ps(); ps_ = B.ps()
            for c in range(8):
                B.mm(pa.a, wg.a[:, c, cb * 128:(cb + 1) * 128], hT.a[:, c, :], c == 0, c == 7, [wg, hT], [pa])
            for c in range(8):
                B.mm(pbb.a, wg.a[:, c, (8 + cb) * 128:(9 + cb) * 128], hT.a[:, c, :], c == 0, c == 7, [wg, hT], [pbb])
            for j in range(4):
                B.mm(po.a, wbr.a[:, j, cb * 128:(cb + 1) * 128], yrt.a[:, j, :], j == 0, j == 3, [wbr, yrt], [po])
            for j in range(4):
                B.mm(ps_.a, wbs.a[:, j, cb * 128:(cb + 1) * 128], ygt.a[:, j, :], j == 0, j == 3, [wbs, ygt], [ps_])
            B.act(gA.a, pa.a, AF.Sigmoid, [pa, bgc], [gA], bias=bgc.a[:, cb:cb + 1])
            B.act(gB.a, pbb.a, AF.Sigmoid, [pbb, bgc], [gB], bias=bgc.a[:, 8 + cb:9 + cb])
            B.tt("dve", mt1.a, po.a, gA.a, ALU.mult, [po, gA], [mt1])
            B.tt("dve", mt2.a, ps_.a, gB.a, ALU.mult, [ps_, gB], [mt2])
            B.tt("pool", mixb.a[:, cb, :], mt1.a, mt2.a, ALU.add, [mt1, mt2], [mixb])

    def part_c(ti):
        q = sets[ti % 2]
        xt, hb, hT, scr, st, yrt, ygt, mixb = q.xt, q.hb, q.hT, q.scr, q.st, q.yrt, q.ygt, q.mixb
        for s_ in range(NS):
            pbs = [B.ps(), B.ps()]
            for half in range(2):
                for c8 in range(8):
                    B.mm(pbs[half].a, mixb.a[:, c8, s_ * 128:(s_ + 1) * 128], wout.a[:, c8, half * 512:(half + 1) * 512],
                         c8 == 0, c8 == 7, [mixb, wout], [pbs[half]])
            for half in range(2):
                B.act(scr.a[:, 0:512], pbs[half].a, AF.Square, [pbs[half]], [scr, st], accum_out=st.a[:, half:half + 1])
            B.tt("dve", st.a[:, 2:3], st.a[:, 0:1], st.a[:, 1:2], ALU.add, [st], [st])
            B.act(st.a[:, 2:3], st.a[:, 2:3], AF.Ln, [st, epsc], [st], scale=1.0 / D, bias=epsc.a[:, 0:1])
            B.act(st.a[:, 3:4], st.a[:, 2:3], AF.Exp, [st], [st], scale=-0.5)
            for half in range(2):
                hsl = slice(half * 512, (half + 1) * 512)
                B.stt(scr.a[:, hsl], pbs[half].a, st.a[:, 3:4], gpost.a[:, hsl], ALU.mult, ALU.mult, [pbs[half], st, gpost], [scr])
            B.tt("pool", xt.a[:, s_, :], xt.a[:, s_, :], scr.a, ALU.add, [xt, scr], [xt])
        S.dma("sp", x1s.a[ti * TB:(ti + 1) * TB, :].rearrange("(s p) d -> p s d", p=128), xt.a, reads=[xt], writes=[x1s], dreg=xt)
        if "x1" in dbo:
            S.dma("sp", dbo["x1"].a[ti * TB:(ti + 1) * TB, :].rearrange("(s p) d -> p s d", p=128), xt.a, reads=[xt],
                  writes=[dbo["x1"]], dreg=xt)
        frontend(None, ti, NS, gffn, xt, hb, hT, scr, st, load=False)
        S.dma("sp", H2T.a[:, :, ti * TB:(ti + 1) * TB], hT.a, reads=[hT], writes=[H2T], dreg=hT)

    part_a(0)
    for ti in range(ntiles):
        part_b(ti)
        if ti + 1 < ntiles:
            part_a(ti + 1)
        part_c(ti)
    B.pop()


def phase_2(B, W, x1s, H2T, out, epsc, frontend, bc_load, dbo, ntiles):
    nc, S = B.nc, B.S
    TB = 512
    NS = TB // 128
    ntiles = ntiles * (512 // TB)
    B.push()
    wup = B.sb("wup", [128, 8, 2 * FF], BF16)
    wsrc = W["ffn_w_up"].a.rearrange("(c p) n -> p c n", p=128)
    for c in range(8):
        for (a, b) in ((0, 2048), (2048, 4096), (4096, 2 * FF)):
            S.dma("pool", wup.a[:, c, a:b], wsrc[:, c, a:b], reads=[W["ffn_w_up"]], writes=[wup])
    wdn = B.sb("wdn", [128, 22, D], BF16)
    for i in range(22):
        S.dma("pool", wdn.a[:, i, :], W["ffn_w_down"].a[i * 128:(i + 1) * 128, :], reads=[W["ffn_w_down"]], writes=[wdn])
    g2 = bc_load("norm_ffn_post", D)
    cw = B.sb("cw", [128, 3, 44]); cbias = B.sb("cbias", [128, 44])
    S.dma("sp", cw.a, W["ffn_conv_w"].a.rearrange("j (b p) -> p j b", p=128), reads=[W["ffn_conv_w"]], writes=[cw],
          allow_slow_non_contiguous=True)
    S.dma("sp", cbias.a, W["ffn_conv_b"].a.rearrange("(b p) -> p b", p=128), reads=[W["ffn_conv_b"]], writes=[cbias],
          allow_slow_non_contiguous=True)
    halo = B.sb("halo", [128, 44, 2]); B.memset("pool", halo.a, 0.0, [halo])
    epsc = B.sb("epsc2", [128, 1]); B.memset("pool", epsc.a, 1e-6, [epsc])
    class BS:
        pass
    sets = []
    hTs = [B.sb("hT2_%d" % w, [128, 8, TB], BF16) for w in range(2)]
    for w in range(1):
        q = BS()
        q.xt = B.sb("xt2_%d" % w, [128, NS, D])
        q.actb = B.sb("actb%d" % w, [128, 22, TB], BF16)
        q.st = B.sb("st2_%d" % w, [128, 12])
        sets.append(q)
    scrF_ = B.sb("scrF", [128, D])
    a0s_ = (B.sb("a0g", [128, TB]), B.sb("a0v", [128, TB]))
    a1s_ = ((B.sb("a1g0", [128, TB]), B.sb("a1v0", [128, TB])), (B.sb("a1g1", [128, TB]), B.sb("a1v1", [128, TB])))
    for q in sets:
        q.scrF, q.a0s, q.a1s = scrF_, a0s_, a1s_
    def tile(ti):
        q = sets[0]
        xt, actb, st, scrF, a0s, a1s = q.xt, q.actb, q.st, q.scrF, q.a0s, q.a1s
        hT = hTs[ti % 2]
        if ti == 0:
            S.dma("sp", hT.a, H2T.a[:, :, 0:TB], reads=[H2T], writes=[hT])
        if ti + 1 < ntiles:
            S.dma("sp", hTs[(ti + 1) % 2].a, H2T.a[:, :, (ti + 1) * TB:(ti + 2) * TB], reads=[H2T], writes=[hTs[(ti + 1) % 2]])
        S.dma("sp", xt.a, x1s.a[ti * TB:(ti + 1) * TB, :].rearrange("(s p) d -> p s d", p=128), reads=[x1s], writes=[xt])
        def finish(i):
            accg, accv = a1s[i % 2]
            B.act(accg.a, accg.a, AF.Gelu_apprx_tanh, [accg], [accg])
            B.tt("pool", actb.a[:, i, :], accg.a, accv.a, ALU.mult, [accg, accv], [actb])

        for i in range(22):
            accs = a1s[i % 2]
            for gv in range(2):
                b = i + 22 * gv
                pb = B.ps()
                for c in range(8):
                    B.mm(pb.a[:, 0:TB], wup.a[:, c, b * 128:(b + 1) * 128], hT.a[:, c, :], c == 0, c == 7, [wup, hT], [pb])
                acc = accs[gv]; a0 = a0s[gv]; a1 = accs[gv]
                B.act(a0.a, pb.a[:, 0:TB], AF.Identity, [pb, cw, cbias], [a0], scale=cw.a[:, 2, b:b + 1], bias=cbias.a[:, b:b + 1])
                B.act(a1.a[:, 1:TB], pb.a[:, 0:TB - 1], AF.Copy, [pb, cw], [a1], scale=cw.a[:, 1, b:b + 1])
                B.stt(a0.a[:, 2:TB], pb.a[:, 0:TB - 2], cw.a[:, 0, b:b + 1], a0.a[:, 2:TB], ALU.mult, ALU.add, [pb, cw, a0], [a0])
                B.ts("dve", a1.a[:, 0:1], halo.a[:, b, 1:2], cw.a[:, 1, b:b + 1], ALU.mult, [halo, cw], [a1])
                B.stt(a0.a[:, 0:2], halo.a[:, b, 0:2], cw.a[:, 0, b:b + 1], a0.a[:, 0:2], ALU.mult, ALU.add, [halo, cw, a0], [a0])
                B.cp("dve", halo.a[:, b, :], pb.a[:, TB - 2:TB], [pb], [halo])
                B.tt("pool", acc.a, a0.a, a1.a, ALU.add, [a0, a1], [acc])
            if "zc" in dbo and ti == 0 and i == 0:
                S.dma("sp", dbo["zc"].a[:, 0:TB], accs[0].a, reads=[accs[0]], writes=[dbo["zc"]], dreg=accs[0])
            if i > 0:
                finish(i - 1)
        finish(21)
        for s_ in range(NS):
            pbs = [B.ps(), B.ps()]
            for half in range(2):
                for i in range(22):
                    B.mm(pbs[half].a, actb.a[:, i, s_ * 128:(s_ + 1) * 128], wdn.a[:, i, half * 512:(half + 1) * 512],
                         i == 0, i == 21, [actb, wdn], [pbs[half]])
            for half in range(2):
                B.act(scrF.a[:, 0:512], pbs[half].a, AF.Square, [pbs[half]], [scrF, st], accum_out=st.a[:, half:half + 1])
            B.tt("dve", st.a[:, 2:3], st.a[:, 0:1], st.a[:, 1:2], ALU.add, [st], [st])
            B.act(st.a[:, 2:3], st.a[:, 2:3], AF.Ln, [st, epsc], [st], scale=1.0 / D, bias=epsc.a[:, 0:1])
            B.act(st.a[:, 3:4], st.a[:, 2:3], AF.Exp, [st], [st], scale=-0.5)
            for half in range(2):
                hsl = slice(half * 512, (half + 1) * 512)
                B.stt(scrF.a[:, hsl], pbs[half].a, st.a[:, 3:4], g2.a[:, hsl], ALU.mult, ALU.mult, [pbs[half], st, g2], [scrF])
            B.tt("pool", xt.a[:, s_, :], xt.a[:, s_, :], scrF.a, ALU.add, [xt, scrF], [xt])
        S.dma("sp", out.a[ti * TB:(ti + 1) * TB, :].rearrange("(s p) d -> p s d", p=128), xt.a, reads=[xt], writes=[out], dreg=xt)

    S.rec = []
    for ti in range(ntiles):
        tile(ti)
    items = S.rec
    S.rec = None
    if os.environ.get("K_P2SCHED", "0") == "1":
        S.schedule_emit(items, window=int(os.environ.get("K_WIN2", "1500")))
    else:
        for it in items:
            S.replay(it)
    B.pop()


_CACHE = {}


def kernel(**inputs):
    if "B" not in _CACHE:
        _CACHE["B"] = build()
    Bd = _CACHE["B"]
    x = np.ascontiguousarray(inputs["x"], dtype=np.float32)
    wmap = {k: np.ascontiguousarray(np.asarray(inputs[k], dtype=np.float32).reshape(shp)) for k, shp in Bd.Wshapes.items()}
    in_maps = []
    for c in range(8):
        m = dict(wmap)
        m["x"] = x[c]
        in_maps.append(m)
    res = run_bass_kernel_spmd(Bd.nc, in_maps, core_ids=list(range(8)))
    return np.stack([np.asarray(res.results[c]["out"], dtype=np.float32) for c in range(8)], axis=0)
```

```python
import contextlib
import math
import os
import numpy as np
import concourse.bass as bass
import concourse.mybir as mybir
from concourse.bass_utils import run_bass_kernel_spmd

F32 = mybir.dt.float32
BF16 = mybir.dt.bfloat16
I32 = mybir.dt.int32
ALU = mybir.AluOpType
AF = mybir.ActivationFunctionType

L = 4096
D = 1024
NR = 1792
NS5 = 512
NIN = 4352
FF = 2816
PI = math.pi


class Reg:
    __slots__ = ("name", "w", "r", "dsem", "dcnt", "last", "excl")

    def __init__(self, name=""):
        self.name = name
        self.w = None
        self.r = []
        self.dsem = None
        self.dcnt = 0
        self.last = 0
        self.excl = False


class Sched:
    def __init__(self, nc):
        self.nc = nc
        self.eng = {"pe": nc.tensor, "dve": nc.vector, "act": nc.scalar,
                    "pool": nc.gpsimd, "sp": nc.sync}
        self.sem = {k: nc.alloc_semaphore(name="sem_" + k) for k in self.eng}
        self.cnt = {k: 0 for k in self.eng}
        self.seen = {k: {} for k in self.eng}
        self.ninst = 0
        self.nds = 0
        self.dregs = []
        self.dmap = {}
        self.maxops = int(os.environ.get("K_MAXOPS", "100000000"))
        self.rec = None

    def _wait(self, e, tok):
        sem, val = tok
        key = sem.name
        if key in self.dmap:
            val = max(val, self.dmap[key].dcnt)
        if self.seen[e].get(key, 0) >= val:
            return
        if e == "pe" and sem is self.sem["pe"]:
            return
        self.eng[e].wait_ge(sem, val)
        self.seen[e][key] = val

    def _deps(self, e, reads, writes, skip=None):
        for r in reads:
            if r.w is not None:
                self._wait(e, r.w)
        for w in writes:
            if w.w is not None and w.w[0] is not skip:
                self._wait(e, w.w)
            for t in w.r:
                self._wait(e, t)

    def _commit(self, tok, reads, writes):
        for r in reads:
            r.last = self.ninst
            r.r.append(tok)
            if len(r.r) > 16:
                d = {}
                for s, v in r.r:
                    if d.get(s.name, (None, -1))[1] < v:
                        d[s.name] = (s, v)
                r.r = list(d.values())
        for w in writes:
            w.last = self.ninst
            w.w = tok
            w.r = []

    def op(self, e, fn, reads=(), writes=(), cost=None):
        if self.rec is not None:
            self.rec.append(("op", e, fn, list(reads), list(writes), None, cost))
            return None
        if self.ninst >= self.maxops:
            return None
        reads = [x.r if isinstance(x, Buf) else x for x in reads]
        writes = [x.r if isinstance(x, Buf) else x for x in writes]
        ex = [x for x in reads if x.excl and x not in writes]
        if ex:
            reads = [x for x in reads if not x.excl]
            writes = list(writes) + ex
        self._deps(e, reads, writes)
        ins = fn(self.eng[e])
        self.cnt[e] += 1
        ins.then_inc(self.sem[e], 1)
        tok = (self.sem[e], self.cnt[e])
        self._commit(tok, reads, writes)
        self.ninst += 1
        return tok

    def dma(self, e, out, in_, reads=(), writes=(), dreg=None, **kw):
        if self.rec is not None:
            self.rec.append(("dma", e, (out, in_), list(reads), list(writes), (dreg, kw), None))
            return None
        if self.ninst >= self.maxops and not kw.pop("force", False):
            return None
        kw.pop("force", None)
        reads = [x.r if isinstance(x, Buf) else x for x in reads]
        writes = [x.r if isinstance(x, Buf) else x for x in writes]
        if dreg is None:
            dreg = writes[0] if writes else reads[0]
        elif isinstance(dreg, Buf):
            dreg = dreg.r
        if dreg.dsem is None:
            self.nds += 1
            dreg.dsem = self.nc.alloc_semaphore(name="ds%d_%s" % (self.nds, dreg.name))
            self.dregs.append(dreg)
            self.dmap[dreg.dsem.name] = dreg
        self._deps(e, reads, writes, skip=dreg.dsem)
        ins = self.eng[e].dma_start(out=out, in_=in_, **kw)
        dreg.dcnt += 16
        ins.then_inc(dreg.dsem, 16)
        tok = (dreg.dsem, dreg.dcnt)
        self._commit(tok, reads, writes)
        self.ninst += 1
        return tok

    def replay(self, item):
        kind, e, a, reads, writes, extra = item[:6]
        if kind == "op":
            return self.op(e, a, reads, writes)
        dreg, kw = extra
        return self.dma(e, a[0], a[1], reads=reads, writes=writes, dreg=dreg, **kw)

    def schedule_emit(self, items, window=1500):
        import heapq
        n = len(items)
        norm = []
        for it in items:
            kind, e, a, reads, writes, extra, cost = it
            reads = [x.r if isinstance(x, Buf) else x for x in reads]
            writes = [x.r if isinstance(x, Buf) else x for x in writes]
            dreg = None
            if kind == "dma":
                dreg = extra[0]
                if dreg is None:
                    dreg = writes[0] if writes else reads[0]
                elif isinstance(dreg, Buf):
                    dreg = dreg.r
            ex = [x for x in reads if x.excl and x not in writes]
            if ex:
                reads = [x for x in reads if not x.excl]
                writes = list(writes) + ex
            norm.append((kind, e, a, reads, writes, extra, cost, dreg))
        lw = {}
        rd = {}
        first = [[] for _ in range(n)]
        preds = [set() for _ in range(n)]
        for i, (kind, e, a, reads, writes, extra, cost, dreg) in enumerate(norm):
            for r in reads:
                k = id(r)
                if k in lw:
                    preds[i].add(lw[k])
                else:
                    first[i].append((r, "r"))
            for w in writes:
                k = id(w)
                if k in lw:
                    if not (kind == "dma" and norm[lw[k]][0] == "dma" and norm[lw[k]][7] is dreg):
                        preds[i].add(lw[k])
                else:
                    first[i].append((w, "w"))
                for j in rd.get(k, ()):
                    preds[i].add(j)
            for r in reads:
                rd.setdefault(id(r), []).append(i)
            for w in writes:
                lw[id(w)] = i
                rd[id(w)] = []
            preds[i].discard(i)
        succs = [[] for _ in range(n)]
        npred = [len(p) for p in preds]
        for i, p in enumerate(preds):
            for j in p:
                succs[j].append(i)
        def dur(it):
            kind, e, a, reads, writes, extra, cost, dreg = it
            if cost is not None:
                return cost
            if kind == "op":
                return {"pe": 0.08, "dve": 0.4, "act": 0.4, "pool": 1.0, "sp": 2.0}[e]
            o = a[0]
            nbytes = 1
            for d in list(o.shape):
                nbytes *= int(d)
            nbytes *= mybir.dt.size(o.dtype)
            return 2.5 + nbytes / 140e3
        LAT = float(os.environ.get("K_LAT", "0.15"))
        free = {e: 0.0 for e in self.eng}
        fin = [0.0] * n
        ready_t = [0.0] * n
        heaps = {e: [] for e in self.eng}
        low = 0
        done = [False] * n
        avail = [False] * n
        for i in range(n):
            if npred[i] == 0:
                heapq.heappush(heaps[norm[i][1]], (0.0, i)); avail[i] = True
        order = []
        deferred = {e: [] for e in self.eng}
        while len(order) < n:
            best = None
            for e, h in heaps.items():
                while h and h[0][1] >= low + window:
                    deferred[e].append(heapq.heappop(h))
                if not h:
                    continue
                rt, i = h[0]
                st = max(rt, free[e])
                if best is None or (st, i) < (best[0], best[1]):
                    best = (st, i, e)
            if best is None:
                for e in deferred:
                    for x in deferred[e]:
                        heapq.heappush(heaps[e], x)
                    deferred[e] = []
                window *= 2
                continue
            st, i, e = best
            heapq.heappop(heaps[e])
            order.append(i)
            done[i] = True
            f = st + dur(norm[i])
            fin[i] = f
            free[e] = f
            for j in succs[i]:
                npred[j] -= 1
                ready_t[j] = max(ready_t[j], f + LAT)
                if npred[j] == 0:
                    heapq.heappush(heaps[norm[j][1]], (ready_t[j], j)); avail[j] = True
            if i == low:
                while low < n and done[low]:
                    low += 1
                for e2 in deferred:
                    keep = []
                    for x in deferred[e2]:
                        if x[1] < low + window:
                            heapq.heappush(heaps[e2], x)
                        else:
                            keep.append(x)
                    deferred[e2] = keep
        self.sched_makespan = max(fin) if n else 0.0
        toks = [None] * n
        for i in order:
            kind, e, a, reads, writes, extra, cost, dreg = norm[i]
            for (reg, mode) in first[i]:
                if reg.w is not None and not (kind == "dma" and mode == "w" and reg.w[0] is (dreg.dsem if dreg is not None else None)):
                    self._wait(e, reg.w)
                if mode == "w":
                    for t in reg.r:
                        self._wait(e, t)
            for j in preds[i]:
                self._wait(e, toks[j])
            if kind == "op":
                ins = a(self.eng[e])
                self.cnt[e] += 1
                ins.then_inc(self.sem[e], 1)
                toks[i] = (self.sem[e], self.cnt[e])
            else:
                dg, kw = extra
                kw = dict(kw); kw.pop("force", None)
                if dreg.dsem is None:
                    self.nds += 1
                    dreg.dsem = self.nc.alloc_semaphore(name="ds%d_%s" % (self.nds, dreg.name))
                    self.dregs.append(dreg)
                    self.dmap[dreg.dsem.name] = dreg
                ins = self.eng[e].dma_start(out=a[0], in_=a[1], **kw)
                dreg.dcnt += 16
                ins.then_inc(dreg.dsem, 16)
                toks[i] = (dreg.dsem, dreg.dcnt)
            self.ninst += 1
        touched = {}
        for i, it in enumerate(norm):
            for r in it[3]:
                touched[id(r)] = r
            for w in it[4]:
                touched[id(w)] = w
        for k, reg in touched.items():
            if k in lw:
                reg.w = toks[lw[k]]
                reg.r = [toks[j] for j in rd.get(k, ())]
            else:
                reg.r = list(reg.r) + [toks[j] for j in rd.get(k, ())]
            reg.last = self.ninst

    def barrier(self):
        for e in self.eng:
            for f in self.eng:
                if f != e and self.cnt[f] > 0:
                    self._wait(e, (self.sem[f], self.cnt[f]))
            for d in self.dregs:
                if d.dcnt > 0:
                    self._wait(e, (d.dsem, d.dcnt))

    def final_wait(self, e, regs):
        for r in regs:
            r = r.r if isinstance(r, Buf) else r
            if r.w is not None:
                self._wait(e, r.w)
            for t in r.r:
                self._wait(e, t)


class Buf:
    def __init__(self, t, name):
        self.t = t
        self.a = t.ap()
        self.r = Reg(name)


class Builder:
    def __init__(self, dbg=None, ntiles=8):
        self.nc = bass.Bass("TRN2", target_bir_lowering=False)
        self.S = Sched(self.nc)
        self.dbg = dbg or {}
        self.ntiles = ntiles
        self.din = {}
        self.dout = {}
        self.nbuf = 0
        self.psb = None
        self.psi = 0
        self.ps_pool = None
        self.pclock = 0
        self.scopes = []

    def push(self):
        self.scopes.append(contextlib.ExitStack())

    def pop(self):
        self.S.barrier()
        self.scopes.pop().close()

    def inp(self, name, shape):
        b = Buf(self.nc.dram_tensor(name, list(shape), F32, kind="ExternalInput"), name)
        self.din[name] = b
        return b

    def outp(self, name, shape, dt=F32):
        b = Buf(self.nc.dram_tensor(name, list(shape), dt, kind="ExternalOutput"), name)
        self.dout[name] = b
        return b

    def sb(self, name, shape, dt=F32):
        self.nbuf += 1
        nm = "%s_%d" % (name, self.nbuf)
        if self.scopes:
            return Buf(self.scopes[-1].enter_context(self.nc.sbuf_tensor(nm, list(shape), dt)), name)
        return Buf(self.nc.alloc_sbuf_tensor(nm, list(shape), dt), name)

    def view(self, buf, ap):
        v = Buf.__new__(Buf)
        v.t = buf.t
        v.a = ap
        v.r = buf.r
        return v

    def init_psum(self):
        self.psb = []
        for i in range(8):
            t = self.nc.alloc_psum_tensor("psb%d" % i, [128, 512], F32)
            self.psb.append(Buf(t, "psb%d" % i))
            self.psb[-1].r.excl = True

    def ps(self):
        if self.ps_pool is not None:
            b = self.psb[self.ps_pool[self.psi % len(self.ps_pool)]]
            self.psi += 1
            return b
        b = min(self.psb, key=lambda t: t.r.last)
        self.pclock = max(self.pclock, self.S.ninst) + 1
        b.r.last = self.pclock
        return b

    def op(self, e, fn, reads=(), writes=(), cost=None):
        return self.S.op(e, fn, reads, writes, cost=cost)

    @staticmethod
    def fsz(ap):
        n = 1
        for d in list(ap.shape)[1:]:
            n *= int(d)
        return n

    def mm(self, out, lhsT, rhs, start, stop, reads, writes, **kw):
        return self.op("pe", lambda e: e.matmul(out, lhsT=lhsT, rhs=rhs, start=start, stop=stop, **kw), reads, writes,
                       cost=0.03 + max(self.fsz(rhs), 64) * 0.00052)

    def tr(self, out, in_, ident, reads, writes):
        return self.op("pe", lambda e: e.transpose(out, in_, ident), reads, writes, cost=0.1)

    def act(self, eng_out, in_, func, reads, writes, **kw):
        return self.op("act", lambda e: e.activation(out=eng_out, in_=in_, func=func, **kw), reads, writes,
                       cost=0.22 + self.fsz(in_) * 0.00075)

    def tt(self, e, out, in0, in1, op, reads, writes):
        c = (0.1 + self.fsz(in0) * 0.00115) if e == "dve" else (0.6 + self.fsz(in0) * 0.0013)
        return self.op(e, lambda g: g.tensor_tensor(out=out, in0=in0, in1=in1, op=op), reads, writes, cost=c)

    def ts(self, e, out, in0, s1, op0, reads, writes, s2=None, op1=None):
        c = (0.1 + self.fsz(in0) * 0.0008) if e == "dve" else (0.6 + self.fsz(in0) * 0.0013)
        if op1 is None:
            return self.op(e, lambda g: g.tensor_scalar(out=out, in0=in0, scalar1=s1, scalar2=None, op0=op0), reads, writes, cost=c)
        return self.op(e, lambda g: g.tensor_scalar(out=out, in0=in0, scalar1=s1, scalar2=s2, op0=op0, op1=op1), reads, writes, cost=c)

    def stt(self, out, in0, scalar, in1, op0, op1, reads, writes):
        return self.op("dve", lambda g: g.scalar_tensor_tensor(out=out, in0=in0, scalar=scalar, in1=in1, op0=op0, op1=op1), reads, writes,
                       cost=0.1 + self.fsz(in0) * 0.00115)

    def cp(self, e, out, in_, reads, writes):
        if e == "act":
            return self.op("act", lambda g: g.activation(out=out, in_=in_, func=AF.Copy), reads, writes, cost=0.22 + self.fsz(in_) * 0.00075)
        c = (0.1 + self.fsz(in_) * 0.0008) if e == "dve" else (0.6 + self.fsz(in_) * 0.0013)
        return self.op(e, lambda g: g.tensor_copy(out, in_), reads, writes, cost=c)

    def memset(self, e, out, val, writes):
        return self.op(e, lambda g: g.memset(out, val), (), writes)

    def cmul(self, e, o_re, o_im, a_re, a_im, b_re, b_im, t1, t2, reads, writes, tmp):
        R = list(reads)
        self.tt(e, t1, a_re, b_re, ALU.mult, R, [tmp])
        self.tt(e, t2, a_im, b_im, ALU.mult, R, [tmp])
        self.tt(e, o_re, t1, t2, ALU.subtract, [tmp], writes)
        self.tt(e, t1, a_re, b_im, ALU.mult, R, [tmp])
        self.tt(e, t2, a_im, b_re, ALU.mult, R, [tmp])
        self.tt(e, o_im, t1, t2, ALU.add, [tmp], writes)


def build(dbg=None, ntiles=8, phases=("1a", "1b", "1c", "2")):
    B = Builder(dbg, ntiles)
    nc, S = B.nc, B.S
    dbg = B.dbg

    x = B.inp("x", [L, D])
    shapes = {
        "norm_mix_pre": [D], "norm_mix_post": [D], "norm_ffn_pre": [D], "norm_ffn_post": [D],
        "w_in": [D, NIN], "b_gate": [2048], "rwkv_shift_mu": [NR], "rwkv_w0": [512],
        "rwkv_w2": [64, 512], "rwkv_a0": [512], "rwkv_a2": [64, 512], "rwkv_g2": [128, 512],
        "rwkv_k_k": [512], "rwkv_k_a": [512], "rwkv_r_k": [512], "rwkv_lnx_w": [512],
        "rwkv_lnx_b": [512], "s5_a_re": [32, 64], "s5_a_im": [32, 64], "s5_b_re": [32, 64, 16],
        "s5_b_im": [32, 64, 16], "s5_c_re": [32, 16, 64], "s5_c_im": [32, 16, 64], "s5_d": [512],
        "s5_log_step": [32], "s5_w_glu": [512, 512], "s5_b_glu": [512], "w_branch_rwkv": [512, D],
        "w_branch_s5": [512, D], "w_out": [D, D], "ffn_w_up": [D, 2 * FF], "ffn_conv_w": [3, 2 * FF],
        "ffn_conv_b": [2 * FF], "ffn_w_down": [FF, D],
    }
    W = {k: B.inp(k, v) for k, v in shapes.items()}
    out = B.outp("out", [L, D])
    dbo = {k: B.outp("dbg_" + k, shp, dt) for k, (shp, dt) in dbg.items()}

    B.init_psum()

    B.push()
    ident_f = B.sb("ident_f", [128, 128], F32)
    ident_b = B.sb("ident_b", [128, 128], BF16)
    B.memset("pool", ident_f.a, 1.0, [ident_f])
    B.op("pool", lambda e: e.affine_select(out=ident_f.a, in_=ident_f.a, pattern=[[-1, 128]],
                                           compare_op=ALU.is_equal, fill=0.0, base=0, channel_multiplier=1),
         [ident_f], [ident_f])
    B.cp("pool", ident_b.a, ident_f.a, [ident_f], [ident_b])
    epsc = B.sb("epsc", [128, 4])
    B.memset("pool", epsc.a[:, 0:1], 1e-6, [epsc]); B.memset("pool", epsc.a[:, 1:2], 1e-24, [epsc])
    B.memset("pool", epsc.a[:, 2:3], 64e-5, [epsc]); B.memset("pool", epsc.a[:, 3:4], 0.0, [epsc])

    def bc_load(name, n, q="sp"):
        t = B.sb(name + "_bc", [128, n], F32)
        S.dma(q, t.a, W[name].a.partition_broadcast(128), reads=[W[name]], writes=[t])
        return t

    def col_load(name, nt, q="sp"):
        t = B.sb(name + "_col", [128, nt], F32)
        S.dma(q, t.a, W[name].a.rearrange("(t p) -> p t", p=128), reads=[W[name]], writes=[t],
              allow_slow_non_contiguous=True)
        return t

    def frontend(src, ti, ntok_tiles, gbc, xt, hb, hT, scr, st, load=True):
        nt = ntok_tiles
        T = 128 * nt
        if load:
            S.dma("sp", xt.a, src.a[ti * T:(ti + 1) * T, :].rearrange("(s p) d -> p s d", p=128),
                  reads=[src], writes=[xt])
        for s in range(nt):
            B.act(scr.a, xt.a[:, s, :], AF.Square, [xt], [scr, st], accum_out=st.a[:, s:s + 1])
        B.act(st.a[:, nt:2 * nt], st.a[:, 0:nt], AF.Ln, [st, epsc], [st], scale=1.0 / D, bias=epsc.a[:, 0:1])
        B.act(st.a[:, 2 * nt:3 * nt], st.a[:, nt:2 * nt], AF.Exp, [st], [st], scale=-0.5)
        for s in range(nt):
            B.stt(hb.a[:, s, :], xt.a[:, s, :], st.a[:, 2 * nt + s:2 * nt + s + 1], gbc.a, ALU.mult, ALU.mult,
                  [xt, st, gbc], [hb])
        for c in range(8):
            pb = B.ps()
            pv = pb.a.bitcast(BF16)
            for s in range(nt):
                B.tr(pv[:, s * 128:(s + 1) * 128], hb.a[:, s, c * 128:(c + 1) * 128], ident_b.a,
                     [hb, ident_b], [pb])
            B.cp("act" if c % 2 == 0 else "dve", hT.a[:, c, :], pv[:, 0:T], [pb], [hT])

    T1 = 512
    YGLU = Buf(nc.dram_tensor("yglu_d", [128, 4, L], BF16, kind="Internal"), "yglu_d")

    gpre = bc_load("norm_mix_pre", D)
    x1s = Buf(nc.dram_tensor("x1s", [L, D], F32, kind="Internal"), "x1s")
    YR = Buf(nc.dram_tensor("yr_d", [128, 4, L], BF16, kind="Internal"), "yr_d")
    H2T = Buf(nc.dram_tensor("h2t_d", [128, 8, L], BF16, kind="Internal"), "h2t_d")
    if "1a" in phases:
        B.push()
        ws5 = B.sb("ws5", [128, 8, 512], BF16)
        S.dma("pool", ws5.a, W["w_in"].a.rearrange("(c p) n -> p c n", p=128)[:, :, NR:NR + 512],
              reads=[W["w_in"]], writes=[ws5])
        wglu = B.sb("wglu", [128, 4, 512], BF16)
        S.dma("pool", wglu.a, W["s5_w_glu"].a.rearrange("(c p) n -> p c n", p=128), reads=[W["s5_w_glu"]], writes=[wglu])
        bglu = col_load("s5_b_glu", 4)
        dcol = col_load("s5_d", 4)

        MS = B.sb("ms", [128, 16, 2, 2])
        ER = B.sb("ER", [128, 16, 64]); EI = B.sb("EI", [128, 16, 64]); R8 = B.sb("R8", [128, 16])
        Wt = B.sb("Wt", [128, 4, 8, 2, 128], BF16)
        CAW = B.sb("CAW", [128, 16, 2, 9, 64], BF16)
        mk = B.sb("mk", [128, 2])
        B.memset("pool", mk.a[:, 0:1], 0.0, [mk]); B.memset("pool", mk.a[0:32, 0:1], 1.0, [mk]); B.memset("pool", mk.a[64:96, 0:1], 1.0, [mk])
        mk4 = B.sb("mk4", [128, 4])
        B.memset("pool", mk4.a, 0.0, [mk4])
        B.memset("pool", mk4.a[0:32, 0:1], 1.0, [mk4]); B.memset("pool", mk4.a[32:64, 1:2], 1.0, [mk4])
        B.memset("pool", mk4.a[64:96, 2:3], 1.0, [mk4]); B.memset("pool", mk4.a[64:128, 3:4], 1.0, [mk4]); B.memset("pool", mk4.a[64:96, 3:4], 0.0, [mk4])
        B.memset("pool", mk.a[:, 1:2], 1.0, [mk]); B.memset("pool", mk.a[0:32, 1:2], 0.0, [mk]); B.memset("pool", mk.a[64:96, 1:2], 0.0, [mk])
        Kbd = B.sb("Kbd", [128, 4, 8, 128], BF16)
        B.push()
        are = B.sb("are", [128, 16]); aim = B.sb("aim", [128, 16]); ls = B.sb("ls", [128, 16])
        for gl in range(2):
            S.dma("sp", are.a[gl * 64:(gl + 1) * 64, :], W["s5_a_re"].a.rearrange("(q gl) n -> gl n q", gl=2)[gl],
                  reads=[W["s5_a_re"]], writes=[are], allow_slow_non_contiguous=True)
            S.dma("sp", aim.a[gl * 64:(gl + 1) * 64, :], W["s5_a_im"].a.rearrange("(q gl) n -> gl n q", gl=2)[gl],
                  reads=[W["s5_a_im"]], writes=[aim], allow_slow_non_contiguous=True)
            S.dma("sp", ls.a[gl * 64:(gl + 1) * 64, :],
                  W["s5_log_step"].a.rearrange("(q gl) -> gl q", gl=2)[gl].partition_broadcast(64),
                  reads=[W["s5_log_step"]], writes=[ls], allow_slow_non_contiguous=True)
        braw = [B.sb("braw%d" % i, [128, 16, 16]) for i in range(2)]
        for i, nm in enumerate(("s5_b_re", "s5_b_im")):
            S.dma("sp", braw[i].a, W[nm].a.rearrange("(q gl) n c -> (gl n) q c", gl=2), reads=[W[nm]], writes=[braw[i]])
        craw = [B.sb("craw%d" % i, [128, 16, 16]) for i in range(2)]
        ctmp = B.sb("ctmp", [128, 128])
        for i, nm in enumerate(("s5_c_re", "s5_c_im")):
            for blk in range(2):
                src = W[nm].a.rearrange("(b qq gl) c n -> b qq c gl n", b=2, gl=2)[blk]
                for qq in range(8):
                    S.dma("sp", ctmp.a[qq * 16:(qq + 1) * 16, :].rearrange("p (gl n) -> p gl n", gl=2),
                          src[qq], reads=[W[nm]], writes=[ctmp])
                pb = B.ps()
                B.tr(pb.a[:, 0:128], ctmp.a, ident_f.a, [ctmp, ident_f], [pb])
                B.cp("dve", craw[i].a[:, blk * 8:(blk + 1) * 8, :],
                     pb.a[:, 0:128].rearrange("p (qq c) -> p qq c", c=16), [pb], [craw[i]])

        tm = B.sb("s5tmp", [128, 12, 32])
        tmr = tm.r

        def row(i, n=16):
            return tm.a[:, i, 0:n]

        dt_ = row(0)
        B.act(dt_, ls.a, AF.Exp, [ls], [tm])
        xr = row(1)
        B.tt("dve", xr, are.a, dt_, ALU.mult, [are, tm], [tm])
        rho = row(2)
        B.ts("dve", rho, xr, 1.0 / 720, ALU.mult, [tm], [tm], s2=1.0 / 120, op1=ALU.add)
        for cf in (1.0 / 24, 1.0 / 6, 0.5, 1.0, 1.0):
            B.tt("dve", rho, rho, xr, ALU.mult, [tm], [tm])
            B.ts("dve", rho, rho, cf, ALU.add, [tm], [tm])
        th2 = tm.a[:, 3, :]
        B.tt("dve", th2[:, 0:16], aim.a, dt_, ALU.mult, [aim, tm], [tm])
        B.ts("dve", th2[:, 16:32], th2[:, 0:16], PI / 2, ALU.add, [tm], [tm])
        kf = tm.a[:, 4, :]
        B.ts("dve", kf, th2, 1.0 / (2 * PI), ALU.mult, [tm], [tm])
        ki = B.sb("ki", [128, 32], I32)
        B.cp("dve", ki.a, kf, [tm], [ki])
        B.cp("dve", kf, ki.a, [ki], [tm])
        r1 = tm.a[:, 5, :]
        B.stt(r1, kf, -2 * PI, th2, ALU.mult, ALU.add, [tm], [tm])
        B.ts("dve", kf, r1, PI, ALU.is_gt, [tm], [tm], s2=-2 * PI, op1=ALU.mult)
        B.tt("dve", r1, r1, kf, ALU.add, [tm], [tm])
        B.ts("dve", kf, r1, -PI, ALU.is_lt, [tm], [tm], s2=2 * PI, op1=ALU.mult)
        B.tt("dve", r1, r1, kf, ALU.add, [tm], [tm])
        sc = tm.a[:, 6, :]
        B.act(sc, r1, AF.Sin, [tm], [tm])
        n2 = row(7)
        B.tt("dve", kf, sc, sc, ALU.mult, [tm], [tm])
        B.tt("dve", n2, kf[:, 0:16], kf[:, 16:32], ALU.add, [tm], [tm])
        B.ts("dve", n2, n2, -0.5, ALU.mult, [tm], [tm], s2=1.5, op1=ALU.add)
        B.tt("dve", n2, n2, rho, ALU.mult, [tm], [tm])
        PW = B.sb("pw", [128, 9, 2, 16])
        B.memset("dve", PW.a[:, 0, 0, :], 1.0, [PW])
        B.memset("dve", PW.a[:, 0, 1, :], 0.0, [PW])
        B.tt("dve", PW.a[:, 1, 0, :], sc[:, 16:32], n2, ALU.mult, [tm], [PW])
        B.tt("dve", PW.a[:, 1, 1, :], sc[:, 0:16], n2, ALU.mult, [tm], [PW])
        pt = B.sb("ptmp", [128, 2, 4, 16])
        for (lo, n, s) in ((2, 1, 1), (3, 2, 2), (5, 4, 4)):
            bre = PW.a[:, s:s + 1, 0, :].broadcast_to([128, n, 16])
            bim = PW.a[:, s:s + 1, 1, :].broadcast_to([128, n, 16])
            B.cmul("dve", PW.a[:, lo:lo + n, 0, :], PW.a[:, lo:lo + n, 1, :],
                   PW.a[:, lo - s:lo - s + n, 0, :], PW.a[:, lo - s:lo - s + n, 1, :], bre, bim,
                   pt.a[:, 0, 0:n, :], pt.a[:, 1, 0:n, :], [PW], [PW], pt)
        B.cp("dve", MS.a[:, :, 0, 0], PW.a[:, 8, 0, :], [PW], [MS])
        B.cp("dve", MS.a[:, :, 1, 1], PW.a[:, 8, 0, :], [PW], [MS])
        B.cp("dve", MS.a[:, :, 1, 0], PW.a[:, 8, 1, :], [PW], [MS])
        B.ts("dve", MS.a[:, :, 0, 1], PW.a[:, 8, 1, :], -1.0, ALU.mult, [PW], [MS])
        B.tt("dve", R8.a, rho, rho, ALU.mult, [tm], [R8])
        B.tt("dve", R8.a, R8.a, R8.a, ALU.mult, [R8], [R8])
        B.tt("dve", R8.a, R8.a, R8.a, ALU.mult, [R8], [R8])
        r8i = row(8)
        B.op("dve", lambda e: e.reciprocal(r8i, R8.a), [R8], [tm])
        B.tt("dve", ER.a[:, :, 0], PW.a[:, 8, 0, :], r8i, ALU.mult, [PW, tm], [ER])
        B.tt("dve", EI.a[:, :, 0], PW.a[:, 8, 1, :], r8i, ALU.mult, [PW, tm], [EI])
        et = B.sb("etmp", [128, 2, 16, 32])
        n_ = 1
        while n_ < 64:
            bre = ER.a[:, :, n_ - 1:n_].broadcast_to([128, 16, n_]); bim = EI.a[:, :, n_ - 1:n_].broadcast_to([128, 16, n_])
            B.cmul("dve", ER.a[:, :, n_:2 * n_], EI.a[:, :, n_:2 * n_], ER.a[:, :, 0:n_], EI.a[:, :, 0:n_], bre, bim,
                   et.a[:, 0, :, 0:n_], et.a[:, 1, :, 0:n_], [ER, EI], [ER, EI], et)
            n_ *= 2
        am1 = row(8); nre = row(9); nim = row(10); den = row(11); t0 = row(4); t1 = row(5)
        B.ts("dve", am1, PW.a[:, 1, 0, :], -1.0, ALU.add, [PW], [tm])
        B.tt("dve", nre, am1, are.a, ALU.mult, [tm, are], [tm])
        B.tt("dve", t0, PW.a[:, 1, 1, :], aim.a, ALU.mult, [PW, aim], [tm])
        B.tt("dve", nre, nre, t0, ALU.add, [tm], [tm])
        B.tt("dve", nim, PW.a[:, 1, 1, :], are.a, ALU.mult, [PW, are], [tm])
        B.tt("dve", t0, am1, aim.a, ALU.mult, [tm, aim], [tm])
        B.tt("dve", nim, nim, t0, ALU.subtract, [tm], [tm])
        B.tt("dve", den, are.a, are.a, ALU.mult, [are], [tm])
        B.tt("dve", t0, aim.a, aim.a, ALU.mult, [aim], [tm])
        B.tt("dve", den, den, t0, ALU.add, [tm], [tm])
        B.op("dve", lambda e: e.reciprocal(t1, den), [tm], [tm])
        B.tt("dve", nre, nre, t1, ALU.mult, [tm], [tm])
        B.tt("dve", nim, nim, t1, ALU.mult, [tm], [tm])
        bb = [B.sb("bb%d" % i, [128, 16, 16]) for i in range(2)]
        btmp = B.sb("btmp", [128, 2, 16, 16])
        cre = nre[:, :, None].broadcast_to([128, 16, 16]); cim = nim[:, :, None].broadcast_to([128, 16, 16])
        B.cmul("dve", bb[0].a, bb[1].a, cre, cim, braw[0].a, braw[1].a, btmp.a[:, 0], btmp.a[:, 1],
               [tm, braw[0], braw[1]], [bb[0], bb[1]], btmp)
        X = [B.sb("X%d" % i, [128, 16, 32]) for i in range(2)]
        Xb = [B.sb("Xb%d" % i, [128, 16, 64], BF16) for i in range(2)]
        for i in range(2):
            B.memset("dve", X[i].a, 0.0, [X[i]])
            for gl in range(2):
                B.cp("dve", X[i].a[gl * 64:(gl + 1) * 64, :, gl * 16:(gl + 1) * 16], bb[i].a[gl * 64:(gl + 1) * 64], [bb[i]], [X[i]])
            B.memset("dve", Xb[i].a, 0.0, [Xb[i]])
            for kk in range(2):
                B.cp("act", Xb[i].a[:, kk::2, 32 * kk:32 * kk + 32], X[i].a[:, kk::2, :], [X[i]], [Xb[i]])
        B.push()
        WX = [B.sb("WX%d" % i, [128, 8, 16, 32]) for i in range(2)]
        wtmp = B.sb("wtmp", [128, 2, 8, 16, 32])
        pre = PW.a[:, 0:8, 0, :][:, :, :, None].broadcast_to([128, 8, 16, 32])
        pim = PW.a[:, 0:8, 1, :][:, :, :, None].broadcast_to([128, 8, 16, 32])
        xre = X[0].a[:, None, :, :].broadcast_to([128, 8, 16, 32])
        xim = X[1].a[:, None, :, :].broadcast_to([128, 8, 16, 32])
        B.cmul("dve", WX[0].a, WX[1].a, pre, pim, xre, xim, wtmp.a[:, 0], wtmp.a[:, 1], [PW, X[0], X[1]], [WX[0], WX[1]], wtmp)
        for tile in range(4):
            for e_ in range(8):
                pb = B.ps()
                for ri in range(2):
                    B.tr(pb.a[:, ri * 128:(ri + 1) * 128],
                         WX[ri].a[:, e_, 4 * tile:4 * tile + 4, :].rearrange("p k c -> p (k c)"), ident_f.a,
                         [WX[ri], ident_f], [pb])
                B.cp("act" if e_ % 2 else "dve", Wt.a[:, tile, e_, :, :],
                     pb.a[:, 0:256].rearrange("p (r n) -> p r n", r=2), [pb], [Wt])
        B.pop()
        B.push()
        CA = [B.sb("CA%d" % i, [128, 9, 16, 16]) for i in range(2)]
        catmp = B.sb("catmp", [128, 2, 9, 16, 16])
        pre9 = PW.a[:, :, 0, :][:, :, :, None].broadcast_to([128, 9, 16, 16])
        pim9 = PW.a[:, :, 1, :][:, :, :, None].broadcast_to([128, 9, 16, 16])
        cre9 = craw[0].a[:, None, :, :].broadcast_to([128, 9, 16, 16])
        cim9 = craw[1].a[:, None, :, :].broadcast_to([128, 9, 16, 16])
        B.cmul("dve", CA[0].a, CA[1].a, pre9, pim9, cre9, cim9, catmp.a[:, 0], catmp.a[:, 1],
               [PW, craw[0], craw[1]], [CA[0], CA[1]], catmp)
        B.memset("dve", CAW.a, 0.0, [CAW])
        for gl in range(2):
            hs = slice(gl * 64, (gl + 1) * 64)
            for kk in range(2):
                o = 32 * kk + 16 * gl
                B.cp("dve", CAW.a[hs, kk::2, 0, :, o:o + 16], CA[0].a[hs, :, kk::2, :].rearrange("p t q c -> p q t c"), [CA[0]], [CAW])
                B.ts("dve", CAW.a[hs, kk::2, 1, :, o:o + 16], CA[1].a[hs, :, kk::2, :].rearrange("p t q c -> p q t c"), -1.0, ALU.mult,
                     [CA[1]], [CAW])
        B.memset("dve", Kbd.a, 0.0, [Kbd])
        k0 = B.sb("k0", [128, 128])
        for tile in range(4):
            pb = B.ps()
            for h in range(2):
                for tau in range(8):
                    n = 0
                    for kk in range(2):
                        q = 4 * tile + 2 * h + kk
                        o = 32 * kk
                        for ri in range(2):
                            B.mm(pb.a[64 * h:64 * h + 64, tau * 32:(tau + 1) * 32], Xb[ri].a[:, q, :],
                                 CAW.a[:, q, ri, tau, o:o + 32], n == 0, n == 3, [Xb[ri], CAW], [pb])
                            n += 1
            for h in range(2):
                hs = slice(64 * h, 64 * h + 64)
                for kk in range(2):
                    cs = slice(64 * h + 32 * kk, 64 * h + 32 * kk + 32)
                    B.ts("dve", Kbd.a[hs, tile, 1:8, cs], pb.a[hs, 32:256].rearrange("p (t c) -> p t c", c=32),
                         mk.a[hs, kk:kk + 1], ALU.mult, [pb, mk], [Kbd])
            B.memset("dve", k0.a, 0.0, [k0])
            for h in range(2):
                hs = slice(64 * h, 64 * h + 64)
                for kk in range(2):
                    cs = slice(64 * h + 32 * kk, 64 * h + 32 * kk + 32)
                    B.ts("dve", k0.a[hs, cs], pb.a[hs, 0:32], mk.a[hs, kk:kk + 1], ALU.mult, [pb, mk], [k0])
            B.stt(k0.a, ident_f.a, dcol.a[:, tile:tile + 1], k0.a, ALU.mult, ALU.add, [ident_f, dcol, k0], [k0])
            B.cp("dve", Kbd.a[:, tile, 0, :], k0.a, [k0], [Kbd])
        B.pop()
        B.pop()
        for nm_, b_ in (("Wt", Wt), ("CAW", CAW), ("Kbd", Kbd), ("MS", MS)):
            if nm_ in dbo:
                S.dma("sp", dbo[nm_].a, b_.a, reads=[b_], writes=[dbo[nm_]], dreg=b_)
        NM = T1 // 8
        xt = B.sb("xt", [128, 4, D]); hT = B.sb("hT", [128, 8, T1], BF16); st = B.sb("st", [128, 12])
        hb = B.sb("hb", [128, 4, D], BF16)
        Sfin = [B.sb("Sfin%d" % h, [128, 8, 2]) for h in range(2)]
        for h in range(2):
            B.memset("pool", Sfin[h].a, 0.0, [Sfin[h]])

        class BS:
            pass
        hv = []
        for h in range(2):
            q = BS()
            q.u = B.sb("u%d" % h, [128, 2, 8, NM], BF16)
            q.um = B.sb("um%d" % h, [128, 4, 2, 8, NM], BF16)
            q.Dm = B.sb("Dm%d" % h, [128, 8, 2, NM]); q.St = B.sb("St%d" % h, [128, 8, 2, NM + 1])
            q.Sb = B.sb("Sb%d" % h, [128, 8, 2, NM], BF16)
            q.rt1 = B.sb("rt1%d" % h, [128, 8, NM]); q.rt2 = B.sb("rt2%d" % h, [128, 8, NM])
            q.Dr = B.sb("Dr%d" % h, [128, 8, 2, NM]); q.Qs = B.sb("Qs%d" % h, [128, 8, 2, NM])
            q.y2 = B.sb("y2%d" % h, [128, 2, T1]); q.yg = B.sb("yg%d" % h, [128, 2, T1])
            q.yy = B.view(q.Dm, q.Dm.a.rearrange("p a b c -> p (a b c)").rearrange("p (t n) -> p t n", t=2))
            q.ygb = B.sb("ygb%d" % h, [128, 2, T1], BF16)
            hv.append(q)
        sgb = B.sb("sgb", [128, T1]); ygl = B.sb("ygl", [128, 4, T1], BF16)
        scr = B.view(hb, hb.a.rearrange("p a b -> p (a b)")[:, 0:D])

        def tile1a(ti):
            frontend(x, ti, 4, gpre, xt, hb, hT, scr, st)
            for cb in range(4):
                q = hv[cb // 2]; tl = cb % 2
                pb = B.ps()
                for c in range(8):
                    B.mm(pb.a, ws5.a[:, c, cb * 128:(cb + 1) * 128], hT.a[:, c, :], c == 0, c == 7, [ws5, hT], [pb])
                pperm = pb.a.rearrange("p (m t) -> p t m", t=8)
                B.cp("act", q.u.a[:, tl, :, :], pperm, [pb], [q.u])
                for k in range(4):
                    B.act(q.um.a[:, k, tl, :, :], pperm, AF.Copy, [pb, mk4], [q.um], scale=mk4.a[:, k:k + 1])
            for h in range(2):
                q = hv[h]
                u, um, Dm, St, Sb, rt1, rt2, Dr, Qs, y2, yg, yy, ygb = (q.u, q.um, q.Dm, q.St, q.Sb, q.rt1, q.rt2, q.Dr, q.Qs,
                                                                        q.y2, q.yg, q.yy, q.ygb)
                sf = Sfin[h]
                for qb in range(2):
                    pb = B.ps()
                    for qq in range(4):
                        ql = 4 * qb + qq
                        tl, k = ql // 4, ql % 4
                        gt = 2 * h + tl
                        for ri in range(2):
                            col = (qq * 2 + ri) * NM
                            for j0 in range(8):
                                B.mm(pb.a[:, col:col + NM], Wt.a[:, gt, 7 - j0, ri, :], um.a[:, k, tl, j0, :],
                                     j0 == 0, j0 == 7, [Wt, um], [pb])
                    B.cp("dve", Dm.a[:, 4 * qb:4 * qb + 4, :, :],
                         pb.a.rearrange("p (q r m) -> p q r m", q=4, r=2), [pb], [Dm])
                erb = ER.a[:, 8 * h:8 * h + 8, :]; eib = EI.a[:, 8 * h:8 * h + 8, :]
                B.cp("pool", St.a[:, :, :, 0], sf.a, [sf], [St])
                B.tt("dve", rt1.a, erb, Dm.a[:, :, 0, :], ALU.mult, [ER, Dm], [rt1])
                B.tt("dve", rt2.a, eib, Dm.a[:, :, 1, :], ALU.mult, [EI, Dm], [rt2])
                B.tt("dve", Dr.a[:, :, 0, :], rt1.a, rt2.a, ALU.add, [rt1, rt2], [Dr])
                B.tt("dve", rt1.a, erb, Dm.a[:, :, 1, :], ALU.mult, [ER, Dm], [rt1])
                B.tt("dve", rt2.a, eib, Dm.a[:, :, 0, :], ALU.mult, [EI, Dm], [rt2])
                B.tt("dve", Dr.a[:, :, 1, :], rt1.a, rt2.a, ALU.subtract, [rt1, rt2], [Dr])
                for ql in range(8):
                    qi = 8 * h + ql
                    for ri in range(2):
                        B.op("dve", lambda e, ql=ql, qi=qi, ri=ri, Qs=Qs, Dr=Dr, sf=sf: e.tensor_tensor_scan(
                            out=Qs.a[:, ql, ri, :], data0=R8.a[:, qi:qi + 1].broadcast_to([128, NM]), data1=Dr.a[:, ql, ri, :],
                            initial=sf.a[:, ql, ri:ri + 1], op0=ALU.mult, op1=ALU.add), [R8, Dr, sf], [Qs], cost=0.4)
                B.tt("dve", rt1.a, erb, Qs.a[:, :, 0, :], ALU.mult, [ER, Qs], [rt1])
                B.tt("dve", rt2.a, eib, Qs.a[:, :, 1, :], ALU.mult, [EI, Qs], [rt2])
                B.tt("dve", St.a[:, :, 0, 1:NM + 1], rt1.a, rt2.a, ALU.subtract, [rt1, rt2], [St])
                B.tt("dve", rt1.a, erb, Qs.a[:, :, 1, :], ALU.mult, [ER, Qs], [rt1])
                B.tt("dve", rt2.a, eib, Qs.a[:, :, 0, :], ALU.mult, [EI, Qs], [rt2])
                B.tt("dve", St.a[:, :, 1, 1:NM + 1], rt1.a, rt2.a, ALU.add, [rt1, rt2], [St])
                B.cp("act", Sb.a, St.a[:, :, :, 0:NM], [St], [Sb])
                B.cp("act", sf.a, St.a[:, :, :, NM], [St], [sf])
                for tl in range(2):
                    gt = 2 * h + tl
                    pb = B.ps()
                    for hh in range(2):
                        hs = slice(64 * hh, 64 * hh + 64)
                        for t0_ in range(8):
                            n = 0
                            for kk in range(2):
                                ql = 4 * tl + 2 * hh + kk
                                qi = 8 * h + ql
                                for ri in range(2):
                                    B.mm(pb.a[hs, t0_ * NM:(t0_ + 1) * NM], CAW.a[:, qi, ri, t0_ + 1, :], Sb.a[:, ql, ri, :], n == 0, n == 3,
                                         [CAW, Sb], [pb], skip_group_check=True)
                                    n += 1
                    B.cp("act", y2.a[:, tl, :], pb.a, [pb], [y2])
                    pb = B.ps()
                    for t0o in range(8):
                        for tau in range(t0o + 1):
                            B.mm(pb.a[:, t0o * NM:(t0o + 1) * NM], Kbd.a[:, gt, tau, :], u.a[:, tl, t0o - tau, :],
                                 tau == 0, tau == t0o, [Kbd, u], [pb])
                    B.tt("dve", yy.a[:, tl, :].rearrange("p (m t) -> p t m", t=8), pb.a.rearrange("p (t m) -> p t m", t=8),
                         y2.a[:, tl, :].rearrange("p (t m) -> p t m", t=8), ALU.add, [pb, y2], [yy])
                if "s5y" in dbo:
                    S.dma("sp", dbo["s5y"].a[:, 2 * h:2 * h + 2, ti * T1:(ti + 1) * T1], yy.a, reads=[yy], writes=[dbo["s5y"]], dreg=yy)
                B.act(yg.a, yy.a, AF.Gelu_apprx_tanh, [yy], [yg])
                B.cp("act", ygb.a, yg.a, [yg], [ygb])
            for cb in range(4):
                pb = B.ps()
                for c in range(4):
                    B.mm(pb.a, wglu.a[:, c, cb * 128:(cb + 1) * 128], hv[c // 2].ygb.a[:, c % 2, :], c == 0, c == 3,
                         [wglu, hv[0].ygb, hv[1].ygb], [pb])
                B.act(sgb.a, pb.a, AF.Sigmoid, [pb, bglu], [sgb], bias=bglu.a[:, cb:cb + 1])
                B.tt("dve", ygl.a[:, cb, :], hv[cb // 2].yg.a[:, cb % 2, :], sgb.a, ALU.mult, [hv[cb // 2].yg, sgb], [ygl])
            S.dma("sp", YGLU.a[:, :, ti * T1:(ti + 1) * T1], ygl.a, reads=[ygl], writes=[YGLU], dreg=ygl)
            if "yglu" in dbo:
                S.dma("sp", dbo["yglu"].a[:, :, ti * T1:(ti + 1) * T1], ygl.a, reads=[ygl], writes=[dbo["yglu"]], dreg=ygl)

        S.rec = []
        for ti in range(ntiles):
            tile1a(ti)
        items = S.rec
        S.rec = None
        if os.environ.get("K_P1ASCHED", "1") == "1":
            S.schedule_emit(items, window=int(os.environ.get("K_WIN1A", "2500")))
        else:
            for it in items:
                S.replay(it)
        B.pop()

    if "1b" in phases:
        phase_1b(B, W, x, YR, epsc, gpre, ident_f, ident_b, frontend, bc_load, col_load, dbo, ntiles * (T1 // 128))

    if "1c" in phases:
        phase_1c(B, W, x, x1s, H2T, YR, YGLU, epsc, gpre, frontend, bc_load, col_load, dbo, ntiles)

    B.pop()
    if "2" in phases:
        phase_2(B, W, x1s, H2T, out, epsc, frontend, bc_load, dbo, ntiles)

    S.final_wait("sp", list(B.dout.values()))
    B.Wshapes = shapes
    return B


def phase_1b(B, W, x, YR, epsc, gpre, ident_f, ident_b, frontend, bc_load, col_load, dbo, ntiles):
    nc, S = B.nc, B.S
    TB = 128
    C = 128
    NW = 3
    C0 = math.exp(-0.5)
    B.push()
    wrg = B.sb("wrg", [128, 8, NR], BF16)
    win = W["w_in"].a.rearrange("(c p) n -> p c n", p=128)
    for c in range(8):
        S.dma("pool", wrg.a[:, c, :], win[:, c, 0:NR], reads=[W["w_in"]], writes=[wrg])
    w2p = B.sb("w2p", [128, 512], BF16); a2p = B.sb("a2p", [128, 512], BF16); g2b = B.sb("g2b", [128, 512], BF16)
    B.memset("pool", w2p.a, 0.0, [w2p]); B.memset("pool", a2p.a, 0.0, [a2p])
    S.dma("pool", w2p.a[0:64, :], W["rwkv_w2"].a, reads=[W["rwkv_w2"]], writes=[w2p])
    S.dma("pool", a2p.a[64:128, :], W["rwkv_a2"].a, reads=[W["rwkv_a2"]], writes=[a2p])
    S.dma("pool", g2b.a, W["rwkv_g2"].a, reads=[W["rwkv_g2"]], writes=[g2b])
    mu = col_load("rwkv_shift_mu", 14); w0c = col_load("rwkv_w0", 4); a0c = col_load("rwkv_a0", 4)
    kkc = col_load("rwkv_k_k", 4); kac = col_load("rwkv_k_a", 4); rkc = col_load("rwkv_r_k", 4)
    lwc = col_load("rwkv_lnx_w", 4); lbc = col_load("rwkv_lnx_b", 4)
    omu = B.sb("omu", [128, 14]); oka = B.sb("oka", [128, 4])
    B.ts("dve", omu.a, mu.a, -1.0, ALU.mult, [mu], [omu], s2=1.0, op1=ALU.add)
    B.ts("dve", oka.a, kac.a, -1.0, ALU.mult, [kac], [oka], s2=1.0, op1=ALU.add)
    bones = B.sb("bones", [128, 128]); bavg = B.sb("bavg", [128, 128]); ones = B.sb("ones", [128, 128])
    B.memset("pool", ones.a, 1.0, [ones])
    B.memset("pool", bones.a, 0.0, [bones])
    B.memset("pool", bones.a[0:64, 0:64], 1.0, [bones]); B.memset("pool", bones.a[64:128, 64:128], 1.0, [bones])
    B.ts("pool", bavg.a, bones.a, 1.0 / 64, ALU.mult, [bones], [bavg])
    hm = B.sb("hm", [128, 2])
    B.memset("pool", hm.a, 0.0, [hm]); B.memset("pool", hm.a[0:64, 0:1], 1.0, [hm]); B.memset("pool", hm.a[64:128, 1:2], 1.0, [hm])
    mf = B.sb("mf", [128, 128])
    MU4 = B.sb("MU4", [128, 4, 128], BF16); MI4 = B.sb("MI4", [128, 4, 128], BF16)
    ML4 = B.sb("ML4", [128, 4, 128], BF16); I4 = B.sb("I4", [128, 4, 128], BF16)
    for (mt, pat, cm, cop) in ((MU4, 1, -1, ALU.is_gt), (MI4, 1, -1, ALU.is_ge), (ML4, -1, 1, ALU.is_gt)):
        B.memset("pool", mf.a, 1.0, [mf])
        B.op("pool", lambda e, pat=pat, cm=cm, cop=cop: e.affine_select(out=mf.a, in_=mf.a, pattern=[[pat, 128]], compare_op=cop,
                                                                      fill=0.0, base=0, channel_multiplier=cm), [mf], [mf])
        for h in range(4):
            B.cp("pool", mt.a[:, h, :], mf.a, [mf], [mt])
    for h in range(4):
        B.cp("pool", I4.a[:, h, :], ident_f.a, [ident_f], [I4])
    pc = B.sb("pc", [128, 14]); B.memset("pool", pc.a, 0.0, [pc])
    H32 = B.sb("H32", [128, 4, 64]); Hb = B.sb("Hb", [128, 4, 64], BF16); Ht = B.sb("Ht", [128, 4, 64])
    B.memset("pool", H32.a, 0.0, [H32]); B.memset("pool", Hb.a, 0.0, [Hb])

    def bc4(col):
        return col.a[:, :, None].broadcast_to([128, 4, TB])

    class BS:
        pass

    sets = []
    for w in range(NW):
        b = BS()
        f4 = lambda nm: B.sb(nm + str(w), [128, 4, TB])
        h4 = lambda nm: B.sb(nm + str(w), [128, 4, TB], BF16)
        b.xt = B.sb("xt%d" % w, [128, 1, D]); b.hb = B.sb("hb%d" % w, [128, 1, D], BF16); b.hT = B.sb("hT%d" % w, [128, 8, TB], BF16)
        b.st = B.sb("st%d" % w, [128, 6])
        b.PS = B.sb("PS%d" % w, [128, 14, TB]); b.t1 = B.sb("t1%d" % w, [128, 4, TB]); b.t2 = B.sb("t2%d" % w, [128, 4, TB])
        b.scr = B.view(b.PS, b.PS.a.rearrange("p a b -> p (a b)")[:, 0:D])
        b.twa = B.sb("twa%d" % w, [128, TB], BF16); b.sgd = B.sb("sgd%d" % w, [128, TB], BF16)
        b.sgw = f4("sgw"); b.asg = f4("asg"); b.gg = f4("gg"); b.kk = f4("kk"); b.tq = f4("tq"); b.kmod = f4("kmod")
        b.cum = f4("cum"); b.eg = b.t1; b.egx = f4("egx"); b.eng = b.t2; b.bon = f4("bon")
        b.am = B.sb("am%d" % w, [128, 4, 2, TB], BF16); b.rm = B.sb("rm%d" % w, [128, 4, 2, TB], BF16)
        b.bf = h4("bf"); b.kf = h4("kf"); b.vb = h4("vb")
        b.Btok = B.sb("Btok%d" % w, [128, 512], BF16); b.Ktok = B.sb("Ktok%d" % w, [128, 512], BF16); b.Vtok = B.sb("Vtok%d" % w, [128, 512], BF16)
        for nm, src in (("Pm", b.sgw), ("Qm", b.asg), ("Rm", b.kk), ("Nak", b.kmod), ("Nrb", b.egx), ("Nrk", b.eng)):
            setattr(b, nm, B.view(src, src.a.rearrange("p a b -> p (a b)").bitcast(BF16).rearrange("p (h t) -> p h t", h=8)))
        hTf = b.hT.a.rearrange("p a b -> p (a b)")
        b.Xb = B.view(b.hT, hTf[:, 0:512]); b.Ub = B.view(b.hT, hTf[:, 512:1024])
        b.Yf = B.view(b.xt, b.xt.a.rearrange("p a b -> p (a b)")[:, 0:4 * TB].rearrange("p (j t) -> p j t", j=4))
        b.dd = B.view(b.hb, b.hb.a.rearrange("p a b -> p (a b)").bitcast(F32).rearrange("p (j t) -> p j t", j=4))
        b.yrb = h4("yrb")
        sets.append(b)

    def chunk(ci):
        b = sets[ci % NW]
        PS, tq, cum, kk, kmod, eg, egx, eng, asg, sgw, gg, bon = b.PS, b.tq, b.cum, b.kk, b.kmod, b.eg, b.egx, b.eng, b.asg, b.sgw, b.gg, b.bon
        am, rm, Pm, Qm, Rm, Nak, Nrb, Nrk = b.am, b.rm, b.Pm, b.Qm, b.Rm, b.Nak, b.Nrb, b.Nrk
        frontend(x, ci, 1, gpre, b.xt, b.hb, b.hT, b.scr, b.st)
        yield
        for j0 in range(0, 14, 4):
            nj = min(4, 14 - j0)
            pb = B.ps()
            for jj in range(nj):
                for c in range(8):
                    B.mm(pb.a[:, jj * TB:(jj + 1) * TB], wrg.a[:, c, (j0 + jj) * 128:(j0 + jj + 1) * 128], b.hT.a[:, c, :],
                         c == 0, c == 7, [wrg, b.hT], [pb])
            pv = pb.a[:, 0:nj * TB].rearrange("p (j t) -> p j t", j=nj)
            om_b = omu.a[:, j0:j0 + nj, None].broadcast_to([128, nj, TB])
            mu_b = mu.a[:, j0:j0 + nj, None].broadcast_to([128, nj, TB - 1])
            B.tt("dve", b.t1.a[:, 0:nj, :], pv, om_b, ALU.mult, [pb, omu], [b.t1])
            B.tt("dve", b.t2.a[:, 0:nj, 1:TB], pv[:, :, 0:TB - 1], mu_b, ALU.mult, [pb, mu], [b.t2])
            B.tt("dve", b.t2.a[:, 0:nj, 0:1], pc.a[:, j0:j0 + nj, None], mu.a[:, j0:j0 + nj, None], ALU.mult, [pc, mu], [b.t2])
            B.cp("act", pc.a[:, j0:j0 + nj, None], pv[:, :, TB - 1:TB], [pb], [pc])
            B.tt("pool", PS.a[:, j0:j0 + nj, :], b.t1.a[:, 0:nj, :], b.t2.a[:, 0:nj, :], ALU.add, [b.t1, b.t2], [PS])
            yield
        if "pshift" in dbo:
            S.dma("sp", dbo["pshift"].a[:, :, ci * TB:(ci + 1) * TB], PS.a, reads=[PS], writes=[dbo["pshift"]], dreg=PS)
        r_ = PS.a[:, 0:4, :]; k_ = PS.a[:, 4:8, :]; v_ = PS.a[:, 8:12, :]
        B.act(b.twa.a[0:64, :], PS.a[0:64, 12, :], AF.Tanh, [PS], [b.twa])
        B.cp("act", b.twa.a[64:128, :], PS.a[64:128, 12, :], [PS], [b.twa])
        B.act(b.sgd.a, PS.a[:, 13, :], AF.Sigmoid, [PS], [b.sgd])
        pw_ = B.ps()
        for j in range(4):
            B.mm(pw_.a[:, j * TB:(j + 1) * TB], w2p.a[:, j * 128:(j + 1) * 128], b.twa.a, True, True, [w2p, b.twa], [pw_])
        for j in range(4):
            B.act(sgw.a[:, j, :], pw_.a[:, j * TB:(j + 1) * TB], AF.Sigmoid, [pw_, w0c], [sgw], bias=w0c.a[:, j:j + 1])
        pa_ = B.ps()
        for j in range(4):
            B.mm(pa_.a[:, j * TB:(j + 1) * TB], a2p.a[:, j * 128:(j + 1) * 128], b.twa.a, True, True, [a2p, b.twa], [pa_])
        for j in range(4):
            B.act(asg.a[:, j, :], pa_.a[:, j * TB:(j + 1) * TB], AF.Sigmoid, [pa_, a0c], [asg], bias=a0c.a[:, j:j + 1])
        pg_ = B.ps()
        for j in range(4):
            B.mm(pg_.a[:, j * TB:(j + 1) * TB], g2b.a[:, j * 128:(j + 1) * 128], b.sgd.a, True, True, [g2b, b.sgd], [pg_])
        B.cp("act", gg.a, pg_.a.rearrange("p (j t) -> p j t", j=4), [pg_], [gg])
        yield
        B.tt("dve", kk.a, k_, bc4(kkc), ALU.mult, [PS, kkc], [kk])
        B.tt("pool", tq.a, kk.a, kk.a, ALU.mult, [kk], [tq])
        pb = B.ps()
        for j in range(4):
            B.mm(pb.a[:, j * TB:(j + 1) * TB], bones.a, tq.a[:, j, :], True, True, [bones, tq], [pb])
        B.act(cum.a, pb.a.rearrange("p (j t) -> p j t", j=4), AF.Ln, [pb, epsc], [cum], bias=epsc.a[:, 1:2])
        B.act(cum.a, cum.a, AF.Exp, [cum], [cum], scale=-0.5)
        B.tt("pool", kk.a, kk.a, cum.a, ALU.mult, [kk, cum], [kk])
        yield
        B.tt("dve", tq.a, asg.a, bc4(kac), ALU.mult, [asg, kac], [tq])
        B.tt("dve", tq.a, tq.a, bc4(oka), ALU.add, [tq, oka], [tq])
        B.tt("dve", kmod.a, k_, tq.a, ALU.mult, [PS, tq], [kmod])
        B.tt("pool", tq.a, r_, kmod.a, ALU.mult, [PS, kmod], [tq])
        B.tt("dve", tq.a, tq.a, bc4(rkc), ALU.mult, [tq, rkc], [tq])
        pbon = B.ps()
        for j in range(4):
            B.mm(pbon.a[:, j * TB:(j + 1) * TB], bones.a, tq.a[:, j, :], True, True, [bones, tq], [pbon])
        B.tt("dve", bon.a, pbon.a.rearrange("p (j t) -> p j t", j=4), v_, ALU.mult, [pbon, PS], [bon])
        yield
        for j in range(4):
            B.op("dve", lambda e, j=j: e.tensor_tensor_scan(out=cum.a[:, j, :], data0=ones.a, data1=sgw.a[:, j, :], initial=0.0,
                                                            op0=ALU.mult, op1=ALU.add), [ones, sgw], [cum])
        B.act(eg.a, cum.a, AF.Exp, [cum], [eg], scale=-C0)
        B.act(eng.a, cum.a, AF.Exp, [cum], [eng], scale=C0)
        B.tt("pool", tq.a, cum.a, sgw.a, ALU.subtract, [cum, sgw], [tq])
        B.act(egx.a, tq.a, AF.Exp, [tq], [egx], scale=-C0)
        yield
        B.stt(tq.a, kk.a, -1.0, egx.a, ALU.mult, ALU.mult, [kk, egx], [tq])
        for hh in range(2):
            B.act(am.a[:, :, hh, :], tq.a, AF.Copy, [tq, hm], [am], scale=hm.a[:, hh:hh + 1])
        B.tt("dve", egx.a, r_, eg.a, ALU.mult, [PS, eg], [egx])
        for hh in range(2):
            B.act(rm.a[:, :, hh, :], egx.a, AF.Copy, [egx, hm], [rm], scale=hm.a[:, hh:hh + 1])
        B.tt("pool", tq.a, kk.a, asg.a, ALU.mult, [kk, asg], [tq])
        B.tt("dve", b.bf.a, tq.a, eng.a, ALU.mult, [tq, eng], [b.bf])
        B.tt("dve", b.kf.a, kmod.a, eng.a, ALU.mult, [kmod, eng], [b.kf])
        B.cp("act", b.vb.a, v_, [PS], [b.vb])
        yield
        cs = slice(0, C)
        for n_, (src, dst) in enumerate(((b.bf, b.Btok), (b.kf, b.Ktok), (b.vb, b.Vtok))):
            pb = B.ps()
            pv = pb.a.bitcast(BF16)
            for j in range(4):
                B.tr(pv[:, j * 128:(j + 1) * 128], src.a[:, j, cs], ident_b.a, [src, ident_b], [pb])
            B.cp("act" if n_ != 1 else "dve", dst.a, pv[:, 0:512], [pb], [dst])
        yield
        for g in range(2):
            gs = slice(4 * g, 4 * g + 4)
            for kind in range(5):
                pbk = B.ps()
                for hq in range(4):
                    h = 4 * g + hq
                    j, hh = h // 2, h % 2
                    o = slice(hq * 128, (hq + 1) * 128)
                    bfs, kfs, ams, rms = b.bf.a[:, j, cs], b.kf.a[:, j, cs], am.a[:, j, hh, cs], rm.a[:, j, hh, cs]
                    lhsT, rhs, rd = ((bfs, ams, [b.bf, am]), (ams, bfs, [b.bf, am]), (kfs, ams, [b.kf, am]),
                                     (bfs, rms, [b.bf, rm]), (kfs, rms, [b.kf, rm]))[kind]
                    B.mm(pbk.a[:, o], lhsT, rhs, True, True, rd, [pbk])
                msk, dst = ((MU4, Pm), (ML4, Qm), (MU4, Nak), (MI4, Nrb), (MI4, Nrk))[kind]
                B.tt("dve", dst.a[:, gs, :], pbk.a.rearrange("p (h t) -> p h t", h=4), msk.a, ALU.mult, [pbk, msk], [dst])
            B.tt("pool", Rm.a[:, gs, :], Pm.a[:, gs, :], I4.a, ALU.add, [Pm, I4], [Rm])
            yield
        for lvl in range(1, 7):
            for g in range(2):
                gs = slice(4 * g, 4 * g + 4)
                pq = B.ps()
                pp = B.ps() if lvl < 6 else None
                for hq in range(4):
                    h = 4 * g + hq
                    o = slice(hq * 128, (hq + 1) * 128)
                    B.mm(pq.a[:, o], Pm.a[:, h, :], Qm.a[:, h, :], True, True, [Pm, Qm], [pq])
                    if pp is not None:
                        B.mm(pp.a[:, o], Qm.a[:, h, :], Pm.a[:, h, :], True, True, [Pm, Qm], [pp])
                B.cp("act", Qm.a[:, gs, :], pq.a.rearrange("p (h t) -> p h t", h=4), [pq], [Qm])
                if pp is not None:
                    B.cp("act", Pm.a[:, gs, :], pp.a.rearrange("p (h t) -> p h t", h=4), [pp], [Pm])
                pr = B.ps()
                for hq in range(4):
                    h = 4 * g + hq
                    o = slice(hq * 128, (hq + 1) * 128)
                    B.mm(pr.a[:, o], Qm.a[:, h, :], Rm.a[:, h, :], True, True, [Qm, Rm], [pr])
                B.tt("dve", Rm.a[:, gs, :], pr.a.rearrange("p (h t) -> p h t", h=4), Rm.a[:, gs, :], ALU.add, [pr, Rm], [Rm])
                yield
        px = B.ps()
        for h in range(8):
            j, hh = h // 2, h % 2
            o = slice(h * 64, (h + 1) * 64)
            B.mm(px.a[:, o], am.a[:, j, hh, cs], Hb.a[:, j, :], True, False, [am, Hb], [px])
            B.mm(px.a[:, o], Nak.a[:, h, :], b.Vtok.a[:, o], False, True, [Nak, b.Vtok], [px])
        B.cp("act", b.Xb.a, px.a, [px], [b.Xb])
        pu = B.ps()
        for h in range(8):
            o = slice(h * 64, (h + 1) * 64)
            B.mm(pu.a[:, o], Rm.a[:, h, :], b.Xb.a[:, o], True, True, [Rm, b.Xb], [pu])
        B.cp("act", b.Ub.a, pu.a, [pu], [b.Ub])
        ph = B.ps()
        for h in range(8):
            j, hh = h // 2, h % 2
            o = slice(h * 64, (h + 1) * 64)
            ho = ph.a[hh * 64:(hh + 1) * 64, j * 64:(j + 1) * 64]
            B.mm(ho, b.Btok.a[:, o], b.Ub.a[:, o], True, False, [b.Btok, b.Ub], [ph])
            B.mm(ho, b.Ktok.a[:, o], b.Vtok.a[:, o], False, True, [b.Ktok, b.Vtok], [ph])
        py = B.ps()
        for h in range(8):
            j, hh = h // 2, h % 2
            o = slice(h * 64, (h + 1) * 64)
            yo = py.a[hh * 64:(hh + 1) * 64, j * 128:(j + 1) * 128]
            B.mm(yo, Hb.a[:, j, :], rm.a[:, j, hh, cs], True, False, [Hb, rm], [py])
            B.mm(yo, b.Ub.a[:, o], Nrb.a[:, h, :], False, False, [b.Ub, Nrb], [py])
            B.mm(yo, b.Vtok.a[:, o], Nrk.a[:, h, :], False, True, [b.Vtok, Nrk], [py])
        B.tt("dve", Ht.a, ph.a[:, 0:256].rearrange("p (j i) -> p j i", j=4), H32.a, ALU.add, [ph, H32], [Ht])
        gC = eg.a[:, :, C - 1:C].broadcast_to([128, 4, 64])
        B.tt("dve", H32.a, Ht.a, gC, ALU.mult, [Ht, eg], [H32])
        B.cp("act", Hb.a, H32.a, [H32], [Hb])
        B.cp("act", b.Yf.a, py.a.rearrange("p (j t) -> p j t", j=4), [py], [b.Yf])
        yield
        if "wkv" in dbo:
            S.dma("sp", dbo["wkv"].a[:, :, ci * TB:(ci + 1) * TB], b.Yf.a, reads=[b.Yf], writes=[dbo["wkv"]], dreg=b.Yf)
        Yf, dd = b.Yf, b.dd
        pm_ = B.ps()
        for j in range(4):
            B.mm(pm_.a[:, j * TB:(j + 1) * TB], bavg.a, Yf.a[:, j, :], True, True, [bavg, Yf], [pm_])
        B.tt("dve", dd.a, Yf.a, pm_.a.rearrange("p (j t) -> p j t", j=4), ALU.subtract, [Yf, pm_], [dd])
        B.act(tq.a, dd.a, AF.Square, [dd], [tq])
        pv_ = B.ps()
        for j in range(4):
            B.mm(pv_.a[:, j * TB:(j + 1) * TB], bavg.a, tq.a[:, j, :], True, True, [bavg, tq], [pv_])
        B.act(cum.a, pv_.a.rearrange("p (j t) -> p j t", j=4), AF.Ln, [pv_, epsc], [cum], bias=epsc.a[:, 2:3])
        B.act(cum.a, cum.a, AF.Exp, [cum], [cum], scale=-0.5)
        yield
        B.tt("pool", dd.a, dd.a, cum.a, ALU.mult, [dd, cum], [dd])
        B.tt("dve", dd.a, dd.a, bc4(lwc), ALU.mult, [dd, lwc], [dd])
        B.tt("dve", dd.a, dd.a, bc4(lbc), ALU.add, [dd, lbc], [dd])
        B.tt("pool", dd.a, dd.a, bon.a, ALU.add, [dd, bon], [dd])
        B.tt("dve", b.yrb.a, dd.a, gg.a, ALU.mult, [dd, gg], [b.yrb])
        if "rwkv_y" in dbo:
            S.dma("sp", dbo["rwkv_y"].a[:, :, ci * TB:(ci + 1) * TB], b.yrb.a, reads=[b.yrb], writes=[dbo["rwkv_y"]], dreg=b.yrb)
        S.dma("sp", YR.a[:, :, ci * TB:(ci + 1) * TB], b.yrb.a, reads=[b.yrb], writes=[YR], dreg=b.yrb)
        yield

    pools = [[0, 1, 2], [3, 4, 5], [6, 7]] if NW == 3 else ([[0, 1, 2, 3], [4, 5, 6, 7]] if NW == 2 else [list(range(8))])
    S.rec = []
    for ci in range(ntiles):
        B.ps_pool = pools[ci % NW]
        for _ in chunk(ci):
            pass
    items = S.rec
    S.rec = None
    B.ps_pool = None
    S.schedule_emit(items, window=int(os.environ.get("K_WIN", "1400")))
    B.pop()


def phase_1c(B, W, x, x1s, H2T, YR, YGLU, epsc, gpre, frontend, bc_load, col_load, dbo, ntiles):
    nc, S = B.nc, B.S
    TB = 512
    NS = TB // 128
    B.push()
    wg = B.sb("wg", [128, 8, 2048], BF16)
    win = W["w_in"].a.rearrange("(c p) n -> p c n", p=128)
    for c in range(8):
        S.dma("pool", wg.a[:, c, :], win[:, c, NR + 512:NIN], reads=[W["w_in"]], writes=[wg])
    wbr = B.sb("wbr", [128, 4, D], BF16); wbs = B.sb("wbs", [128, 4, D], BF16); wout = B.sb("wout", [128, 8, D], BF16)
    S.dma("pool", wbr.a, W["w_branch_rwkv"].a.rearrange("(c p) n -> p c n", p=128), reads=[W["w_branch_rwkv"]], writes=[wbr])
    S.dma("pool", wbs.a, W["w_branch_s5"].a.rearrange("(c p) n -> p c n", p=128), reads=[W["w_branch_s5"]], writes=[wbs])
    for c in range(8):
        S.dma("pool", wout.a[:, c, :], W["w_out"].a[c * 128:(c + 1) * 128, :], reads=[W["w_out"]], writes=[wout])
    gpost = bc_load("norm_mix_post", D)
    gffn = bc_load("norm_ffn_pre", D)
    bgc = col_load("b_gate", 16)
    class BS:
        pass
    sets = []
    for w in range(2):
        q = BS()
        q.xt = B.sb("xt%d" % w, [128, NS, D]); q.hb = B.sb("hb%d" % w, [128, NS, D], BF16); q.hT = B.sb("hT%d" % w, [128, 8, TB], BF16)
        q.scr = B.sb("scr%d" % w, [128, D]); q.st = B.sb("st%d" % w, [128, 12])
        q.yrt = B.sb("yrt%d" % w, [128, 4, TB], BF16); q.ygt = B.sb("ygt%d" % w, [128, 4, TB], BF16)
        q.mixb = B.sb("mixb%d" % w, [128, 8, TB], BF16)
        sets.append(q)
    gA = B.sb("gA", [128, TB]); gB = B.sb("gB", [128, TB]); mt1 = B.sb("mt1", [128, TB]); mt2 = B.sb("mt2", [128, TB])

    def part_a(ti):
        q = sets[ti % 2]
        frontend(x, ti, NS, gpre, q.xt, q.hb, q.hT, q.scr, q.st)
        S.dma("sp", q.yrt.a, YR.a[:, :, ti * TB:(ti + 1) * TB], reads=[YR], writes=[q.yrt])
        S.dma("sp", q.ygt.a, YGLU.a[:, :, ti * TB:(ti + 1) * TB], reads=[YGLU], writes=[q.ygt])

    def part_b(ti):
        q = sets[ti % 2]
        xt, hb, hT, scr, st, yrt, ygt, mixb = q.xt, q.hb, q.hT, q.scr, q.st, q.yrt, q.ygt, q.mixb
        for cb in range(8):
            pa = B.ps(); pbb = B.ps(); po = B.ps(); ps_ = B.ps()
            for c in range(8):
                B.mm(pa.a, wg.a[:, c, cb * 128:(cb + 1) * 128], hT.a[:, c, :], c == 0, c == 7, [wg, hT], [pa])
            for c in range(8):
                B.mm(pbb.a, wg.a[:, c, (8 + cb) * 128:(9 + cb) * 128], hT.a[:, c, :], c == 0, c == 7, [wg, hT], [pbb])
            for j in range(4):
                B.mm(po.a, wbr.a[:, j, cb * 128:(cb + 1) * 128], yrt.a[:, j, :], j == 0, j == 3, [wbr, yrt], [po])
            for j in range(4):
                B.mm(ps_.a, wbs.a[:, j, cb * 128:(cb + 1) * 128], ygt.a[:, j, :], j == 0, j == 3, [wbs, ygt], [ps_])
            B.act(gA.a, pa.a, AF.Sigmoid, [pa, bgc], [gA], bias=bgc.a[:, cb:cb + 1])
            B.act(gB.a, pbb.a, AF.Sigmoid, [pbb, bgc], [gB], bias=bgc.a[:, 8 + cb:9 + cb])
            B.tt("dve", mt1.a, po.a, gA.a, ALU.mult, [po, gA], [mt1])
            B.tt("dve", mt2.a, ps_.a, gB.a, ALU.mult, [ps_, gB], [mt2])
            B.tt(os.environ.get("K_MIXENG", "dve"), mixb.a[:, cb, :], mt1.a, mt2.a, ALU.add, [mt1, mt2], [mixb])

    def part_c(ti):
        q = sets[ti % 2]
        xt, hb, hT, scr, st, yrt, ygt, mixb = q.xt, q.hb, q.hT, q.scr, q.st, q.yrt, q.ygt, q.mixb
        for s_ in range(NS):
            pbs = [B.ps(), B.ps()]
            for half in range(2):
                for c8 in range(8):
                    B.mm(pbs[half].a, mixb.a[:, c8, s_ * 128:(s_ + 1) * 128], wout.a[:, c8, half * 512:(half + 1) * 512],
                         c8 == 0, c8 == 7, [mixb, wout], [pbs[half]])
            for half in range(2):
                B.act(scr.a[:, 0:512], pbs[half].a, AF.Square, [pbs[half]], [scr, st], accum_out=st.a[:, half:half + 1])
            B.tt("dve", st.a[:, 2:3], st.a[:, 0:1], st.a[:, 1:2], ALU.add, [st], [st])
            B.act(st.a[:, 2:3], st.a[:, 2:3], AF.Ln, [st, epsc], [st], scale=1.0 / D, bias=epsc.a[:, 0:1])
            B.act(st.a[:, 3:4], st.a[:, 2:3], AF.Exp, [st], [st], scale=-0.5)
            for half in range(2):
                hsl = slice(half * 512, (half + 1) * 512)
                B.stt(scr.a[:, hsl], pbs[half].a, st.a[:, 3:4], gpost.a[:, hsl], ALU.mult, ALU.mult, [pbs[half], st, gpost], [scr])
            B.tt("pool", xt.a[:, s_, :], xt.a[:, s_, :], scr.a, ALU.add, [xt, scr], [xt])
        S.dma("sp", x1s.a[ti * TB:(ti + 1) * TB, :].rearrange("(s p) d -> p s d", p=128), xt.a, reads=[xt], writes=[x1s], dreg=xt)
        if "x1" in dbo:
            S.dma("sp", dbo["x1"].a[ti * TB:(ti + 1) * TB, :].rearrange("(s p) d -> p s d", p=128), xt.a, reads=[xt],
                  writes=[dbo["x1"]], dreg=xt)
        frontend(None, ti, NS, gffn, xt, hb, hT, scr, st, load=False)
        S.dma("sp", H2T.a[:, :, ti * TB:(ti + 1) * TB], hT.a, reads=[hT], writes=[H2T], dreg=hT)

    if os.environ.get("K_P1CSCHED", "0") == "1":
        S.rec = []
        for ti in range(ntiles):
            part_a(ti); part_b(ti); part_c(ti)
        items = S.rec
        S.rec = None
        S.schedule_emit(items, window=int(os.environ.get("K_WIN1C", "1500")))
    else:
        part_a(0)
        for ti in range(ntiles):
            part_b(ti)
            if ti + 1 < ntiles:
                part_a(ti + 1)
            part_c(ti)
    B.pop()


def phase_2(B, W, x1s, H2T, out, epsc, frontend, bc_load, dbo, ntiles):
    nc, S = B.nc, B.S
    TB = 512
    NS = TB // 128
    ntiles = ntiles * (512 // TB)
    B.push()
    wup = B.sb("wup", [128, 8, 2 * FF], BF16)
    wsrc = W["ffn_w_up"].a.rearrange("(c p) n -> p c n", p=128)
    for c in range(8):
        for (a, b) in ((0, 2048), (2048, 4096), (4096, 2 * FF)):
            S.dma("pool", wup.a[:, c, a:b], wsrc[:, c, a:b], reads=[W["ffn_w_up"]], writes=[wup])
    wdn = B.sb("wdn", [128, 22, D], BF16)
    for i in range(22):
        S.dma("pool", wdn.a[:, i, :], W["ffn_w_down"].a[i * 128:(i + 1) * 128, :], reads=[W["ffn_w_down"]], writes=[wdn])
    g2 = bc_load("norm_ffn_post", D)
    cw = B.sb("cw", [128, 3, 44]); cbias = B.sb("cbias", [128, 44])
    S.dma("sp", cw.a, W["ffn_conv_w"].a.rearrange("j (b p) -> p j b", p=128), reads=[W["ffn_conv_w"]], writes=[cw],
          allow_slow_non_contiguous=True)
    S.dma("sp", cbias.a, W["ffn_conv_b"].a.rearrange("(b p) -> p b", p=128), reads=[W["ffn_conv_b"]], writes=[cbias],
          allow_slow_non_contiguous=True)
    halo = B.sb("halo", [128, 44, 2]); B.memset("pool", halo.a, 0.0, [halo])
    epsc = B.sb("epsc2", [128, 1]); B.memset("pool", epsc.a, 1e-6, [epsc])
    class BS:
        pass
    sets = []
    hTs = [B.sb("hT2_%d" % w, [128, 8, TB], BF16) for w in range(2)]
    for w in range(1):
        q = BS()
        q.xt = B.sb("xt2_%d" % w, [128, NS, D])
        q.actb = B.sb("actb%d" % w, [128, 22, TB], BF16)
        q.st = B.sb("st2_%d" % w, [128, 12])
        sets.append(q)
    scrF_ = B.sb("scrF", [128, D])
    a0s_ = (B.sb("a0g", [128, TB]), B.sb("a0v", [128, TB]))
    a1s_ = ((B.sb("a1g0", [128, TB]), B.sb("a1v0", [128, TB])), (B.sb("a1g1", [128, TB]), B.sb("a1v1", [128, TB])))
    for q in sets:
        q.scrF, q.a0s, q.a1s = scrF_, a0s_, a1s_
    def tile(ti):
        q = sets[0]
        xt, actb, st, scrF, a0s, a1s = q.xt, q.actb, q.st, q.scrF, q.a0s, q.a1s
        hT = hTs[ti % 2]
        if ti == 0:
            S.dma("sp", hT.a, H2T.a[:, :, 0:TB], reads=[H2T], writes=[hT])
        if ti + 1 < ntiles:
            S.dma("sp", hTs[(ti + 1) % 2].a, H2T.a[:, :, (ti + 1) * TB:(ti + 2) * TB], reads=[H2T], writes=[hTs[(ti + 1) % 2]])
        S.dma("sp", xt.a, x1s.a[ti * TB:(ti + 1) * TB, :].rearrange("(s p) d -> p s d", p=128), reads=[x1s], writes=[xt])
        def finish(i):
            accg, accv = a1s[i % 2]
            B.act(accg.a, accg.a, AF.Gelu_apprx_tanh, [accg], [accg])
            me = os.environ.get("K_MULENG", "pool")
            me = ("dve" if i % 2 else "pool") if me == "alt" else me
            B.tt(me, actb.a[:, i, :], accg.a, accv.a, ALU.mult, [accg, accv], [actb])

        for i in range(22):
            accs = a1s[i % 2]
            for gv in range(2):
                b = i + 22 * gv
                pb = B.ps()
                for c in range(8):
                    B.mm(pb.a[:, 0:TB], wup.a[:, c, b * 128:(b + 1) * 128], hT.a[:, c, :], c == 0, c == 7, [wup, hT], [pb])
                acc = accs[gv]; a0 = a0s[gv]; a1 = accs[gv]
                B.act(a0.a, pb.a[:, 0:TB], AF.Identity, [pb, cw, cbias], [a0], scale=cw.a[:, 2, b:b + 1], bias=cbias.a[:, b:b + 1])
                B.act(a1.a[:, 1:TB], pb.a[:, 0:TB - 1], AF.Copy, [pb, cw], [a1], scale=cw.a[:, 1, b:b + 1])
                B.stt(a0.a[:, 2:TB], pb.a[:, 0:TB - 2], cw.a[:, 0, b:b + 1], a0.a[:, 2:TB], ALU.mult, ALU.add, [pb, cw, a0], [a0])
                B.ts("dve", a1.a[:, 0:1], halo.a[:, b, 1:2], cw.a[:, 1, b:b + 1], ALU.mult, [halo, cw], [a1])
                B.stt(a0.a[:, 0:2], halo.a[:, b, 0:2], cw.a[:, 0, b:b + 1], a0.a[:, 0:2], ALU.mult, ALU.add, [halo, cw, a0], [a0])
                B.cp("dve", halo.a[:, b, :], pb.a[:, TB - 2:TB], [pb], [halo])
                B.tt(os.environ.get("K_ADDENG", "dve"), acc.a, a0.a, a1.a, ALU.add, [a0, a1], [acc])
            if "zc" in dbo and ti == 0 and i == 0:
                S.dma("sp", dbo["zc"].a[:, 0:TB], accs[0].a, reads=[accs[0]], writes=[dbo["zc"]], dreg=accs[0])
            if i > 0:
                finish(i - 1)
        finish(21)
        for s_ in range(NS):
            pbs = [B.ps(), B.ps()]
            for half in range(2):
                for i in range(22):
                    B.mm(pbs[half].a, actb.a[:, i, s_ * 128:(s_ + 1) * 128], wdn.a[:, i, half * 512:(half + 1) * 512],
                         i == 0, i == 21, [actb, wdn], [pbs[half]])
            for half in range(2):
                B.act(scrF.a[:, 0:512], pbs[half].a, AF.Square, [pbs[half]], [scrF, st], accum_out=st.a[:, half:half + 1])
            B.tt("dve", st.a[:, 2:3], st.a[:, 0:1], st.a[:, 1:2], ALU.add, [st], [st])
            B.act(st.a[:, 2:3], st.a[:, 2:3], AF.Ln, [st, epsc], [st], scale=1.0 / D, bias=epsc.a[:, 0:1])
            B.act(st.a[:, 3:4], st.a[:, 2:3], AF.Exp, [st], [st], scale=-0.5)
            for half in range(2):
                hsl = slice(half * 512, (half + 1) * 512)
                B.stt(scrF.a[:, hsl], pbs[half].a, st.a[:, 3:4], g2.a[:, hsl], ALU.mult, ALU.mult, [pbs[half], st, g2], [scrF])
            B.tt("pool", xt.a[:, s_, :], xt.a[:, s_, :], scrF.a, ALU.add, [xt, scrF], [xt])
        S.dma("sp", out.a[ti * TB:(ti + 1) * TB, :].rearrange("(s p) d -> p s d", p=128), xt.a, reads=[xt], writes=[out], dreg=xt)

    S.rec = []
    for ti in range(ntiles):
        tile(ti)
    items = S.rec
    S.rec = None
    if os.environ.get("K_P2SCHED", "0") == "1":
        S.schedule_emit(items, window=int(os.environ.get("K_WIN2", "1500")))
    else:
        for it in items:
            S.replay(it)
    B.pop()


_CACHE = {}


def kernel(**inputs):
    if "B" not in _CACHE:
        _CACHE["B"] = build()
    Bd = _CACHE["B"]
    x = np.ascontiguousarray(inputs["x"], dtype=np.float32)
    wmap = {k: np.ascontiguousarray(np.asarray(inputs[k], dtype=np.float32).reshape(shp)) for k, shp in Bd.Wshapes.items()}
    in_maps = []
    for c in range(8):
        m = dict(wmap)
        m["x"] = x[c]
        in_maps.append(m)
    res = run_bass_kernel_spmd(Bd.nc, in_maps, core_ids=list(range(8)))
    return np.stack([np.asarray(res.results[c]["out"], dtype=np.float32) for c in range(8)], axis=0)
```

```python
import contextlib
import math
import os
import numpy as np
import concourse.bass as bass
import concourse.mybir as mybir
from concourse.bass_utils import run_bass_kernel_spmd

F32 = mybir.dt.float32
BF16 = mybir.dt.bfloat16
I32 = mybir.dt.int32
ALU = mybir.AluOpType
AF = mybir.ActivationFunctionType

L = 4096
D = 1024
NR = 1792
NS5 = 512
NIN = 4352
FF = 2816
PI = math.pi


class Reg:
    __slots__ = ("name", "w", "r", "dsem", "dcnt", "last", "excl")

    def __init__(self, name=""):
        self.name = name
        self.w = None
        self.r = []
        self.dsem = None
        self.dcnt = 0
        self.last = 0
        self.excl = False


class Sched:
    def __init__(self, nc):
        self.nc = nc
        self.eng = {"pe": nc.tensor, "dve": nc.vector, "act": nc.scalar,
                    "pool": nc.gpsimd, "sp": nc.sync}
        self.sem = {k: nc.alloc_semaphore(name="sem_" + k) for k in self.eng}
        self.cnt = {k: 0 for k in self.eng}
        self.seen = {k: {} for k in self.eng}
        self.ninst = 0
        self.nds = 0
        self.dregs = []
        self.dmap = {}
        self.maxops = int(os.environ.get("K_MAXOPS", "100000000"))
        self.rec = None

    def _wait(self, e, tok):
        sem, val = tok
        key = sem.name
        if key in self.dmap:
            val = max(val, self.dmap[key].dcnt)
        if self.seen[e].get(key, 0) >= val:
            return
        if e == "pe" and sem is self.sem["pe"]:
            return
        self.eng[e].wait_ge(sem, val)
        self.seen[e][key] = val

    def _deps(self, e, reads, writes, skip=None):
        for r in reads:
            if r.w is not None:
                self._wait(e, r.w)
        for w in writes:
            if w.w is not None and w.w[0] is not skip:
                self._wait(e, w.w)
            for t in w.r:
                self._wait(e, t)

    def _commit(self, tok, reads, writes):
        for r in reads:
            r.last = self.ninst
            r.r.append(tok)
            if len(r.r) > 16:
                d = {}
                for s, v in r.r:
                    if d.get(s.name, (None, -1))[1] < v:
                        d[s.name] = (s, v)
                r.r = list(d.values())
        for w in writes:
            w.last = self.ninst
            w.w = tok
            w.r = []

    def op(self, e, fn, reads=(), writes=(), cost=None):
        if self.rec is not None:
            self.rec.append(("op", e, fn, list(reads), list(writes), None, cost))
            return None
        if self.ninst >= self.maxops:
            return None
        reads = [x.r if isinstance(x, Buf) else x for x in reads]
        writes = [x.r if isinstance(x, Buf) else x for x in writes]
        ex = [x for x in reads if x.excl and x not in writes]
        if ex:
            reads = [x for x in reads if not x.excl]
            writes = list(writes) + ex
        self._deps(e, reads, writes)
        ins = fn(self.eng[e])
        self.cnt[e] += 1
        ins.then_inc(self.sem[e], 1)
        tok = (self.sem[e], self.cnt[e])
        self._commit(tok, reads, writes)
        self.ninst += 1
        return tok

    def dma(self, e, out, in_, reads=(), writes=(), dreg=None, **kw):
        if self.rec is not None:
            self.rec.append(("dma", e, (out, in_), list(reads), list(writes), (dreg, kw), None))
            return None
        if self.ninst >= self.maxops and not kw.pop("force", False):
            return None
        kw.pop("force", None)
        reads = [x.r if isinstance(x, Buf) else x for x in reads]
        writes = [x.r if isinstance(x, Buf) else x for x in writes]
        if dreg is None:
            dreg = writes[0] if writes else reads[0]
        elif isinstance(dreg, Buf):
            dreg = dreg.r
        if dreg.dsem is None:
            self.nds += 1
            dreg.dsem = self.nc.alloc_semaphore(name="ds%d_%s" % (self.nds, dreg.name))
            self.dregs.append(dreg)
            self.dmap[dreg.dsem.name] = dreg
        self._deps(e, reads, writes, skip=dreg.dsem)
        ins = self.eng[e].dma_start(out=out, in_=in_, **kw)
        dreg.dcnt += 16
        ins.then_inc(dreg.dsem, 16)
        tok = (dreg.dsem, dreg.dcnt)
        self._commit(tok, reads, writes)
        self.ninst += 1
        return tok

    def replay(self, item):
        kind, e, a, reads, writes, extra = item[:6]
        if kind == "op":
            return self.op(e, a, reads, writes)
        dreg, kw = extra
        return self.dma(e, a[0], a[1], reads=reads, writes=writes, dreg=dreg, **kw)

    def schedule_emit(self, items, window=1500):
        import heapq
        n = len(items)
        norm = []
        for it in items:
            kind, e, a, reads, writes, extra, cost = it
            reads = [x.r if isinstance(x, Buf) else x for x in reads]
            writes = [x.r if isinstance(x, Buf) else x for x in writes]
            dreg = None
            if kind == "dma":
                dreg = extra[0]
                if dreg is None:
                    dreg = writes[0] if writes else reads[0]
                elif isinstance(dreg, Buf):
                    dreg = dreg.r
            ex = [x for x in reads if x.excl and x not in writes]
            if ex:
                reads = [x for x in reads if not x.excl]
                writes = list(writes) + ex
            norm.append((kind, e, a, reads, writes, extra, cost, dreg))
        lw = {}
        rd = {}
        first = [[] for _ in range(n)]
        preds = [set() for _ in range(n)]
        for i, (kind, e, a, reads, writes, extra, cost, dreg) in enumerate(norm):
            for r in reads:
                k = id(r)
                if k in lw:
                    preds[i].add(lw[k])
                else:
                    first[i].append((r, "r"))
            for w in writes:
                k = id(w)
                if k in lw:
                    if not (kind == "dma" and norm[lw[k]][0] == "dma" and norm[lw[k]][7] is dreg):
                        preds[i].add(lw[k])
                else:
                    first[i].append((w, "w"))
                for j in rd.get(k, ()):
                    preds[i].add(j)
            for r in reads:
                rd.setdefault(id(r), []).append(i)
            for w in writes:
                lw[id(w)] = i
                rd[id(w)] = []
            preds[i].discard(i)
        succs = [[] for _ in range(n)]
        npred = [len(p) for p in preds]
        for i, p in enumerate(preds):
            for j in p:
                succs[j].append(i)
        def dur(it):
            kind, e, a, reads, writes, extra, cost, dreg = it
            if cost is not None:
                return cost
            if kind == "op":
                return {"pe": 0.08, "dve": 0.4, "act": 0.4, "pool": 1.0, "sp": 2.0}[e]
            o = a[0]
            nbytes = 1
            for d in list(o.shape):
                nbytes *= int(d)
            nbytes *= mybir.dt.size(o.dtype)
            return 2.5 + nbytes / 140e3
        LAT = float(os.environ.get("K_LAT", "0.15"))
        free = {e: 0.0 for e in self.eng}
        fin = [0.0] * n
        ready_t = [0.0] * n
        heaps = {e: [] for e in self.eng}
        low = 0
        done = [False] * n
        avail = [False] * n
        for i in range(n):
            if npred[i] == 0:
                heapq.heappush(heaps[norm[i][1]], (0.0, i)); avail[i] = True
        order = []
        deferred = {e: [] for e in self.eng}
        while len(order) < n:
            best = None
            for e, h in heaps.items():
                while h and h[0][1] >= low + window:
                    deferred[e].append(heapq.heappop(h))
                if not h:
                    continue
                rt, i = h[0]
                st = max(rt, free[e])
                if best is None or (st, i) < (best[0], best[1]):
                    best = (st, i, e)
            if best is None:
                for e in deferred:
                    for x in deferred[e]:
                        heapq.heappush(heaps[e], x)
                    deferred[e] = []
                window *= 2
                continue
            st, i, e = best
            heapq.heappop(heaps[e])
            order.append(i)
            done[i] = True
            f = st + dur(norm[i])
            fin[i] = f
            free[e] = f
            for j in succs[i]:
                npred[j] -= 1
                ready_t[j] = max(ready_t[j], f + LAT)
                if npred[j] == 0:
                    heapq.heappush(heaps[norm[j][1]], (ready_t[j], j)); avail[j] = True
            if i == low:
                while low < n and done[low]:
                    low += 1
                for e2 in deferred:
                    keep = []
                    for x in deferred[e2]:
                        if x[1] < low + window:
                            heapq.heappush(heaps[e2], x)
                        else:
                            keep.append(x)
                    deferred[e2] = keep
        self.sched_makespan = max(fin) if n else 0.0
        toks = [None] * n
        for i in order:
            kind, e, a, reads, writes, extra, cost, dreg = norm[i]
            for (reg, mode) in first[i]:
                if reg.w is not None and not (kind == "dma" and mode == "w" and reg.w[0] is (dreg.dsem if dreg is not None else None)):
                    self._wait(e, reg.w)
                if mode == "w":
                    for t in reg.r:
                        self._wait(e, t)
            for j in preds[i]:
                self._wait(e, toks[j])
            if kind == "op":
                ins = a(self.eng[e])
                self.cnt[e] += 1
                ins.then_inc(self.sem[e], 1)
                toks[i] = (self.sem[e], self.cnt[e])
            else:
                dg, kw = extra
                kw = dict(kw); kw.pop("force", None)
                if dreg.dsem is None:
                    self.nds += 1
                    dreg.dsem = self.nc.alloc_semaphore(name="ds%d_%s" % (self.nds, dreg.name))
                    self.dregs.append(dreg)
                    self.dmap[dreg.dsem.name] = dreg
                ins = self.eng[e].dma_start(out=a[0], in_=a[1], **kw)
                dreg.dcnt += 16
                ins.then_inc(dreg.dsem, 16)
                toks[i] = (dreg.dsem, dreg.dcnt)
            self.ninst += 1
        touched = {}
        for i, it in enumerate(norm):
            for r in it[3]:
                touched[id(r)] = r
            for w in it[4]:
                touched[id(w)] = w
        for k, reg in touched.items():
            if k in lw:
                reg.w = toks[lw[k]]
                reg.r = [toks[j] for j in rd.get(k, ())]
            else:
                reg.r = list(reg.r) + [toks[j] for j in rd.get(k, ())]
            reg.last = self.ninst

    def barrier(self):
        for e in self.eng:
            for f in self.eng:
                if f != e and self.cnt[f] > 0:
                    self._wait(e, (self.sem[f], self.cnt[f]))
            for d in self.dregs:
                if d.dcnt > 0:
                    self._wait(e, (d.dsem, d.dcnt))

    def final_wait(self, e, regs):
        for r in regs:
            r = r.r if isinstance(r, Buf) else r
            if r.w is not None:
                self._wait(e, r.w)
            for t in r.r:
                self._wait(e, t)


class Buf:
    def __init__(self, t, name):
        self.t = t
        self.a = t.ap()
        self.r = Reg(name)


class Builder:
    def __init__(self, dbg=None, ntiles=8):
        self.nc = bass.Bass("TRN2", target_bir_lowering=False)
        self.S = Sched(self.nc)
        self.dbg = dbg or {}
        self.ntiles = ntiles
        self.din = {}
        self.dout = {}
        self.nbuf = 0
        self.psb = None
        self.psi = 0
        self.ps_pool = None
        self.pclock = 0
        self.scopes = []

    def push(self):
        self.scopes.append(contextlib.ExitStack())

    def pop(self):
        self.S.barrier()
        self.scopes.pop().close()

    def inp(self, name, shape):
        b = Buf(self.nc.dram_tensor(name, list(shape), F32, kind="ExternalInput"), name)
        self.din[name] = b
        return b

    def outp(self, name, shape, dt=F32):
        b = Buf(self.nc.dram_tensor(name, list(shape), dt, kind="ExternalOutput"), name)
        self.dout[name] = b
        return b

    def sb(self, name, shape, dt=F32):
        self.nbuf += 1
        nm = "%s_%d" % (name, self.nbuf)
        if self.scopes:
            return Buf(self.scopes[-1].enter_context(self.nc.sbuf_tensor(nm, list(shape), dt)), name)
        return Buf(self.nc.alloc_sbuf_tensor(nm, list(shape), dt), name)

    def view(self, buf, ap):
        v = Buf.__new__(Buf)
        v.t = buf.t
        v.a = ap
        v.r = buf.r
        return v

    def init_psum(self):
        self.psb = []
        for i in range(8):
            t = self.nc.alloc_psum_tensor("psb%d" % i, [128, 512], F32)
            self.psb.append(Buf(t, "psb%d" % i))
            self.psb[-1].r.excl = True

    def ps(self):
        if self.ps_pool is not None:
            b = self.psb[self.ps_pool[self.psi % len(self.ps_pool)]]
            self.psi += 1
            return b
        b = min(self.psb, key=lambda t: t.r.last)
        self.pclock = max(self.pclock, self.S.ninst) + 1
        b.r.last = self.pclock
        return b

    def op(self, e, fn, reads=(), writes=(), cost=None):
        return self.S.op(e, fn, reads, writes, cost=cost)

    @staticmethod
    def fsz(ap):
        n = 1
        for d in list(ap.shape)[1:]:
            n *= int(d)
        return n

    def mm(self, out, lhsT, rhs, start, stop, reads, writes, **kw):
        return self.op("pe", lambda e: e.matmul(out, lhsT=lhsT, rhs=rhs, start=start, stop=stop, **kw), reads, writes,
                       cost=0.03 + max(self.fsz(rhs), 64) * 0.00052)

    def tr(self, out, in_, ident, reads, writes):
        return self.op("pe", lambda e: e.transpose(out, in_, ident), reads, writes, cost=0.1)

    def act(self, eng_out, in_, func, reads, writes, **kw):
        return self.op("act", lambda e: e.activation(out=eng_out, in_=in_, func=func, **kw), reads, writes,
                       cost=0.22 + self.fsz(in_) * 0.00075)

    def tt(self, e, out, in0, in1, op, reads, writes):
        c = (0.1 + self.fsz(in0) * 0.00115) if e == "dve" else (0.6 + self.fsz(in0) * 0.0013)
        return self.op(e, lambda g: g.tensor_tensor(out=out, in0=in0, in1=in1, op=op), reads, writes, cost=c)

    def ts(self, e, out, in0, s1, op0, reads, writes, s2=None, op1=None):
        c = (0.1 + self.fsz(in0) * 0.0008) if e == "dve" else (0.6 + self.fsz(in0) * 0.0013)
        if op1 is None:
            return self.op(e, lambda g: g.tensor_scalar(out=out, in0=in0, scalar1=s1, scalar2=None, op0=op0), reads, writes, cost=c)
        return self.op(e, lambda g: g.tensor_scalar(out=out, in0=in0, scalar1=s1, scalar2=s2, op0=op0, op1=op1), reads, writes, cost=c)

    def stt(self, out, in0, scalar, in1, op0, op1, reads, writes):
        return self.op("dve", lambda g: g.scalar_tensor_tensor(out=out, in0=in0, scalar=scalar, in1=in1, op0=op0, op1=op1), reads, writes,
                       cost=0.1 + self.fsz(in0) * 0.00115)

    def cp(self, e, out, in_, reads, writes):
        if e == "act":
            return self.op("act", lambda g: g.activation(out=out, in_=in_, func=AF.Copy), reads, writes, cost=0.22 + self.fsz(in_) * 0.00075)
        c = (0.1 + self.fsz(in_) * 0.0008) if e == "dve" else (0.6 + self.fsz(in_) * 0.0013)
        return self.op(e, lambda g: g.tensor_copy(out, in_), reads, writes, cost=c)

    def memset(self, e, out, val, writes):
        return self.op(e, lambda g: g.memset(out, val), (), writes)

    def cmul(self, e, o_re, o_im, a_re, a_im, b_re, b_im, t1, t2, reads, writes, tmp):
        R = list(reads)
        self.tt(e, t1, a_re, b_re, ALU.mult, R, [tmp])
        self.tt(e, t2, a_im, b_im, ALU.mult, R, [tmp])
        self.tt(e, o_re, t1, t2, ALU.subtract, [tmp], writes)
        self.tt(e, t1, a_re, b_im, ALU.mult, R, [tmp])
        self.tt(e, t2, a_im, b_re, ALU.mult, R, [tmp])
        self.tt(e, o_im, t1, t2, ALU.add, [tmp], writes)


def build(dbg=None, ntiles=8, phases=("1a", "1b", "1c", "2")):
    B = Builder(dbg, ntiles)
    nc, S = B.nc, B.S
    dbg = B.dbg

    x = B.inp("x", [L, D])
    shapes = {
        "norm_mix_pre": [D], "norm_mix_post": [D], "norm_ffn_pre": [D], "norm_ffn_post": [D],
        "w_in": [D, NIN], "b_gate": [2048], "rwkv_shift_mu": [NR], "rwkv_w0": [512],
        "rwkv_w2": [64, 512], "rwkv_a0": [512], "rwkv_a2": [64, 512], "rwkv_g2": [128, 512],
        "rwkv_k_k": [512], "rwkv_k_a": [512], "rwkv_r_k": [512], "rwkv_lnx_w": [512],
        "rwkv_lnx_b": [512], "s5_a_re": [32, 64], "s5_a_im": [32, 64], "s5_b_re": [32, 64, 16],
        "s5_b_im": [32, 64, 16], "s5_c_re": [32, 16, 64], "s5_c_im": [32, 16, 64], "s5_d": [512],
        "s5_log_step": [32], "s5_w_glu": [512, 512], "s5_b_glu": [512], "w_branch_rwkv": [512, D],
        "w_branch_s5": [512, D], "w_out": [D, D], "ffn_w_up": [D, 2 * FF], "ffn_conv_w": [3, 2 * FF],
        "ffn_conv_b": [2 * FF], "ffn_w_down": [FF, D],
    }
    W = {k: B.inp(k, v) for k, v in shapes.items()}
    out = B.outp("out", [L, D])
    dbo = {k: B.outp("dbg_" + k, shp, dt) for k, (shp, dt) in dbg.items()}

    B.init_psum()

    B.push()
    ident_f = B.sb("ident_f", [128, 128], F32)
    ident_b = B.sb("ident_b", [128, 128], BF16)
    B.memset("pool", ident_f.a, 1.0, [ident_f])
    B.op("pool", lambda e: e.affine_select(out=ident_f.a, in_=ident_f.a, pattern=[[-1, 128]],
                                           compare_op=ALU.is_equal, fill=0.0, base=0, channel_multiplier=1),
         [ident_f], [ident_f])
    B.cp("pool", ident_b.a, ident_f.a, [ident_f], [ident_b])
    epsc = B.sb("epsc", [128, 4])
    B.memset("pool", epsc.a[:, 0:1], 1e-6, [epsc]); B.memset("pool", epsc.a[:, 1:2], 1e-24, [epsc])
    B.memset("pool", epsc.a[:, 2:3], 64e-5, [epsc]); B.memset("pool", epsc.a[:, 3:4], 0.0, [epsc])

    def bc_load(name, n, q="sp"):
        t = B.sb(name + "_bc", [128, n], F32)
        S.dma(q, t.a, W[name].a.partition_broadcast(128), reads=[W[name]], writes=[t])
        return t

    def col_load(name, nt, q="sp"):
        t = B.sb(name + "_col", [128, nt], F32)
        S.dma(q, t.a, W[name].a.rearrange("(t p) -> p t", p=128), reads=[W[name]], writes=[t],
              allow_slow_non_contiguous=True)
        return t

    def frontend(src, ti, ntok_tiles, gbc, xt, hb, hT, scr, st, load=True):
        nt = ntok_tiles
        T = 128 * nt
        if load:
            S.dma("sp", xt.a, src.a[ti * T:(ti + 1) * T, :].rearrange("(s p) d -> p s d", p=128),
                  reads=[src], writes=[xt])
        for s in range(nt):
            B.act(scr.a, xt.a[:, s, :], AF.Square, [xt], [scr, st], accum_out=st.a[:, s:s + 1])
        B.act(st.a[:, nt:2 * nt], st.a[:, 0:nt], AF.Ln, [st, epsc], [st], scale=1.0 / D, bias=epsc.a[:, 0:1])
        B.act(st.a[:, 2 * nt:3 * nt], st.a[:, nt:2 * nt], AF.Exp, [st], [st], scale=-0.5)
        for s in range(nt):
            B.stt(hb.a[:, s, :], xt.a[:, s, :], st.a[:, 2 * nt + s:2 * nt + s + 1], gbc.a, ALU.mult, ALU.mult,
                  [xt, st, gbc], [hb])
        for c in range(8):
            pb = B.ps()
            pv = pb.a.bitcast(BF16)
            for s in range(nt):
                B.tr(pv[:, s * 128:(s + 1) * 128], hb.a[:, s, c * 128:(c + 1) * 128], ident_b.a,
                     [hb, ident_b], [pb])
            B.cp("act" if c % 2 == 0 else "dve", hT.a[:, c, :], pv[:, 0:T], [pb], [hT])

    T1 = 512
    YGLU = Buf(nc.dram_tensor("yglu_d", [128, 4, L], BF16, kind="Internal"), "yglu_d")

    gpre = bc_load("norm_mix_pre", D)
    x1s = Buf(nc.dram_tensor("x1s", [L, D], F32, kind="Internal"), "x1s")
    YR = Buf(nc.dram_tensor("yr_d", [128, 4, L], BF16, kind="Internal"), "yr_d")
    H2T = Buf(nc.dram_tensor("h2t_d", [128, 8, L], BF16, kind="Internal"), "h2t_d")
    if "1a" in phases:
        B.push()
        ws5 = B.sb("ws5", [128, 8, 512], BF16)
        S.dma("pool", ws5.a, W["w_in"].a.rearrange("(c p) n -> p c n", p=128)[:, :, NR:NR + 512],
              reads=[W["w_in"]], writes=[ws5])
        wglu = B.sb("wglu", [128, 4, 512], BF16)
        S.dma("pool", wglu.a, W["s5_w_glu"].a.rearrange("(c p) n -> p c n", p=128), reads=[W["s5_w_glu"]], writes=[wglu])
        bglu = col_load("s5_b_glu", 4)
        dcol = col_load("s5_d", 4)

        MS = B.sb("ms", [128, 16, 2, 2])
        ER = B.sb("ER", [128, 16, 64]); EI = B.sb("EI", [128, 16, 64]); R8 = B.sb("R8", [128, 16])
        Wt = B.sb("Wt", [128, 4, 8, 2, 128], BF16)
        CAW = B.sb("CAW", [128, 16, 2, 9, 64], BF16)
        mk = B.sb("mk", [128, 2])
        B.memset("pool", mk.a[:, 0:1], 0.0, [mk]); B.memset("pool", mk.a[0:32, 0:1], 1.0, [mk]); B.memset("pool", mk.a[64:96, 0:1], 1.0, [mk])
        mk4 = B.sb("mk4", [128, 4])
        B.memset("pool", mk4.a, 0.0, [mk4])
        B.memset("pool", mk4.a[0:32, 0:1], 1.0, [mk4]); B.memset("pool", mk4.a[32:64, 1:2], 1.0, [mk4])
        B.memset("pool", mk4.a[64:96, 2:3], 1.0, [mk4]); B.memset("pool", mk4.a[64:128, 3:4], 1.0, [mk4]); B.memset("pool", mk4.a[64:96, 3:4], 0.0, [mk4])
        B.memset("pool", mk.a[:, 1:2], 1.0, [mk]); B.memset("pool", mk.a[0:32, 1:2], 0.0, [mk]); B.memset("pool", mk.a[64:96, 1:2], 0.0, [mk])
        Kbd = B.sb("Kbd", [128, 4, 8, 128], BF16)
        B.push()
        are = B.sb("are", [128, 16]); aim = B.sb("aim", [128, 16]); ls = B.sb("ls", [128, 16])
        for gl in range(2):
            S.dma("sp", are.a[gl * 64:(gl + 1) * 64, :], W["s5_a_re"].a.rearrange("(q gl) n -> gl n q", gl=2)[gl],
                  reads=[W["s5_a_re"]], writes=[are], allow_slow_non_contiguous=True)
            S.dma("sp", aim.a[gl * 64:(gl + 1) * 64, :], W["s5_a_im"].a.rearrange("(q gl) n -> gl n q", gl=2)[gl],
                  reads=[W["s5_a_im"]], writes=[aim], allow_slow_non_contiguous=True)
            S.dma("sp", ls.a[gl * 64:(gl + 1) * 64, :],
                  W["s5_log_step"].a.rearrange("(q gl) -> gl q", gl=2)[gl].partition_broadcast(64),
                  reads=[W["s5_log_step"]], writes=[ls], allow_slow_non_contiguous=True)
        braw = [B.sb("braw%d" % i, [128, 16, 16]) for i in range(2)]
        for i, nm in enumerate(("s5_b_re", "s5_b_im")):
            S.dma("sp", braw[i].a, W[nm].a.rearrange("(q gl) n c -> (gl n) q c", gl=2), reads=[W[nm]], writes=[braw[i]])
        craw = [B.sb("craw%d" % i, [128, 16, 16]) for i in range(2)]
        ctmp = B.sb("ctmp", [128, 128])
        for i, nm in enumerate(("s5_c_re", "s5_c_im")):
            for blk in range(2):
                src = W[nm].a.rearrange("(b qq gl) c n -> b qq c gl n", b=2, gl=2)[blk]
                for qq in range(8):
                    S.dma("sp", ctmp.a[qq * 16:(qq + 1) * 16, :].rearrange("p (gl n) -> p gl n", gl=2),
                          src[qq], reads=[W[nm]], writes=[ctmp])
                pb = B.ps()
                B.tr(pb.a[:, 0:128], ctmp.a, ident_f.a, [ctmp, ident_f], [pb])
                B.cp("dve", craw[i].a[:, blk * 8:(blk + 1) * 8, :],
                     pb.a[:, 0:128].rearrange("p (qq c) -> p qq c", c=16), [pb], [craw[i]])

        tm = B.sb("s5tmp", [128, 12, 32])
        tmr = tm.r

        def row(i, n=16):
            return tm.a[:, i, 0:n]

        dt_ = row(0)
        B.act(dt_, ls.a, AF.Exp, [ls], [tm])
        xr = row(1)
        B.tt("dve", xr, are.a, dt_, ALU.mult, [are, tm], [tm])
        rho = row(2)
        B.ts("dve", rho, xr, 1.0 / 720, ALU.mult, [tm], [tm], s2=1.0 / 120, op1=ALU.add)
        for cf in (1.0 / 24, 1.0 / 6, 0.5, 1.0, 1.0):
            B.tt("dve", rho, rho, xr, ALU.mult, [tm], [tm])
            B.ts("dve", rho, rho, cf, ALU.add, [tm], [tm])
        th2 = tm.a[:, 3, :]
        B.tt("dve", th2[:, 0:16], aim.a, dt_, ALU.mult, [aim, tm], [tm])
        B.ts("dve", th2[:, 16:32], th2[:, 0:16], PI / 2, ALU.add, [tm], [tm])
        kf = tm.a[:, 4, :]
        B.ts("dve", kf, th2, 1.0 / (2 * PI), ALU.mult, [tm], [tm])
        ki = B.sb("ki", [128, 32], I32)
        B.cp("dve", ki.a, kf, [tm], [ki])
        B.cp("dve", kf, ki.a, [ki], [tm])
        r1 = tm.a[:, 5, :]
        B.stt(r1, kf, -2 * PI, th2, ALU.mult, ALU.add, [tm], [tm])
        B.ts("dve", kf, r1, PI, ALU.is_gt, [tm], [tm], s2=-2 * PI, op1=ALU.mult)
        B.tt("dve", r1, r1, kf, ALU.add, [tm], [tm])
        B.ts("dve", kf, r1, -PI, ALU.is_lt, [tm], [tm], s2=2 * PI, op1=ALU.mult)
        B.tt("dve", r1, r1, kf, ALU.add, [tm], [tm])
        sc = tm.a[:, 6, :]
        B.act(sc, r1, AF.Sin, [tm], [tm])
        n2 = row(7)
        B.tt("dve", kf, sc, sc, ALU.mult, [tm], [tm])
        B.tt("dve", n2, kf[:, 0:16], kf[:, 16:32], ALU.add, [tm], [tm])
        B.ts("dve", n2, n2, -0.5, ALU.mult, [tm], [tm], s2=1.5, op1=ALU.add)
        B.tt("dve", n2, n2, rho, ALU.mult, [tm], [tm])
        PW = B.sb("pw", [128, 9, 2, 16])
        B.memset("dve", PW.a[:, 0, 0, :], 1.0, [PW])
        B.memset("dve", PW.a[:, 0, 1, :], 0.0, [PW])
        B.tt("dve", PW.a[:, 1, 0, :], sc[:, 16:32], n2, ALU.mult, [tm], [PW])
        B.tt("dve", PW.a[:, 1, 1, :], sc[:, 0:16], n2, ALU.mult, [tm], [PW])
        pt = B.sb("ptmp", [128, 2, 4, 16])
        for (lo, n, s) in ((2, 1, 1), (3, 2, 2), (5, 4, 4)):
            bre = PW.a[:, s:s + 1, 0, :].broadcast_to([128, n, 16])
            bim = PW.a[:, s:s + 1, 1, :].broadcast_to([128, n, 16])
            B.cmul("dve", PW.a[:, lo:lo + n, 0, :], PW.a[:, lo:lo + n, 1, :],
                   PW.a[:, lo - s:lo - s + n, 0, :], PW.a[:, lo - s:lo - s + n, 1, :], bre, bim,
                   pt.a[:, 0, 0:n, :], pt.a[:, 1, 0:n, :], [PW], [PW], pt)
        B.cp("dve", MS.a[:, :, 0, 0], PW.a[:, 8, 0, :], [PW], [MS])
        B.cp("dve", MS.a[:, :, 1, 1], PW.a[:, 8, 0, :], [PW], [MS])
        B.cp("dve", MS.a[:, :, 1, 0], PW.a[:, 8, 1, :], [PW], [MS])
        B.ts("dve", MS.a[:, :, 0, 1], PW.a[:, 8, 1, :], -1.0, ALU.mult, [PW], [MS])
        B.tt("dve", R8.a, rho, rho, ALU.mult, [tm], [R8])
        B.tt("dve", R8.a, R8.a, R8.a, ALU.mult, [R8], [R8])
        B.tt("dve", R8.a, R8.a, R8.a, ALU.mult, [R8], [R8])
        r8i = row(8)
        B.op("dve", lambda e: e.reciprocal(r8i, R8.a), [R8], [tm])
        B.tt("dve", ER.a[:, :, 0], PW.a[:, 8, 0, :], r8i, ALU.mult, [PW, tm], [ER])
        B.tt("dve", EI.a[:, :, 0], PW.a[:, 8, 1, :], r8i, ALU.mult, [PW, tm], [EI])
        et = B.sb("etmp", [128, 2, 16, 32])
        n_ = 1
        while n_ < 64:
            bre = ER.a[:, :, n_ - 1:n_].broadcast_to([128, 16, n_]); bim = EI.a[:, :, n_ - 1:n_].broadcast_to([128, 16, n_])
            B.cmul("dve", ER.a[:, :, n_:2 * n_], EI.a[:, :, n_:2 * n_], ER.a[:, :, 0:n_], EI.a[:, :, 0:n_], bre, bim,
                   et.a[:, 0, :, 0:n_], et.a[:, 1, :, 0:n_], [ER, EI], [ER, EI], et)
            n_ *= 2
        am1 = row(8); nre = row(9); nim = row(10); den = row(11); t0 = row(4); t1 = row(5)
        B.ts("dve", am1, PW.a[:, 1, 0, :], -1.0, ALU.add, [PW], [tm])
        B.tt("dve", nre, am1, are.a, ALU.mult, [tm, are], [tm])
        B.tt("dve", t0, PW.a[:, 1, 1, :], aim.a, ALU.mult, [PW, aim], [tm])
        B.tt("dve", nre, nre, t0, ALU.add, [tm], [tm])
        B.tt("dve", nim, PW.a[:, 1, 1, :], are.a, ALU.mult, [PW, are], [tm])
        B.tt("dve", t0, am1, aim.a, ALU.mult, [tm, aim], [tm])
        B.tt("dve", nim, nim, t0, ALU.subtract, [tm], [tm])
        B.tt("dve", den, are.a, are.a, ALU.mult, [are], [tm])
        B.tt("dve", t0, aim.a, aim.a, ALU.mult, [aim], [tm])
        B.tt("dve", den, den, t0, ALU.add, [tm], [tm])
        B.op("dve", lambda e: e.reciprocal(t1, den), [tm], [tm])
        B.tt("dve", nre, nre, t1, ALU.mult, [tm], [tm])
        B.tt("dve", nim, nim, t1, ALU.mult, [tm], [tm])
        bb = [B.sb("bb%d" % i, [128, 16, 16]) for i in range(2)]
        btmp = B.sb("btmp", [128, 2, 16, 16])
        cre = nre[:, :, None].broadcast_to([128, 16, 16]); cim = nim[:, :, None].broadcast_to([128, 16, 16])
        B.cmul("dve", bb[0].a, bb[1].a, cre, cim, braw[0].a, braw[1].a, btmp.a[:, 0], btmp.a[:, 1],
               [tm, braw[0], braw[1]], [bb[0], bb[1]], btmp)
        X = [B.sb("X%d" % i, [128, 16, 32]) for i in range(2)]
        Xb = [B.sb("Xb%d" % i, [128, 16, 64], BF16) for i in range(2)]
        for i in range(2):
            B.memset("dve", X[i].a, 0.0, [X[i]])
            for gl in range(2):
                B.cp("dve", X[i].a[gl * 64:(gl + 1) * 64, :, gl * 16:(gl + 1) * 16], bb[i].a[gl * 64:(gl + 1) * 64], [bb[i]], [X[i]])
            B.memset("dve", Xb[i].a, 0.0, [Xb[i]])
            for kk in range(2):
                B.cp("act", Xb[i].a[:, kk::2, 32 * kk:32 * kk + 32], X[i].a[:, kk::2, :], [X[i]], [Xb[i]])
        B.push()
        WX = [B.sb("WX%d" % i, [128, 8, 16, 32]) for i in range(2)]
        wtmp = B.sb("wtmp", [128, 2, 8, 16, 32])
        pre = PW.a[:, 0:8, 0, :][:, :, :, None].broadcast_to([128, 8, 16, 32])
        pim = PW.a[:, 0:8, 1, :][:, :, :, None].broadcast_to([128, 8, 16, 32])
        xre = X[0].a[:, None, :, :].broadcast_to([128, 8, 16, 32])
        xim = X[1].a[:, None, :, :].broadcast_to([128, 8, 16, 32])
        B.cmul("dve", WX[0].a, WX[1].a, pre, pim, xre, xim, wtmp.a[:, 0], wtmp.a[:, 1], [PW, X[0], X[1]], [WX[0], WX[1]], wtmp)
        for tile in range(4):
            for e_ in range(8):
                pb = B.ps()
                for ri in range(2):
                    B.tr(pb.a[:, ri * 128:(ri + 1) * 128],
                         WX[ri].a[:, e_, 4 * tile:4 * tile + 4, :].rearrange("p k c -> p (k c)"), ident_f.a,
                         [WX[ri], ident_f], [pb])
                B.cp("act" if e_ % 2 else "dve", Wt.a[:, tile, e_, :, :],
                     pb.a[:, 0:256].rearrange("p (r n) -> p r n", r=2), [pb], [Wt])
        B.pop()
        B.push()
        CA = [B.sb("CA%d" % i, [128, 9, 16, 16]) for i in range(2)]
        catmp = B.sb("catmp", [128, 2, 9, 16, 16])
        pre9 = PW.a[:, :, 0, :][:, :, :, None].broadcast_to([128, 9, 16, 16])
        pim9 = PW.a[:, :, 1, :][:, :, :, None].broadcast_to([128, 9, 16, 16])
        cre9 = craw[0].a[:, None, :, :].broadcast_to([128, 9, 16, 16])
        cim9 = craw[1].a[:, None, :, :].broadcast_to([128, 9, 16, 16])
        B.cmul("dve", CA[0].a, CA[1].a, pre9, pim9, cre9, cim9, catmp.a[:, 0], catmp.a[:, 1],
               [PW, craw[0], craw[1]], [CA[0], CA[1]], catmp)
        B.memset("dve", CAW.a, 0.0, [CAW])
        for gl in range(2):
            hs = slice(gl * 64, (gl + 1) * 64)
            for kk in range(2):
                o = 32 * kk + 16 * gl
                B.cp("dve", CAW.a[hs, kk::2, 0, :, o:o + 16], CA[0].a[hs, :, kk::2, :].rearrange("p t q c -> p q t c"), [CA[0]], [CAW])
                B.ts("dve", CAW.a[hs, kk::2, 1, :, o:o + 16], CA[1].a[hs, :, kk::2, :].rearrange("p t q c -> p q t c"), -1.0, ALU.mult,
                     [CA[1]], [CAW])
        B.memset("dve", Kbd.a, 0.0, [Kbd])
        k0 = B.sb("k0", [128, 128])
        for tile in range(4):
            pb = B.ps()
            for h in range(2):
                for tau in range(8):
                    n = 0
                    for kk in range(2):
                        q = 4 * tile + 2 * h + kk
                        o = 32 * kk
                        for ri in range(2):
                            B.mm(pb.a[64 * h:64 * h + 64, tau * 32:(tau + 1) * 32], Xb[ri].a[:, q, :],
                                 CAW.a[:, q, ri, tau, o:o + 32], n == 0, n == 3, [Xb[ri], CAW], [pb])
                            n += 1
            for h in range(2):
                hs = slice(64 * h, 64 * h + 64)
                for kk in range(2):
                    cs = slice(64 * h + 32 * kk, 64 * h + 32 * kk + 32)
                    B.ts("dve", Kbd.a[hs, tile, 1:8, cs], pb.a[hs, 32:256].rearrange("p (t c) -> p t c", c=32),
                         mk.a[hs, kk:kk + 1], ALU.mult, [pb, mk], [Kbd])
            B.memset("dve", k0.a, 0.0, [k0])
            for h in range(2):
                hs = slice(64 * h, 64 * h + 64)
                for kk in range(2):
                    cs = slice(64 * h + 32 * kk, 64 * h + 32 * kk + 32)
                    B.ts("dve", k0.a[hs, cs], pb.a[hs, 0:32], mk.a[hs, kk:kk + 1], ALU.mult, [pb, mk], [k0])
            B.stt(k0.a, ident_f.a, dcol.a[:, tile:tile + 1], k0.a, ALU.mult, ALU.add, [ident_f, dcol, k0], [k0])
            B.cp("dve", Kbd.a[:, tile, 0, :], k0.a, [k0], [Kbd])
        B.pop()
        B.pop()
        for nm_, b_ in (("Wt", Wt), ("CAW", CAW), ("Kbd", Kbd), ("MS", MS)):
            if nm_ in dbo:
                S.dma("sp", dbo[nm_].a, b_.a, reads=[b_], writes=[dbo[nm_]], dreg=b_)
        NM = T1 // 8
        xt = B.sb("xt", [128, 4, D]); hT = B.sb("hT", [128, 8, T1], BF16); st = B.sb("st", [128, 12])
        hb = B.sb("hb", [128, 4, D], BF16)
        Sfin = [B.sb("Sfin%d" % h, [128, 8, 2]) for h in range(2)]
        for h in range(2):
            B.memset("pool", Sfin[h].a, 0.0, [Sfin[h]])

        class BS:
            pass
        hv = []
        for h in range(2):
            q = BS()
            q.u = B.sb("u%d" % h, [128, 2, 8, NM], BF16)
            q.um = B.sb("um%d" % h, [128, 4, 2, 8, NM], BF16)
            q.Dm = B.sb("Dm%d" % h, [128, 8, 2, NM]); q.St = B.sb("St%d" % h, [128, 8, 2, NM + 1])
            q.Sb = B.sb("Sb%d" % h, [128, 8, 2, NM], BF16)
            q.rt1 = B.sb("rt1%d" % h, [128, 8, NM]); q.rt2 = B.sb("rt2%d" % h, [128, 8, NM])
            q.Dr = B.sb("Dr%d" % h, [128, 8, 2, NM]); q.Qs = B.sb("Qs%d" % h, [128, 8, 2, NM])
            q.y2 = B.sb("y2%d" % h, [128, 2, T1]); q.yg = B.sb("yg%d" % h, [128, 2, T1])
            q.yy = B.view(q.Dm, q.Dm.a.rearrange("p a b c -> p (a b c)").rearrange("p (t n) -> p t n", t=2))
            q.ygb = B.sb("ygb%d" % h, [128, 2, T1], BF16)
            hv.append(q)
        sgb = B.sb("sgb", [128, T1]); ygl = B.sb("ygl", [128, 4, T1], BF16)
        scr = B.view(hb, hb.a.rearrange("p a b -> p (a b)")[:, 0:D])

        def tile1a(ti):
            frontend(x, ti, 4, gpre, xt, hb, hT, scr, st)
            for cb in range(4):
                q = hv[cb // 2]; tl = cb % 2
                pb = B.ps()
                for c in range(8):
                    B.mm(pb.a, ws5.a[:, c, cb * 128:(cb + 1) * 128], hT.a[:, c, :], c == 0, c == 7, [ws5, hT], [pb])
                pperm = pb.a.rearrange("p (m t) -> p t m", t=8)
                B.cp("act", q.u.a[:, tl, :, :], pperm, [pb], [q.u])
                for k in range(4):
                    B.act(q.um.a[:, k, tl, :, :], pperm, AF.Copy, [pb, mk4], [q.um], scale=mk4.a[:, k:k + 1])
            for h in range(2):
                q = hv[h]
                u, um, Dm, St, Sb, rt1, rt2, Dr, Qs, y2, yg, yy, ygb = (q.u, q.um, q.Dm, q.St, q.Sb, q.rt1, q.rt2, q.Dr, q.Qs,
                                                                        q.y2, q.yg, q.yy, q.ygb)
                sf = Sfin[h]
                for qb in range(2):
                    pb = B.ps()
                    for qq in range(4):
                        ql = 4 * qb + qq
                        tl, k = ql // 4, ql % 4
                        gt = 2 * h + tl
                        for ri in range(2):
                            col = (qq * 2 + ri) * NM
                            for j0 in range(8):
                                B.mm(pb.a[:, col:col + NM], Wt.a[:, gt, 7 - j0, ri, :], um.a[:, k, tl, j0, :],
                                     j0 == 0, j0 == 7, [Wt, um], [pb])
                    B.cp("dve", Dm.a[:, 4 * qb:4 * qb + 4, :, :],
                         pb.a.rearrange("p (q r m) -> p q r m", q=4, r=2), [pb], [Dm])
                erb = ER.a[:, 8 * h:8 * h + 8, :]; eib = EI.a[:, 8 * h:8 * h + 8, :]
                B.cp("pool", St.a[:, :, :, 0], sf.a, [sf], [St])
                B.tt("dve", rt1.a, erb, Dm.a[:, :, 0, :], ALU.mult, [ER, Dm], [rt1])
                B.tt("dve", rt2.a, eib, Dm.a[:, :, 1, :], ALU.mult, [EI, Dm], [rt2])
                B.tt("dve", Dr.a[:, :, 0, :], rt1.a, rt2.a, ALU.add, [rt1, rt2], [Dr])
                B.tt("dve", rt1.a, erb, Dm.a[:, :, 1, :], ALU.mult, [ER, Dm], [rt1])
                B.tt("dve", rt2.a, eib, Dm.a[:, :, 0, :], ALU.mult, [EI, Dm], [rt2])
                B.tt("dve", Dr.a[:, :, 1, :], rt1.a, rt2.a, ALU.subtract, [rt1, rt2], [Dr])
                for ql in range(8):
                    qi = 8 * h + ql
                    for ri in range(2):
                        B.op("dve", lambda e, ql=ql, qi=qi, ri=ri, Qs=Qs, Dr=Dr, sf=sf: e.tensor_tensor_scan(
                            out=Qs.a[:, ql, ri, :], data0=R8.a[:, qi:qi + 1].broadcast_to([128, NM]), data1=Dr.a[:, ql, ri, :],
                            initial=sf.a[:, ql, ri:ri + 1], op0=ALU.mult, op1=ALU.add), [R8, Dr, sf], [Qs], cost=0.4)
                B.tt("dve", rt1.a, erb, Qs.a[:, :, 0, :], ALU.mult, [ER, Qs], [rt1])
                B.tt("dve", rt2.a, eib, Qs.a[:, :, 1, :], ALU.mult, [EI, Qs], [rt2])
                B.tt("dve", St.a[:, :, 0, 1:NM + 1], rt1.a, rt2.a, ALU.subtract, [rt1, rt2], [St])
                B.tt("dve", rt1.a, erb, Qs.a[:, :, 1, :], ALU.mult, [ER, Qs], [rt1])
                B.tt("dve", rt2.a, eib, Qs.a[:, :, 0, :], ALU.mult, [EI, Qs], [rt2])
                B.tt("dve", St.a[:, :, 1, 1:NM + 1], rt1.a, rt2.a, ALU.add, [rt1, rt2], [St])
                B.cp("act", Sb.a, St.a[:, :, :, 0:NM], [St], [Sb])
                B.cp("act", sf.a, St.a[:, :, :, NM], [St], [sf])
                for tl in range(2):
                    gt = 2 * h + tl
                    pb = B.ps()
                    for hh in range(2):
                        hs = slice(64 * hh, 64 * hh + 64)
                        for t0_ in range(8):
                            n = 0
                            for kk in range(2):
                                ql = 4 * tl + 2 * hh + kk
                                qi = 8 * h + ql
                                for ri in range(2):
                                    B.mm(pb.a[hs, t0_ * NM:(t0_ + 1) * NM], CAW.a[:, qi, ri, t0_ + 1, :], Sb.a[:, ql, ri, :], n == 0, n == 3,
                                         [CAW, Sb], [pb], skip_group_check=True)
                                    n += 1
                    B.cp("act", y2.a[:, tl, :], pb.a, [pb], [y2])
                    pb = B.ps()
                    for t0o in range(8):
                        for tau in range(t0o + 1):
                            B.mm(pb.a[:, t0o * NM:(t0o + 1) * NM], Kbd.a[:, gt, tau, :], u.a[:, tl, t0o - tau, :],
                                 tau == 0, tau == t0o, [Kbd, u], [pb])
                    B.tt("dve", yy.a[:, tl, :].rearrange("p (m t) -> p t m", t=8), pb.a.rearrange("p (t m) -> p t m", t=8),
                         y2.a[:, tl, :].rearrange("p (t m) -> p t m", t=8), ALU.add, [pb, y2], [yy])
                if "s5y" in dbo:
                    S.dma("sp", dbo["s5y"].a[:, 2 * h:2 * h + 2, ti * T1:(ti + 1) * T1], yy.a, reads=[yy], writes=[dbo["s5y"]], dreg=yy)
                B.act(yg.a, yy.a, AF.Gelu_apprx_tanh, [yy], [yg])
                B.cp("act", ygb.a, yg.a, [yg], [ygb])
            for cb in range(4):
                pb = B.ps()
                for c in range(4):
                    B.mm(pb.a, wglu.a[:, c, cb * 128:(cb + 1) * 128], hv[c // 2].ygb.a[:, c % 2, :], c == 0, c == 3,
                         [wglu, hv[0].ygb, hv[1].ygb], [pb])
                B.act(sgb.a, pb.a, AF.Sigmoid, [pb, bglu], [sgb], bias=bglu.a[:, cb:cb + 1])
                B.tt("dve", ygl.a[:, cb, :], hv[cb // 2].yg.a[:, cb % 2, :], sgb.a, ALU.mult, [hv[cb // 2].yg, sgb], [ygl])
            S.dma("sp", YGLU.a[:, :, ti * T1:(ti + 1) * T1], ygl.a, reads=[ygl], writes=[YGLU], dreg=ygl)
            if "yglu" in dbo:
                S.dma("sp", dbo["yglu"].a[:, :, ti * T1:(ti + 1) * T1], ygl.a, reads=[ygl], writes=[dbo["yglu"]], dreg=ygl)

        S.rec = []
        for ti in range(ntiles):
            tile1a(ti)
        items = S.rec
        S.rec = None
        if os.environ.get("K_P1ASCHED", "1") == "1":
            S.schedule_emit(items, window=int(os.environ.get("K_WIN1A", "2500")))
        else:
            for it in items:
                S.replay(it)
        B.pop()

    if "1b" in phases:
        phase_1b(B, W, x, YR, epsc, gpre, ident_f, ident_b, frontend, bc_load, col_load, dbo, ntiles * (T1 // 128))

    if "1c" in phases:
        phase_1c(B, W, x, x1s, H2T, YR, YGLU, epsc, gpre, frontend, bc_load, col_load, dbo, ntiles)

    B.pop()
    if "2" in phases:
        phase_2(B, W, x1s, H2T, out, epsc, frontend, bc_load, dbo, ntiles)

    S.final_wait("sp", list(B.dout.values()))
    B.Wshapes = shapes
    return B


def phase_1b(B, W, x, YR, epsc, gpre, ident_f, ident_b, frontend, bc_load, col_load, dbo, ntiles):
    nc, S = B.nc, B.S
    TB = 128
    C = 128
    NW = 3
    C0 = math.exp(-0.5)
    B.push()
    wrg = B.sb("wrg", [128, 8, NR], BF16)
    win = W["w_in"].a.rearrange("(c p) n -> p c n", p=128)
    for c in range(8):
        S.dma("pool", wrg.a[:, c, :], win[:, c, 0:NR], reads=[W["w_in"]], writes=[wrg])
    w2p = B.sb("w2p", [128, 512], BF16); a2p = B.sb("a2p", [128, 512], BF16); g2b = B.sb("g2b", [128, 512], BF16)
    B.memset("pool", w2p.a, 0.0, [w2p]); B.memset("pool", a2p.a, 0.0, [a2p])
    S.dma("pool", w2p.a[0:64, :], W["rwkv_w2"].a, reads=[W["rwkv_w2"]], writes=[w2p])
    S.dma("pool", a2p.a[64:128, :], W["rwkv_a2"].a, reads=[W["rwkv_a2"]], writes=[a2p])
    S.dma("pool", g2b.a, W["rwkv_g2"].a, reads=[W["rwkv_g2"]], writes=[g2b])
    mu = col_load("rwkv_shift_mu", 14); w0c = col_load("rwkv_w0", 4); a0c = col_load("rwkv_a0", 4)
    kkc = col_load("rwkv_k_k", 4); kac = col_load("rwkv_k_a", 4); rkc = col_load("rwkv_r_k", 4)
    lwc = col_load("rwkv_lnx_w", 4); lbc = col_load("rwkv_lnx_b", 4)
    omu = B.sb("omu", [128, 14]); oka = B.sb("oka", [128, 4])
    B.ts("dve", omu.a, mu.a, -1.0, ALU.mult, [mu], [omu], s2=1.0, op1=ALU.add)
    B.ts("dve", oka.a, kac.a, -1.0, ALU.mult, [kac], [oka], s2=1.0, op1=ALU.add)
    bones = B.sb("bones", [128, 128]); bavg = B.sb("bavg", [128, 128]); ones = B.sb("ones", [128, 128])
    B.memset("pool", ones.a, 1.0, [ones])
    B.memset("pool", bones.a, 0.0, [bones])
    B.memset("pool", bones.a[0:64, 0:64], 1.0, [bones]); B.memset("pool", bones.a[64:128, 64:128], 1.0, [bones])
    B.ts("pool", bavg.a, bones.a, 1.0 / 64, ALU.mult, [bones], [bavg])
    hm = B.sb("hm", [128, 2])
    B.memset("pool", hm.a, 0.0, [hm]); B.memset("pool", hm.a[0:64, 0:1], 1.0, [hm]); B.memset("pool", hm.a[64:128, 1:2], 1.0, [hm])
    mf = B.sb("mf", [128, 128])
    MU4 = B.sb("MU4", [128, 4, 128], BF16); MI4 = B.sb("MI4", [128, 4, 128], BF16)
    ML4 = B.sb("ML4", [128, 4, 128], BF16); I4 = B.sb("I4", [128, 4, 128], BF16)
    for (mt, pat, cm, cop) in ((MU4, 1, -1, ALU.is_gt), (MI4, 1, -1, ALU.is_ge), (ML4, -1, 1, ALU.is_gt)):
        B.memset("pool", mf.a, 1.0, [mf])
        B.op("pool", lambda e, pat=pat, cm=cm, cop=cop: e.affine_select(out=mf.a, in_=mf.a, pattern=[[pat, 128]], compare_op=cop,
                                                                      fill=0.0, base=0, channel_multiplier=cm), [mf], [mf])
        for h in range(4):
            B.cp("pool", mt.a[:, h, :], mf.a, [mf], [mt])
    for h in range(4):
        B.cp("pool", I4.a[:, h, :], ident_f.a, [ident_f], [I4])
    pc = B.sb("pc", [128, 14]); B.memset("pool", pc.a, 0.0, [pc])
    H32 = B.sb("H32", [128, 4, 64]); Hb = B.sb("Hb", [128, 4, 64], BF16); Ht = B.sb("Ht", [128, 4, 64])
    B.memset("pool", H32.a, 0.0, [H32]); B.memset("pool", Hb.a, 0.0, [Hb])

    def bc4(col):
        return col.a[:, :, None].broadcast_to([128, 4, TB])

    class BS:
        pass

    sets = []
    for w in range(NW):
        b = BS()
        f4 = lambda nm: B.sb(nm + str(w), [128, 4, TB])
        h4 = lambda nm: B.sb(nm + str(w), [128, 4, TB], BF16)
        b.xt = B.sb("xt%d" % w, [128, 1, D]); b.hb = B.sb("hb%d" % w, [128, 1, D], BF16); b.hT = B.sb("hT%d" % w, [128, 8, TB], BF16)
        b.st = B.sb("st%d" % w, [128, 6])
        b.PS = B.sb("PS%d" % w, [128, 14, TB]); b.t1 = B.sb("t1%d" % w, [128, 4, TB]); b.t2 = B.sb("t2%d" % w, [128, 4, TB])
        b.scr = B.view(b.PS, b.PS.a.rearrange("p a b -> p (a b)")[:, 0:D])
        b.twa = B.sb("twa%d" % w, [128, TB], BF16); b.sgd = B.sb("sgd%d" % w, [128, TB], BF16)
        b.sgw = f4("sgw"); b.asg = f4("asg"); b.gg = f4("gg"); b.kk = f4("kk"); b.tq = f4("tq"); b.kmod = f4("kmod")
        b.cum = f4("cum"); b.eg = b.t1; b.egx = f4("egx"); b.eng = b.t2; b.bon = f4("bon")
        b.am = B.sb("am%d" % w, [128, 4, 2, TB], BF16); b.rm = B.sb("rm%d" % w, [128, 4, 2, TB], BF16)
        b.bf = h4("bf"); b.kf = h4("kf"); b.vb = h4("vb")
        b.Btok = B.sb("Btok%d" % w, [128, 512], BF16); b.Ktok = B.sb("Ktok%d" % w, [128, 512], BF16); b.Vtok = B.sb("Vtok%d" % w, [128, 512], BF16)
        for nm, src in (("Pm", b.sgw), ("Qm", b.asg), ("Rm", b.kk), ("Nak", b.kmod), ("Nrb", b.egx), ("Nrk", b.eng)):
            setattr(b, nm, B.view(src, src.a.rearrange("p a b -> p (a b)").bitcast(BF16).rearrange("p (h t) -> p h t", h=8)))
        hTf = b.hT.a.rearrange("p a b -> p (a b)")
        b.Xb = B.view(b.hT, hTf[:, 0:512]); b.Ub = B.view(b.hT, hTf[:, 512:1024])
        b.Yf = B.view(b.xt, b.xt.a.rearrange("p a b -> p (a b)")[:, 0:4 * TB].rearrange("p (j t) -> p j t", j=4))
        b.dd = B.view(b.hb, b.hb.a.rearrange("p a b -> p (a b)").bitcast(F32).rearrange("p (j t) -> p j t", j=4))
        b.yrb = h4("yrb")
        sets.append(b)

    PL = os.environ.get("K_1BPOOL", "dve")

    def chunk(ci):
        b = sets[ci % NW]
        PS, tq, cum, kk, kmod, eg, egx, eng, asg, sgw, gg, bon = b.PS, b.tq, b.cum, b.kk, b.kmod, b.eg, b.egx, b.eng, b.asg, b.sgw, b.gg, b.bon
        am, rm, Pm, Qm, Rm, Nak, Nrb, Nrk = b.am, b.rm, b.Pm, b.Qm, b.Rm, b.Nak, b.Nrb, b.Nrk
        frontend(x, ci, 1, gpre, b.xt, b.hb, b.hT, b.scr, b.st)
        yield
        for j0 in range(0, 14, 4):
            nj = min(4, 14 - j0)
            pb = B.ps()
            for jj in range(nj):
                for c in range(8):
                    B.mm(pb.a[:, jj * TB:(jj + 1) * TB], wrg.a[:, c, (j0 + jj) * 128:(j0 + jj + 1) * 128], b.hT.a[:, c, :],
                         c == 0, c == 7, [wrg, b.hT], [pb])
            pv = pb.a[:, 0:nj * TB].rearrange("p (j t) -> p j t", j=nj)
            om_b = omu.a[:, j0:j0 + nj, None].broadcast_to([128, nj, TB])
            mu_b = mu.a[:, j0:j0 + nj, None].broadcast_to([128, nj, TB - 1])
            B.tt("dve", b.t1.a[:, 0:nj, :], pv, om_b, ALU.mult, [pb, omu], [b.t1])
            B.tt("dve", b.t2.a[:, 0:nj, 1:TB], pv[:, :, 0:TB - 1], mu_b, ALU.mult, [pb, mu], [b.t2])
            B.tt("dve", b.t2.a[:, 0:nj, 0:1], pc.a[:, j0:j0 + nj, None], mu.a[:, j0:j0 + nj, None], ALU.mult, [pc, mu], [b.t2])
            B.cp("act", pc.a[:, j0:j0 + nj, None], pv[:, :, TB - 1:TB], [pb], [pc])
            B.tt(PL, PS.a[:, j0:j0 + nj, :], b.t1.a[:, 0:nj, :], b.t2.a[:, 0:nj, :], ALU.add, [b.t1, b.t2], [PS])
            yield
        if "pshift" in dbo:
            S.dma("sp", dbo["pshift"].a[:, :, ci * TB:(ci + 1) * TB], PS.a, reads=[PS], writes=[dbo["pshift"]], dreg=PS)
        r_ = PS.a[:, 0:4, :]; k_ = PS.a[:, 4:8, :]; v_ = PS.a[:, 8:12, :]
        B.act(b.twa.a[0:64, :], PS.a[0:64, 12, :], AF.Tanh, [PS], [b.twa])
        B.cp("act", b.twa.a[64:128, :], PS.a[64:128, 12, :], [PS], [b.twa])
        B.act(b.sgd.a, PS.a[:, 13, :], AF.Sigmoid, [PS], [b.sgd])
        pw_ = B.ps()
        for j in range(4):
            B.mm(pw_.a[:, j * TB:(j + 1) * TB], w2p.a[:, j * 128:(j + 1) * 128], b.twa.a, True, True, [w2p, b.twa], [pw_])
        for j in range(4):
            B.act(sgw.a[:, j, :], pw_.a[:, j * TB:(j + 1) * TB], AF.Sigmoid, [pw_, w0c], [sgw], bias=w0c.a[:, j:j + 1])
        pa_ = B.ps()
        for j in range(4):
            B.mm(pa_.a[:, j * TB:(j + 1) * TB], a2p.a[:, j * 128:(j + 1) * 128], b.twa.a, True, True, [a2p, b.twa], [pa_])
        for j in range(4):
            B.act(asg.a[:, j, :], pa_.a[:, j * TB:(j + 1) * TB], AF.Sigmoid, [pa_, a0c], [asg], bias=a0c.a[:, j:j + 1])
        pg_ = B.ps()
        for j in range(4):
            B.mm(pg_.a[:, j * TB:(j + 1) * TB], g2b.a[:, j * 128:(j + 1) * 128], b.sgd.a, True, True, [g2b, b.sgd], [pg_])
        B.cp("act", gg.a, pg_.a.rearrange("p (j t) -> p j t", j=4), [pg_], [gg])
        yield
        B.tt("dve", kk.a, k_, bc4(kkc), ALU.mult, [PS, kkc], [kk])
        B.tt(PL, tq.a, kk.a, kk.a, ALU.mult, [kk], [tq])
        pb = B.ps()
        for j in range(4):
            B.mm(pb.a[:, j * TB:(j + 1) * TB], bones.a, tq.a[:, j, :], True, True, [bones, tq], [pb])
        B.act(cum.a, pb.a.rearrange("p (j t) -> p j t", j=4), AF.Ln, [pb, epsc], [cum], bias=epsc.a[:, 1:2])
        B.act(cum.a, cum.a, AF.Exp, [cum], [cum], scale=-0.5)
        B.tt(PL, kk.a, kk.a, cum.a, ALU.mult, [kk, cum], [kk])
        yield
        B.tt("dve", tq.a, asg.a, bc4(kac), ALU.mult, [asg, kac], [tq])
        B.tt("dve", tq.a, tq.a, bc4(oka), ALU.add, [tq, oka], [tq])
        B.tt("dve", kmod.a, k_, tq.a, ALU.mult, [PS, tq], [kmod])
        B.tt(PL, tq.a, r_, kmod.a, ALU.mult, [PS, kmod], [tq])
        B.tt("dve", tq.a, tq.a, bc4(rkc), ALU.mult, [tq, rkc], [tq])
        pbon = B.ps()
        for j in range(4):
            B.mm(pbon.a[:, j * TB:(j + 1) * TB], bones.a, tq.a[:, j, :], True, True, [bones, tq], [pbon])
        B.tt("dve", bon.a, pbon.a.rearrange("p (j t) -> p j t", j=4), v_, ALU.mult, [pbon, PS], [bon])
        yield
        for j in range(4):
            B.op("dve", lambda e, j=j: e.tensor_tensor_scan(out=cum.a[:, j, :], data0=ones.a, data1=sgw.a[:, j, :], initial=0.0,
                                                            op0=ALU.mult, op1=ALU.add), [ones, sgw], [cum])
        B.act(eg.a, cum.a, AF.Exp, [cum], [eg], scale=-C0)
        B.act(eng.a, cum.a, AF.Exp, [cum], [eng], scale=C0)
        B.tt(PL, tq.a, cum.a, sgw.a, ALU.subtract, [cum, sgw], [tq])
        B.act(egx.a, tq.a, AF.Exp, [tq], [egx], scale=-C0)
        yield
        B.stt(tq.a, kk.a, -1.0, egx.a, ALU.mult, ALU.mult, [kk, egx], [tq])
        for hh in range(2):
            B.act(am.a[:, :, hh, :], tq.a, AF.Copy, [tq, hm], [am], scale=hm.a[:, hh:hh + 1])
        B.tt("dve", egx.a, r_, eg.a, ALU.mult, [PS, eg], [egx])
        for hh in range(2):
            B.act(rm.a[:, :, hh, :], egx.a, AF.Copy, [egx, hm], [rm], scale=hm.a[:, hh:hh + 1])
        B.tt(PL, tq.a, kk.a, asg.a, ALU.mult, [kk, asg], [tq])
        B.tt("dve", b.bf.a, tq.a, eng.a, ALU.mult, [tq, eng], [b.bf])
        B.tt("dve", b.kf.a, kmod.a, eng.a, ALU.mult, [kmod, eng], [b.kf])
        B.cp("act", b.vb.a, v_, [PS], [b.vb])
        yield
        cs = slice(0, C)
        for n_, (src, dst) in enumerate(((b.bf, b.Btok), (b.kf, b.Ktok), (b.vb, b.Vtok))):
            pb = B.ps()
            pv = pb.a.bitcast(BF16)
            for j in range(4):
                B.tr(pv[:, j * 128:(j + 1) * 128], src.a[:, j, cs], ident_b.a, [src, ident_b], [pb])
            B.cp("act" if n_ != 1 else "dve", dst.a, pv[:, 0:512], [pb], [dst])
        yield
        for g in range(2):
            gs = slice(4 * g, 4 * g + 4)
            for kind in range(5):
                pbk = B.ps()
                for hq in range(4):
                    h = 4 * g + hq
                    j, hh = h // 2, h % 2
                    o = slice(hq * 128, (hq + 1) * 128)
                    bfs, kfs, ams, rms = b.bf.a[:, j, cs], b.kf.a[:, j, cs], am.a[:, j, hh, cs], rm.a[:, j, hh, cs]
                    lhsT, rhs, rd = ((bfs, ams, [b.bf, am]), (ams, bfs, [b.bf, am]), (kfs, ams, [b.kf, am]),
                                     (bfs, rms, [b.bf, rm]), (kfs, rms, [b.kf, rm]))[kind]
                    B.mm(pbk.a[:, o], lhsT, rhs, True, True, rd, [pbk])
                msk, dst = ((MU4, Pm), (ML4, Qm), (MU4, Nak), (MI4, Nrb), (MI4, Nrk))[kind]
                B.tt("dve", dst.a[:, gs, :], pbk.a.rearrange("p (h t) -> p h t", h=4), msk.a, ALU.mult, [pbk, msk], [dst])
            B.tt(PL, Rm.a[:, gs, :], Pm.a[:, gs, :], I4.a, ALU.add, [Pm, I4], [Rm])
            yield
        for lvl in range(1, 7):
            for g in range(2):
                gs = slice(4 * g, 4 * g + 4)
                pq = B.ps()
                pp = B.ps() if lvl < 6 else None
                for hq in range(4):
                    h = 4 * g + hq
                    o = slice(hq * 128, (hq + 1) * 128)
                    B.mm(pq.a[:, o], Pm.a[:, h, :], Qm.a[:, h, :], True, True, [Pm, Qm], [pq])
                    if pp is not None:
                        B.mm(pp.a[:, o], Qm.a[:, h, :], Pm.a[:, h, :], True, True, [Pm, Qm], [pp])
                B.cp("act", Qm.a[:, gs, :], pq.a.rearrange("p (h t) -> p h t", h=4), [pq], [Qm])
                if pp is not None:
                    B.cp("act", Pm.a[:, gs, :], pp.a.rearrange("p (h t) -> p h t", h=4), [pp], [Pm])
                pr = B.ps()
                for hq in range(4):
                    h = 4 * g + hq
                    o = slice(hq * 128, (hq + 1) * 128)
                    B.mm(pr.a[:, o], Qm.a[:, h, :], Rm.a[:, h, :], True, True, [Qm, Rm], [pr])
                B.tt("dve", Rm.a[:, gs, :], pr.a.rearrange("p (h t) -> p h t", h=4), Rm.a[:, gs, :], ALU.add, [pr, Rm], [Rm])
                yield
        px = B.ps()
        for h in range(8):
            j, hh = h // 2, h % 2
            o = slice(h * 64, (h + 1) * 64)
            B.mm(px.a[:, o], am.a[:, j, hh, cs], Hb.a[:, j, :], True, False, [am, Hb], [px])
            B.mm(px.a[:, o], Nak.a[:, h, :], b.Vtok.a[:, o], False, True, [Nak, b.Vtok], [px])
        B.cp("act", b.Xb.a, px.a, [px], [b.Xb])
        pu = B.ps()
        for h in range(8):
            o = slice(h * 64, (h + 1) * 64)
            B.mm(pu.a[:, o], Rm.a[:, h, :], b.Xb.a[:, o], True, True, [Rm, b.Xb], [pu])
        B.cp("act", b.Ub.a, pu.a, [pu], [b.Ub])
        ph = B.ps()
        for h in range(8):
            j, hh = h // 2, h % 2
            o = slice(h * 64, (h + 1) * 64)
            ho = ph.a[hh * 64:(hh + 1) * 64, j * 64:(j + 1) * 64]
            B.mm(ho, b.Btok.a[:, o], b.Ub.a[:, o], True, False, [b.Btok, b.Ub], [ph])
            B.mm(ho, b.Ktok.a[:, o], b.Vtok.a[:, o], False, True, [b.Ktok, b.Vtok], [ph])
        py = B.ps()
        for h in range(8):
            j, hh = h // 2, h % 2
            o = slice(h * 64, (h + 1) * 64)
            yo = py.a[hh * 64:(hh + 1) * 64, j * 128:(j + 1) * 128]
            B.mm(yo, Hb.a[:, j, :], rm.a[:, j, hh, cs], True, False, [Hb, rm], [py])
            B.mm(yo, b.Ub.a[:, o], Nrb.a[:, h, :], False, False, [b.Ub, Nrb], [py])
            B.mm(yo, b.Vtok.a[:, o], Nrk.a[:, h, :], False, True, [b.Vtok, Nrk], [py])
        B.tt("dve", Ht.a, ph.a[:, 0:256].rearrange("p (j i) -> p j i", j=4), H32.a, ALU.add, [ph, H32], [Ht])
        gC = eg.a[:, :, C - 1:C].broadcast_to([128, 4, 64])
        B.tt("dve", H32.a, Ht.a, gC, ALU.mult, [Ht, eg], [H32])
        B.cp("act", Hb.a, H32.a, [H32], [Hb])
        B.cp("act", b.Yf.a, py.a.rearrange("p (j t) -> p j t", j=4), [py], [b.Yf])
        yield
        if "wkv" in dbo:
            S.dma("sp", dbo["wkv"].a[:, :, ci * TB:(ci + 1) * TB], b.Yf.a, reads=[b.Yf], writes=[dbo["wkv"]], dreg=b.Yf)
        Yf, dd = b.Yf, b.dd
        pm_ = B.ps()
        for j in range(4):
            B.mm(pm_.a[:, j * TB:(j + 1) * TB], bavg.a, Yf.a[:, j, :], True, True, [bavg, Yf], [pm_])
        B.tt("dve", dd.a, Yf.a, pm_.a.rearrange("p (j t) -> p j t", j=4), ALU.subtract, [Yf, pm_], [dd])
        B.act(tq.a, dd.a, AF.Square, [dd], [tq])
        pv_ = B.ps()
        for j in range(4):
            B.mm(pv_.a[:, j * TB:(j + 1) * TB], bavg.a, tq.a[:, j, :], True, True, [bavg, tq], [pv_])
        B.act(cum.a, pv_.a.rearrange("p (j t) -> p j t", j=4), AF.Ln, [pv_, epsc], [cum], bias=epsc.a[:, 2:3])
        B.act(cum.a, cum.a, AF.Exp, [cum], [cum], scale=-0.5)
        yield
        B.tt(PL, dd.a, dd.a, cum.a, ALU.mult, [dd, cum], [dd])
        B.tt("dve", dd.a, dd.a, bc4(lwc), ALU.mult, [dd, lwc], [dd])
        B.tt("dve", dd.a, dd.a, bc4(lbc), ALU.add, [dd, lbc], [dd])
        B.tt(PL, dd.a, dd.a, bon.a, ALU.add, [dd, bon], [dd])
        B.tt("dve", b.yrb.a, dd.a, gg.a, ALU.mult, [dd, gg], [b.yrb])
        if "rwkv_y" in dbo:
            S.dma("sp", dbo["rwkv_y"].a[:, :, ci * TB:(ci + 1) * TB], b.yrb.a, reads=[b.yrb], writes=[dbo["rwkv_y"]], dreg=b.yrb)
        S.dma("sp", YR.a[:, :, ci * TB:(ci + 1) * TB], b.yrb.a, reads=[b.yrb], writes=[YR], dreg=b.yrb)
        yield

    pools = [[0, 1, 2], [3, 4, 5], [6, 7]] if NW == 3 else ([[0, 1, 2, 3], [4, 5, 6, 7]] if NW == 2 else [list(range(8))])
    S.rec = []
    for ci in range(ntiles):
        B.ps_pool = pools[ci % NW]
        for _ in chunk(ci):
            pass
    items = S.rec
    S.rec = None
    B.ps_pool = None
    S.schedule_emit(items, window=int(os.environ.get("K_WIN", "1400")))
    B.pop()


def phase_1c(B, W, x, x1s, H2T, YR, YGLU, epsc, gpre, frontend, bc_load, col_load, dbo, ntiles):
    nc, S = B.nc, B.S
    TB = 512
    NS = TB // 128
    B.push()
    wg = B.sb("wg", [128, 8, 2048], BF16)
    win = W["w_in"].a.rearrange("(c p) n -> p c n", p=128)
    for c in range(8):
        S.dma("pool", wg.a[:, c, :], win[:, c, NR + 512:NIN], reads=[W["w_in"]], writes=[wg])
    wbr = B.sb("wbr", [128, 4, D], BF16); wbs = B.sb("wbs", [128, 4, D], BF16); wout = B.sb("wout", [128, 8, D], BF16)
    S.dma("pool", wbr.a, W["w_branch_rwkv"].a.rearrange("(c p) n -> p c n", p=128), reads=[W["w_branch_rwkv"]], writes=[wbr])
    S.dma("pool", wbs.a, W["w_branch_s5"].a.rearrange("(c p) n -> p c n", p=128), reads=[W["w_branch_s5"]], writes=[wbs])
    for c in range(8):
        S.dma("pool", wout.a[:, c, :], W["w_out"].a[c * 128:(c + 1) * 128, :], reads=[W["w_out"]], writes=[wout])
    gpost = bc_load("norm_mix_post", D)
    gffn = bc_load("norm_ffn_pre", D)
    bgc = col_load("b_gate", 16)
    class BS:
        pass
    sets = []
    for w in range(2):
        q = BS()
        q.xt = B.sb("xt%d" % w, [128, NS, D]); q.hb = B.sb("hb%d" % w, [128, NS, D], BF16); q.hT = B.sb("hT%d" % w, [128, 8, TB], BF16)
        q.scr = B.sb("scr%d" % w, [128, D]); q.st = B.sb("st%d" % w, [128, 12])
        q.yrt = B.sb("yrt%d" % w, [128, 4, TB], BF16); q.ygt = B.sb("ygt%d" % w, [128, 4, TB], BF16)
        q.mixb = B.sb("mixb%d" % w, [128, 8, TB], BF16)
        sets.append(q)
    gA = B.sb("gA", [128, TB]); gB = B.sb("gB", [128, TB]); mt1 = B.sb("mt1", [128, TB]); mt2 = B.sb("mt2", [128, TB])

    def part_a(ti):
        q = sets[ti % 2]
        frontend(x, ti, NS, gpre, q.xt, q.hb, q.hT, q.scr, q.st)
        S.dma("sp", q.yrt.a, YR.a[:, :, ti * TB:(ti + 1) * TB], reads=[YR], writes=[q.yrt])
        S.dma("sp", q.ygt.a, YGLU.a[:, :, ti * TB:(ti + 1) * TB], reads=[YGLU], writes=[q.ygt])

    def part_b(ti):
        q = sets[ti % 2]
        xt, hb, hT, scr, st, yrt, ygt, mixb = q.xt, q.hb, q.hT, q.scr, q.st, q.yrt, q.ygt, q.mixb
        for cb in range(8):
            pa = B.ps(); pbb = B.ps(); po = B.ps(); ps_ = B.ps()
            for c in range(8):
                B.mm(pa.a, wg.a[:, c, cb * 128:(cb + 1) * 128], hT.a[:, c, :], c == 0, c == 7, [wg, hT], [pa])
            for c in range(8):
                B.mm(pbb.a, wg.a[:, c, (8 + cb) * 128:(9 + cb) * 128], hT.a[:, c, :], c == 0, c == 7, [wg, hT], [pbb])
            for j in range(4):
                B.mm(po.a, wbr.a[:, j, cb * 128:(cb + 1) * 128], yrt.a[:, j, :], j == 0, j == 3, [wbr, yrt], [po])
            for j in range(4):
                B.mm(ps_.a, wbs.a[:, j, cb * 128:(cb + 1) * 128], ygt.a[:, j, :], j == 0, j == 3, [wbs, ygt], [ps_])
            B.act(gA.a, pa.a, AF.Sigmoid, [pa, bgc], [gA], bias=bgc.a[:, cb:cb + 1])
            B.act(gB.a, pbb.a, AF.Sigmoid, [pbb, bgc], [gB], bias=bgc.a[:, 8 + cb:9 + cb])
            B.tt("dve", mt1.a, po.a, gA.a, ALU.mult, [po, gA], [mt1])
            B.tt("dve", mt2.a, ps_.a, gB.a, ALU.mult, [ps_, gB], [mt2])
            B.tt(os.environ.get("K_MIXENG", "dve"), mixb.a[:, cb, :], mt1.a, mt2.a, ALU.add, [mt1, mt2], [mixb])

    def part_c(ti):
        q = sets[ti % 2]
        xt, hb, hT, scr, st, yrt, ygt, mixb = q.xt, q.hb, q.hT, q.scr, q.st, q.yrt, q.ygt, q.mixb
        for s_ in range(NS):
            pbs = [B.ps(), B.ps()]
            for half in range(2):
                for c8 in range(8):
                    B.mm(pbs[half].a, mixb.a[:, c8, s_ * 128:(s_ + 1) * 128], wout.a[:, c8, half * 512:(half + 1) * 512],
                         c8 == 0, c8 == 7, [mixb, wout], [pbs[half]])
            for half in range(2):
                B.act(scr.a[:, 0:512], pbs[half].a, AF.Square, [pbs[half]], [scr, st], accum_out=st.a[:, half:half + 1])
            B.tt("dve", st.a[:, 2:3], st.a[:, 0:1], st.a[:, 1:2], ALU.add, [st], [st])
            B.act(st.a[:, 2:3], st.a[:, 2:3], AF.Ln, [st, epsc], [st], scale=1.0 / D, bias=epsc.a[:, 0:1])
            B.act(st.a[:, 3:4], st.a[:, 2:3], AF.Exp, [st], [st], scale=-0.5)
            for half in range(2):
                hsl = slice(half * 512, (half + 1) * 512)
                B.stt(scr.a[:, hsl], pbs[half].a, st.a[:, 3:4], gpost.a[:, hsl], ALU.mult, ALU.mult, [pbs[half], st, gpost], [scr])
            B.tt(os.environ.get("K_RESENG", "dve"), xt.a[:, s_, :], xt.a[:, s_, :], scr.a, ALU.add, [xt, scr], [xt])
        S.dma("sp", x1s.a[ti * TB:(ti + 1) * TB, :].rearrange("(s p) d -> p s d", p=128), xt.a, reads=[xt], writes=[x1s], dreg=xt)
        if "x1" in dbo:
            S.dma("sp", dbo["x1"].a[ti * TB:(ti + 1) * TB, :].rearrange("(s p) d -> p s d", p=128), xt.a, reads=[xt],
                  writes=[dbo["x1"]], dreg=xt)
        frontend(None, ti, NS, gffn, xt, hb, hT, scr, st, load=False)
        S.dma("sp", H2T.a[:, :, ti * TB:(ti + 1) * TB], hT.a, reads=[hT], writes=[H2T], dreg=hT)

    if os.environ.get("K_P1CSCHED", "0") == "1":
        S.rec = []
        for ti in range(ntiles):
            part_a(ti); part_b(ti); part_c(ti)
        items = S.rec
        S.rec = None
        S.schedule_emit(items, window=int(os.environ.get("K_WIN1C", "1500")))
    else:
        part_a(0)
        for ti in range(ntiles):
            part_b(ti)
            if ti + 1 < ntiles:
                part_a(ti + 1)
            part_c(ti)
    B.pop()


def phase_2(B, W, x1s, H2T, out, epsc, frontend, bc_load, dbo, ntiles):
    nc, S = B.nc, B.S
    TB = 512
    NS = TB // 128
    ntiles = ntiles * (512 // TB)
    B.push()
    wup = B.sb("wup", [128, 8, 2 * FF], BF16)
    wsrc = W["ffn_w_up"].a.rearrange("(c p) n -> p c n", p=128)
    for c in range(8):
        for (a, b) in ((0, 2048), (2048, 4096), (4096, 2 * FF)):
            S.dma("pool", wup.a[:, c, a:b], wsrc[:, c, a:b], reads=[W["ffn_w_up"]], writes=[wup])
    wdn = B.sb("wdn", [128, 22, D], BF16)
    for i in range(22):
        S.dma("pool", wdn.a[:, i, :], W["ffn_w_down"].a[i * 128:(i + 1) * 128, :], reads=[W["ffn_w_down"]], writes=[wdn])
    g2 = bc_load("norm_ffn_post", D)
    cw = B.sb("cw", [128, 3, 44]); cbias = B.sb("cbias", [128, 44])
    S.dma("sp", cw.a, W["ffn_conv_w"].a.rearrange("j (b p) -> p j b", p=128), reads=[W["ffn_conv_w"]], writes=[cw],
          allow_slow_non_contiguous=True)
    S.dma("sp", cbias.a, W["ffn_conv_b"].a.rearrange("(b p) -> p b", p=128), reads=[W["ffn_conv_b"]], writes=[cbias],
          allow_slow_non_contiguous=True)
    halo = B.sb("halo", [128, 44, 2]); B.memset("pool", halo.a, 0.0, [halo])
    epsc = B.sb("epsc2", [128, 1]); B.memset("pool", epsc.a, 1e-6, [epsc])
    class BS:
        pass
    sets = []
    hTs = [B.sb("hT2_%d" % w, [128, 8, TB], BF16) for w in range(2)]
    for w in range(1):
        q = BS()
        q.xt = B.sb("xt2_%d" % w, [128, NS, D])
        q.actb = B.sb("actb%d" % w, [128, 22, TB], BF16)
        q.st = B.sb("st2_%d" % w, [128, 12])
        sets.append(q)
    scrF_ = B.sb("scrF", [128, D])
    a0s_ = (B.sb("a0g", [128, TB]), B.sb("a0v", [128, TB]))
    a1s_ = ((B.sb("a1g0", [128, TB]), B.sb("a1v0", [128, TB])), (B.sb("a1g1", [128, TB]), B.sb("a1v1", [128, TB])))
    for q in sets:
        q.scrF, q.a0s, q.a1s = scrF_, a0s_, a1s_
    def tile(ti):
        q = sets[0]
        xt, actb, st, scrF, a0s, a1s = q.xt, q.actb, q.st, q.scrF, q.a0s, q.a1s
        hT = hTs[ti % 2]
        if ti == 0:
            S.dma("sp", hT.a, H2T.a[:, :, 0:TB], reads=[H2T], writes=[hT])
        if ti + 1 < ntiles:
            S.dma("sp", hTs[(ti + 1) % 2].a, H2T.a[:, :, (ti + 1) * TB:(ti + 2) * TB], reads=[H2T], writes=[hTs[(ti + 1) % 2]])
        S.dma("sp", xt.a, x1s.a[ti * TB:(ti + 1) * TB, :].rearrange("(s p) d -> p s d", p=128), reads=[x1s], writes=[xt])
        def finish(i):
            accg, accv = a1s[i % 2]
            B.act(accg.a, accg.a, AF.Gelu_apprx_tanh, [accg], [accg])
            me = os.environ.get("K_MULENG", "pool")
            me = ("dve" if i % 2 else "pool") if me == "alt" else me
            B.tt(me, actb.a[:, i, :], accg.a, accv.a, ALU.mult, [accg, accv], [actb])

        for i in range(22):
            accs = a1s[i % 2]
            for gv in range(2):
                b = i + 22 * gv
                pb = B.ps()
                for c in range(8):
                    B.mm(pb.a[:, 0:TB], wup.a[:, c, b * 128:(b + 1) * 128], hT.a[:, c, :], c == 0, c == 7, [wup, hT], [pb])
                acc = accs[gv]; a0 = a0s[gv]; a1 = accs[gv]
                B.act(a0.a, pb.a[:, 0:TB], AF.Identity, [pb, cw, cbias], [a0], scale=cw.a[:, 2, b:b + 1], bias=cbias.a[:, b:b + 1])
                B.act(a1.a[:, 1:TB], pb.a[:, 0:TB - 1], AF.Copy, [pb, cw], [a1], scale=cw.a[:, 1, b:b + 1])
                B.stt(a0.a[:, 2:TB], pb.a[:, 0:TB - 2], cw.a[:, 0, b:b + 1], a0.a[:, 2:TB], ALU.mult, ALU.add, [pb, cw, a0], [a0])
                B.ts("dve", a1.a[:, 0:1], halo.a[:, b, 1:2], cw.a[:, 1, b:b + 1], ALU.mult, [halo, cw], [a1])
                B.stt(a0.a[:, 0:2], halo.a[:, b, 0:2], cw.a[:, 0, b:b + 1], a0.a[:, 0:2], ALU.mult, ALU.add, [halo, cw, a0], [a0])
                B.cp("dve", halo.a[:, b, :], pb.a[:, TB - 2:TB], [pb], [halo])
                B.tt(os.environ.get("K_ADDENG", "dve"), acc.a, a0.a, a1.a, ALU.add, [a0, a1], [acc])
            if "zc" in dbo and ti == 0 and i == 0:
                S.dma("sp", dbo["zc"].a[:, 0:TB], accs[0].a, reads=[accs[0]], writes=[dbo["zc"]], dreg=accs[0])
            if i > 0:
                finish(i - 1)
        finish(21)
        for s_ in range(NS):
            pbs = [B.ps(), B.ps()]
            for half in range(2):
                for i in range(22):
                    B.mm(pbs[half].a, actb.a[:, i, s_ * 128:(s_ + 1) * 128], wdn.a[:, i, half * 512:(half + 1) * 512],
                         i == 0, i == 21, [actb, wdn], [pbs[half]])
            for half in range(2):
                B.act(scrF.a[:, 0:512], pbs[half].a, AF.Square, [pbs[half]], [scrF, st], accum_out=st.a[:, half:half + 1])
            B.tt("dve", st.a[:, 2:3], st.a[:, 0:1], st.a[:, 1:2], ALU.add, [st], [st])
            B.act(st.a[:, 2:3], st.a[:, 2:3], AF.Ln, [st, epsc], [st], scale=1.0 / D, bias=epsc.a[:, 0:1])
            B.act(st.a[:, 3:4], st.a[:, 2:3], AF.Exp, [st], [st], scale=-0.5)
            for half in range(2):
                hsl = slice(half * 512, (half + 1) * 512)
                B.stt(scrF.a[:, hsl], pbs[half].a, st.a[:, 3:4], g2.a[:, hsl], ALU.mult, ALU.mult, [pbs[half], st, g2], [scrF])
            B.tt(os.environ.get("K_RESENG", "dve"), xt.a[:, s_, :], xt.a[:, s_, :], scrF.a, ALU.add, [xt, scrF], [xt])
        S.dma("sp", out.a[ti * TB:(ti + 1) * TB, :].rearrange("(s p) d -> p s d", p=128), xt.a, reads=[xt], writes=[out], dreg=xt)

    S.rec = []
    for ti in range(ntiles):
        tile(ti)
    items = S.rec
    S.rec = None
    if os.environ.get("K_P2SCHED", "0") == "1":
        S.schedule_emit(items, window=int(os.environ.get("K_WIN2", "1500")))
    else:
        for it in items:
            S.replay(it)
    B.pop()


_CACHE = {}


def kernel(**inputs):
    if "B" not in _CACHE:
        _CACHE["B"] = build()
    Bd = _CACHE["B"]
    x = np.ascontiguousarray(inputs["x"], dtype=np.float32)
    wmap = {k: np.ascontiguousarray(np.asarray(inputs[k], dtype=np.float32).reshape(shp)) for k, shp in Bd.Wshapes.items()}
    in_maps = []
    for c in range(8):
        m = dict(wmap)
        m["x"] = x[c]
        in_maps.append(m)
    res = run_bass_kernel_spmd(Bd.nc, in_maps, core_ids=list(range(8)))
    return np.stack([np.asarray(res.results[c]["out"], dtype=np.float32) for c in range(8)], axis=0)
```
